# Optimizing a Trainium2 kernel written in Bass

```python
import math
import jax, jax.numpy as jnp
from jax import lax
import numpy as np

D_MODEL = 1024
BATCH = 32
SEQ = 256
DEPTH = 2
DEC_BATCH = 2
DEC_SEQ = 2048
PAST_LEN = 512

GRID_W = 64
CHUNK = 128
SHORT_CONV_W = 5
FFN_CONV_W = 3
N_MOD = 6
EPS = 1e-6

SSD_HEADS = 16
SSD_HEAD_DIM = 64
SSD_INNER = SSD_HEADS * SSD_HEAD_DIM
SSD_GROUPS = 4
SSD_STATE = 128
DN_HEADS = 8
DN_KEY_DIM = 128
DN_VAL_DIM = 128
DN_QK = DN_HEADS * DN_KEY_DIM
DN_V = DN_HEADS * DN_VAL_DIM
SG_GROUPS = 4
SG_WIDTH = 1024
SG_GROUP_DIM = SG_WIDTH // SG_GROUPS
ATTN_HEADS = 8
ATTN_KV_HEADS = 2
ATTN_GROUP = ATTN_HEADS // ATTN_KV_HEADS
ATTN_HEAD_DIM = 128
WINDOW = 128
ROPE_THETA = 10000.0
ROPE_PAIRS_AXIS = ATTN_HEAD_DIM // 4
D_FF = 2816

L0_SIZES = (SSD_INNER, SSD_INNER + 2 * SSD_GROUPS * SSD_STATE, 2 * SSD_HEADS,
            2 * DN_QK + DN_V, DN_V, 2 * DN_HEADS, 2 * DN_HEADS)
L0_PROJ = sum(L0_SIZES)
L0_MIX = SSD_INNER + DN_V
L1_SIZES = (2 * SG_WIDTH, ATTN_HEADS * ATTN_HEAD_DIM, ATTN_KV_HEADS * ATTN_HEAD_DIM, ATTN_KV_HEADS * ATTN_HEAD_DIM)
L1_PROJ = sum(L1_SIZES)
L1_MIX = SG_WIDTH + ATTN_HEADS * ATTN_HEAD_DIM

F32 = jnp.float32

kernel_name = 'hybrid_diffusion_prefix_step'


def _offsets(sizes):
    out, acc = [], 0
    for s in sizes[:-1]:
        acc += s
        out.append(acc)
    return out


def flip(t):
    return t[:, ::-1]


def rmsnorm(x, w):
    xf = x.astype(F32)
    y = xf * lax.rsqrt(jnp.mean(xf * xf, axis=-1, keepdims=True) + EPS)
    return (y * w.astype(F32)).astype(x.dtype)


def layernorm(x, w, b):
    xf = x.astype(F32)
    xc = xf - jnp.mean(xf, axis=-1, keepdims=True)
    y = xc * lax.rsqrt(jnp.mean(xc * xc, axis=-1, keepdims=True) + EPS)
    return (y * w.astype(F32) + b.astype(F32)).astype(x.dtype)


def l2norm(x):
    xf = x.astype(F32)
    return xf * lax.rsqrt(jnp.sum(xf * xf, axis=-1, keepdims=True) + EPS)


def dwconv(x, w, b=None):
    width = w.shape[0]
    pad = width // 2
    length = x.shape[1]
    xp = jnp.pad(x, ((0, 0), (pad, pad), (0, 0)))
    y = xp[:, 0:length] * w[0]
    for tap in range(1, width):
        y = y + xp[:, tap:tap + length] * w[tap]
    return y if b is None else y + b


def modulated_rmsnorm(x, w, shift, scale):
    return rmsnorm(x, w) * (1.0 + scale[:, None, :]) + shift[:, None, :]


def adaln(cond, w, b):
    return (jax.nn.silu(cond.astype(F32)) @ w + b).reshape(cond.shape[0], N_MOD, D_MODEL)


def ssd_scan(x, dt, A, Bm, Cm, h0):
    bsz, length, nh, hp = x.shape
    ns = Bm.shape[-1]
    nc = length // CHUNK
    xc = x.astype(F32).reshape(bsz, nc, CHUNK, nh, hp)
    bc = Bm.astype(F32).reshape(bsz, nc, CHUNK, nh, ns)
    cc = Cm.astype(F32).reshape(bsz, nc, CHUNK, nh, ns)
    dtc = dt.astype(F32).reshape(bsz, nc, CHUNK, nh)
    acs = jnp.cumsum(dtc * A.astype(F32), axis=2)
    tril = jnp.tril(jnp.ones((CHUNK, CHUNK), bool))[:, :, None]
    seg = acs[:, :, :, None, :] - acs[:, :, None, :, :]
    lmat = jnp.exp(jnp.where(tril, seg, -jnp.inf))
    xdt = xc * dtc[..., None]
    scores = jnp.einsum('bcihn,bcjhn->bcijh', cc, bc) * lmat
    y_diag = jnp.einsum('bcijh,bcjhp->bcihp', scores, xdt)
    decay_to_end = jnp.exp(acs[:, :, -1:, :] - acs)
    chunk_states = jnp.einsum('bcjhn,bcjh,bcjhp->bchpn', bc, decay_to_end, xdt)
    chunk_decay = jnp.exp(acs[:, :, -1, :])

    def step(h, inp):
        st, dec = inp
        return h * dec[:, :, None, None] + st, h

    h_fin, h_prev = lax.scan(step, h0.astype(F32),
                             (jnp.moveaxis(chunk_states, 1, 0), jnp.moveaxis(chunk_decay, 1, 0)))
    h_prev = jnp.moveaxis(h_prev, 0, 1)
    y_off = jnp.einsum('bcihn,bchpn,bcih->bcihp', cc, h_prev, jnp.exp(acs))
    return (y_diag + y_off).reshape(bsz, length, nh, hp), h_fin


def gated_delta_scan(q, k, v, g, beta, s0):
    bsz, length, nh, dk = q.shape
    dv = v.shape[-1]
    nc = length // CHUNK

    def to_chunks(t):
        t = t.astype(F32).reshape((bsz, nc, CHUNK) + t.shape[2:])
        return jnp.moveaxis(t, 3, 2)

    qc = to_chunks(q) * (dk ** -0.5)
    kc = to_chunks(k)
    vc = to_chunks(v)
    bc = to_chunks(beta)
    gcs = jnp.cumsum(to_chunks(g), axis=-1)
    tril = jnp.tril(jnp.ones((CHUNK, CHUNK), bool))
    strict = jnp.tril(jnp.ones((CHUNK, CHUNK), bool), -1)
    decay = jnp.exp(jnp.where(tril, gcs[..., :, None] - gcs[..., None, :], -jnp.inf))
    kbeta = kc * bc[..., None]
    a_strict = jnp.where(strict, jnp.einsum('bchid,bchjd->bchij', kbeta, kc) * decay, 0.0)
    ia = a_strict + jnp.eye(CHUNK, dtype=F32)
    u = lax.linalg.triangular_solve(ia, vc * bc[..., None], left_side=True, lower=True)
    w = lax.linalg.triangular_solve(ia, kbeta * jnp.exp(gcs)[..., None], left_side=True, lower=True)
    qk = jnp.einsum('bchid,bchjd->bchij', qc, kc) * decay
    q_dec = qc * jnp.exp(gcs)[..., None]
    k_end = kc * jnp.exp(gcs[..., -1:] - gcs)[..., None]
    last = jnp.exp(gcs[..., -1])

    def step(s, inp):
        qk_c, qd_c, w_c, u_c, ke_c, last_c = inp
        v_new = u_c - jnp.einsum('bhik,bhkv->bhiv', w_c, s)
        o_c = jnp.einsum('bhik,bhkv->bhiv', qd_c, s) + jnp.einsum('bhij,bhjv->bhiv', qk_c, v_new)
        s = s * last_c[..., None, None] + jnp.einsum('bhjk,bhjv->bhkv', ke_c, v_new)
        return s, o_c

    xs = (jnp.moveaxis(qk, 1, 0), jnp.moveaxis(q_dec, 1, 0), jnp.moveaxis(w, 1, 0),
          jnp.moveaxis(u, 1, 0), jnp.moveaxis(k_end, 1, 0), jnp.moveaxis(last, 1, 0))
    s_fin, o = lax.scan(step, s0.astype(F32), xs)
    o = jnp.moveaxis(jnp.moveaxis(o, 0, 1), 2, 3).reshape(bsz, length, nh, dv)
    return o, s_fin


def ssd_delta_mixer(h, ssd_h0, dn_h0, w_in, w_out, ssd_conv_w, ssd_conv_b, ssd_dt_bias, ssd_A_log,
                    ssd_D, ssd_norm_w, dn_conv_w, dn_dt_bias, dn_A_log, dn_norm_w):
    bsz, length, _ = h.shape
    z, xbc, dt_raw, qkv, gate, a_raw, b_raw = jnp.split(h @ w_in, _offsets(L0_SIZES), axis=-1)
    xbc = jax.nn.silu(dwconv(xbc, ssd_conv_w, ssd_conv_b))
    xs, bm, cm = jnp.split(xbc, [SSD_INNER, SSD_INNER + SSD_GROUPS * SSD_STATE], axis=-1)
    xs = xs.reshape(bsz, length, SSD_HEADS, SSD_HEAD_DIM)
    rep = SSD_HEADS // SSD_GROUPS
    bh = jnp.repeat(bm.reshape(bsz, length, SSD_GROUPS, SSD_STATE), rep, axis=2)
    ch = jnp.repeat(cm.reshape(bsz, length, SSD_GROUPS, SSD_STATE), rep, axis=2)
    dt = jax.nn.softplus(dt_raw.astype(F32).reshape(bsz, length, 2, SSD_HEADS) + ssd_dt_bias.astype(F32))
    a_ssd = -jnp.exp(ssd_A_log.astype(F32))
    y_f, h_f = ssd_scan(xs, dt[:, :, 0], a_ssd[0], bh, ch, ssd_h0[:, 0])
    y_b, h_b = ssd_scan(flip(xs), flip(dt[:, :, 1]), a_ssd[1], flip(bh), flip(ch), ssd_h0[:, 1])
    y = y_f + flip(y_b) + ssd_D.astype(F32)[:, None] * xs.astype(F32)
    y = rmsnorm(y.reshape(bsz, length, SSD_INNER) * jax.nn.silu(z.astype(F32)), ssd_norm_w)
    qkv = jax.nn.silu(dwconv(qkv, dn_conv_w))
    q, k, v = jnp.split(qkv, [DN_QK, 2 * DN_QK], axis=-1)
    q = l2norm(q.reshape(bsz, length, DN_HEADS, DN_KEY_DIM))
    k = l2norm(k.reshape(bsz, length, DN_HEADS, DN_KEY_DIM))
    v = v.reshape(bsz, length, DN_HEADS, DN_VAL_DIM)
    g = -jnp.exp(dn_A_log.astype(F32)) * jax.nn.softplus(
        a_raw.astype(F32).reshape(bsz, length, 2, DN_HEADS) + dn_dt_bias.astype(F32))
    beta = jax.nn.sigmoid(b_raw.astype(F32).reshape(bsz, length, 2, DN_HEADS))
    o_f, s_f = gated_delta_scan(q, k, v, g[:, :, 0], beta[:, :, 0], dn_h0[:, 0])
    o_b, s_b = gated_delta_scan(flip(q), flip(k), flip(v), flip(g[:, :, 1]), flip(beta[:, :, 1]), dn_h0[:, 1])
    o = rmsnorm(o_f + flip(o_b), dn_norm_w) * jax.nn.silu(gate.astype(F32)).reshape(bsz, length, DN_HEADS, DN_VAL_DIM)
    mixed = jnp.concatenate([y, o.reshape(bsz, length, DN_V)], axis=-1).astype(h.dtype)
    return mixed @ w_out, jnp.stack([h_f, h_b], axis=1), jnp.stack([s_f, s_b], axis=1)


def axial_rope_tables(length):
    rows = length // GRID_W
    row = jnp.repeat(jnp.arange(rows, dtype=F32), GRID_W)
    col = jnp.tile(jnp.arange(GRID_W, dtype=F32), rows)
    inv = ROPE_THETA ** (-jnp.arange(ROPE_PAIRS_AXIS, dtype=F32) / ROPE_PAIRS_AXIS)
    ang = jnp.concatenate([row[:, None] * inv, col[:, None] * inv], axis=-1)
    return jnp.cos(ang), jnp.sin(ang)


def apply_rope(x, cos, sin):
    half = x.shape[-1] // 2
    x1 = x[..., :half].astype(F32)
    x2 = x[..., half:].astype(F32)
    c = cos[None, :, None, :]
    s = sin[None, :, None, :]
    return jnp.concatenate([x1 * c - x2 * s, x2 * c + x1 * s], axis=-1).astype(x.dtype)


def sink_attend(q, k, v, mask, sink):
    s = jnp.einsum('bqkgd,bskd->bkgqs', q.astype(F32), k.astype(F32)) * (ATTN_HEAD_DIM ** -0.5)
    if mask is not None:
        s = jnp.where(mask, s, -jnp.inf)
    sink_col = jnp.broadcast_to(sink.astype(F32).reshape(ATTN_KV_HEADS, ATTN_GROUP)[None, :, :, None, None],
                                s.shape[:-1] + (1,))
    p = jax.nn.softmax(jnp.concatenate([s, sink_col], axis=-1), axis=-1)[..., :-1]
    o = jnp.einsum('bkgqs,bskd->bqkgd', p, v.astype(F32))
    return o.reshape(o.shape[0], o.shape[1], ATTN_HEADS * ATTN_HEAD_DIM)


def context_attention(q, k, v, sink):
    bsz, length = q.shape[:2]
    nb = length // CHUNK
    qb = jnp.swapaxes(q.reshape(bsz, nb, CHUNK, ATTN_KV_HEADS, ATTN_GROUP, ATTN_HEAD_DIM), 0, 1)
    ob = lax.map(lambda qq: sink_attend(qq, k, v, None, sink), qb)
    return jnp.swapaxes(ob, 0, 1).reshape(bsz, length, ATTN_HEADS * ATTN_HEAD_DIM)


def latent_attention(q, k, v, k_ctx, v_ctx, sink):
    bsz, length = q.shape[:2]
    nb = length // CHUNK
    band = 3 * CHUNK
    pad = ((0, 0), (CHUNK, CHUNK), (0, 0), (0, 0))
    kp = jnp.pad(k, pad)
    vp = jnp.pad(v, pad)
    k_ctx = k_ctx.astype(k.dtype)
    v_ctx = v_ctx.astype(v.dtype)
    n_ctx = k_ctx.shape[1]
    qb = jnp.swapaxes(q.reshape(bsz, nb, CHUNK, ATTN_KV_HEADS, ATTN_GROUP, ATTN_HEAD_DIM), 0, 1)

    def block(args):
        i, qq = args
        start = i * CHUNK
        kb = jnp.concatenate([lax.dynamic_slice_in_dim(kp, start, band, axis=1), k_ctx], axis=1)
        vb = jnp.concatenate([lax.dynamic_slice_in_dim(vp, start, band, axis=1), v_ctx], axis=1)
        qpos = start + jnp.arange(CHUNK)
        kpos = start - CHUNK + jnp.arange(band)
        near = ((jnp.abs(qpos[:, None] - kpos[None, :]) <= WINDOW)
                & (kpos >= 0)[None, :] & (kpos < length)[None, :])
        mask = jnp.concatenate([near, jnp.ones((CHUNK, n_ctx), bool)], axis=1)
        return sink_attend(qq, kb, vb, mask, sink)

    ob = lax.map(block, (jnp.arange(nb), qb))
    return jnp.swapaxes(ob, 0, 1).reshape(bsz, length, ATTN_HEADS * ATTN_HEAD_DIM)


def _l1_branches(h, w_in, sg_ln_w, sg_ln_b, sg_w_s, sg_b_s):
    bsz, length, _ = h.shape
    uv, q, k, v = jnp.split(h @ w_in, _offsets(L1_SIZES), axis=-1)
    u, gv = jnp.split(jax.nn.gelu(uv), 2, axis=-1)
    gv = layernorm(gv, sg_ln_w, sg_ln_b).reshape(bsz, length // CHUNK, CHUNK, SG_GROUPS, SG_GROUP_DIM)
    sv = jnp.einsum('gij,bcjgd->bcigd', sg_w_s, gv) + sg_b_s.T[:, :, None]
    mlp_out = u * sv.reshape(bsz, length, SG_WIDTH)
    q = q.reshape(bsz, length, ATTN_HEADS, ATTN_HEAD_DIM)
    k = k.reshape(bsz, length, ATTN_KV_HEADS, ATTN_HEAD_DIM)
    v = v.reshape(bsz, length, ATTN_KV_HEADS, ATTN_HEAD_DIM)
    return mlp_out, q, k, v


def l1_mixer_context(h, w_in, w_out, sg_ln_w, sg_ln_b, sg_w_s, sg_b_s, attn_sink):
    mlp_out, q, k, v = _l1_branches(h, w_in, sg_ln_w, sg_ln_b, sg_w_s, sg_b_s)
    attn_out = context_attention(q, k, v, attn_sink)
    out = jnp.concatenate([mlp_out, attn_out.astype(mlp_out.dtype)], axis=-1) @ w_out
    return out, k, v


def l1_mixer_latent(h, k_ctx, v_ctx, w_in, w_out, sg_ln_w, sg_ln_b, sg_w_s, sg_b_s, attn_sink):
    mlp_out, q, k, v = _l1_branches(h, w_in, sg_ln_w, sg_ln_b, sg_w_s, sg_b_s)
    cos, sin = axial_rope_tables(h.shape[1])
    attn_out = latent_attention(apply_rope(q, cos, sin), apply_rope(k, cos, sin), v, k_ctx, v_ctx, attn_sink)
    return jnp.concatenate([mlp_out, attn_out.astype(mlp_out.dtype)], axis=-1) @ w_out


def conv_ffn(h, w_up, conv_w, conv_b, w_down):
    a, b = jnp.split(dwconv(h @ w_up, conv_w, conv_b), 2, axis=-1)
    return (jax.nn.silu(a) * b) @ w_down


def setup_inputs(seed: int = 0) -> dict:
    key = jax.random.key(seed)
    keys = iter(jax.random.split(key, 64))

    def nrm(shape, scale=1.0):
        return scale * jax.random.normal(next(keys), shape, F32)

    def gain(n):
        return 1.0 + 0.1 * jax.random.normal(next(keys), (n,), F32)

    def dt_bias(shape):
        dt = jnp.exp(jax.random.uniform(next(keys), shape, F32, math.log(1e-3), math.log(1e-1)))
        return dt + jnp.log(-jnp.expm1(-dt))

    def a_log(shape):
        return jnp.log(jax.random.uniform(next(keys), shape, F32, 1.0, 16.0))

    inp = {}
    inp['x_prompt'] = nrm((BATCH, SEQ, D_MODEL))
    inp['x_sample'] = nrm((DEC_BATCH, DEC_SEQ, D_MODEL))
    inp['state_l0_ssd'] = nrm((DEC_BATCH, 2, SSD_HEADS, SSD_HEAD_DIM, SSD_STATE), 0.5)
    inp['state_l0_dn'] = nrm((DEC_BATCH, 2, DN_HEADS, DN_KEY_DIM, DN_VAL_DIM), 0.5)
    inp['cache_l1_k'] = nrm((DEC_BATCH, PAST_LEN, ATTN_KV_HEADS, ATTN_HEAD_DIM))
    inp['cache_l1_v'] = nrm((DEC_BATCH, PAST_LEN, ATTN_KV_HEADS, ATTN_HEAD_DIM))
    inp['c'] = nrm((DEC_BATCH, D_MODEL))
    inp['c_ctx'] = nrm((D_MODEL,))
    for l in range(DEPTH):
        inp['mod_w_l%d' % l] = nrm((D_MODEL, N_MOD * D_MODEL), 0.5 * D_MODEL ** -0.5)
        inp['mod_b_l%d' % l] = nrm((N_MOD * D_MODEL,), 0.02)
        inp['norm_mix_pre_l%d' % l] = gain(D_MODEL)
        inp['norm_mix_post_l%d' % l] = gain(D_MODEL)
        inp['norm_ffn_pre_l%d' % l] = gain(D_MODEL)
        inp['norm_ffn_post_l%d' % l] = gain(D_MODEL)
        inp['ffn_up_l%d' % l] = nrm((D_MODEL, 2 * D_FF), D_MODEL ** -0.5)
        inp['ffn_conv_w_l%d' % l] = nrm((FFN_CONV_W, 2 * D_FF), FFN_CONV_W ** -0.5)
        inp['ffn_conv_b_l%d' % l] = nrm((2 * D_FF,), 0.02)
        inp['ffn_down_l%d' % l] = nrm((D_FF, D_MODEL), D_FF ** -0.5)
    inp['mix_in_l0'] = nrm((D_MODEL, L0_PROJ), D_MODEL ** -0.5)
    inp['mix_out_l0'] = nrm((L0_MIX, D_MODEL), L0_MIX ** -0.5)
    inp['ssd_conv_w'] = nrm((SHORT_CONV_W, SSD_INNER + 2 * SSD_GROUPS * SSD_STATE), SHORT_CONV_W ** -0.5)
    inp['ssd_conv_b'] = nrm((SSD_INNER + 2 * SSD_GROUPS * SSD_STATE,), 0.02)
    inp['ssd_dt_bias'] = dt_bias((2, SSD_HEADS))
    inp['ssd_A_log'] = a_log((2, SSD_HEADS))
    inp['ssd_D'] = gain(SSD_HEADS)
    inp['ssd_norm_w'] = gain(SSD_INNER)
    inp['dn_conv_w'] = nrm((SHORT_CONV_W, 2 * DN_QK + DN_V), SHORT_CONV_W ** -0.5)
    inp['dn_dt_bias'] = dt_bias((2, DN_HEADS))
    inp['dn_A_log'] = a_log((2, DN_HEADS))
    inp['dn_norm_w'] = gain(DN_VAL_DIM)
    inp['mix_in_l1'] = nrm((D_MODEL, L1_PROJ), D_MODEL ** -0.5)
    inp['mix_out_l1'] = nrm((L1_MIX, D_MODEL), L1_MIX ** -0.5)
    inp['sg_ln_w'] = gain(SG_WIDTH)
    inp['sg_ln_b'] = nrm((SG_WIDTH,), 0.02)
    inp['sg_w_s'] = nrm((SG_GROUPS, CHUNK, CHUNK), CHUNK ** -0.5)
    inp['sg_b_s'] = 1.0 + nrm((SG_GROUPS, CHUNK), 0.1)
    inp['attn_sink'] = nrm((ATTN_HEADS,))
    return inp


def reference(x_prompt, x_sample, state_l0_ssd, state_l0_dn, cache_l1_k, cache_l1_v, c, c_ctx,
              mod_w_l0, mod_b_l0, norm_mix_pre_l0, norm_mix_post_l0, norm_ffn_pre_l0, norm_ffn_post_l0,
              ffn_up_l0, ffn_conv_w_l0, ffn_conv_b_l0, ffn_down_l0,
              mod_w_l1, mod_b_l1, norm_mix_pre_l1, norm_mix_post_l1, norm_ffn_pre_l1, norm_ffn_post_l1,
              ffn_up_l1, ffn_conv_w_l1, ffn_conv_b_l1, ffn_down_l1,
              mix_in_l0, mix_out_l0, ssd_conv_w, ssd_conv_b, ssd_dt_bias, ssd_A_log, ssd_D, ssd_norm_w,
              dn_conv_w, dn_dt_bias, dn_A_log, dn_norm_w,
              mix_in_l1, mix_out_l1, sg_ln_w, sg_ln_b, sg_w_s, sg_b_s, attn_sink):
    mod_w = (mod_w_l0, mod_w_l1)
    mod_b = (mod_b_l0, mod_b_l1)
    n_mix_pre = (norm_mix_pre_l0, norm_mix_pre_l1)
    n_mix_post = (norm_mix_post_l0, norm_mix_post_l1)
    n_ffn_pre = (norm_ffn_pre_l0, norm_ffn_pre_l1)
    n_ffn_post = (norm_ffn_post_l0, norm_ffn_post_l1)
    ffn = ((ffn_up_l0, ffn_conv_w_l0, ffn_conv_b_l0, ffn_down_l0),
           (ffn_up_l1, ffn_conv_w_l1, ffn_conv_b_l1, ffn_down_l1))
    l0_mix = (mix_in_l0, mix_out_l0, ssd_conv_w, ssd_conv_b, ssd_dt_bias, ssd_A_log, ssd_D, ssd_norm_w,
              dn_conv_w, dn_dt_bias, dn_A_log, dn_norm_w)
    l1_mix = (mix_in_l1, mix_out_l1, sg_ln_w, sg_ln_b, sg_w_s, sg_b_s, attn_sink)

    xp, xs = x_prompt, x_sample
    for layer in range(DEPTH):
        mod_p = adaln(c_ctx[None, :], mod_w[layer], mod_b[layer])
        mod_s = adaln(c, mod_w[layer], mod_b[layer])
        hp = modulated_rmsnorm(xp, n_mix_pre[layer], mod_p[:, 0], mod_p[:, 1])
        hs = modulated_rmsnorm(xs, n_mix_pre[layer], mod_s[:, 0], mod_s[:, 1])
        if layer % 2 == 0:
            zero_ssd = jnp.zeros((xp.shape[0],) + state_l0_ssd.shape[1:], F32)
            zero_dn = jnp.zeros((xp.shape[0],) + state_l0_dn.shape[1:], F32)
            op, new_ssd, new_dn = ssd_delta_mixer(hp, zero_ssd, zero_dn, *l0_mix)
            os_, _, _ = ssd_delta_mixer(hs, state_l0_ssd, state_l0_dn, *l0_mix)
        else:
            op, new_k, new_v = l1_mixer_context(hp, *l1_mix)
            os_ = l1_mixer_latent(hs, cache_l1_k, cache_l1_v, *l1_mix)
        xp = xp + mod_p[:, 2, None, :] * rmsnorm(op, n_mix_post[layer])
        xs = xs + mod_s[:, 2, None, :] * rmsnorm(os_, n_mix_post[layer])
        hp = modulated_rmsnorm(xp, n_ffn_pre[layer], mod_p[:, 3], mod_p[:, 4])
        hs = modulated_rmsnorm(xs, n_ffn_pre[layer], mod_s[:, 3], mod_s[:, 4])
        xp = xp + mod_p[:, 5, None, :] * rmsnorm(conv_ffn(hp, *ffn[layer]), n_ffn_post[layer])
        xs = xs + mod_s[:, 5, None, :] * rmsnorm(conv_ffn(hs, *ffn[layer]), n_ffn_post[layer])
    return (xp, xs, new_ssd, new_dn, new_k, new_v)
```

```python
import contextlib
import numpy as np
import concourse.bass as bass
import concourse.mybir as mybir

F32 = mybir.dt.float32
BF16 = mybir.dt.bfloat16
AF = mybir.ActivationFunctionType
ALU = mybir.AluOpType
AX = mybir.AxisListType

SAME_ENGINE_SYNC = True


class V:
    __slots__ = ("ap", "keys")

    def __init__(self, ap, keys):
        self.ap = ap
        self.keys = keys


class Buf:
    _n = 0

    def __init__(self, kb, t, shape, gran=None):
        self.kb = kb
        self.t = t
        self.shape = list(shape)
        Buf._n += 1
        self.id = Buf._n
        f0 = self.shape[1] if len(self.shape) > 1 else 1
        self.gran = gran if gran else f0
        self.ng = (f0 + self.gran - 1) // self.gran

    def _keys(self, idx):
        if not isinstance(idx, tuple):
            idx = (idx,)
        lo, hi = 0, self.ng - 1
        if len(idx) > 1:
            i1 = idx[1]
            if isinstance(i1, slice):
                a = 0 if i1.start is None else i1.start
                b = self.shape[1] if i1.stop is None else i1.stop
                lo, hi = a // self.gran, (b - 1) // self.gran
            elif isinstance(i1, int):
                lo = hi = i1 // self.gran
        return [(self.id, g) for g in range(lo, hi + 1)]

    def __getitem__(self, idx):
        return V(self.t[idx], self._keys(idx))

    def v(self, ap, idx=None):
        return V(ap, self._keys(idx) if idx is not None else [(self.id, g) for g in range(self.ng)])


class KB:
    ENG = ("pe", "act", "dve", "pool", "sp")

    def __init__(self):
        self.nc = bass.Bass("TRN2", target_bir_lowering=False)
        self.es = contextlib.ExitStack()
        nc = self.nc
        self.E = {"pe": nc.tensor, "act": nc.scalar, "dve": nc.vector, "pool": nc.gpsimd, "sp": nc.sync}
        self.sems = {}
        self.cnt = {}
        for e in self.ENG:
            self.sems["e_" + e] = self.es.enter_context(nc.semaphore("s_" + e))
            self.cnt["e_" + e] = 0
        self.NRING = 12
        for q in ("hw", "sw"):
            for i in range(self.NRING):
                nm = "d_%s%d" % (q, i)
                self.sems[nm] = self.es.enter_context(nc.semaphore(nm))
                self.cnt[nm] = 0
        self.ring_i = {"hw": 0, "sw": 0}
        self.seen = {e: {} for e in self.ENG}
        self.last_w = {}
        self.readers = {}
        self.n_inst = 0
        self.n_wait = 0
        self.out_deps = []
        self.stack = [self.es]
        self.waits_by = {}
        self.cur_birth = None
        self.birth = {}
        self.birth_done = set()

    @contextlib.contextmanager
    def scope(self):
        es = contextlib.ExitStack()
        self.stack.append(es)
        try:
            yield
        finally:
            self.stack.pop()
            es.close()
            self.cur_birth = tuple((s, v) for s, v in self.cnt.items() if v > 0)

    def sbuf(self, name, shape, dtype=F32, gran=None):
        self._nalloc = getattr(self, "_nalloc", 0) + 1
        t = self.stack[-1].enter_context(self.nc.sbuf_tensor("%s_%d" % (name, self._nalloc), list(shape), dtype))
        b = Buf(self, t, shape, gran)
        if self.cur_birth:
            self.birth[b.id] = self.cur_birth
        return b

    def psum(self, name, shape, dtype=F32, gran=None):
        t = self.stack[-1].enter_context(self.nc.psum_tensor(name, list(shape), dtype))
        return Buf(self, t, shape, gran)

    def dram(self, name, shape, dtype=F32, kind="Internal", gran=None):
        t = self.nc.dram_tensor(name, list(shape), dtype, kind=kind)
        return Buf(self, t.ap() if hasattr(t, "ap") else t, shape, gran)

    def _collect(self, reads, writes, eng=None):
        deps = {}

        def add(d):
            if d is None:
                return
            s, v = d
            if deps.get(s, 0) < v:
                deps[s] = v
        if self.birth:
            for r in list(reads) + list(writes):
                for k in r.keys:
                    bid = k[0]
                    if bid in self.birth and (eng, bid) not in self.birth_done:
                        self.birth_done.add((eng, bid))
                        for d in self.birth[bid]:
                            add(d)
        for r in reads:
            for k in r.keys:
                add(self.last_w.get(k))
        for w in writes:
            for k in w.keys:
                add(self.last_w.get(k))
                for d in self.readers.get(k, ()):
                    add(d)
        return deps

    def _emit_waits(self, eng, deps, war_only_same=None):
        need = []
        own = "e_" + eng
        for s, v in deps.items():
            if s == own and (not SAME_ENGINE_SYNC or eng == "pe"):
                continue
            if self.seen[eng].get(s, 0) >= v:
                continue
            need.append((s, v))
        return need

    def _record(self, mark, reads, writes):
        for r in reads:
            for k in r.keys:
                self.readers.setdefault(k, []).append(mark)
        for w in writes:
            for k in w.keys:
                self.last_w[k] = mark
                self.readers[k] = []

    def op(self, eng, fn, reads, writes):
        deps = self._collect(reads, writes, eng)
        need = self._emit_waits(eng, deps)
        E = self.E[eng]
        for s, v in need[:-1]:
            E.wait_ge(self.sems[s], v)
            self.n_wait += 1
            self.waits_by[eng] = self.waits_by.get(eng, 0) + 1
        inst = fn()
        if need:
            s, v = need[-1]
            inst._wait_ge(self.sems[s], v)
        for s, v in need:
            self.seen[eng][s] = v
        own = "e_" + eng
        self.cnt[own] += 1
        inst.then_inc(self.sems[own], 1)
        mark = (own, self.cnt[own])
        self._record(mark, reads, writes)
        self.n_inst += 1
        return inst

    def dma(self, queue, out, in_, is_output=False, **kw):
        ring = "sw" if queue == "pool" else "hw"
        i = self.ring_i[ring]
        self.ring_i[ring] += 1
        nm = "d_%s%d" % (ring, i % self.NRING)
        deps = self._collect([in_], [out], queue)
        if self.cnt[nm] > 0:
            if deps.get(nm, 0) < self.cnt[nm]:
                deps[nm] = self.cnt[nm]
        need = self._emit_waits(queue, deps)
        E = self.E[queue]
        for s, v in need[:-1]:
            E.wait_ge(self.sems[s], v)
            self.n_wait += 1
        inst = E.dma_start(out=out.ap, in_=in_.ap, **kw)
        if need:
            s, v = need[-1]
            inst._wait_ge(self.sems[s], v)
        for s, v in need:
            self.seen[queue][s] = v
        self.cnt[nm] += 16
        inst.then_inc(self.sems[nm], 16)
        mark = (nm, self.cnt[nm])
        self._record(mark, [in_], [out])
        if is_output:
            self.out_deps.append(mark)
        self.n_inst += 1
        return inst

    def finish(self):
        for s, v in self.cnt.items():
            if v > 0:
                self.E["sp"].wait_ge(self.sems[s], v)

    def mm(self, out, lhsT, rhs, start=True, stop=True, **kw):
        return self.op("pe", lambda: self.nc.tensor.matmul(out.ap, lhsT=lhsT.ap, rhs=rhs.ap, start=start, stop=stop, **kw),
                       [lhsT, rhs] + ([] if start else [out]), [out])

    def transpose(self, out, in_, ident):
        return self.op("pe", lambda: self.nc.tensor.transpose(out.ap, in_.ap, ident.ap), [in_, ident], [out])

    def act(self, out, in_, func, bias=None, scale=None, accum_out=None, eng="act"):
        reads = [in_]
        kw = {}
        if bias is not None:
            if isinstance(bias, V):
                reads.append(bias); kw["bias"] = bias.ap
            else:
                kw["bias"] = bias
        if scale is not None:
            if isinstance(scale, V):
                reads.append(scale); kw["scale"] = scale.ap
            else:
                kw["scale"] = scale
        writes = [out]
        if accum_out is not None:
            writes.append(accum_out); kw["accum_out"] = accum_out.ap
        return self.op("act", lambda: self.nc.scalar.activation(out=out.ap, in_=in_.ap, func=func, **kw), reads, writes)

    def tt(self, out, in0, in1, op, eng="dve"):
        E = self.E[eng]
        return self.op(eng, lambda: E.tensor_tensor(out=out.ap, in0=in0.ap, in1=in1.ap, op=op), [in0, in1], [out])

    def ts(self, out, in0, s1, op0, s2=None, op1=None, eng="dve", accum_out=None):
        E = self.E[eng]
        reads = [in0]
        a1 = s1.ap if isinstance(s1, V) else s1
        a2 = s2.ap if isinstance(s2, V) else s2
        if isinstance(s1, V): reads.append(s1)
        if isinstance(s2, V): reads.append(s2)
        kw = {}
        writes = [out]
        if op1 is not None:
            kw["op1"] = op1
        if accum_out is not None:
            kw["accum_out"] = accum_out.ap; writes.append(accum_out)
        return self.op(eng, lambda: E.tensor_scalar(out=out.ap, in0=in0.ap, scalar1=a1, scalar2=a2, op0=op0, **kw), reads, writes)

    def stt(self, out, in0, scalar, in1, op0, op1, eng="dve"):
        E = self.E[eng]
        reads = [in0, in1]
        a = scalar.ap if isinstance(scalar, V) else scalar
        if isinstance(scalar, V): reads.append(scalar)
        return self.op(eng, lambda: E.scalar_tensor_tensor(out=out.ap, in0=in0.ap, scalar=a, in1=in1.ap, op0=op0, op1=op1), reads, [out])

    def copy(self, out, in_, eng="dve"):
        if eng == "act":
            return self.op("act", lambda: self.nc.scalar.copy(out=out.ap, in_=in_.ap), [in_], [out])
        E = self.E[eng]
        return self.op(eng, lambda: E.tensor_copy(out=out.ap, in_=in_.ap), [in_], [out])

    def memset(self, out, val, eng="dve"):
        E = self.E[eng]
        return self.op(eng, lambda: E.memset(out.ap, val), [], [out])

    def recip(self, out, in_):
        return self.op("dve", lambda: self.nc.vector.reciprocal(out=out.ap, in_=in_.ap), [in_], [out])

T = 2048
D = 1024
NCH = 16
NTT = 4
NSEG = 8
DFF = 2816
NJ = DFF // 128
EPS = 1e-6
PAD = 2
L0_PROJ = 7232
L1_PROJ = 3584


class SubBuf:
    def __init__(self, buf, off, width):
        self.buf = buf; self.off = off; self.width = width
        self.id = buf.id
        self.t = buf.t[:, off:off + width]
        self._k = buf._keys((slice(None), slice(off, off + width)))

    def __getitem__(self, idx):
        return V(self.t[idx], self._k)


class Net(KB):
    def __init__(self, dbg=None):
        super().__init__()
        self.dbg = dbg or {}
        self.inp = {}
        self.banks = [self.psum("bank%d" % i, [128, 512], F32) for i in range(7)]
        self.psum_bf(0)
        self.banks.append(None)

    def psum_bf(self, i):
        if not hasattr(self, "_bfb"):
            big = self.psum("bfbig", [128, 1024], BF16)
            self._bfb = [SubBuf(big, j * 512, 512) for j in range(2)]
        return self._bfb[i]

    def din(self, name, shape, dtype=F32, gran=None):
        b = self.dram(name, shape, dtype, kind="ExternalInput", gran=gran)
        self.inp[name] = b
        return b

    def dout(self, name, shape, dtype=F32, gran=None):
        return self.dram(name, shape, dtype, kind="ExternalOutput", gran=gran)

    def load_cols(self, dst, src_ap, src_keys, n):
        k = self
        st = k.colstage[k.ncst % 2]; k.ncst += 1
        k.dma("sp", st[0:n, :], V(src_ap, src_keys))
        bk = k.banks[6]
        k.transpose(bk[:, 0:n], st[0:n, :], k.ident_f[0:n, 0:n])
        k.copy(dst, bk[:, 0:n], eng="dve")

    def load_consts(self):
        k = self
        c = k.din("cst_ident", [128, 128])
        k.ident_f = k.sbuf("ident_f", [128, 128], F32)
        k.dma("sp", k.ident_f[:, :], c[:, :])
        k.ident_b = k.sbuf("ident_b", [128, 128], BF16)
        k.copy(k.ident_b[:, :], k.ident_f[:, :], eng="dve")
        k.ones_b = k.sbuf("ones_b", [128, 128], BF16)
        k.memset(k.ones_b[:, :], 1.0, eng="pool")
        k.ones_f = k.sbuf("ones_f", [128, 128], F32)
        k.memset(k.ones_f[:, :], 1.0, eng="pool")
        k.colstage = [k.sbuf("colstage%d" % i, [128, 128], F32) for i in range(2)]
        k.ncst = 0
        k.eps_t = k.sbuf("eps_t", [128, 1], F32)
        k.memset(k.eps_t[:, :], EPS, eng="pool")
        fl = k.din("flags", [128])
        k.flags = k.sbuf("flags_t", [128, 128], F32)
        k.dma("sp", k.flags[:, :], V(fl.t.partition_broadcast(128), fl[:].keys))
        k.xs = [k.dram("xs%d" % fc, [128, T], F32, gran=512) for fc in range(8)]
        k.hT = [k.sbuf("hT%d" % fc, [128, T + 2 * PAD], BF16, gran=None) for fc in range(8)]
        for fc in range(8):
            k.memset(k.hT[fc][:, 0:PAD], 0.0, eng="pool")
            k.memset(k.hT[fc][:, T + PAD:T + 2 * PAD], 0.0, eng="pool")

    def load_x(self):
        k = self
        x = k.din("x", [T, D], gran=None)
        sc = k.scope(); sc.__enter__()
        xin = [k.sbuf("xin%d" % i, [128, 4, D], F32) for i in range(2)]
        xt = [k.sbuf("xt%d" % i, [128, 512], F32) for i in range(4)]
        n = 0
        for tt in range(NTT):
            xi = xin[tt % 2]
            k.dma("sp", xi[:, :, :], V(x.t[tt * 512:(tt + 1) * 512, :].rearrange("(c p) d -> p c d", p=128), x[:].keys))
            for fc in range(8):
                bk = k.banks[n % 4]
                for c in range(4):
                    k.transpose(bk[:, c * 128:(c + 1) * 128], xi[:, c, fc * 128:(fc + 1) * 128], k.ident_f[:, :])
                xo = xt[n % 4]
                if n % 2 == 0:
                    k.copy(xo[:, :], bk[:, :], eng="act")
                else:
                    k.copy(xo[:, :], bk[:, :], eng="dve")
                k.dma("sp", k.xs[fc][:, tt * 512:(tt + 1) * 512], xo[:, :])
                n += 1
        sc.__exit__(None, None, None)

    def store_y(self):
        k = self
        y = k.dout("y", [T, D])
        sc = k.scope(); sc.__enter__()
        xl = [k.sbuf("yl%d" % i, [128, 512], F32) for i in range(4)]
        yo = [k.sbuf("yo%d" % i, [128, 4, D], F32) for i in range(2)]
        n = 0
        for tt in range(NTT):
            yt = yo[tt % 2]
            for fc in range(8):
                xi = xl[n % 4]
                k.dma("sp", xi[:, :], k.xs[fc][:, tt * 512:(tt + 1) * 512])
                bk = k.banks[n % 4]
                for c in range(4):
                    k.transpose(bk[:, c * 128:(c + 1) * 128], xi[:, c * 128:(c + 1) * 128], k.ident_f[:, :])
                o = V(yt.t[:, :, fc * 128:(fc + 1) * 128], yt[:].keys)
                src = V(bk.t[:, :].rearrange("p (c f) -> p c f", c=4), bk[:].keys)
                if n % 2 == 0:
                    k.copy(o, src, eng="act")
                else:
                    k.copy(o, src, eng="dve")
                n += 1
            k.dma("sp", V(y.t[tt * 512:(tt + 1) * 512, :].rearrange("(c p) d -> p c d", p=128), y[:].keys), yt[:, :, :], is_output=True)
        sc.__exit__(None, None, None)

    def modulation(self, l):
        k = self
        L = "l%d" % l
        cond = k.inp.get("cond") or k.din("cond", [D])
        mw = k.din("mod_w_" + L, [D, 6 * D])
        mb = k.din("mod_b_" + L, [6 * D])
        nws = [k.din(n + L, [D]) for n in ("norm_mix_pre_", "norm_mix_post_", "norm_ffn_pre_", "norm_ffn_post_")]
        cols = k.sbuf("modcols_" + L, [128, 6, 8], F32)
        sc = k.scope(); sc.__enter__()
        cs = k.sbuf("cond_" + L, [128, 8], F32)
        k.load_cols(cs[:, :], cond.t.rearrange("(c p) -> c p", p=128), cond[:].keys, 8)
        mbt = k.sbuf("modb_" + L, [128, 48], F32)
        k.load_cols(mbt[:, :], mb.t.rearrange("(j p) -> j p", p=128), mb[:].keys, 48)
        nwt = k.sbuf("nw_" + L, [128, 4, 8], F32)
        for i, nw in enumerate(nws):
            k.load_cols(nwt[:, i, :], nw.t.rearrange("(c p) -> c p", p=128), nw[:].keys, 8)
        sb = k.sbuf("scond_" + L, [128, 8], BF16)
        k.act(sb[:, :], cs[:, :], AF.Silu)
        wbuf = [k.sbuf("modw%d_%s" % (i, L), [128, 8, 512], BF16) for i in range(2)]
        mps = k.banks[6]
        mwv = mw.t.rearrange("(c p) n -> p c n", p=128)
        for blk in range(12):
            wb = wbuf[blk % 2]
            k.dma("pool", wb[:, :, :], V(mwv[:, :, blk * 512:(blk + 1) * 512], mw[:].keys))
            for nn in range(4):
                j = blk * 4 + nn
                for kc in range(8):
                    k.mm(mps[:, j:j + 1], wb[:, kc, nn * 128:(nn + 1) * 128], sb[:, kc:kc + 1], start=(kc == 0), stop=(kc == 7))
        mod = k.sbuf("mod_" + L, [128, 48], F32)
        k.tt(mod[:, :], mps[:, 0:48], mbt[:, :], ALU.add)
        k.stt(cols[:, 0, :], mod[:, 8:16], 1.0, nwt[:, 0, :], ALU.add, ALU.mult)
        k.copy(cols[:, 1, :], mod[:, 0:8], eng="dve")
        k.tt(cols[:, 2, :], mod[:, 16:24], nwt[:, 1, :], ALU.mult)
        k.stt(cols[:, 3, :], mod[:, 32:40], 1.0, nwt[:, 2, :], ALU.add, ALU.mult)
        k.copy(cols[:, 4, :], mod[:, 24:32], eng="dve")
        k.tt(cols[:, 5, :], mod[:, 40:48], nwt[:, 3, :], ALU.mult)
        sc.__exit__(None, None, None)
        return cols

    def rstd_from(self, out, ss, n, tmp):
        k = self
        k.act(tmp, ss, AF.Ln, bias=k.eps_t[:, 0:1], scale=1.0 / n)
        k.act(out, tmp, AF.Exp, scale=-0.5)

    def prenorm(self, cols, ia, ib):
        k = self
        sc = k.scope(); sc.__enter__()
        k.pn_x = [k.sbuf("pn_x%d" % i, [128, 512], F32) for i in range(10)]
        k.pn_sq = [k.sbuf("pn_sq%d" % i, [128, 512], BF16) for i in range(3)]
        k.pn_r = [k.sbuf("pn_r%d" % i, [128, 512], F32) for i in range(2)]
        k.pn_t = [k.sbuf("pn_t%d" % i, [128, 512], F32) for i in range(3)]
        n = 0
        for tt in range(NTT):
            sl = slice(tt * 512, (tt + 1) * 512)
            ss = k.banks[tt % 2]
            xs_t = []
            for fc in range(8):
                xb = k.pn_x[(tt * 8 + fc) % 10]
                k.dma("sp", xb[:, :], k.xs[fc][:, sl])
                sq = k.pn_sq[n % 3]; n += 1
                k.act(sq[:, :], xb[:, :], AF.Square)
                k.mm(ss[:, :], k.ones_b[:, :], sq[:, :], start=(fc == 0), stop=(fc == 7))
                xs_t.append(xb)
            r = k.pn_r[tt % 2]
            k.rstd_from(r[:, :], ss[:, :], float(D), k.pn_t[0][:, :])
            for fc in range(8):
                tm = k.pn_t[1 + fc % 2]
                k.stt(tm[:, :], xs_t[fc][:, :], cols[:, ia, fc:fc + 1], r[:, :], ALU.mult, ALU.mult)
                k.act(k.hT[fc][:, PAD + tt * 512:PAD + (tt + 1) * 512], tm[:, :], AF.Identity, bias=cols[:, ib, fc:fc + 1])
        sc.__exit__(None, None, None)

    def postnorm_alloc(self):
        k = self
        k.po_x = [k.sbuf("po_x%d" % i, [128, 512], F32) for i in range(3)]
        k.po_r = k.sbuf("po_r", [128, 512], F32)
        k.po_t = [k.sbuf("po_t%d" % i, [128, 512], F32) for i in range(3)]

    def postnorm_tile(self, tt, o_tile, ss, cols, ig):
        k = self
        sl = slice(tt * 512, (tt + 1) * 512)
        k.rstd_from(k.po_r[:, :], ss, float(D), k.po_t[0][:, :])
        for fc in range(8):
            xb = k.po_x[fc % 3]
            k.dma("sp", xb[:, :], k.xs[fc][:, sl])
            tm = k.po_t[1 + fc % 2]
            k.tt(tm[:, :], o_tile[fc], k.po_r[:, :], ALU.mult)
            k.stt(xb[:, :], tm[:, :], cols[:, ig, fc:fc + 1], xb[:, :], ALU.mult, ALU.add)
            k.dma("sp", k.xs[fc][:, sl], xb[:, :])

    def ffn(self, l, cols):
        k = self
        L = "l%d" % l
        wup = k.din("ffn_up_" + L, [D, 2 * DFF])
        wcv = k.din("ffn_conv_w_" + L, [3, 2 * DFF])
        bcv = k.din("ffn_conv_b_" + L, [2 * DFF])
        wdn = k.din("ffn_down_" + L, [DFF, D])
        k.prenorm(cols, 3, 4)
        sc = k.scope(); sc.__enter__()
        k.ff_g = k.sbuf("ff_g", [128, NJ, 1024], BF16, gran=1)
        k.ff_wu = [k.sbuf("ff_wu%d" % i, [128, 8, 2, 512], BF16) for i in range(2)]
        k.ff_wd = [k.sbuf("ff_wd%d" % i, [128, 512], BF16) for i in range(4)]
        k.ff_u = [k.sbuf("ff_u%d" % i, [128, 258], BF16) for i in range(4)]
        k.ff_dgall = k.sbuf("ff_dgall", [128, 2 * NJ, 3, 128], BF16, gran=1)
        k.ff_sa = [k.sbuf("ff_sa%d" % i, [128, 256], F32) for i in range(2)]
        k.ff_o = [k.sbuf("ff_o%d" % i, [128, 512], F32) for i in range(8)]
        k.ff_sq = [k.sbuf("ff_sq%d" % i, [128, 512], BF16) for i in range(2)]
        k.postnorm_alloc()
        cw = k.sbuf("ff_cw_" + L, [128, 44, 3], F32)
        cb = k.sbuf("ff_cb_" + L, [128, 44], F32)
        for tap in range(3):
            k.load_cols(V(cw.t[:, :, tap], cw[:].keys), wcv.t[tap, :].rearrange("(c p) -> c p", p=128), wcv[:].keys, 44)
        k.load_cols(cb[:, :], bcv.t.rearrange("(c p) -> c p", p=128), bcv[:].keys, 44)
        for ch in range(2 * NJ):
            for tap in range(3):
                k.act(V(k.ff_dgall.t[:, ch, tap, :], k.ff_dgall[:, ch].keys), k.ident_b[:, :], AF.Copy, scale=cw[:, ch, tap:tap + 1])
        wupv = wup.t.rearrange("(c p) n -> p c n", p=128)
        nwu = 0
        nu = 0
        nd = 0
        import os
        dbg_tt = int(os.environ.get("FFN_TT", NTT)); dbg_jp = int(os.environ.get("FFN_JP", NJ // 2)); dbg_part = int(os.environ.get("FFN_PART", 9))
        for tp in range(dbg_tt // 2):
            pending = None

            def conv_stage(item):
                j, sg, us, nu_ = item
                dgs = [V(k.ff_dgall.t[:, ab * NJ + j, :, :], k.ff_dgall[:, ab * NJ + j].keys) for ab in range(2)]
                pcs = []
                for ab in range(2):
                    pc = k.banks[4 + ab]
                    for tap in range(3):
                        k.mm(pc[:, 0:256], V(dgs[ab].ap[:, tap, :], dgs[ab].keys), us[ab][:, tap:tap + 256], start=(tap == 0), stop=(tap == 2))
                    pcs.append(pc)
                sa = k.ff_sa[nu_ % 2]
                k.act(sa[:, :], pcs[0][:, 0:256], AF.Silu, bias=cb[:, j:j + 1])
                k.stt(V(k.ff_g.t[:, j, sg * 256:(sg + 1) * 256], k.ff_g[:, j].keys), pcs[1][:, 0:256], cb[:, NJ + j:NJ + j + 1], sa[:, :],
                      ALU.add, ALU.mult)

            for jp in range((NJ + 3) // 4):
                wu = k.ff_wu[nwu % 2]; nwu += 1
                nj_here = min(4, NJ - jp * 4)
                for ab in range(2):
                    c0 = ab * DFF + jp * 512
                    k.dma("pool", V(wu.t[:, :, ab, 0:nj_here * 128], wu[:].keys), V(wupv[:, :, c0:c0 + nj_here * 128], wup[:].keys))
                for jj in range(nj_here):
                    j = jp * 4 + jj
                    for sg in range(4):
                        seg = tp * 4 + sg
                        c_lo = PAD + seg * 256 - 1
                        us = []
                        for ab in range(2):
                            pb = k.banks[(nu * 2 + ab) % 4]
                            for kc in range(8):
                                k.mm(pb[:, 0:258], wu[:, kc, ab, jj * 128:(jj + 1) * 128], k.hT[kc][:, c_lo:c_lo + 258],
                                     start=(kc == 0), stop=(kc == 7))
                            u = k.ff_u[(nu * 2 + ab) % 4]
                            if ab == 0:
                                k.copy(u[:, 0:258], pb[:, 0:258], eng="act")
                            else:
                                k.copy(u[:, 0:258], pb[:, 0:258], eng="dve")
                            uv = V(u.t[:, 0:258:257], u[:].keys)
                            k.tt(uv, uv, k.flags[:, seg * 2:seg * 2 + 2], ALU.mult)
                            us.append(u)
                        if pending is not None:
                            conv_stage(pending)
                        pending = (j, sg, us, nu)
                        nu += 1
            if pending is not None:
                conv_stage(pending)
            for st_ in range(2):
                tt = tp * 2 + st_
                ss = k.banks[6]
                for half in range(2):
                    for j in range(NJ):
                        wd = k.ff_wd[nd % 4]; nd += 1
                        k.dma("pool", wd[:, :], V(wdn.t[j * 128:(j + 1) * 128, half * 512:(half + 1) * 512], wdn[:].keys))
                        for nn in range(4):
                            k.mm(k.banks[nn][:, :], wd[:, nn * 128:(nn + 1) * 128], V(k.ff_g.t[:, j, st_ * 512:(st_ + 1) * 512], k.ff_g[:, j].keys),
                                 start=(j == 0), stop=(j == NJ - 1))
                    for nn in range(4):
                        n = half * 4 + nn
                        k.copy(k.ff_o[n][:, :], k.banks[nn][:, :], eng="dve")
                        sq = k.ff_sq[n % 2]
                        k.act(sq[:, :], k.ff_o[n][:, :], AF.Square)
                        k.mm(ss[:, :], k.ones_b[:, :], sq[:, :], start=(n == 0), stop=(n == 7))
                k.postnorm_tile(tt, [k.ff_o[n][:, :] for n in range(8)], ss[:, :], cols, 5)
        sc.__exit__(None, None, None)

    def gelu(self, out, src_psum, n, scr):
        k = self
        x, t, s = scr
        k.copy(x[:, 0:n], src_psum, eng="act")
        k.act(t[:, 0:n], x[:, 0:n], AF.Square)
        k.ts(t[:, 0:n], t[:, 0:n], 0.044715, ALU.mult, 1.0, ALU.add)
        k.tt(t[:, 0:n], t[:, 0:n], x[:, 0:n], ALU.mult)
        k.act(s[:, 0:n], t[:, 0:n], AF.Sigmoid, scale=1.5957691216057308)
        k.tt(out, s[:, 0:n], x[:, 0:n], ALU.mult)

    def outproj_post(self, wname, cols):
        k = self
        wo = k.din(wname, [2048, D])
        sc = k.scope(); sc.__enter__()
        wt = k.sbuf("wo", [128, 16, D], BF16)
        wov = wo.t.rearrange("(c p) n -> p c n", p=128)
        for q in range(4):
            k.dma("pool", V(wt.t[:, q * 4:(q + 1) * 4, :], wt[:].keys), V(wov[:, q * 4:(q + 1) * 4, :], wo[:].keys))
        o_t = [k.sbuf("op_o%d" % i, [128, 512], F32) for i in range(8)]
        sqs = [k.sbuf("op_sq%d" % i, [128, 512], BF16) for i in range(2)]
        k.postnorm_alloc()
        nb = 0
        for tt in range(NTT):
            ss = k.banks[6]
            for n in range(8):
                bk = k.banks[nb % 4]; nb += 1
                for kc in range(16):
                    k.mm(bk[:, :], wt[:, kc, n * 128:(n + 1) * 128], V(k.mixedT.t[:, kc, tt * 512:(tt + 1) * 512], k.mixedT[:, kc].keys),
                         start=(kc == 0), stop=(kc == 15))
                k.copy(o_t[n][:, :], bk[:, :], eng="dve")
                sq = sqs[n % 2]
                k.act(sq[:, :], o_t[n][:, :], AF.Square)
                k.mm(ss[:, :], k.ones_b[:, :], sq[:, :], start=(n == 0), stop=(n == 7))
            k.postnorm_tile(tt, [o_t[n][:, :] for n in range(8)], ss[:, :], cols, 2)
        sc.__exit__(None, None, None)

    def mixer_l1(self, cols):
        k = self
        win = k.din("mix_in_l1", [D, L1_PROJ])
        winv = win.t.rearrange("(c p) n -> p c n", p=128)
        lnw = k.din("sg_ln_w", [1024]); lnb = k.din("sg_ln_b", [1024])
        sws = k.din("sg_w_s", [4, 128, 128]); sbs = k.din("sg_b_s", [4, 128])
        sink = k.din("attn_sink", [8])
        ck = k.din("ctx_k", [512, 256]); cv = k.din("ctx_v", [512, 256])
        rope = k.din("rope", [2, 128, T])
        cband = k.din("cst_band", [2, 128, 512])
        crot = k.din("cst_rot", [128, 128])
        nk = k.dout("nk", [T, 256]); nv = k.dout("nv", [T, 256])
        k.prenorm(cols, 0, 1)
        sc0 = k.scope(); sc0.__enter__()
        k.mixedT = k.sbuf("mixedT", [128, 16, T], BF16, gran=1)
        sc = k.scope(); sc.__enter__()
        wgv = k.sbuf("wgv", [128, 8, 1024], BF16)
        for q in range(2):
            k.dma("pool", V(wgv.t[:, :, q * 512:(q + 1) * 512], wgv[:].keys), V(winv[:, :, 1024 + q * 512:1024 + (q + 1) * 512], win[:].keys))
        lnw_t = k.sbuf("lnw_t", [128, 1024], F32); lnb_t = k.sbuf("lnb_t", [128, 1024], F32)
        k.dma("sp", lnw_t[:, :], V(lnw.t.partition_broadcast(128), lnw[:].keys))
        k.dma("sp", lnb_t[:, :], V(lnb.t.partition_broadcast(128), lnb[:].keys))
        wsT = k.sbuf("wsT", [128, 4, 128], BF16)
        wstage = k.sbuf("wstage", [128, 4, 128], F32)
        k.dma("sp", wstage[:, :, :], V(sws.t.rearrange("g i j -> i g j"), sws[:].keys))
        for g in range(4):
            bk = k.banks[g % 2]
            k.transpose(bk[:, 0:128], wstage[:, g, :], k.ident_f[:, :])
            k.copy(wsT[:, g, :], bk[:, 0:128], eng="dve")
        bsf = k.sbuf("bsf", [1, 512], F32); bsb = k.sbuf("bsb", [1, 512], BF16)
        k.dma("sp", bsf[:, :], V(sbs.t.rearrange("(o g) i -> o (g i)", o=1), sbs[:].keys))
        k.copy(bsb[:, :], bsf[:, :], eng="dve")
        gsc = [(k.sbuf("g_x%d" % i, [128, 512], F32), k.sbuf("g_t%d" % i, [128, 512], F32), k.sbuf("g_s%d" % i, [128, 512], F32)) for i in range(2)]
        wu = [k.sbuf("wu1_%d" % i, [128, 8, 128], BF16) for i in range(2)]
        ng = 0
        for n in range(8):
            w = wu[n % 2]
            k.dma("pool", w[:, :, :], V(winv[:, :, n * 128:(n + 1) * 128], win[:].keys))
            for tt in range(NTT):
                bk = k.banks[ng % 2]
                for kc in range(8):
                    k.mm(bk[:, :], w[:, kc, :], k.hT[kc][:, PAD + tt * 512:PAD + (tt + 1) * 512], start=(kc == 0), stop=(kc == 7))
                k.gelu(V(k.mixedT.t[:, n, tt * 512:(tt + 1) * 512], k.mixedT[:, n].keys), bk[:, :], 512, gsc[ng % 2])
                ng += 1
        gv = [k.sbuf("gv%d" % i, [128, 1024], F32) for i in range(2)]
        gvn = [k.sbuf("gvn%d" % i, [128, 1024], BF16) for i in range(2)]
        st6 = k.sbuf("st6", [128, 2, 6], F32); mv = k.sbuf("mv", [128, 2], F32); rs = k.sbuf("rs", [128, 2], F32)
        for c in range(NCH):
            g_ = gv[c % 2]
            for hf in range(2):
                bk = k.banks[2 + hf]
                for kc in range(8):
                    k.mm(bk[:, :], k.hT[kc][:, PAD + c * 128:PAD + (c + 1) * 128], wgv[:, kc, hf * 512:(hf + 1) * 512], start=(kc == 0), stop=(kc == 7))
                k.gelu(g_[:, hf * 512:(hf + 1) * 512], bk[:, :], 512, gsc[ng % 2]); ng += 1
                k.op("dve", lambda g_=g_, hf=hf: k.nc.vector.bn_stats(out=st6.t[:, hf, :], in_=g_.t[:, hf * 512:(hf + 1) * 512]), [g_[:, :]], [st6[:, :, :]])
            k.op("dve", lambda: k.nc.vector.bn_aggr(out=mv.t[:, :], in_=st6.t[:, :, :].rearrange("p a b -> p (a b)")), [st6[:, :, :]], [mv[:, :]])
            k.act(rs[:, 0:1], mv[:, 1:2], AF.Sqrt, bias=k.eps_t[:, 0:1], scale=1.0)
            k.recip(rs[:, 1:2], rs[:, 0:1])
            k.ts(g_[:, :], g_[:, :], mv[:, 0:1], ALU.subtract, rs[:, 1:2], ALU.mult)
            k.tt(g_[:, :], g_[:, :], lnw_t[:, :], ALU.mult, eng="pool")
            gn = gvn[c % 2]
            k.tt(gn[:, :], g_[:, :], lnb_t[:, :], ALU.add)
            for hf in range(2):
                bk = k.banks[4 + hf]
                for q in range(4):
                    dch = hf * 4 + q
                    g = dch // 2
                    k.mm(bk[:, q * 128:(q + 1) * 128], gn[:, dch * 128:(dch + 1) * 128], wsT[:, g, :], start=True, stop=False)
                    k.mm(bk[:, q * 128:(q + 1) * 128], k.ones_b[0:1, :], bsb[0:1, g * 128:(g + 1) * 128], start=False, stop=True)
                mo = V(k.mixedT.t[:, hf * 4:hf * 4 + 4, c * 128:(c + 1) * 128], [(k.mixedT.id, hf * 4 + q) for q in range(4)])
                k.tt(mo, V(bk.t[:, :].rearrange("p (q i) -> p q i", q=4), bk[:].keys), mo, ALU.mult)
        sc.__exit__(None, None, None)
        sc = k.scope(); sc.__enter__()
        qT = k.sbuf("qT", [128, 8, T], BF16, gran=1)
        kT = k.sbuf("kT", [128, 2, T], BF16, gran=1)
        vtok = k.sbuf("vtok", [128, NCH, 256], BF16, gran=1)
        kcT = k.sbuf("kcT", [128, 2, 512], BF16)
        vc = k.sbuf("vc", [128, 4, 256], BF16)
        skr = k.sbuf("skr", [1, 8, 128], BF16)
        band = k.sbuf("band", [128, 2, 512], BF16)
        scA = k.scope(); scA.__enter__()
        ropeC = k.sbuf("ropeC", [128, T], F32); ropeS = k.sbuf("ropeS", [128, T], F32)
        k.dma("sp", ropeC[:, :], V(rope.t[0], rope[:].keys)); k.dma("sp", ropeS[:, :], V(rope.t[1], rope[:].keys))
        rotf = k.sbuf("rotf", [128, 128], F32); rotb = k.sbuf("rotb", [128, 128], BF16)
        k.dma("sp", rotf[:, :], crot[:, :]); k.copy(rotb[:, :], rotf[:, :], eng="dve")
        wq = [k.sbuf("wq%d" % i, [128, 8, 128], BF16) for i in range(2)]
        qs = [k.sbuf("q_s%d" % i, [128, 512], BF16) for i in range(2)]
        t1 = [k.sbuf("q_t1%d" % i, [128, 512], F32) for i in range(2)]
        t2 = [k.sbuf("q_t2%d" % i, [128, 512], F32) for i in range(2)]
        nq = 0
        for hh in range(10):
            w = wq[hh % 2]
            c0 = 2048 + hh * 128
            k.dma("pool", w[:, :, :], V(winv[:, :, c0:c0 + 128], win[:].keys))
            for tt in range(NTT):
                sl = slice(tt * 512, (tt + 1) * 512)
                bk = k.banks[nq % 2]; bk2 = k.banks[2 + nq % 2]
                for kc in range(8):
                    k.mm(bk[:, :], w[:, kc, :], k.hT[kc][:, PAD + tt * 512:PAD + (tt + 1) * 512], start=(kc == 0), stop=(kc == 7))
                q_ = qs[nq % 2]
                k.copy(q_[:, :], bk[:, :], eng="act")
                k.mm(bk2[:, :], rotb[:, :], q_[:, :])
                a = t1[nq % 2]; b = t2[nq % 2]
                k.tt(a[:, :], q_[:, :], ropeC[:, sl], ALU.mult, eng="pool")
                k.tt(b[:, :], bk2[:, :], ropeS[:, sl], ALU.mult)
                dst = V(qT.t[:, hh, sl], qT[:, hh].keys) if hh < 8 else V(kT.t[:, hh - 8, sl], kT[:, hh - 8].keys)
                k.tt(dst, a[:, :], b[:, :], ALU.add)
                nq += 1
        wkv = k.sbuf("wkv", [128, 8, 512], BF16)
        k.dma("pool", wkv[:, :, :], V(winv[:, :, 3072:3584], win[:].keys))
        kvo = [k.sbuf("kvo%d" % i, [128, 512], F32) for i in range(2)]
        for c in range(NCH):
            bk = k.banks[c % 2]
            for kc in range(8):
                k.mm(bk[:, :], k.hT[kc][:, PAD + c * 128:PAD + (c + 1) * 128], wkv[:, kc, :], start=(kc == 0), stop=(kc == 7))
            o = kvo[c % 2]
            k.copy(o[:, :], bk[:, :], eng="act")
            k.copy(V(vtok.t[:, c, :], vtok[:, c].keys), o[:, 256:512], eng="dve")
            k.dma("sp", V(nk.t[c * 128:(c + 1) * 128, :], nk[:].keys), o[:, 0:256], is_output=True)
            k.dma("sp", V(nv.t[c * 128:(c + 1) * 128, :], nv[:].keys), o[:, 256:512], is_output=True)
        kcs = k.sbuf("kcs", [128, 4, 256], F32)
        k.dma("sp", kcs[:, :, :], V(ck.t.rearrange("(c p) d -> p c d", p=128), ck[:].keys))
        for kvh in range(2):
            bk = k.banks[kvh]
            for sc_ in range(4):
                k.transpose(bk[:, sc_ * 128:(sc_ + 1) * 128], kcs[:, sc_, kvh * 128:(kvh + 1) * 128], k.ident_f[:, :])
            k.copy(kcT[:, kvh, :], bk[:, :], eng="dve")
        k.dma("pool", vc[:, :, :], V(cv.t.rearrange("(c p) d -> p c d", p=128), cv[:].keys))
        skf = k.sbuf("skf", [1, 8], F32); ske = k.sbuf("ske", [1, 8], F32)
        k.dma("sp", skf[:, :], V(sink.t.rearrange("(o h) -> o h", o=1), sink[:].keys))
        k.act(ske[:, :], skf[:, :], AF.Exp)
        k.copy(skr[:, :, :], V(ske.t[:, :].unsqueeze(2).to_broadcast([1, 8, 128]), ske[:].keys), eng="dve")
        bandf = k.sbuf("bandf", [128, 2, 512], F32)
        k.dma("sp", bandf[:, :, :], V(cband.t.rearrange("a p n -> p a n"), cband[:].keys))
        k.copy(band[:, :, :], bandf[:, :, :], eng="dve")
        scA.__exit__(None, None, None)
        pT = [k.sbuf("pT%d" % i, [128, 512], BF16) for i in range(3)]
        mk = [k.sbuf("mk%d" % i, [128, 512], BF16) for i in range(2)]
        rden = [k.sbuf("rden%d" % i, [128, 512], F32) for i in range(2)]
        scale = 128.0 ** -0.5
        npb = 0; nmk = 0; nu = 0
        for c in range(NCH):
            for kvh in range(2):
                blocks = []
                if c > 0: blocks.append(("prev", c - 1))
                blocks.append(("same", c))
                if c < NCH - 1: blocks.append(("next", c + 1))
                for s4 in range(4): blocks.append(("ctx", s4))
                po = k.banks[3 + nu % 2]; pd = k.banks[5 + nu % 2]
                rhs_q = V(qT.t[:, kvh * 4:kvh * 4 + 4, c * 128:(c + 1) * 128], [(qT.id, kvh * 4 + i) for i in range(4)])
                for bi, (kind, idx) in enumerate(blocks):
                    ps = k.banks[npb % 3]
                    p_ = pT[npb % 3]; npb += 1
                    if kind == "ctx":
                        k.mm(V(ps.t[:, :].rearrange("p (h q) -> p h q", h=4), ps[:].keys), kcT[:, kvh, idx * 128:(idx + 1) * 128], rhs_q)
                        k.act(p_[:, :], ps[:, :], AF.Exp, bias=k.flags[:, 112:113], scale=scale)
                        lv = vc[:, idx, kvh * 128:(kvh + 1) * 128]
                    else:
                        k.mm(V(ps.t[:, :].rearrange("p (h q) -> p h q", h=4), ps[:].keys), V(kT.t[:, kvh, idx * 128:(idx + 1) * 128], kT[:, kvh].keys), rhs_q)
                        k.act(p_[:, :], ps[:, :], AF.Exp, scale=scale)
                        if kind != "same":
                            m = mk[nmk % 2]; nmk += 1
                            bsel = 0 if kind == "prev" else 1
                            f0 = 48 + (0 if kind == "prev" else 32) + c
                            k.ts(m[:, :], band[:, bsel, :], k.flags[:, f0:f0 + 1], ALU.mult, k.flags[:, f0 + 16:f0 + 17], ALU.add, eng="pool")
                            k.tt(p_[:, :], p_[:, :], m[:, :], ALU.mult)
                        lv = V(vtok.t[:, idx, kvh * 128:(kvh + 1) * 128], vtok[:, idx].keys)
                    k.mm(po[:, :], lv, p_[:, :], start=(bi == 0), stop=(bi == len(blocks) - 1))
                    k.mm(pd[:, :], k.ones_b[:, :], p_[:, :], start=(bi == 0), stop=False)
                k.mm(pd[:, :], k.ones_b[0:1, :], V(skr.t[0:1, kvh * 4:kvh * 4 + 4, :].rearrange("o h q -> o (h q)"), skr[:].keys), start=False, stop=True)
                rd = rden[nu % 2]
                k.act(rd[:, :], pd[:, :], AF.Ln)
                k.act(rd[:, :], rd[:, :], AF.Exp, scale=-1.0)
                mo = V(k.mixedT.t[:, 8 + kvh * 4:8 + kvh * 4 + 4, c * 128:(c + 1) * 128], [(k.mixedT.id, 8 + kvh * 4 + i) for i in range(4)])
                k.tt(mo, V(po.t[:, :].rearrange("p (h q) -> p h q", h=4), po[:].keys), V(rd.t[:, :].rearrange("p (h q) -> p h q", h=4), rd[:].keys), ALU.mult)
                nu += 1
        sc.__exit__(None, None, None)
        k.outproj_post("mix_out_l1", cols)
        sc0.__exit__(None, None, None)

    def proj_conv5(self, win, winv, col0, cw5, cb, dst_fn):
        k = self
        w = k.pc_w[k.pc_n % 2]
        k.dma("pool", w[:, :, :], V(winv[:, :, col0:col0 + 128], win[:].keys))
        dg = k.pc_dg[k.pc_n % 2]
        for tap in range(5):
            k.act(dg[:, tap, :], k.ident_b[:, :], AF.Copy, scale=cw5(tap))
        k.pc_n += 1
        pending = None

        def conv_stage(item):
            seg, u, pc = item
            for tap in range(5):
                k.mm(pc[:, 0:256], dg[:, tap, :], u[:, tap:tap + 256], start=(tap == 0), stop=(tap == 4))
            if cb is not None:
                k.act(dst_fn(seg), pc[:, 0:256], AF.Silu, bias=cb)
            else:
                k.act(dst_fn(seg), pc[:, 0:256], AF.Silu)

        for seg in range(NSEG):
            pb = k.banks[k.pc_m % 2]; pc = k.banks[2 + k.pc_m % 2]
            u = k.pc_u[k.pc_m % 2]; k.pc_m += 1
            c_lo = seg * 256
            for kc in range(8):
                k.mm(pb[:, 0:260], w[:, kc, :], k.hT[kc][:, c_lo:c_lo + 260], start=(kc == 0), stop=(kc == 7))
            k.copy(u[:, 0:260], pb[:, 0:260], eng="act")
            k.ts(u[:, 0:2], u[:, 0:2], k.flags[:, 2 * seg:2 * seg + 1], ALU.mult)
            k.ts(u[:, 258:260], u[:, 258:260], k.flags[:, 2 * seg + 1:2 * seg + 2], ALU.mult)
            if pending is not None:
                conv_stage(pending)
            pending = (seg, u, pc)
        conv_stage(pending)

    def pc_alloc(self):
        k = self
        k.pc_w = [k.sbuf("pc_w%d" % i, [128, 8, 128], BF16) for i in range(2)]
        k.pc_dg = [k.sbuf("pc_dg%d" % i, [128, 5, 128], BF16) for i in range(2)]
        k.pc_u = [k.sbuf("pc_u%d" % i, [128, 260], BF16) for i in range(2)]
        k.pc_n = 0; k.pc_m = 0

    def mixer_l0(self, cols):
        k = self
        win = k.din("mix_in_l0", [D, L0_PROJ])
        winv = win.t.rearrange("(c p) n -> p c n", p=128)
        k.prenorm(cols, 0, 1)
        sc0 = k.scope(); sc0.__enter__()
        k.mixedT = k.sbuf("mixedT", [128, 16, T], BF16, gran=1)
        k.l0_consts()
        k.dtab = {nm: k.sbuf("dn_" + nm, [128, NCH, 2, 8], F32) for nm in ("av", "ncs", "ecs", "necs", "dte", "etot")}
        k.beta = k.sbuf("beta", [128, NCH, 16], F32)
        tri = k.din("cst_tri", [2, 128, 128])
        k.tri = k.sbuf("tri", [128, 2, 128], F32)
        k.dma("sp", k.tri[:, :, :], V(tri.t.rearrange("a p n -> p a n"), tri[:].keys))
        import os
        part = os.environ.get("L0_PART", "both")
        s1 = k.scope(); s1.__enter__()
        k.l0_small(win, winv)
        if part in ("both", "ssd"):
            sc = k.scope(); sc.__enter__()
            k.ssd(win, winv)
            sc.__exit__(None, None, None)
        else:
            for fc in range(8):
                k.memset(V(k.mixedT.t[:, fc, :], k.mixedT[:, fc].keys), 0.0, eng="pool")
        s1.__exit__(None, None, None)
        if part in ("both", "dn"):
            sc = k.scope(); sc.__enter__()
            k.dn(win, winv)
            sc.__exit__(None, None, None)
        else:
            for fc in range(8, 16):
                k.memset(V(k.mixedT.t[:, fc, :], k.mixedT[:, fc].keys), 0.0, eng="pool")
        k.outproj_post("mix_out_l0", cols)
        sc0.__exit__(None, None, None)

    def l0_small(self, win, winv):
        k = self
        dtb = k.din("ssd_dt_bias", [2, 16]); alog = k.din("ssd_A_log", [2, 16])
        ddtb = k.din("dn_dt_bias", [2, 8]); dalog = k.din("dn_A_log", [2, 8])
        k.dt_t = k.sbuf("dt_t", [128, NCH, 32], F32)
        k.av = k.sbuf("av", [128, NCH, 2, 24], F32)
        k.cs = k.sbuf("cs", [128, NCH, 2, 24], F32)
        k.ncs = k.sbuf("ncs", [128, NCH, 2, 24], F32)
        k.ecs = k.sbuf("ecs", [128, NCH, 2, 24], F32)
        k.necs = k.sbuf("necs", [128, NCH, 2, 24], F32)
        k.dte = k.sbuf("dte", [128, NCH, 2, 24], F32)
        k.etot = k.sbuf("etot", [128, NCH, 2, 24], F32)
        sc = k.scope(); sc.__enter__()
        wsm = k.sbuf("wsm", [128, 8, 64], BF16)
        k.dma("pool", V(wsm.t[:, :, 0:32], wsm[:].keys), V(winv[:, :, 3072:3104], win[:].keys))
        k.dma("pool", V(wsm.t[:, :, 32:64], wsm[:].keys), V(winv[:, :, 7200:7232], win[:].keys))
        sm = k.sbuf("sm", [128, NCH, 64], F32)
        for c in range(NCH):
            bk = k.banks[c % 2]
            for kc in range(8):
                k.mm(bk[:, 0:64], k.hT[kc][:, PAD + c * 128:PAD + (c + 1) * 128], wsm[:, kc, :], start=(kc == 0), stop=(kc == 7))
            k.copy(V(sm.t[:, c, :], sm[:].keys), bk[:, 0:64], eng="act" if c % 2 == 0 else "dve")
        bias48 = k.sbuf("bias48", [128, 48], F32); al48 = k.sbuf("al48", [128, 48], F32)
        k.dma("sp", bias48[:, 0:32], V(dtb.t.rearrange("a h -> (a h)").partition_broadcast(128), dtb[:].keys))
        k.dma("sp", bias48[:, 32:48], V(ddtb.t.rearrange("a h -> (a h)").partition_broadcast(128), ddtb[:].keys))
        k.dma("sp", al48[:, 0:32], V(alog.t.rearrange("a h -> (a h)").partition_broadcast(128), alog[:].keys))
        k.dma("sp", al48[:, 32:48], V(dalog.t.rearrange("a h -> (a h)").partition_broadcast(128), dalog[:].keys))
        nega = k.sbuf("nega", [128, 48], F32)
        k.act(nega[:, :], al48[:, :], AF.Exp)
        k.ts(nega[:, :], nega[:, :], -1.0, ALU.mult)
        sp_ = k.sbuf("sp_", [128, NCH, 48], F32)
        bb = V(bias48.t[:, :].unsqueeze(1).to_broadcast([128, NCH, 48]), bias48[:].keys)
        k.tt(sp_[:, :, :], V(sm.t[:, :, 0:48], sm[:].keys), bb, ALU.add)
        k.act(sp_[:, :, :], sp_[:, :, :], AF.Exp)
        k.ts(sp_[:, :, :], sp_[:, :, :], 1.0, ALU.add)
        k.act(sp_[:, :, :], sp_[:, :, :], AF.Ln)
        k.copy(k.dt_t[:, :, :], V(sp_.t[:, :, 0:32], sp_[:].keys), eng="dve")
        k.act(k.beta[:, :, :], V(sm.t[:, :, 48:64], sm[:].keys), AF.Sigmoid)
        nb = V(nega.t[:, :].unsqueeze(1).to_broadcast([128, NCH, 48]), nega[:].keys)
        k.tt(sp_[:, :, :], sp_[:, :, :], nb, ALU.mult)
        for d in range(2):
            k.copy(V(k.av.t[:, :, d, 0:16], k.av[:].keys), V(sp_.t[:, :, d * 16:(d + 1) * 16], sp_[:].keys), eng="dve")
            k.copy(V(k.av.t[:, :, d, 16:24], k.av[:].keys), V(sp_.t[:, :, 32 + d * 8:32 + (d + 1) * 8], sp_[:].keys), eng="dve")
        tot = k.sbuf("tot", [128, NCH, 2, 24], F32)
        for d in range(2):
            bk = k.banks[d]; bk2 = k.banks[2 + d]
            rhs = V(k.av.t[:, :, d, :], k.av[:].keys)
            k.mm(V(bk.t[:, 0:384].rearrange("p (c n) -> p c n", c=NCH), bk[:].keys), k.tri[:, d, :], rhs)
            k.mm(V(bk2.t[:, 0:384].rearrange("p (c n) -> p c n", c=NCH), bk2[:].keys), k.ones_f[:, :], rhs)
            k.copy(V(k.cs.t[:, :, d, :], k.cs[:].keys), V(bk.t[:, 0:384].rearrange("p (c n) -> p c n", c=NCH), bk[:].keys), eng="dve")
            k.copy(V(tot.t[:, :, d, :], tot[:].keys), V(bk2.t[:, 0:384].rearrange("p (c n) -> p c n", c=NCH), bk2[:].keys), eng="dve")
        k.ts(k.ncs[:, :, :, :], k.cs[:, :, :, :], -1.0, ALU.mult)
        k.act(k.ecs[:, :, :, :], k.cs[:, :, :, :], AF.Exp)
        k.ts(k.necs[:, :, :, :], k.ecs[:, :, :, :], -1.0, ALU.mult)
        k.act(k.etot[:, :, :, :], tot[:, :, :, :], AF.Exp)
        k.tt(tot[:, :, :, :], tot[:, :, :, :], k.cs[:, :, :, :], ALU.subtract)
        k.act(k.dte[:, :, :, :], tot[:, :, :, :], AF.Exp)
        for nm, src in (("av", k.av), ("ncs", k.ncs), ("ecs", k.ecs), ("necs", k.necs), ("dte", k.dte), ("etot", k.etot)):
            k.copy(k.dtab[nm][:, :, :, :], V(src.t[:, :, :, 16:24], src[:].keys), eng="pool")
        sc.__exit__(None, None, None)

    def build_Lt(self, lt, ps, d, acol, ncol):
        k = self
        abc = V(acol.ap.to_broadcast([128, 128]), acol.keys)
        k.mm(ps, abc, k.tri[:, d, :], start=True, stop=False)
        k.mm(ps, k.ident_b[:, :], k.mbias[:, d, :], start=False, stop=True)
        k.act(lt, ps, AF.Exp, bias=ncol)

    def l0_consts(self):
        k = self
        mb = k.din("cst_mbias", [2, 128, 128]); st = k.din("cst_strict", [2, 128, 128]); blk = k.din("cst_blk", [4, 128, 128])
        k.mbias = k.sbuf("mbias", [128, 2, 128], BF16)
        k.strict = k.sbuf("strict", [128, 2, 128], F32)
        k.blk = k.sbuf("blk", [128, 4, 128], F32)
        k.dma("pool", k.mbias[:, :, :], V(mb.t.rearrange("a p n -> p a n"), mb[:].keys))
        k.blk_b = k.sbuf("blk_b", [128, 4, 128], BF16)
        k.dma("pool", k.blk_b[:, :, :], V(blk.t.rearrange("a p n -> p a n"), blk[:].keys))
        k.dma("sp", k.strict[:, :, :], V(st.t.rearrange("a p n -> p a n"), st[:].keys))
        k.dma("sp", k.blk[:, :, :], V(blk.t.rearrange("a p n -> p a n"), blk[:].keys))

    def ssd(self, win, winv):
        k = self
        cwd = k.din("ssd_conv_w", [5, 2048]); cbd = k.din("ssd_conv_b", [2048])
        dD = k.din("ssd_D", [16]); nwd = k.din("ssd_norm_w", [1024])
        h0d = k.din("ssd_h0", [2, 16, 64, 128])
        hout = k.dout("ssd_out", [NSEG, 2, 16, 64, 128])
        k.pc_alloc()
        cw = k.sbuf("s_cw", [128, 16, 5], F32); cb = k.sbuf("s_cb", [128, 16], F32)
        for tap in range(5):
            k.load_cols(V(cw.t[:, :, tap], cw[:].keys), cwd.t[tap, :].rearrange("(c p) -> c p", p=128), cwd[:].keys, 16)
        k.load_cols(cb[:, :], cbd.t.rearrange("(c p) -> c p", p=128), cbd[:].keys, 16)
        Dbc = k.sbuf("Dbc", [128, 16], F32)
        k.dma("sp", Dbc[:, :], V(dD.t.partition_broadcast(128), dD[:].keys))
        nwc = k.sbuf("s_nw", [128, 8], F32)
        k.load_cols(nwc[:, :], nwd.t.rearrange("(c p) -> c p", p=128), nwd[:].keys, 8)
        scI = k.scope(); scI.__enter__()
        BT = k.sbuf("s_BT", [128, T], BF16); CT = k.sbuf("s_CT", [128, T], BF16)
        xtok = k.sbuf("s_xtok", [128, NCH, 256], BF16, gran=1)
        Btok = k.sbuf("s_Btok", [128, NCH, 128], BF16, gran=1)
        sz = k.sbuf("s_sz", [128, NCH, 256], BF16, gran=1)
        yacc = k.sbuf("s_yacc", [128, NCH, 256], BF16, gran=1)
        hm = [k.sbuf("s_hm%d" % d, [128, 256], F32) for d in range(2)]
        hb = [k.sbuf("s_hb%d" % d, [128, 256], BF16) for d in range(2)]
        bfb = [k.psum_bf(i) for i in range(2)]
        n = 0
        for g in range(4):
            scA = k.scope(); scA.__enter__()
            xT = k.sbuf("s_xT", [128, 2, T], BF16, gran=1)
            wz = [k.sbuf("s_wz%d" % i, [128, 8, 256], BF16) for i in range(1)]
            for q in range(2):
                ch = 2 * g + q
                k.proj_conv5(win, winv, 1024 + ch * 128, lambda tap, ch=ch: cw[:, ch, tap:tap + 1], cb[:, ch:ch + 1],
                             lambda seg, q=q: V(xT.t[:, q, seg * 256:(seg + 1) * 256], xT[:, q].keys))
            chB = 8 + g; chC = 12 + g
            k.proj_conv5(win, winv, 1024 + chB * 128, lambda tap: cw[:, chB, tap:tap + 1], cb[:, chB:chB + 1], lambda seg: BT[:, seg * 256:(seg + 1) * 256])
            k.proj_conv5(win, winv, 1024 + chC * 128, lambda tap: cw[:, chC, tap:tap + 1], cb[:, chC:chC + 1], lambda seg: CT[:, seg * 256:(seg + 1) * 256])
            w = wz[0]
            k.dma("pool", w[:, :, :], V(winv[:, :, g * 256:(g + 1) * 256], win[:].keys))
            for c in range(NCH):
                cs_ = slice(c * 128, (c + 1) * 128)
                bk = k.banks[4 + c % 2]
                for kc in range(8):
                    k.mm(bk[:, 0:256], k.hT[kc][:, PAD + c * 128:PAD + (c + 1) * 128], w[:, kc, :], start=(kc == 0), stop=(kc == 7))
                k.act(V(sz.t[:, c, :], sz[:, c].keys), bk[:, 0:256], AF.Silu)
                pb = bfb[c % 2]
                for q in range(2):
                    k.transpose(pb[:, q * 128:(q + 1) * 128], V(xT.t[:, q, cs_], xT[:, q].keys), k.ident_b[:, :])
                k.transpose(pb[:, 256:384], BT[:, cs_], k.ident_b[:, :])
                k.copy(V(xtok.t[:, c, :], xtok[:, c].keys), pb[:, 0:256], eng="dve")
                k.copy(V(Btok.t[:, c, :], Btok[:, c].keys), pb[:, 256:384], eng="dve")
            scA.__exit__(None, None, None)
            scB = k.scope(); scB.__enter__()
            Gs = [k.sbuf("s_G%d" % i, [128, 128], F32) for i in range(2)]
            Lt = [k.sbuf("s_Lt%d" % i, [128, 4, 128], F32) for i in range(2)]
            St = [k.sbuf("s_St%d" % i, [128, 4, 128], BF16) for i in range(2)]
            xdt = [k.sbuf("s_xdt%d" % i, [128, 256], BF16) for i in range(2)]
            xde = [k.sbuf("s_xde%d" % i, [128, 256], BF16) for i in range(2)]
            xD = [k.sbuf("s_xD%d" % i, [128, 256], BF16) for i in range(2)]
            htmp = [k.sbuf("s_ht%d" % i, [128, 256], F32) for i in range(2)]
            hstage = [k.sbuf("s_hs%d" % i, [128, 2, 128], F32) for i in range(2)]
            ytmp = [k.sbuf("s_yt%d" % i, [128, 256], F32) for i in range(2)]
            yz = [k.sbuf("s_yz%d" % i, [128, 256], BF16) for i in range(2)]
            for d in range(2):
                hs = hstage[d]
                k.dma("sp", hs[:, :, :], V(h0d.t[d, 4 * g:4 * g + 4].rearrange("(a h) p n -> (h p) a n", a=2), h0d[:].keys))
                bk = k.banks[6]
                for a in range(2):
                    k.transpose(bk[:, a * 128:(a + 1) * 128], hs[:, a, :], k.ident_f[:, :])
                k.copy(hm[d][:, :], bk[:, 0:256], eng="dve")
                k.copy(hb[d][:, :], hm[d][:, :], eng="act")
            for step in range(NCH):
                for d in range(2):
                    c = step if d == 0 else NCH - 1 - step
                    second = (d == 0 and c >= 8) or (d == 1 and c < 8)
                    cs_ = slice(c * 128, (c + 1) * 128)
                    i2 = d
                    bg = k.banks[0]
                    k.mm(bg[:, 0:128], BT[:, cs_], CT[:, cs_])
                    k.copy(Gs[i2][:, :], bg[:, 0:128], eng="act")
                    bl = k.banks[1 + i2]
                    for hh in range(4):
                        acol = V(k.av.t[:, c, d, 4 * g + hh:4 * g + hh + 1], k.av[:].keys)
                        abc = V(acol.ap.to_broadcast([128, 128]), acol.keys)
                        k.mm(bl[:, hh * 128:(hh + 1) * 128], abc, k.tri[:, d, :], start=True, stop=False)
                        k.mm(bl[:, hh * 128:(hh + 1) * 128], k.ident_b[:, :], k.mbias[:, d, :], start=False, stop=True)
                    for hh in range(4):
                        k.act(V(Lt[i2].t[:, hh, :], Lt[i2][:].keys), bl[:, hh * 128:(hh + 1) * 128], AF.Exp,
                              bias=V(k.ncs.t[:, c, d, 4 * g + hh:4 * g + hh + 1], k.ncs[:].keys))
                    k.tt(St[i2][:, :, :], Lt[i2][:, :, :], V(Gs[i2].t[:, :].unsqueeze(1).to_broadcast([128, 4, 128]), Gs[i2][:].keys), ALU.mult)
                    dtb_ = V(k.dt_t.t[:, c, d * 16 + 4 * g:d * 16 + 4 * g + 4].unsqueeze(2).to_broadcast([128, 4, 64]), k.dt_t[:].keys)
                    dte_ = V(k.dte.t[:, c, d, 4 * g:4 * g + 4].unsqueeze(2).to_broadcast([128, 4, 64]), k.dte[:].keys)
                    ecs_ = V(k.ecs.t[:, c, d, 4 * g:4 * g + 4].unsqueeze(2).to_broadcast([128, 4, 64]), k.ecs[:].keys)
                    eto_ = V(k.etot.t[:, c, d, 4 * g:4 * g + 4].unsqueeze(2).to_broadcast([128, 4, 64]), k.etot[:].keys)
                    x3 = V(xtok.t[:, c, :].rearrange("p (h q) -> p h q", h=4), xtok[:, c].keys)
                    v3 = lambda b: V(b.t[:, :].rearrange("p (h q) -> p h q", h=4), b[:].keys)
                    k.tt(v3(xdt[i2]), x3, dtb_, ALU.mult)
                    k.tt(v3(xde[i2]), v3(xdt[i2]), dte_, ALU.mult, eng="pool")
                    yd = k.banks[3 + 2 * d]; yo = SubBuf(k.banks[4 + 2 * d], 0, 256); ps = SubBuf(k.banks[4 + 2 * d], 256, 256)
                    if second:
                        Db = V(Dbc.t[:, 4 * g:4 * g + 4].unsqueeze(2).to_broadcast([128, 4, 64]), Dbc[:].keys)
                        k.tt(v3(xD[i2]), x3, Db, ALU.mult, eng="pool")
                    for hh in range(4):
                        k.mm(yd[:, hh * 64:(hh + 1) * 64], V(St[i2].t[:, hh, :], St[i2][:].keys), xdt[i2][:, hh * 64:(hh + 1) * 64],
                             start=True, stop=(not second))
                        if second:
                            k.mm(yd[:, hh * 64:(hh + 1) * 64], k.ident_b[:, :], xD[i2][:, hh * 64:(hh + 1) * 64], start=False, stop=True)
                    k.mm(yo[:, 0:256], CT[:, cs_], hb[d][:, :])
                    yt = ytmp[i2]
                    k.tt(v3(yt), V(yo.t[:, 0:256].rearrange("p (h q) -> p h q", h=4), yo[:].keys), ecs_, ALU.mult)
                    ya = V(yacc.t[:, c, :], yacc[:, c].keys)
                    if not second:
                        k.tt(ya, yd[:, 0:256], yt[:, :], ALU.add)
                    else:
                        k.tt(yt[:, :], yd[:, 0:256], yt[:, :], ALU.add)
                        k.tt(yt[:, :], yt[:, :], ya, ALU.add, eng="pool")
                        k.tt(yz[i2][:, :], yt[:, :], V(sz.t[:, c, :], sz[:, c].keys), ALU.mult)
                        pb = bfb[i2]
                        for q in range(2):
                            k.transpose(pb[:, q * 128:(q + 1) * 128], yz[i2][:, q * 128:(q + 1) * 128], k.ident_b[:, :])
                        mo = V(k.mixedT.t[:, 2 * g:2 * g + 2, cs_], [(k.mixedT.id, 2 * g), (k.mixedT.id, 2 * g + 1)])
                        k.copy(mo, V(pb.t[:, 0:256].rearrange("p (q t) -> p q t", q=2), pb[:].keys), eng="act")
                    k.mm(ps[:, 0:256], V(Btok.t[:, c, :], Btok[:, c].keys), xde[i2][:, :])
                    ht = htmp[i2]
                    k.tt(v3(ht), v3(hm[d]), eto_, ALU.mult, eng="pool")
                    k.tt(hm[d][:, :], ps[:, 0:256], ht[:, :], ALU.add)
                    last = (c % 2 == 1) if d == 0 else (c % 2 == 0)
                    if last:
                        seg = c // 2
                        bk = k.banks[6]
                        hs2 = hstage[i2]
                        for a in range(2):
                            k.transpose(bk[:, a * 128:(a + 1) * 128], hm[d][:, a * 128:(a + 1) * 128], k.ident_f[:, :])
                        k.copy(V(hs2.t[:, :, :], hs2[:].keys), V(bk.t[:, 0:256].rearrange("p (a n) -> p a n", a=2), bk[:].keys), eng="act")
                        k.dma("sp", V(hout.t[seg, d, 4 * g:4 * g + 4].rearrange("(a h) p n -> (h p) a n", a=2), hout[:].keys), hs2[:, :, :], is_output=True)
                    cn = c + 1 if d == 0 else c - 1
                    if 0 <= cn < NCH:
                        fcol = (16 if d == 0 else 32) + cn
                        k.ts(hm[d][:, :], hm[d][:, :], k.flags[:, fcol:fcol + 1], ALU.mult)
                        k.copy(hb[d][:, :], hm[d][:, :], eng="act")
            scB.__exit__(None, None, None)
        scI.__exit__(None, None, None)
        sq = [k.sbuf("s_sq%d" % i, [128, 512], BF16) for i in range(2)]
        rr = k.sbuf("s_rr", [128, 512], F32); rt = k.sbuf("s_rt", [128, 512], F32)
        for tt in range(NTT):
            sl = slice(tt * 512, (tt + 1) * 512)
            ss = k.banks[tt % 2]
            for fc in range(8):
                s_ = sq[fc % 2]
                k.act(s_[:, :], V(k.mixedT.t[:, fc, sl], k.mixedT[:, fc].keys), AF.Square)
                k.mm(ss[:, :], k.ones_b[:, :], s_[:, :], start=(fc == 0), stop=(fc == 7))
            k.rstd_from(rr[:, :], ss[:, :], 1024.0, rt[:, :])
            for fc in range(8):
                mv_ = V(k.mixedT.t[:, fc, sl], k.mixedT[:, fc].keys)
                k.stt(mv_, mv_, nwc[:, fc:fc + 1], rr[:, :], ALU.mult, ALU.mult)

    def dn(self, win, winv):
        k = self
        cwd = k.din("dn_conv_w", [5, 3072]); nwd = k.din("dn_norm_w", [128])
        s0d = k.din("dn_h0", [2, 8, 128, 128])
        sout = k.dout("dn_out", [NSEG, 2, 8, 128, 128])
        cw = k.sbuf("d_cw", [128, 24, 5], F32)
        for tap in range(5):
            k.load_cols(V(cw.t[:, :, tap], cw[:].keys), cwd.t[tap, :].rearrange("(c p) -> c p", p=128), cwd[:].keys, 24)
        nwb = k.sbuf("d_nwb", [128, 128], F32)
        k.dma("sp", nwb[:, :], V(nwd.t.partition_broadcast(128), nwd[:].keys))
        lnsc = k.sbuf("d_lnsc", [128, 1], F32)
        k.memset(lnsc[:, :], -0.5 * float(np.log(128.0)), eng="pool")
        zero1 = k.sbuf("d_zero1", [128, 1], F32)
        k.memset(zero1[:, :], 0.0, eng="pool")
        qT = k.sbuf("d_qT", [128, T], BF16, gran=128); kT = k.sbuf("d_kT", [128, T], BF16, gran=128); vT = k.sbuf("d_vT", [128, T], BF16, gran=128)
        khtok = k.sbuf("d_khtok", [128, NCH, 128], BF16, gran=1); vtok = k.sbuf("d_vtok", [128, NCH, 128], BF16, gran=1)
        oacc = k.sbuf("d_oacc", [128, NCH, 128], F32, gran=1)
        Wall = k.sbuf("d_Wall", [128, 2 * NCH, 128], BF16, gran=1)
        QKall = k.sbuf("d_QKall", [128, 2 * NCH, 128], BF16, gran=1)
        wg = [k.sbuf("d_wg%d" % i, [128, 8, 128], BF16) for i in range(2)]
        import os
        IDT = BF16 if os.environ.get("DN_INV", "bf16") == "bf16" else F32
        G = 3
        NT = 6
        tmpf = [k.sbuf("d_tf%d" % i, [128, 128], F32) for i in range(NT)]
        tmpb = [k.sbuf("d_tb%d" % i, [128, 128], BF16) for i in range(12)]
        Sm = [k.sbuf("d_S%d" % d, [128, 128], F32) for d in range(2)]
        Sb = [k.sbuf("d_Sb%d" % d, [128, 128], BF16) for d in range(2)]
        fin16 = k.sbuf("d_fin16", [128, 16], F32); fin16b = k.sbuf("d_fin16b", [128, 16], F32)
        pbf = k.psum_bf(1)
        st = {"bank": 0, "tf": 0, "tb": 0, "ev": 0, "lt": 0}
        T_ = k.dtab

        def nbank():
            b = k.banks[st["bank"] % 7]; st["bank"] += 1
            return b

        def tf():
            t = tmpf[st["tf"] % NT]; st["tf"] += 1
            return t

        def tb():
            t = tmpb[st["tb"] % 12]; st["tb"] += 1
            return t

        def evac(dst, src, scale=None):
            st["ev"] += 1
            if scale is not None:
                k.act(dst, src, AF.Copy, scale=scale)
            elif st["ev"] % 3 != 0:
                k.copy(dst, src, eng="act")
            else:
                k.copy(dst, src, eng="dve")

        I_ = k.ident_b if IDT == BF16 else k.ident_f

        def mm1(lhsT, rhs, dst):
            b = nbank()
            k.mm(b[:, 0:128], lhsT, rhs)
            evac(dst, b[:, 0:128])
            return dst

        def mmadd(lhsT, rhs, sb, dst, op=ALU.add, first_sb=False):
            b = nbank()
            k.mm(b[:, 0:128], lhsT, rhs)
            if first_sb:
                k.tt(dst, sb, b[:, 0:128], op)
            else:
                k.tt(dst, b[:, 0:128], sb, op)
            return dst

        F_ = lambda Tq: Tq[:, :, :]
        Q_ = lambda Tq, q: V(Tq.t[:, q, :], Tq[:].keys)
        bc_ = lambda v2: V(v2.ap.unsqueeze(1).to_broadcast([128, 4, 128]), v2.keys)
        bank3 = lambda b: V(b.t[:, :].rearrange("p (q i) -> p q i", q=4), b[:].keys)
        opq = lambda x, q: Q_(x, q) if isinstance(x, Buf) else x

        def mmq(lhsT, rhs, dst):
            b = nbank()
            for q in range(4):
                k.mm(b[:, q * 128:(q + 1) * 128], opq(lhsT, q), opq(rhs, q))
            evac(F_(dst), bank3(b))
            return dst

        def mmaddq(lhsT, rhs, sb, dstv, op=ALU.add, first_sb=False):
            b = nbank()
            for q in range(4):
                k.mm(b[:, q * 128:(q + 1) * 128], opq(lhsT, q), opq(rhs, q))
            if first_sb:
                k.tt(dstv, sb, bank3(b), op)
            else:
                k.tt(dstv, bank3(b), sb, op)

        def quad_gen(h, d, c0, S, LtT):
            u0 = d * NCH + c0
            css = [slice((c0 + q) * 128, (c0 + q + 1) * 128) for q in range(4)]
            U, UT = S[0], S[1]
            s_ = S[2:10]
            Iv = I_[:, :]
            bL = nbank()
            for q in range(4):
                c = c0 + q
                acol = V(T_["av"].t[:, c, d, h:h + 1], T_["av"][:].keys)
                abc = V(acol.ap.to_broadcast([128, 128]), acol.keys)
                k.mm(bL[:, q * 128:(q + 1) * 128], abc, k.tri[:, d, :], start=True, stop=False)
                k.mm(bL[:, q * 128:(q + 1) * 128], k.ident_b[:, :], k.mbias[:, d, :], start=False, stop=True)
            for q in range(4):
                c = c0 + q
                k.act(Q_(LtT, q), bL[:, q * 128:(q + 1) * 128], AF.Exp, bias=V(T_["ncs"].t[:, c, d, h:h + 1], T_["ncs"][:].keys))
            yield
            Ls = s_[7]
            k.tt(F_(Ls), F_(LtT), bc_(k.strict[:, d, :]), ALU.mult, eng="pool")
            bQ = nbank()
            for q in range(4):
                k.mm(bQ[:, q * 128:(q + 1) * 128], kT[:, css[q]], qT[:, css[q]])
            k.tt(V(QKall.t[:, u0:u0 + 4, :], [(QKall.id, u0 + q) for q in range(4)]), bank3(bQ), F_(LtT), ALU.mult)
            yield
            bA = nbank()
            for q in range(4):
                k.mm(bA[:, q * 128:(q + 1) * 128], kT[:, css[q]], kT[:, css[q]])
            for q in range(4):
                c = c0 + q
                beta_ = V(k.beta.t[:, c, d * 8 + h:d * 8 + h + 1], k.beta[:].keys)
                k.stt(Q_(U, q), bA[:, q * 128:(q + 1) * 128], beta_, Q_(Ls, q), ALU.mult, ALU.mult)
            yield
            mmq(U, Iv, UT)
            Ud, UdT, P = s_[0], s_[1], s_[2]
            k.tt(F_(Ud), F_(U), bc_(k.blk_b[:, 0, :]), ALU.mult, eng="pool")
            yield
            k.tt(F_(UdT), F_(UT), bc_(k.blk_b[:, 0, :]), ALU.mult)
            k.tt(F_(P), bc_(Iv), F_(Ud), ALU.subtract)
            yield
            V1 = mmq(UdT, Ud, s_[3]); V1T = mmq(Ud, UdT, s_[4])
            yield
            A1 = s_[5]; k.tt(F_(A1), F_(V1T), bc_(Iv), ALU.add, eng="pool")
            V2 = mmq(V1T, V1, s_[0]); V2T = mmq(V1, V1T, s_[1])
            yield
            P1 = mmq(A1, P, s_[6])
            A3 = s_[3]; mmaddq(V2, V2T, bc_(Iv), F_(A3))
            yield
            A2 = s_[2]; k.tt(F_(A2), F_(V2T), bc_(Iv), ALU.add, eng="pool")
            yield
            P2 = mmq(A2, P1, s_[5])
            yield
            Wd = mmq(A3, P2, s_[4])
            yield
            WdT = s_[6]; mmq(Wd, Iv, WdT)
            slots = {1: (s_[3], s_[7]), 2: (s_[4], s_[6])}
            for lvl in (1, 2, 3):
                B, BT = s_[0], s_[1]
                k.tt(F_(B), F_(U), bc_(k.blk_b[:, lvl, :]), ALU.mult)
                k.tt(F_(BT), F_(UT), bc_(k.blk_b[:, lvl, :]), ALU.mult, eng="pool")
                yield
                Y = mmq(BT, Wd, s_[2])
                if lvl < 3:
                    Yt = mmq(B, WdT, s_[5])
                    yield
                    nW, nWT = slots[lvl]
                    mmaddq(WdT, Y, F_(Wd), F_(nW), op=ALU.subtract, first_sb=True)
                    mmaddq(Wd, Yt, F_(WdT), F_(nWT), op=ALU.subtract, first_sb=True)
                    Wd, WdT = nW, nWT
                    yield
                else:
                    yield
                    mmaddq(WdT, Y, F_(Wd), V(Wall.t[:, u0:u0 + 4, :], [(Wall.id, u0 + q) for q in range(4)]), op=ALU.subtract, first_sb=True)

        def chain_step(h, d, c):
            cs_ = slice(c * 128, (c + 1) * 128)
            u = d * NCH + c
            col = lambda nm: V(T_[nm].t[:, c, d, h:h + 1], T_[nm][:].keys)
            beta_ = V(k.beta.t[:, c, d * 8 + h:d * 8 + h + 1], k.beta[:].keys)
            Wb = V(Wall.t[:, u, :], Wall[:, u].keys); QKm = V(QKall.t[:, u, :], QKall[:, u].keys)
            kend = tb()[:, :]
            k.act(kend, V(khtok.t[:, c, :], khtok[:, c].keys), AF.Copy, scale=col("dte"))
            bK = nbank(); k.mm(bK[:, 0:128], kT[:, cs_], Sb[d][:, :])
            bS = nbank(); k.mm(bS[:, 0:128], qT[:, cs_], Sb[d][:, :])
            R0 = tb()[:, :]
            k.stt(R0, bK[:, 0:128], col("necs"), V(vtok.t[:, c, :], vtok[:, c].keys), ALU.mult, ALU.add)
            t_ = tf()[:, :]
            k.act(t_, bS[:, 0:128], AF.Copy, scale=col("ecs"))
            bV = nbank(); k.mm(bV[:, 0:128], Wb, R0)
            vnew = tb()[:, :]
            k.act(vnew, bV[:, 0:128], AF.Copy, scale=beta_)
            bU = nbank(); k.mm(bU[:, 0:128], kend, vnew)
            bO = nbank(); k.mm(bO[:, 0:128], QKm, vnew)
            k.stt(Sm[d][:, :], Sm[d][:, :], col("etot"), bU[:, 0:128], ALU.mult, ALU.add)
            last = (c % 2 == 1) if d == 0 else (c % 2 == 0)
            if last:
                k.dma("sp", V(sout.t[c // 2, d, h], sout[:].keys), Sm[d][:, :], is_output=True)
            cn = c + 1 if d == 0 else c - 1
            if 0 <= cn < NCH:
                fcol = (16 if d == 0 else 32) + cn
                k.ts(Sm[d][:, :], Sm[d][:, :], k.flags[:, fcol:fcol + 1], ALU.mult)
                k.copy(Sb[d][:, :], Sm[d][:, :], eng="act")
            oa = V(oacc.t[:, c, :], oacc[:, c].keys)
            first = (d == 0 and c < 8) or (d == 1 and c >= 8)
            if first:
                k.tt(oa, bO[:, 0:128], t_, ALU.add)
            else:
                k.tt(t_, bO[:, 0:128], t_, ALU.add)
                k.tt(oa, oa, t_, ALU.add, eng="pool")

        k.marks = getattr(k, "marks", [])
        mk_ = lambda lab: k.marks.append((lab, k.cnt["e_pe"]))
        for h in range(8):
            mk_("dn%d:proj" % h)
            sc1 = k.scope(); sc1.__enter__()
            k.pc_alloc()
            sqb = [k.sbuf("d_sq%d" % i, [128, 512], BF16) for i in range(2)]
            rr = [k.sbuf("d_rr%d" % i, [128, 512], F32) for i in range(2)]
            for (buf, coff, cc) in ((qT, 3104, h), (kT, 4128, 8 + h), (vT, 5152, 16 + h)):
                k.proj_conv5(win, winv, coff + h * 128, lambda tap, cc=cc: cw[:, cc, tap:tap + 1], None,
                             lambda seg, buf=buf: buf[:, seg * 256:(seg + 1) * 256])
            mk_("dn%d:l2" % h)
            for (buf, isq) in ((qT, True), (kT, False)):
                for tt in range(NTT):
                    sl = slice(tt * 512, (tt + 1) * 512)
                    s2 = sqb[tt % 2]; r_ = rr[tt % 2]
                    k.act(s2[:, :], buf[:, sl], AF.Square)
                    b = nbank()
                    k.mm(b[:, :], k.ones_b[:, :], s2[:, :])
                    k.act(r_[:, :], b[:, :], AF.Ln, bias=k.eps_t[:, 0:1])
                    k.act(r_[:, :], r_[:, :], AF.Exp, scale=-0.5, bias=(lnsc[:, 0:1] if isq else zero1[:, 0:1]))
                    k.tt(buf[:, sl], buf[:, sl], r_[:, :], ALU.mult)
            w = wg[h % 2]
            k.dma("pool", w[:, :, :], V(winv[:, :, 6176 + h * 128:6176 + (h + 1) * 128], win[:].keys))
            for c in range(NCH):
                cs_ = slice(c * 128, (c + 1) * 128)
                k.transpose(pbf[:, 0:128], kT[:, cs_], k.ident_b[:, :])
                k.transpose(pbf[:, 128:256], vT[:, cs_], k.ident_b[:, :])
                k.copy(V(khtok.t[:, c, :], khtok[:, c].keys), pbf[:, 0:128], eng="dve")
                k.copy(V(vtok.t[:, c, :], vtok[:, c].keys), pbf[:, 128:256], eng="dve")
            sc1.__exit__(None, None, None)
            mk_("dn%d:P" % h)
            sc2 = k.scope(); sc2.__enter__()
            scr = [[k.sbuf("d_s%d_%d" % (g_, i), [128, 4, 128], IDT) for i in range(10)] for g_ in range(G)]
            ltf = [k.sbuf("d_lt%d" % i, [128, 4, 128], F32) for i in range(G)]
            quads = [(d, c0) for d in range(2) for c0 in range(0, NCH, 4)]
            for g0 in range(0, len(quads), G):
                gens = [quad_gen(h, d, c0, scr[i], ltf[i]) for i, (d, c0) in enumerate(quads[g0:g0 + G])]
                while gens:
                    for g_ in list(gens):
                        try:
                            next(g_)
                        except StopIteration:
                            gens.remove(g_)
            sc2.__exit__(None, None, None)
            mk_("dn%d:C" % h)
            for d in range(2):
                k.dma("sp", Sm[d][:, :], V(s0d.t[d, h], s0d[:].keys))
                k.copy(Sb[d][:, :], Sm[d][:, :], eng="act")
            for step in range(NCH):
                chain_step(h, 0, step)
                chain_step(h, 1, NCH - 1 - step)
            mk_("dn%d:fin" % h)
            for c in range(NCH):
                oa = V(oacc.t[:, c, :], oacc[:, c].keys)
                junk = tf()[:, :]
                k.act(junk, oa, AF.Square, accum_out=fin16[:, c:c + 1])
            k.act(fin16b[:, :], fin16[:, :], AF.Sqrt, bias=k.eps_t[:, 0:1], scale=1.0 / 128.0)
            k.recip(fin16[:, :], fin16b[:, :])
            for c in range(NCH):
                cs_ = slice(c * 128, (c + 1) * 128)
                b = nbank()
                for kc in range(8):
                    k.mm(b[:, 0:128], k.hT[kc][:, PAD + c * 128:PAD + (c + 1) * 128], w[:, kc, :], start=(kc == 0), stop=(kc == 7))
                sg = tb()[:, :]
                k.act(sg, b[:, 0:128], AF.Silu)
                oa = V(oacc.t[:, c, :], oacc[:, c].keys)
                o1 = tf()[:, :]
                k.stt(o1, oa, fin16[:, c:c + 1], nwb[:, :], ALU.mult, ALU.mult)
                ob = tb()[:, :]
                k.tt(ob, o1, sg, ALU.mult)
                k.transpose(pbf[:, 0:128], ob, k.ident_b[:, :])
                k.copy(V(k.mixedT.t[:, 8 + h, cs_], k.mixedT[:, 8 + h].keys), pbf[:, 0:128], eng="dve")


def core_inputs(inputs, core):
    f32 = np.float32
    d = {}
    flags = np.zeros(128, f32)
    rope = np.zeros((2, 128, T), f32)
    if core < 4:
        rope[0] = 1.0
        d["ctx_k"] = np.zeros((512, 256), f32)
        d["ctx_v"] = np.zeros((512, 256), f32)
        flags[112] = -30000.0
        for c in range(16):
            flags[48 + c] = 0.0; flags[64 + c] = 1.0 if c % 2 == 1 else 0.0
            flags[80 + c] = 0.0; flags[96 + c] = 1.0 if c % 2 == 0 else 0.0
        d["x"] = np.ascontiguousarray(inputs["x_prompt"][core * 8:(core + 1) * 8].reshape(T, D))
        d["cond"] = np.ascontiguousarray(inputs["c_ctx"])
        for c in range(16):
            flags[16 + c] = 0.0 if c % 2 == 0 else 1.0
            flags[32 + c] = 0.0 if c % 2 == 1 else 1.0
    else:
        b = (core - 4) % 2
        d["ctx_k"] = np.ascontiguousarray(inputs["cache_l1_k"][b].reshape(512, 256))
        d["ctx_v"] = np.ascontiguousarray(inputs["cache_l1_v"][b].reshape(512, 256))
        pos = np.arange(T)
        inv = (10000.0 ** (-np.arange(32, dtype=np.float64) / 32.0))
        ang = np.concatenate([(pos // 64)[:, None] * inv[None, :], (pos % 64)[:, None] * inv[None, :]], axis=1)
        ang = ang.astype(f32)
        rope[0] = np.concatenate([np.cos(ang), np.cos(ang)], axis=1).T
        rope[1] = np.concatenate([np.sin(ang), np.sin(ang)], axis=1).T
        for c in range(16):
            flags[48 + c] = 1.0; flags[80 + c] = 1.0
        d["x"] = np.ascontiguousarray(inputs["x_sample"][b])
        d["cond"] = np.ascontiguousarray(inputs["c"][b])
        for s in range(8):
            flags[2 * s] = 0.0 if s == 0 else 1.0
            flags[2 * s + 1] = 0.0 if s == 7 else 1.0
        for c in range(16):
            flags[16 + c] = 0.0 if c == 0 else 1.0
            flags[32 + c] = 0.0 if c == 15 else 1.0
    d["flags"] = flags
    d["rope"] = rope
    d["cst_ident"] = np.eye(128, dtype=f32)
    sq = np.arange(128)
    band = np.zeros((2, 128, 512), f32)
    band[0] = np.tile((sq[:, None] >= sq[None, :]).astype(f32), (1, 4))
    band[1] = np.tile((sq[:, None] <= sq[None, :]).astype(f32), (1, 4))
    d["cst_band"] = band
    rot = np.zeros((128, 128), f32)
    for dp in range(64):
        rot[dp + 64, dp] = -1.0
        rot[dp, dp + 64] = 1.0
    d["cst_rot"] = rot
    tri = np.zeros((2, 128, 128), f32)
    tri[0] = (sq[:, None] <= sq[None, :]).astype(f32)
    tri[1] = (sq[:, None] >= sq[None, :]).astype(f32)
    d["cst_tri"] = tri
    mbias = np.zeros((2, 128, 128), f32)
    mbias[0] = np.where(sq[None, :] >= sq[:, None], 0.0, -30000.0)
    mbias[1] = np.where(sq[None, :] <= sq[:, None], 0.0, -30000.0)
    d["cst_mbias"] = mbias
    strict = np.zeros((2, 128, 128), f32)
    strict[0] = (sq[None, :] > sq[:, None]).astype(f32)
    strict[1] = (sq[None, :] < sq[:, None]).astype(f32)
    d["cst_strict"] = strict
    blk = np.zeros((4, 128, 128), f32)
    bd = lambda b: ((sq[:, None] // b) == (sq[None, :] // b)).astype(f32)
    blk[0] = bd(16); blk[1] = bd(32) - bd(16); blk[2] = bd(64) - bd(32); blk[3] = 1.0 - bd(64)
    d["cst_blk"] = blk
    if core < 4:
        d["ssd_h0"] = np.zeros((2, 16, 64, 128), f32)
        d["dn_h0"] = np.zeros((2, 8, 128, 128), f32)
    else:
        d["ssd_h0"] = np.ascontiguousarray(inputs["state_l0_ssd"][(core - 4) % 2])
        d["dn_h0"] = np.ascontiguousarray(inputs["state_l0_dn"][(core - 4) % 2])
    return d


def build(stages):
    k = Net()
    k.load_consts()
    k.load_x()
    for st in stages:
        if st[0] == "ffn":
            l = st[1]
            cols = k.modulation(l)
            k.ffn(l, cols)
        elif st[0] == "mix0":
            cols = k.modulation(0)
            k.mixer_l0(cols)
        elif st[0] == "mix1":
            cols = k.modulation(1)
            k.mixer_l1(cols)
        elif st[0] == "mod":
            cols = k.modulation(st[1])
        elif st[0] == "pre":
            cols = k.modulation(st[1])
            k.prenorm(cols, 3, 4)
    k.store_y()
    k.finish()
    return k


def run(inputs, stages, n_cores=8):
    from concourse.bass_utils import run_bass_kernel_spmd
    k = build(stages)
    in_maps = []
    for c in range(n_cores):
        d = core_inputs(inputs, c)
        m = {}
        for name in k.inp:
            if name in d:
                m[name] = d[name]
            else:
                m[name] = np.ascontiguousarray(np.asarray(inputs[name], dtype=np.float32))
        in_maps.append(m)
    res = run_bass_kernel_spmd(k.nc, in_maps, core_ids=list(range(n_cores)))
    return k, res


FULL = [("full",)]


def build_full():
    k = Net()
    k.load_consts()
    k.load_x()
    import os
    nsub = int(os.environ.get("FULL_N", 4))
    cols0 = k.modulation(0)
    k.mixer_l0(cols0)
    if nsub >= 2:
        k.ffn(0, cols0)
    if nsub >= 3:
        cols1 = k.modulation(1)
        k.mixer_l1(cols1)
    if nsub >= 4:
        k.ffn(1, cols1)
    k.store_y()
    k.finish()
    return k


def kernel(**inputs):
    from concourse.bass_utils import run_bass_kernel_spmd
    inputs = {n: np.asarray(v) for n, v in inputs.items()}
    used = ("x_prompt", "x_sample", "state_l0_ssd", "state_l0_dn", "cache_l1_k", "cache_l1_v", "c", "c_ctx",
            "mod_w_l0", "mod_b_l0", "norm_mix_pre_l0", "norm_mix_post_l0", "norm_ffn_pre_l0", "norm_ffn_post_l0",
            "ffn_up_l0", "ffn_conv_w_l0", "ffn_conv_b_l0", "ffn_down_l0",
            "mod_w_l1", "mod_b_l1", "norm_mix_pre_l1", "norm_mix_post_l1", "norm_ffn_pre_l1", "norm_ffn_post_l1",
            "ffn_up_l1", "ffn_conv_w_l1", "ffn_conv_b_l1", "ffn_down_l1",
            "mix_in_l0", "mix_out_l0", "ssd_conv_w", "ssd_conv_b", "ssd_dt_bias", "ssd_A_log", "ssd_D", "ssd_norm_w",
            "dn_conv_w", "dn_dt_bias", "dn_A_log", "dn_norm_w",
            "mix_in_l1", "mix_out_l1", "sg_ln_w", "sg_ln_b", "sg_w_s", "sg_b_s", "attn_sink")
    assert all(n in inputs for n in used)
    k = build_full()
    in_maps = []
    for c in range(8):
        d = core_inputs(inputs, c)
        m = {}
        for name in k.inp:
            m[name] = d[name] if name in d else np.ascontiguousarray(np.asarray(inputs[name], dtype=np.float32))
        in_maps.append(m)
    res = run_bass_kernel_spmd(k.nc, in_maps, core_ids=list(range(8)))
    r = res.results
    f32 = np.float32
    y_prompt = np.concatenate([r[c]["y"].reshape(8, 256, D) for c in range(4)], axis=0).astype(f32)
    y_sample = np.stack([r[4]["y"], r[5]["y"]], axis=0).astype(f32)
    new_ssd = np.concatenate([r[c]["ssd_out"] for c in range(4)], axis=0).astype(f32)
    new_dn = np.concatenate([r[c]["dn_out"] for c in range(4)], axis=0).astype(f32)
    new_k = np.concatenate([r[c]["nk"].reshape(8, 256, 2, 128) for c in range(4)], axis=0).astype(f32)
    new_v = np.concatenate([r[c]["nv"].reshape(8, 256, 2, 128) for c in range(4)], axis=0).astype(f32)
    return (y_prompt, y_sample, new_ssd, new_dn, new_k, new_v)
```

```python
import contextlib
import numpy as np
import concourse.bass as bass
import concourse.mybir as mybir

F32 = mybir.dt.float32
BF16 = mybir.dt.bfloat16
AF = mybir.ActivationFunctionType
ALU = mybir.AluOpType
AX = mybir.AxisListType

SAME_ENGINE_SYNC = True


class V:
    __slots__ = ("ap", "keys")

    def __init__(self, ap, keys):
        self.ap = ap
        self.keys = keys


class Buf:
    _n = 0

    def __init__(self, kb, t, shape, gran=None):
        self.kb = kb
        self.t = t
        self.shape = list(shape)
        Buf._n += 1
        self.id = Buf._n
        f0 = self.shape[1] if len(self.shape) > 1 else 1
        self.gran = gran if gran else f0
        self.ng = (f0 + self.gran - 1) // self.gran

    def _keys(self, idx):
        if not isinstance(idx, tuple):
            idx = (idx,)
        lo, hi = 0, self.ng - 1
        if len(idx) > 1:
            i1 = idx[1]
            if isinstance(i1, slice):
                a = 0 if i1.start is None else i1.start
                b = self.shape[1] if i1.stop is None else i1.stop
                lo, hi = a // self.gran, (b - 1) // self.gran
            elif isinstance(i1, int):
                lo = hi = i1 // self.gran
        return [(self.id, g) for g in range(lo, hi + 1)]

    def __getitem__(self, idx):
        return V(self.t[idx], self._keys(idx))

    def v(self, ap, idx=None):
        return V(ap, self._keys(idx) if idx is not None else [(self.id, g) for g in range(self.ng)])


class KB:
    ENG = ("pe", "act", "dve", "pool", "sp")

    def __init__(self):
        self.nc = bass.Bass("TRN2", target_bir_lowering=False)
        self.es = contextlib.ExitStack()
        nc = self.nc
        self.E = {"pe": nc.tensor, "act": nc.scalar, "dve": nc.vector, "pool": nc.gpsimd, "sp": nc.sync}
        self.sems = {}
        self.cnt = {}
        for e in self.ENG:
            self.sems["e_" + e] = self.es.enter_context(nc.semaphore("s_" + e))
            self.cnt["e_" + e] = 0
        self.NRING = 12
        for q in ("hw", "sw"):
            for i in range(self.NRING):
                nm = "d_%s%d" % (q, i)
                self.sems[nm] = self.es.enter_context(nc.semaphore(nm))
                self.cnt[nm] = 0
        self.ring_i = {"hw": 0, "sw": 0}
        self.seen = {e: {} for e in self.ENG}
        self.last_w = {}
        self.readers = {}
        self.n_inst = 0
        self.n_wait = 0
        self.out_deps = []
        self.stack = [self.es]
        self.waits_by = {}
        self.cur_birth = None
        self.birth = {}
        self.birth_done = set()

    @contextlib.contextmanager
    def scope(self):
        es = contextlib.ExitStack()
        self.stack.append(es)
        try:
            yield
        finally:
            self.stack.pop()
            es.close()
            self.cur_birth = tuple((s, v) for s, v in self.cnt.items() if v > 0)

    def sbuf(self, name, shape, dtype=F32, gran=None):
        self._nalloc = getattr(self, "_nalloc", 0) + 1
        t = self.stack[-1].enter_context(self.nc.sbuf_tensor("%s_%d" % (name, self._nalloc), list(shape), dtype))
        b = Buf(self, t, shape, gran)
        if self.cur_birth:
            self.birth[b.id] = self.cur_birth
        return b

    def psum(self, name, shape, dtype=F32, gran=None):
        t = self.stack[-1].enter_context(self.nc.psum_tensor(name, list(shape), dtype))
        return Buf(self, t, shape, gran)

    def dram(self, name, shape, dtype=F32, kind="Internal", gran=None):
        t = self.nc.dram_tensor(name, list(shape), dtype, kind=kind)
        return Buf(self, t.ap() if hasattr(t, "ap") else t, shape, gran)

    def _collect(self, reads, writes, eng=None):
        deps = {}

        def add(d):
            if d is None:
                return
            s, v = d
            if deps.get(s, 0) < v:
                deps[s] = v
        if self.birth:
            for r in list(reads) + list(writes):
                for k in r.keys:
                    bid = k[0]
                    if bid in self.birth and (eng, bid) not in self.birth_done:
                        self.birth_done.add((eng, bid))
                        for d in self.birth[bid]:
                            add(d)
        for r in reads:
            for k in r.keys:
                add(self.last_w.get(k))
        for w in writes:
            for k in w.keys:
                add(self.last_w.get(k))
                for d in self.readers.get(k, ()):
                    add(d)
        return deps

    def _emit_waits(self, eng, deps, war_only_same=None):
        need = []
        own = "e_" + eng
        for s, v in deps.items():
            if s == own and (not SAME_ENGINE_SYNC or eng == "pe"):
                continue
            if self.seen[eng].get(s, 0) >= v:
                continue
            need.append((s, v))
        return need

    def _record(self, mark, reads, writes):
        for r in reads:
            for k in r.keys:
                self.readers.setdefault(k, []).append(mark)
        for w in writes:
            for k in w.keys:
                self.last_w[k] = mark
                self.readers[k] = []

    def op(self, eng, fn, reads, writes):
        deps = self._collect(reads, writes, eng)
        need = self._emit_waits(eng, deps)
        E = self.E[eng]
        for s, v in need[:-1]:
            E.wait_ge(self.sems[s], v)
            self.n_wait += 1
            self.waits_by[eng] = self.waits_by.get(eng, 0) + 1
        inst = fn()
        if need:
            s, v = need[-1]
            inst._wait_ge(self.sems[s], v)
        for s, v in need:
            self.seen[eng][s] = v
        own = "e_" + eng
        self.cnt[own] += 1
        inst.then_inc(self.sems[own], 1)
        mark = (own, self.cnt[own])
        self._record(mark, reads, writes)
        self.n_inst += 1
        return inst

    def dma(self, queue, out, in_, is_output=False, **kw):
        ring = "sw" if queue == "pool" else "hw"
        i = self.ring_i[ring]
        self.ring_i[ring] += 1
        nm = "d_%s%d" % (ring, i % self.NRING)
        deps = self._collect([in_], [out], queue)
        if self.cnt[nm] > 0:
            if deps.get(nm, 0) < self.cnt[nm]:
                deps[nm] = self.cnt[nm]
        need = self._emit_waits(queue, deps)
        E = self.E[queue]
        for s, v in need[:-1]:
            E.wait_ge(self.sems[s], v)
            self.n_wait += 1
        inst = E.dma_start(out=out.ap, in_=in_.ap, **kw)
        if need:
            s, v = need[-1]
            inst._wait_ge(self.sems[s], v)
        for s, v in need:
            self.seen[queue][s] = v
        self.cnt[nm] += 16
        inst.then_inc(self.sems[nm], 16)
        mark = (nm, self.cnt[nm])
        self._record(mark, [in_], [out])
        if is_output:
            self.out_deps.append(mark)
        self.n_inst += 1
        return inst

    def finish(self):
        for s, v in self.cnt.items():
            if v > 0:
                self.E["sp"].wait_ge(self.sems[s], v)

    def mm(self, out, lhsT, rhs, start=True, stop=True, **kw):
        return self.op("pe", lambda: self.nc.tensor.matmul(out.ap, lhsT=lhsT.ap, rhs=rhs.ap, start=start, stop=stop, **kw),
                       [lhsT, rhs] + ([] if start else [out]), [out])

    def transpose(self, out, in_, ident):
        return self.op("pe", lambda: self.nc.tensor.transpose(out.ap, in_.ap, ident.ap), [in_, ident], [out])

    def act(self, out, in_, func, bias=None, scale=None, accum_out=None, eng="act"):
        reads = [in_]
        kw = {}
        if bias is not None:
            if isinstance(bias, V):
                reads.append(bias); kw["bias"] = bias.ap
            else:
                kw["bias"] = bias
        if scale is not None:
            if isinstance(scale, V):
                reads.append(scale); kw["scale"] = scale.ap
            else:
                kw["scale"] = scale
        writes = [out]
        if accum_out is not None:
            writes.append(accum_out); kw["accum_out"] = accum_out.ap
        return self.op("act", lambda: self.nc.scalar.activation(out=out.ap, in_=in_.ap, func=func, **kw), reads, writes)

    def tt(self, out, in0, in1, op, eng="dve"):
        E = self.E[eng]
        return self.op(eng, lambda: E.tensor_tensor(out=out.ap, in0=in0.ap, in1=in1.ap, op=op), [in0, in1], [out])

    def ts(self, out, in0, s1, op0, s2=None, op1=None, eng="dve", accum_out=None):
        E = self.E[eng]
        reads = [in0]
        a1 = s1.ap if isinstance(s1, V) else s1
        a2 = s2.ap if isinstance(s2, V) else s2
        if isinstance(s1, V): reads.append(s1)
        if isinstance(s2, V): reads.append(s2)
        kw = {}
        writes = [out]
        if op1 is not None:
            kw["op1"] = op1
        if accum_out is not None:
            kw["accum_out"] = accum_out.ap; writes.append(accum_out)
        return self.op(eng, lambda: E.tensor_scalar(out=out.ap, in0=in0.ap, scalar1=a1, scalar2=a2, op0=op0, **kw), reads, writes)

    def stt(self, out, in0, scalar, in1, op0, op1, eng="dve"):
        E = self.E[eng]
        reads = [in0, in1]
        a = scalar.ap if isinstance(scalar, V) else scalar
        if isinstance(scalar, V): reads.append(scalar)
        return self.op(eng, lambda: E.scalar_tensor_tensor(out=out.ap, in0=in0.ap, scalar=a, in1=in1.ap, op0=op0, op1=op1), reads, [out])

    def copy(self, out, in_, eng="dve"):
        if eng == "act":
            return self.op("act", lambda: self.nc.scalar.copy(out=out.ap, in_=in_.ap), [in_], [out])
        E = self.E[eng]
        return self.op(eng, lambda: E.tensor_copy(out=out.ap, in_=in_.ap), [in_], [out])

    def memset(self, out, val, eng="dve"):
        E = self.E[eng]
        return self.op(eng, lambda: E.memset(out.ap, val), [], [out])

    def recip(self, out, in_):
        return self.op("dve", lambda: self.nc.vector.reciprocal(out=out.ap, in_=in_.ap), [in_], [out])

T = 2048
D = 1024
NCH = 16
NTT = 4
NSEG = 8
DFF = 2816
NJ = DFF // 128
EPS = 1e-6
PAD = 2
L0_PROJ = 7232
L1_PROJ = 3584


class SubBuf:
    def __init__(self, buf, off, width):
        self.buf = buf; self.off = off; self.width = width
        self.id = buf.id
        self.t = buf.t[:, off:off + width]
        self._k = buf._keys((slice(None), slice(off, off + width)))

    def __getitem__(self, idx):
        return V(self.t[idx], self._k)


class Net(KB):
    def __init__(self, dbg=None):
        super().__init__()
        self.dbg = dbg or {}
        self.inp = {}
        self.banks = [self.psum("bank%d" % i, [128, 512], F32) for i in range(7)]
        self.psum_bf(0)
        self.banks.append(None)

    def psum_bf(self, i):
        if not hasattr(self, "_bfb"):
            big = self.psum("bfbig", [128, 1024], BF16)
            self._bfb = [SubBuf(big, j * 512, 512) for j in range(2)]
        return self._bfb[i]

    def din(self, name, shape, dtype=F32, gran=None):
        b = self.dram(name, shape, dtype, kind="ExternalInput", gran=gran)
        self.inp[name] = b
        return b

    def dout(self, name, shape, dtype=F32, gran=None):
        return self.dram(name, shape, dtype, kind="ExternalOutput", gran=gran)

    def load_cols(self, dst, src_ap, src_keys, n):
        k = self
        st = k.colstage[k.ncst % 2]; k.ncst += 1
        k.dma("sp", st[0:n, :], V(src_ap, src_keys))
        bk = k.banks[6]
        k.transpose(bk[:, 0:n], st[0:n, :], k.ident_f[0:n, 0:n])
        k.copy(dst, bk[:, 0:n], eng="dve")

    def load_consts(self):
        k = self
        c = k.din("cst_ident", [128, 128])
        k.ident_f = k.sbuf("ident_f", [128, 128], F32)
        k.dma("sp", k.ident_f[:, :], c[:, :])
        k.ident_b = k.sbuf("ident_b", [128, 128], BF16)
        k.copy(k.ident_b[:, :], k.ident_f[:, :], eng="dve")
        k.ones_b = k.sbuf("ones_b", [128, 128], BF16)
        k.memset(k.ones_b[:, :], 1.0, eng="pool")
        k.ones_f = k.sbuf("ones_f", [128, 128], F32)
        k.memset(k.ones_f[:, :], 1.0, eng="pool")
        k.colstage = [k.sbuf("colstage%d" % i, [128, 128], F32) for i in range(2)]
        k.ncst = 0
        k.eps_t = k.sbuf("eps_t", [128, 1], F32)
        k.memset(k.eps_t[:, :], EPS, eng="pool")
        fl = k.din("flags", [128])
        k.flags = k.sbuf("flags_t", [128, 128], F32)
        k.dma("sp", k.flags[:, :], V(fl.t.partition_broadcast(128), fl[:].keys))
        k.xs = [k.dram("xs%d" % fc, [128, T], F32, gran=512) for fc in range(8)]
        k.hT = [k.sbuf("hT%d" % fc, [128, T + 2 * PAD], BF16, gran=None) for fc in range(8)]
        for fc in range(8):
            k.memset(k.hT[fc][:, 0:PAD], 0.0, eng="pool")
            k.memset(k.hT[fc][:, T + PAD:T + 2 * PAD], 0.0, eng="pool")

    def load_x(self):
        k = self
        x = k.din("x", [T, D], gran=None)
        sc = k.scope(); sc.__enter__()
        xin = [k.sbuf("xin%d" % i, [128, 4, D], F32) for i in range(2)]
        xt = [k.sbuf("xt%d" % i, [128, 512], F32) for i in range(4)]
        n = 0
        for tt in range(NTT):
            xi = xin[tt % 2]
            k.dma("sp", xi[:, :, :], V(x.t[tt * 512:(tt + 1) * 512, :].rearrange("(c p) d -> p c d", p=128), x[:].keys))
            for fc in range(8):
                bk = k.banks[n % 4]
                for c in range(4):
                    k.transpose(bk[:, c * 128:(c + 1) * 128], xi[:, c, fc * 128:(fc + 1) * 128], k.ident_f[:, :])
                xo = xt[n % 4]
                if n % 2 == 0:
                    k.copy(xo[:, :], bk[:, :], eng="act")
                else:
                    k.copy(xo[:, :], bk[:, :], eng="dve")
                k.dma("sp", k.xs[fc][:, tt * 512:(tt + 1) * 512], xo[:, :])
                n += 1
        sc.__exit__(None, None, None)

    def store_y(self):
        k = self
        y = k.dout("y", [T, D])
        sc = k.scope(); sc.__enter__()
        xl = [k.sbuf("yl%d" % i, [128, 512], F32) for i in range(4)]
        yo = [k.sbuf("yo%d" % i, [128, 4, D], F32) for i in range(2)]
        n = 0
        for tt in range(NTT):
            yt = yo[tt % 2]
            for fc in range(8):
                xi = xl[n % 4]
                k.dma("sp", xi[:, :], k.xs[fc][:, tt * 512:(tt + 1) * 512])
                bk = k.banks[n % 4]
                for c in range(4):
                    k.transpose(bk[:, c * 128:(c + 1) * 128], xi[:, c * 128:(c + 1) * 128], k.ident_f[:, :])
                o = V(yt.t[:, :, fc * 128:(fc + 1) * 128], yt[:].keys)
                src = V(bk.t[:, :].rearrange("p (c f) -> p c f", c=4), bk[:].keys)
                if n % 2 == 0:
                    k.copy(o, src, eng="act")
                else:
                    k.copy(o, src, eng="dve")
                n += 1
            k.dma("sp", V(y.t[tt * 512:(tt + 1) * 512, :].rearrange("(c p) d -> p c d", p=128), y[:].keys), yt[:, :, :], is_output=True)
        sc.__exit__(None, None, None)

    def modulation(self, l):
        k = self
        L = "l%d" % l
        cond = k.inp.get("cond") or k.din("cond", [D])
        mw = k.din("mod_w_" + L, [D, 6 * D])
        mb = k.din("mod_b_" + L, [6 * D])
        nws = [k.din(n + L, [D]) for n in ("norm_mix_pre_", "norm_mix_post_", "norm_ffn_pre_", "norm_ffn_post_")]
        cols = k.sbuf("modcols_" + L, [128, 6, 8], F32)
        sc = k.scope(); sc.__enter__()
        cs = k.sbuf("cond_" + L, [128, 8], F32)
        k.load_cols(cs[:, :], cond.t.rearrange("(c p) -> c p", p=128), cond[:].keys, 8)
        mbt = k.sbuf("modb_" + L, [128, 48], F32)
        k.load_cols(mbt[:, :], mb.t.rearrange("(j p) -> j p", p=128), mb[:].keys, 48)
        nwt = k.sbuf("nw_" + L, [128, 4, 8], F32)
        for i, nw in enumerate(nws):
            k.load_cols(nwt[:, i, :], nw.t.rearrange("(c p) -> c p", p=128), nw[:].keys, 8)
        sb = k.sbuf("scond_" + L, [128, 8], BF16)
        k.act(sb[:, :], cs[:, :], AF.Silu)
        wbuf = [k.sbuf("modw%d_%s" % (i, L), [128, 8, 512], BF16) for i in range(2)]
        mps = k.banks[6]
        mwv = mw.t.rearrange("(c p) n -> p c n", p=128)
        for blk in range(12):
            wb = wbuf[blk % 2]
            k.dma("pool", wb[:, :, :], V(mwv[:, :, blk * 512:(blk + 1) * 512], mw[:].keys))
            for nn in range(4):
                j = blk * 4 + nn
                for kc in range(8):
                    k.mm(mps[:, j:j + 1], wb[:, kc, nn * 128:(nn + 1) * 128], sb[:, kc:kc + 1], start=(kc == 0), stop=(kc == 7))
        mod = k.sbuf("mod_" + L, [128, 48], F32)
        k.tt(mod[:, :], mps[:, 0:48], mbt[:, :], ALU.add)
        k.stt(cols[:, 0, :], mod[:, 8:16], 1.0, nwt[:, 0, :], ALU.add, ALU.mult)
        k.copy(cols[:, 1, :], mod[:, 0:8], eng="dve")
        k.tt(cols[:, 2, :], mod[:, 16:24], nwt[:, 1, :], ALU.mult)
        k.stt(cols[:, 3, :], mod[:, 32:40], 1.0, nwt[:, 2, :], ALU.add, ALU.mult)
        k.copy(cols[:, 4, :], mod[:, 24:32], eng="dve")
        k.tt(cols[:, 5, :], mod[:, 40:48], nwt[:, 3, :], ALU.mult)
        sc.__exit__(None, None, None)
        return cols

    def rstd_from(self, out, ss, n, tmp):
        k = self
        k.act(tmp, ss, AF.Ln, bias=k.eps_t[:, 0:1], scale=1.0 / n)
        k.act(out, tmp, AF.Exp, scale=-0.5)

    def prenorm(self, cols, ia, ib):
        k = self
        sc = k.scope(); sc.__enter__()
        k.pn_x = [k.sbuf("pn_x%d" % i, [128, 512], F32) for i in range(10)]
        k.pn_sq = [k.sbuf("pn_sq%d" % i, [128, 512], BF16) for i in range(3)]
        k.pn_r = [k.sbuf("pn_r%d" % i, [128, 512], F32) for i in range(2)]
        k.pn_t = [k.sbuf("pn_t%d" % i, [128, 512], F32) for i in range(3)]
        n = 0
        for tt in range(NTT):
            sl = slice(tt * 512, (tt + 1) * 512)
            ss = k.banks[tt % 2]
            xs_t = []
            for fc in range(8):
                xb = k.pn_x[(tt * 8 + fc) % 10]
                k.dma("sp", xb[:, :], k.xs[fc][:, sl])
                sq = k.pn_sq[n % 3]; n += 1
                k.act(sq[:, :], xb[:, :], AF.Square)
                k.mm(ss[:, :], k.ones_b[:, :], sq[:, :], start=(fc == 0), stop=(fc == 7))
                xs_t.append(xb)
            r = k.pn_r[tt % 2]
            k.rstd_from(r[:, :], ss[:, :], float(D), k.pn_t[0][:, :])
            for fc in range(8):
                tm = k.pn_t[1 + fc % 2]
                k.stt(tm[:, :], xs_t[fc][:, :], cols[:, ia, fc:fc + 1], r[:, :], ALU.mult, ALU.mult)
                k.act(k.hT[fc][:, PAD + tt * 512:PAD + (tt + 1) * 512], tm[:, :], AF.Identity, bias=cols[:, ib, fc:fc + 1])
        sc.__exit__(None, None, None)

    def postnorm_alloc(self):
        k = self
        k.po_x = [k.sbuf("po_x%d" % i, [128, 512], F32) for i in range(8)]
        k.po_r = k.sbuf("po_r", [128, 512], F32)
        k.po_t = [k.sbuf("po_t%d" % i, [128, 512], F32) for i in range(3)]

    def postnorm_prefetch(self, tt):
        k = self
        sl = slice(tt * 512, (tt + 1) * 512)
        for fc in range(8):
            k.dma("sp", k.po_x[fc][:, :], k.xs[fc][:, sl])

    def postnorm_tile(self, tt, o_tile, ss, cols, ig):
        k = self
        sl = slice(tt * 512, (tt + 1) * 512)
        k.rstd_from(k.po_r[:, :], ss, float(D), k.po_t[0][:, :])
        for fc in range(8):
            xb = k.po_x[fc]
            tm = k.po_t[1 + fc % 2]
            k.tt(tm[:, :], o_tile[fc], k.po_r[:, :], ALU.mult)
            k.stt(xb[:, :], tm[:, :], cols[:, ig, fc:fc + 1], xb[:, :], ALU.mult, ALU.add)
            k.dma("sp", k.xs[fc][:, sl], xb[:, :])

    def ffn(self, l, cols):
        k = self
        L = "l%d" % l
        wup = k.din("ffn_up_" + L, [D, 2 * DFF])
        wcv = k.din("ffn_conv_w_" + L, [3, 2 * DFF])
        bcv = k.din("ffn_conv_b_" + L, [2 * DFF])
        wdn = k.din("ffn_down_" + L, [DFF, D])
        k.prenorm(cols, 3, 4)
        sc = k.scope(); sc.__enter__()
        k.ff_g = k.sbuf("ff_g", [128, NJ, 1024], BF16, gran=1)
        k.ff_wu = [k.sbuf("ff_wu%d" % i, [128, 8, 2, 512], BF16) for i in range(2)]
        k.ff_wd = [k.sbuf("ff_wd%d" % i, [128, 512], BF16) for i in range(4)]
        k.ff_u = [k.sbuf("ff_u%d" % i, [128, 258], BF16) for i in range(4)]
        k.ff_dgall = k.sbuf("ff_dgall", [128, 2 * NJ, 3, 128], BF16, gran=1)
        k.ff_sa = [k.sbuf("ff_sa%d" % i, [128, 256], F32) for i in range(2)]
        k.ff_o = [k.sbuf("ff_o%d" % i, [128, 512], F32) for i in range(8)]
        k.ff_sq = [k.sbuf("ff_sq%d" % i, [128, 512], BF16) for i in range(2)]
        k.postnorm_alloc()
        cw = k.sbuf("ff_cw_" + L, [128, 44, 3], F32)
        cb = k.sbuf("ff_cb_" + L, [128, 44], F32)
        for tap in range(3):
            k.load_cols(V(cw.t[:, :, tap], cw[:].keys), wcv.t[tap, :].rearrange("(c p) -> c p", p=128), wcv[:].keys, 44)
        k.load_cols(cb[:, :], bcv.t.rearrange("(c p) -> c p", p=128), bcv[:].keys, 44)
        for ch in range(2 * NJ):
            for tap in range(3):
                k.act(V(k.ff_dgall.t[:, ch, tap, :], k.ff_dgall[:, ch].keys), k.ident_b[:, :], AF.Copy, scale=cw[:, ch, tap:tap + 1])
        wupv = wup.t.rearrange("(c p) n -> p c n", p=128)
        nwu = 0
        nu = 0
        nd = 0
        import os
        dbg_tt = int(os.environ.get("FFN_TT", NTT)); dbg_jp = int(os.environ.get("FFN_JP", NJ // 2)); dbg_part = int(os.environ.get("FFN_PART", 9))
        for tp in range(dbg_tt // 2):
            pending = None

            def conv_stage(item):
                j, sg, us, nu_ = item
                dgs = [V(k.ff_dgall.t[:, ab * NJ + j, :, :], k.ff_dgall[:, ab * NJ + j].keys) for ab in range(2)]
                pcs = []
                for ab in range(2):
                    pc = k.banks[4 + ab]
                    for tap in range(3):
                        k.mm(pc[:, 0:256], V(dgs[ab].ap[:, tap, :], dgs[ab].keys), us[ab][:, tap:tap + 256], start=(tap == 0), stop=(tap == 2))
                    pcs.append(pc)
                sa = k.ff_sa[nu_ % 2]
                k.act(sa[:, :], pcs[0][:, 0:256], AF.Silu, bias=cb[:, j:j + 1])
                k.stt(V(k.ff_g.t[:, j, sg * 256:(sg + 1) * 256], k.ff_g[:, j].keys), pcs[1][:, 0:256], cb[:, NJ + j:NJ + j + 1], sa[:, :],
                      ALU.add, ALU.mult)

            for jp in range((NJ + 3) // 4):
                wu = k.ff_wu[nwu % 2]; nwu += 1
                nj_here = min(4, NJ - jp * 4)
                for ab in range(2):
                    c0 = ab * DFF + jp * 512
                    k.dma("pool", V(wu.t[:, :, ab, 0:nj_here * 128], wu[:].keys), V(wupv[:, :, c0:c0 + nj_here * 128], wup[:].keys))
                for jj in range(nj_here):
                    j = jp * 4 + jj
                    for sg in range(4):
                        seg = tp * 4 + sg
                        c_lo = PAD + seg * 256 - 1
                        us = []
                        for ab in range(2):
                            pb = k.banks[(nu * 2 + ab) % 4]
                            for kc in range(8):
                                k.mm(pb[:, 0:258], wu[:, kc, ab, jj * 128:(jj + 1) * 128], k.hT[kc][:, c_lo:c_lo + 258],
                                     start=(kc == 0), stop=(kc == 7))
                            u = k.ff_u[(nu * 2 + ab) % 4]
                            if ab == 0:
                                k.copy(u[:, 0:258], pb[:, 0:258], eng="act")
                            else:
                                k.copy(u[:, 0:258], pb[:, 0:258], eng="dve")
                            uv = V(u.t[:, 0:258:257], u[:].keys)
                            k.tt(uv, uv, k.flags[:, seg * 2:seg * 2 + 2], ALU.mult)
                            us.append(u)
                        if pending is not None:
                            conv_stage(pending)
                        pending = (j, sg, us, nu)
                        nu += 1
            if pending is not None:
                conv_stage(pending)
            for st_ in range(2):
                tt = tp * 2 + st_
                k.postnorm_prefetch(tt)
                ss = k.banks[6]
                for half in range(2):
                    for j in range(NJ):
                        wd = k.ff_wd[nd % 4]; nd += 1
                        k.dma("pool", wd[:, :], V(wdn.t[j * 128:(j + 1) * 128, half * 512:(half + 1) * 512], wdn[:].keys))
                        for nn in range(4):
                            k.mm(k.banks[nn][:, :], wd[:, nn * 128:(nn + 1) * 128], V(k.ff_g.t[:, j, st_ * 512:(st_ + 1) * 512], k.ff_g[:, j].keys),
                                 start=(j == 0), stop=(j == NJ - 1))
                    for nn in range(4):
                        n = half * 4 + nn
                        k.copy(k.ff_o[n][:, :], k.banks[nn][:, :], eng="dve")
                        sq = k.ff_sq[n % 2]
                        k.act(sq[:, :], k.ff_o[n][:, :], AF.Square)
                        k.mm(ss[:, :], k.ones_b[:, :], sq[:, :], start=(n == 0), stop=(n == 7))
                k.postnorm_tile(tt, [k.ff_o[n][:, :] for n in range(8)], ss[:, :], cols, 5)
        sc.__exit__(None, None, None)

    def gelu(self, out, src_psum, n, scr):
        k = self
        x, t, s = scr
        k.copy(x[:, 0:n], src_psum, eng="act")
        k.act(t[:, 0:n], x[:, 0:n], AF.Square)
        k.ts(t[:, 0:n], t[:, 0:n], 0.044715, ALU.mult, 1.0, ALU.add)
        k.tt(t[:, 0:n], t[:, 0:n], x[:, 0:n], ALU.mult)
        k.act(s[:, 0:n], t[:, 0:n], AF.Sigmoid, scale=1.5957691216057308)
        k.tt(out, s[:, 0:n], x[:, 0:n], ALU.mult)

    def outproj_post(self, wname, cols):
        k = self
        wo = k.din(wname, [2048, D])
        sc = k.scope(); sc.__enter__()
        wt = k.sbuf("wo", [128, 16, D], BF16)
        wov = wo.t.rearrange("(c p) n -> p c n", p=128)
        for q in range(4):
            k.dma("pool", V(wt.t[:, q * 4:(q + 1) * 4, :], wt[:].keys), V(wov[:, q * 4:(q + 1) * 4, :], wo[:].keys))
        o_t = [k.sbuf("op_o%d" % i, [128, 512], F32) for i in range(8)]
        sqs = [k.sbuf("op_sq%d" % i, [128, 512], BF16) for i in range(2)]
        k.postnorm_alloc()
        nb = 0
        for tt in range(NTT):
            k.postnorm_prefetch(tt)
            ss = k.banks[6]
            for n in range(8):
                bk = k.banks[nb % 4]; nb += 1
                for kc in range(16):
                    k.mm(bk[:, :], wt[:, kc, n * 128:(n + 1) * 128], V(k.mixedT.t[:, kc, tt * 512:(tt + 1) * 512], k.mixedT[:, kc].keys),
                         start=(kc == 0), stop=(kc == 15))
                k.copy(o_t[n][:, :], bk[:, :], eng="dve")
                sq = sqs[n % 2]
                k.act(sq[:, :], o_t[n][:, :], AF.Square)
                k.mm(ss[:, :], k.ones_b[:, :], sq[:, :], start=(n == 0), stop=(n == 7))
            k.postnorm_tile(tt, [o_t[n][:, :] for n in range(8)], ss[:, :], cols, 2)
        sc.__exit__(None, None, None)

    def mixer_l1(self, cols):
        k = self
        win = k.din("mix_in_l1", [D, L1_PROJ])
        winv = win.t.rearrange("(c p) n -> p c n", p=128)
        lnw = k.din("sg_ln_w", [1024]); lnb = k.din("sg_ln_b", [1024])
        sws = k.din("sg_w_s", [4, 128, 128]); sbs = k.din("sg_b_s", [4, 128])
        sink = k.din("attn_sink", [8])
        ck = k.din("ctx_k", [512, 256]); cv = k.din("ctx_v", [512, 256])
        rope = k.din("rope", [2, 128, T])
        cband = k.din("cst_band", [2, 128, 512])
        crot = k.din("cst_rot", [128, 128])
        nk = k.dout("nk", [T, 256]); nv = k.dout("nv", [T, 256])
        k.prenorm(cols, 0, 1)
        sc0 = k.scope(); sc0.__enter__()
        k.mixedT = k.sbuf("mixedT", [128, 16, T], BF16, gran=1)
        sc = k.scope(); sc.__enter__()
        wgv = k.sbuf("wgv", [128, 8, 1024], BF16)
        for q in range(2):
            k.dma("pool", V(wgv.t[:, :, q * 512:(q + 1) * 512], wgv[:].keys), V(winv[:, :, 1024 + q * 512:1024 + (q + 1) * 512], win[:].keys))
        lnw_t = k.sbuf("lnw_t", [128, 1024], F32); lnb_t = k.sbuf("lnb_t", [128, 1024], F32)
        k.dma("sp", lnw_t[:, :], V(lnw.t.partition_broadcast(128), lnw[:].keys))
        k.dma("sp", lnb_t[:, :], V(lnb.t.partition_broadcast(128), lnb[:].keys))
        wsT = k.sbuf("wsT", [128, 4, 128], BF16)
        wstage = k.sbuf("wstage", [128, 4, 128], F32)
        k.dma("sp", wstage[:, :, :], V(sws.t.rearrange("g i j -> i g j"), sws[:].keys))
        for g in range(4):
            bk = k.banks[g % 2]
            k.transpose(bk[:, 0:128], wstage[:, g, :], k.ident_f[:, :])
            k.copy(wsT[:, g, :], bk[:, 0:128], eng="dve")
        bsf = k.sbuf("bsf", [1, 512], F32); bsb = k.sbuf("bsb", [1, 512], BF16)
        k.dma("sp", bsf[:, :], V(sbs.t.rearrange("(o g) i -> o (g i)", o=1), sbs[:].keys))
        k.copy(bsb[:, :], bsf[:, :], eng="dve")
        gsc = [(k.sbuf("g_x%d" % i, [128, 512], F32), k.sbuf("g_t%d" % i, [128, 512], F32), k.sbuf("g_s%d" % i, [128, 512], F32)) for i in range(2)]
        wu = [k.sbuf("wu1_%d" % i, [128, 8, 128], BF16) for i in range(2)]
        ng = 0
        k.dma("pool", wu[0][:, :, :], V(winv[:, :, 0:128], win[:].keys))
        for n in range(8):
            w = wu[n % 2]
            if n + 1 < 8:
                k.dma("pool", wu[(n + 1) % 2][:, :, :], V(winv[:, :, (n + 1) * 128:(n + 2) * 128], win[:].keys))
            for tt in range(NTT):
                bk = k.banks[ng % 2]
                for kc in range(8):
                    k.mm(bk[:, :], w[:, kc, :], k.hT[kc][:, PAD + tt * 512:PAD + (tt + 1) * 512], start=(kc == 0), stop=(kc == 7))
                k.gelu(V(k.mixedT.t[:, n, tt * 512:(tt + 1) * 512], k.mixedT[:, n].keys), bk[:, :], 512, gsc[ng % 2])
                ng += 1
        gv = [k.sbuf("gv%d" % i, [128, 1024], F32) for i in range(2)]
        gvn = [k.sbuf("gvn%d" % i, [128, 1024], BF16) for i in range(2)]
        st6 = k.sbuf("st6", [128, 2, 6], F32); mv = k.sbuf("mv", [128, 2], F32); rs = k.sbuf("rs", [128, 2], F32)
        for c in range(NCH):
            g_ = gv[c % 2]
            for hf in range(2):
                bk = k.banks[2 + hf]
                for kc in range(8):
                    k.mm(bk[:, :], k.hT[kc][:, PAD + c * 128:PAD + (c + 1) * 128], wgv[:, kc, hf * 512:(hf + 1) * 512], start=(kc == 0), stop=(kc == 7))
                k.gelu(g_[:, hf * 512:(hf + 1) * 512], bk[:, :], 512, gsc[ng % 2]); ng += 1
                k.op("dve", lambda g_=g_, hf=hf: k.nc.vector.bn_stats(out=st6.t[:, hf, :], in_=g_.t[:, hf * 512:(hf + 1) * 512]), [g_[:, :]], [st6[:, :, :]])
            k.op("dve", lambda: k.nc.vector.bn_aggr(out=mv.t[:, :], in_=st6.t[:, :, :].rearrange("p a b -> p (a b)")), [st6[:, :, :]], [mv[:, :]])
            k.act(rs[:, 0:1], mv[:, 1:2], AF.Sqrt, bias=k.eps_t[:, 0:1], scale=1.0)
            k.recip(rs[:, 1:2], rs[:, 0:1])
            k.ts(g_[:, :], g_[:, :], mv[:, 0:1], ALU.subtract, rs[:, 1:2], ALU.mult)
            k.tt(g_[:, :], g_[:, :], lnw_t[:, :], ALU.mult, eng="pool")
            gn = gvn[c % 2]
            k.tt(gn[:, :], g_[:, :], lnb_t[:, :], ALU.add)
            for hf in range(2):
                bk = k.banks[4 + hf]
                for q in range(4):
                    dch = hf * 4 + q
                    g = dch // 2
                    k.mm(bk[:, q * 128:(q + 1) * 128], gn[:, dch * 128:(dch + 1) * 128], wsT[:, g, :], start=True, stop=False)
                    k.mm(bk[:, q * 128:(q + 1) * 128], k.ones_b[0:1, :], bsb[0:1, g * 128:(g + 1) * 128], start=False, stop=True)
                mo = V(k.mixedT.t[:, hf * 4:hf * 4 + 4, c * 128:(c + 1) * 128], [(k.mixedT.id, hf * 4 + q) for q in range(4)])
                k.tt(mo, V(bk.t[:, :].rearrange("p (q i) -> p q i", q=4), bk[:].keys), mo, ALU.mult)
        sc.__exit__(None, None, None)
        sc = k.scope(); sc.__enter__()
        qT = k.sbuf("qT", [128, 8, T], BF16, gran=1)
        kT = k.sbuf("kT", [128, 2, T], BF16, gran=1)
        vtok = k.sbuf("vtok", [128, NCH, 256], BF16, gran=1)
        kcT = k.sbuf("kcT", [128, 2, 512], BF16)
        vc = k.sbuf("vc", [128, 4, 256], BF16)
        skr = k.sbuf("skr", [1, 8, 128], BF16)
        band = k.sbuf("band", [128, 2, 512], BF16)
        scA = k.scope(); scA.__enter__()
        ropeC = k.sbuf("ropeC", [128, T], F32); ropeS = k.sbuf("ropeS", [128, T], F32)
        k.dma("sp", ropeC[:, :], V(rope.t[0], rope[:].keys)); k.dma("sp", ropeS[:, :], V(rope.t[1], rope[:].keys))
        rotf = k.sbuf("rotf", [128, 128], F32); rotb = k.sbuf("rotb", [128, 128], BF16)
        k.dma("sp", rotf[:, :], crot[:, :]); k.copy(rotb[:, :], rotf[:, :], eng="dve")
        wq = [k.sbuf("wq%d" % i, [128, 8, 128], BF16) for i in range(2)]
        qs = [k.sbuf("q_s%d" % i, [128, 512], BF16) for i in range(2)]
        t1 = [k.sbuf("q_t1%d" % i, [128, 512], F32) for i in range(2)]
        t2 = [k.sbuf("q_t2%d" % i, [128, 512], F32) for i in range(2)]
        nq = 0
        k.dma("pool", wq[0][:, :, :], V(winv[:, :, 2048:2048 + 128], win[:].keys))
        for hh in range(10):
            w = wq[hh % 2]
            if hh + 1 < 10:
                c1 = 2048 + (hh + 1) * 128
                k.dma("pool", wq[(hh + 1) % 2][:, :, :], V(winv[:, :, c1:c1 + 128], win[:].keys))
            for tt in range(NTT):
                sl = slice(tt * 512, (tt + 1) * 512)
                bk = k.banks[nq % 2]; bk2 = k.banks[2 + nq % 2]
                for kc in range(8):
                    k.mm(bk[:, :], w[:, kc, :], k.hT[kc][:, PAD + tt * 512:PAD + (tt + 1) * 512], start=(kc == 0), stop=(kc == 7))
                q_ = qs[nq % 2]
                k.copy(q_[:, :], bk[:, :], eng="act")
                k.mm(bk2[:, :], rotb[:, :], q_[:, :])
                a = t1[nq % 2]; b = t2[nq % 2]
                k.tt(a[:, :], q_[:, :], ropeC[:, sl], ALU.mult, eng="pool")
                k.tt(b[:, :], bk2[:, :], ropeS[:, sl], ALU.mult)
                dst = V(qT.t[:, hh, sl], qT[:, hh].keys) if hh < 8 else V(kT.t[:, hh - 8, sl], kT[:, hh - 8].keys)
                k.tt(dst, a[:, :], b[:, :], ALU.add)
                nq += 1
        wkv = k.sbuf("wkv", [128, 8, 512], BF16)
        k.dma("pool", wkv[:, :, :], V(winv[:, :, 3072:3584], win[:].keys))
        kvo = [k.sbuf("kvo%d" % i, [128, 512], F32) for i in range(2)]
        for c in range(NCH):
            bk = k.banks[c % 2]
            for kc in range(8):
                k.mm(bk[:, :], k.hT[kc][:, PAD + c * 128:PAD + (c + 1) * 128], wkv[:, kc, :], start=(kc == 0), stop=(kc == 7))
            o = kvo[c % 2]
            k.copy(o[:, :], bk[:, :], eng="act")
            k.copy(V(vtok.t[:, c, :], vtok[:, c].keys), o[:, 256:512], eng="dve")
            k.dma("sp", V(nk.t[c * 128:(c + 1) * 128, :], nk[:].keys), o[:, 0:256], is_output=True)
            k.dma("sp", V(nv.t[c * 128:(c + 1) * 128, :], nv[:].keys), o[:, 256:512], is_output=True)
        kcs = k.sbuf("kcs", [128, 4, 256], F32)
        k.dma("sp", kcs[:, :, :], V(ck.t.rearrange("(c p) d -> p c d", p=128), ck[:].keys))
        for kvh in range(2):
            bk = k.banks[kvh]
            for sc_ in range(4):
                k.transpose(bk[:, sc_ * 128:(sc_ + 1) * 128], kcs[:, sc_, kvh * 128:(kvh + 1) * 128], k.ident_f[:, :])
            k.copy(kcT[:, kvh, :], bk[:, :], eng="dve")
        k.dma("pool", vc[:, :, :], V(cv.t.rearrange("(c p) d -> p c d", p=128), cv[:].keys))
        skf = k.sbuf("skf", [1, 8], F32); ske = k.sbuf("ske", [1, 8], F32)
        k.dma("sp", skf[:, :], V(sink.t.rearrange("(o h) -> o h", o=1), sink[:].keys))
        k.act(ske[:, :], skf[:, :], AF.Exp)
        k.copy(skr[:, :, :], V(ske.t[:, :].unsqueeze(2).to_broadcast([1, 8, 128]), ske[:].keys), eng="dve")
        bandf = k.sbuf("bandf", [128, 2, 512], F32)
        k.dma("sp", bandf[:, :, :], V(cband.t.rearrange("a p n -> p a n"), cband[:].keys))
        k.copy(band[:, :, :], bandf[:, :, :], eng="dve")
        scA.__exit__(None, None, None)
        pT = [k.sbuf("pT%d" % i, [128, 512], BF16) for i in range(3)]
        mk = [k.sbuf("mk%d" % i, [128, 512], BF16) for i in range(2)]
        rden = [k.sbuf("rden%d" % i, [128, 512], F32) for i in range(2)]
        scale = 128.0 ** -0.5
        npb = 0; nmk = 0; nu = 0
        for c in range(NCH):
            for kvh in range(2):
                blocks = []
                if c > 0: blocks.append(("prev", c - 1))
                blocks.append(("same", c))
                if c < NCH - 1: blocks.append(("next", c + 1))
                for s4 in range(4): blocks.append(("ctx", s4))
                po = k.banks[3 + nu % 2]; pd = k.banks[5 + nu % 2]
                rhs_q = V(qT.t[:, kvh * 4:kvh * 4 + 4, c * 128:(c + 1) * 128], [(qT.id, kvh * 4 + i) for i in range(4)])
                for bi, (kind, idx) in enumerate(blocks):
                    ps = k.banks[npb % 3]
                    p_ = pT[npb % 3]; npb += 1
                    if kind == "ctx":
                        k.mm(V(ps.t[:, :].rearrange("p (h q) -> p h q", h=4), ps[:].keys), kcT[:, kvh, idx * 128:(idx + 1) * 128], rhs_q)
                        k.act(p_[:, :], ps[:, :], AF.Exp, bias=k.flags[:, 112:113], scale=scale)
                        lv = vc[:, idx, kvh * 128:(kvh + 1) * 128]
                    else:
                        k.mm(V(ps.t[:, :].rearrange("p (h q) -> p h q", h=4), ps[:].keys), V(kT.t[:, kvh, idx * 128:(idx + 1) * 128], kT[:, kvh].keys), rhs_q)
                        k.act(p_[:, :], ps[:, :], AF.Exp, scale=scale)
                        if kind != "same":
                            m = mk[nmk % 2]; nmk += 1
                            bsel = 0 if kind == "prev" else 1
                            f0 = 48 + (0 if kind == "prev" else 32) + c
                            k.ts(m[:, :], band[:, bsel, :], k.flags[:, f0:f0 + 1], ALU.mult, k.flags[:, f0 + 16:f0 + 17], ALU.add, eng="pool")
                            k.tt(p_[:, :], p_[:, :], m[:, :], ALU.mult)
                        lv = V(vtok.t[:, idx, kvh * 128:(kvh + 1) * 128], vtok[:, idx].keys)
                    k.mm(po[:, :], lv, p_[:, :], start=(bi == 0), stop=(bi == len(blocks) - 1))
                    k.mm(pd[:, :], k.ones_b[:, :], p_[:, :], start=(bi == 0), stop=False)
                k.mm(pd[:, :], k.ones_b[0:1, :], V(skr.t[0:1, kvh * 4:kvh * 4 + 4, :].rearrange("o h q -> o (h q)"), skr[:].keys), start=False, stop=True)
                rd = rden[nu % 2]
                k.act(rd[:, :], pd[:, :], AF.Ln)
                k.act(rd[:, :], rd[:, :], AF.Exp, scale=-1.0)
                mo = V(k.mixedT.t[:, 8 + kvh * 4:8 + kvh * 4 + 4, c * 128:(c + 1) * 128], [(k.mixedT.id, 8 + kvh * 4 + i) for i in range(4)])
                k.tt(mo, V(po.t[:, :].rearrange("p (h q) -> p h q", h=4), po[:].keys), V(rd.t[:, :].rearrange("p (h q) -> p h q", h=4), rd[:].keys), ALU.mult)
                nu += 1
        sc.__exit__(None, None, None)
        k.outproj_post("mix_out_l1", cols)
        sc0.__exit__(None, None, None)

    def pc_prefetch(self, win, winv, col0):
        k = self
        w = k.pc_w[k.pc_nw % len(k.pc_w)]; k.pc_nw += 1
        k.dma("pool", w[:, :, :], V(winv[:, :, col0:col0 + 128], win[:].keys))
        return w

    def proj_conv5(self, win, winv, col0, cw5, cb, dst_fn, w=None):
        k = self
        if w is None:
            w = k.pc_prefetch(win, winv, col0)
        dg = k.pc_dg[k.pc_n % 2]
        for tap in range(5):
            k.act(dg[:, tap, :], k.ident_b[:, :], AF.Copy, scale=cw5(tap))
        k.pc_n += 1
        pending = None

        def conv_stage(item):
            seg, u, pc = item
            for tap in range(5):
                k.mm(pc[:, 0:256], dg[:, tap, :], u[:, tap:tap + 256], start=(tap == 0), stop=(tap == 4))
            if cb is not None:
                k.act(dst_fn(seg), pc[:, 0:256], AF.Silu, bias=cb)
            else:
                k.act(dst_fn(seg), pc[:, 0:256], AF.Silu)

        for seg in range(NSEG):
            pb = k.banks[k.pc_m % 2]; pc = k.banks[2 + k.pc_m % 2]
            u = k.pc_u[k.pc_m % 2]; k.pc_m += 1
            c_lo = seg * 256
            for kc in range(8):
                k.mm(pb[:, 0:260], w[:, kc, :], k.hT[kc][:, c_lo:c_lo + 260], start=(kc == 0), stop=(kc == 7))
            k.copy(u[:, 0:260], pb[:, 0:260], eng="act")
            k.ts(u[:, 0:2], u[:, 0:2], k.flags[:, 2 * seg:2 * seg + 1], ALU.mult)
            k.ts(u[:, 258:260], u[:, 258:260], k.flags[:, 2 * seg + 1:2 * seg + 2], ALU.mult)
            if pending is not None:
                conv_stage(pending)
            pending = (seg, u, pc)
        conv_stage(pending)

    def pc_alloc(self):
        k = self
        k.pc_w = [k.sbuf("pc_w%d" % i, [128, 8, 128], BF16) for i in range(4)]
        k.pc_nw = 0
        k.pc_dg = [k.sbuf("pc_dg%d" % i, [128, 5, 128], BF16) for i in range(2)]
        k.pc_u = [k.sbuf("pc_u%d" % i, [128, 260], BF16) for i in range(2)]
        k.pc_n = 0; k.pc_m = 0

    def mixer_l0(self, cols):
        k = self
        win = k.din("mix_in_l0", [D, L0_PROJ])
        winv = win.t.rearrange("(c p) n -> p c n", p=128)
        k.prenorm(cols, 0, 1)
        sc0 = k.scope(); sc0.__enter__()
        k.mixedT = k.sbuf("mixedT", [128, 16, T], BF16, gran=1)
        k.l0_consts()
        k.dtab = {nm: k.sbuf("dn_" + nm, [128, NCH, 2, 8], F32) for nm in ("av", "ncs", "ecs", "necs", "dte", "etot")}
        k.beta = k.sbuf("beta", [128, NCH, 16], F32)
        tri = k.din("cst_tri", [2, 128, 128])
        k.tri = k.sbuf("tri", [128, 2, 128], F32)
        k.dma("sp", k.tri[:, :, :], V(tri.t.rearrange("a p n -> p a n"), tri[:].keys))
        import os
        part = os.environ.get("L0_PART", "both")
        s1 = k.scope(); s1.__enter__()
        k.l0_small(win, winv)
        if part in ("both", "ssd"):
            sc = k.scope(); sc.__enter__()
            k.ssd(win, winv)
            sc.__exit__(None, None, None)
        else:
            for fc in range(8):
                k.memset(V(k.mixedT.t[:, fc, :], k.mixedT[:, fc].keys), 0.0, eng="pool")
        s1.__exit__(None, None, None)
        if part in ("both", "dn"):
            sc = k.scope(); sc.__enter__()
            k.dn(win, winv)
            sc.__exit__(None, None, None)
        else:
            for fc in range(8, 16):
                k.memset(V(k.mixedT.t[:, fc, :], k.mixedT[:, fc].keys), 0.0, eng="pool")
        k.outproj_post("mix_out_l0", cols)
        sc0.__exit__(None, None, None)

    def l0_small(self, win, winv):
        k = self
        dtb = k.din("ssd_dt_bias", [2, 16]); alog = k.din("ssd_A_log", [2, 16])
        ddtb = k.din("dn_dt_bias", [2, 8]); dalog = k.din("dn_A_log", [2, 8])
        k.dt_t = k.sbuf("dt_t", [128, NCH, 32], F32)
        k.av = k.sbuf("av", [128, NCH, 2, 24], F32)
        k.cs = k.sbuf("cs", [128, NCH, 2, 24], F32)
        k.ncs = k.sbuf("ncs", [128, NCH, 2, 24], F32)
        k.ecs = k.sbuf("ecs", [128, NCH, 2, 24], F32)
        k.necs = k.sbuf("necs", [128, NCH, 2, 24], F32)
        k.dte = k.sbuf("dte", [128, NCH, 2, 24], F32)
        k.etot = k.sbuf("etot", [128, NCH, 2, 24], F32)
        sc = k.scope(); sc.__enter__()
        wsm = k.sbuf("wsm", [128, 8, 64], BF16)
        k.dma("pool", V(wsm.t[:, :, 0:32], wsm[:].keys), V(winv[:, :, 3072:3104], win[:].keys))
        k.dma("pool", V(wsm.t[:, :, 32:64], wsm[:].keys), V(winv[:, :, 7200:7232], win[:].keys))
        sm = k.sbuf("sm", [128, NCH, 64], F32)
        for c in range(NCH):
            bk = k.banks[c % 2]
            for kc in range(8):
                k.mm(bk[:, 0:64], k.hT[kc][:, PAD + c * 128:PAD + (c + 1) * 128], wsm[:, kc, :], start=(kc == 0), stop=(kc == 7))
            k.copy(V(sm.t[:, c, :], sm[:].keys), bk[:, 0:64], eng="act" if c % 2 == 0 else "dve")
        bias48 = k.sbuf("bias48", [128, 48], F32); al48 = k.sbuf("al48", [128, 48], F32)
        k.dma("sp", bias48[:, 0:32], V(dtb.t.rearrange("a h -> (a h)").partition_broadcast(128), dtb[:].keys))
        k.dma("sp", bias48[:, 32:48], V(ddtb.t.rearrange("a h -> (a h)").partition_broadcast(128), ddtb[:].keys))
        k.dma("sp", al48[:, 0:32], V(alog.t.rearrange("a h -> (a h)").partition_broadcast(128), alog[:].keys))
        k.dma("sp", al48[:, 32:48], V(dalog.t.rearrange("a h -> (a h)").partition_broadcast(128), dalog[:].keys))
        nega = k.sbuf("nega", [128, 48], F32)
        k.act(nega[:, :], al48[:, :], AF.Exp)
        k.ts(nega[:, :], nega[:, :], -1.0, ALU.mult)
        sp_ = k.sbuf("sp_", [128, NCH, 48], F32)
        bb = V(bias48.t[:, :].unsqueeze(1).to_broadcast([128, NCH, 48]), bias48[:].keys)
        k.tt(sp_[:, :, :], V(sm.t[:, :, 0:48], sm[:].keys), bb, ALU.add)
        k.act(sp_[:, :, :], sp_[:, :, :], AF.Exp)
        k.ts(sp_[:, :, :], sp_[:, :, :], 1.0, ALU.add)
        k.act(sp_[:, :, :], sp_[:, :, :], AF.Ln)
        k.copy(k.dt_t[:, :, :], V(sp_.t[:, :, 0:32], sp_[:].keys), eng="dve")
        k.act(k.beta[:, :, :], V(sm.t[:, :, 48:64], sm[:].keys), AF.Sigmoid)
        nb = V(nega.t[:, :].unsqueeze(1).to_broadcast([128, NCH, 48]), nega[:].keys)
        k.tt(sp_[:, :, :], sp_[:, :, :], nb, ALU.mult)
        for d in range(2):
            k.copy(V(k.av.t[:, :, d, 0:16], k.av[:].keys), V(sp_.t[:, :, d * 16:(d + 1) * 16], sp_[:].keys), eng="dve")
            k.copy(V(k.av.t[:, :, d, 16:24], k.av[:].keys), V(sp_.t[:, :, 32 + d * 8:32 + (d + 1) * 8], sp_[:].keys), eng="dve")
        tot = k.sbuf("tot", [128, NCH, 2, 24], F32)
        for d in range(2):
            bk = k.banks[d]; bk2 = k.banks[2 + d]
            rhs = V(k.av.t[:, :, d, :], k.av[:].keys)
            k.mm(V(bk.t[:, 0:384].rearrange("p (c n) -> p c n", c=NCH), bk[:].keys), k.tri[:, d, :], rhs)
            k.mm(V(bk2.t[:, 0:384].rearrange("p (c n) -> p c n", c=NCH), bk2[:].keys), k.ones_f[:, :], rhs)
            k.copy(V(k.cs.t[:, :, d, :], k.cs[:].keys), V(bk.t[:, 0:384].rearrange("p (c n) -> p c n", c=NCH), bk[:].keys), eng="dve")
            k.copy(V(tot.t[:, :, d, :], tot[:].keys), V(bk2.t[:, 0:384].rearrange("p (c n) -> p c n", c=NCH), bk2[:].keys), eng="dve")
        k.ts(k.ncs[:, :, :, :], k.cs[:, :, :, :], -1.0, ALU.mult)
        k.act(k.ecs[:, :, :, :], k.cs[:, :, :, :], AF.Exp)
        k.ts(k.necs[:, :, :, :], k.ecs[:, :, :, :], -1.0, ALU.mult)
        k.act(k.etot[:, :, :, :], tot[:, :, :, :], AF.Exp)
        k.tt(tot[:, :, :, :], tot[:, :, :, :], k.cs[:, :, :, :], ALU.subtract)
        k.act(k.dte[:, :, :, :], tot[:, :, :, :], AF.Exp)
        for nm, src in (("av", k.av), ("ncs", k.ncs), ("ecs", k.ecs), ("necs", k.necs), ("dte", k.dte), ("etot", k.etot)):
            k.copy(k.dtab[nm][:, :, :, :], V(src.t[:, :, :, 16:24], src[:].keys), eng="pool")
        sc.__exit__(None, None, None)

    def build_Lt(self, lt, ps, d, acol, ncol):
        k = self
        abc = V(acol.ap.to_broadcast([128, 128]), acol.keys)
        k.mm(ps, abc, k.tri[:, d, :], start=True, stop=False)
        k.mm(ps, k.ident_b[:, :], k.mbias[:, d, :], start=False, stop=True)
        k.act(lt, ps, AF.Exp, bias=ncol)

    def l0_consts(self):
        k = self
        mb = k.din("cst_mbias", [2, 128, 128]); st = k.din("cst_strict", [2, 128, 128]); blk = k.din("cst_blk", [4, 128, 128])
        k.mbias = k.sbuf("mbias", [128, 2, 128], BF16)
        k.strict = k.sbuf("strict", [128, 2, 128], F32)
        k.blk = k.sbuf("blk", [128, 4, 128], F32)
        k.dma("pool", k.mbias[:, :, :], V(mb.t.rearrange("a p n -> p a n"), mb[:].keys))
        k.blk_b = k.sbuf("blk_b", [128, 4, 128], BF16)
        k.dma("pool", k.blk_b[:, :, :], V(blk.t.rearrange("a p n -> p a n"), blk[:].keys))
        k.dma("sp", k.strict[:, :, :], V(st.t.rearrange("a p n -> p a n"), st[:].keys))
        k.dma("sp", k.blk[:, :, :], V(blk.t.rearrange("a p n -> p a n"), blk[:].keys))

    def ssd(self, win, winv):
        k = self
        cwd = k.din("ssd_conv_w", [5, 2048]); cbd = k.din("ssd_conv_b", [2048])
        dD = k.din("ssd_D", [16]); nwd = k.din("ssd_norm_w", [1024])
        h0d = k.din("ssd_h0", [2, 16, 64, 128])
        hout = k.dout("ssd_out", [NSEG, 2, 16, 64, 128])
        k.pc_alloc()
        cw = k.sbuf("s_cw", [128, 16, 5], F32); cb = k.sbuf("s_cb", [128, 16], F32)
        for tap in range(5):
            k.load_cols(V(cw.t[:, :, tap], cw[:].keys), cwd.t[tap, :].rearrange("(c p) -> c p", p=128), cwd[:].keys, 16)
        k.load_cols(cb[:, :], cbd.t.rearrange("(c p) -> c p", p=128), cbd[:].keys, 16)
        Dbc = k.sbuf("Dbc", [128, 16], F32)
        k.dma("sp", Dbc[:, :], V(dD.t.partition_broadcast(128), dD[:].keys))
        nwc = k.sbuf("s_nw", [128, 8], F32)
        k.load_cols(nwc[:, :], nwd.t.rearrange("(c p) -> c p", p=128), nwd[:].keys, 8)
        scI = k.scope(); scI.__enter__()
        BT = k.sbuf("s_BT", [128, T], BF16); CT = k.sbuf("s_CT", [128, T], BF16)
        xtok = k.sbuf("s_xtok", [128, NCH, 256], BF16, gran=1)
        Btok = k.sbuf("s_Btok", [128, NCH, 128], BF16, gran=1)
        sz = k.sbuf("s_sz", [128, NCH, 256], BF16, gran=1)
        yacc = k.sbuf("s_yacc", [128, NCH, 256], BF16, gran=1)
        hm = [k.sbuf("s_hm%d" % d, [128, 256], F32) for d in range(2)]
        hb = [k.sbuf("s_hb%d" % d, [128, 256], BF16) for d in range(2)]
        bfb = [k.psum_bf(i) for i in range(2)]
        n = 0
        for g in range(4):
            scA = k.scope(); scA.__enter__()
            xT = k.sbuf("s_xT", [128, 2, T], BF16, gran=1)
            wz = [k.sbuf("s_wz%d" % i, [128, 8, 256], BF16) for i in range(1)]
            chB = 8 + g; chC = 12 + g
            pw = [k.pc_prefetch(win, winv, 1024 + ch_ * 128) for ch_ in (2 * g, 2 * g + 1, chB, chC)]
            w = wz[0]
            k.dma("pool", w[:, :, :], V(winv[:, :, g * 256:(g + 1) * 256], win[:].keys))
            for q in range(2):
                ch = 2 * g + q
                k.proj_conv5(win, winv, 1024 + ch * 128, lambda tap, ch=ch: cw[:, ch, tap:tap + 1], cb[:, ch:ch + 1],
                             lambda seg, q=q: V(xT.t[:, q, seg * 256:(seg + 1) * 256], xT[:, q].keys), w=pw[q])
            k.proj_conv5(win, winv, 1024 + chB * 128, lambda tap: cw[:, chB, tap:tap + 1], cb[:, chB:chB + 1], lambda seg: BT[:, seg * 256:(seg + 1) * 256], w=pw[2])
            k.proj_conv5(win, winv, 1024 + chC * 128, lambda tap: cw[:, chC, tap:tap + 1], cb[:, chC:chC + 1], lambda seg: CT[:, seg * 256:(seg + 1) * 256], w=pw[3])
            for c in range(NCH):
                cs_ = slice(c * 128, (c + 1) * 128)
                bk = k.banks[4 + c % 2]
                for kc in range(8):
                    k.mm(bk[:, 0:256], k.hT[kc][:, PAD + c * 128:PAD + (c + 1) * 128], w[:, kc, :], start=(kc == 0), stop=(kc == 7))
                k.act(V(sz.t[:, c, :], sz[:, c].keys), bk[:, 0:256], AF.Silu)
                pb = bfb[c % 2]
                for q in range(2):
                    k.transpose(pb[:, q * 128:(q + 1) * 128], V(xT.t[:, q, cs_], xT[:, q].keys), k.ident_b[:, :])
                k.transpose(pb[:, 256:384], BT[:, cs_], k.ident_b[:, :])
                k.copy(V(xtok.t[:, c, :], xtok[:, c].keys), pb[:, 0:256], eng="dve")
                k.copy(V(Btok.t[:, c, :], Btok[:, c].keys), pb[:, 256:384], eng="dve")
            scA.__exit__(None, None, None)
            scB = k.scope(); scB.__enter__()
            Gs = [k.sbuf("s_G%d" % i, [128, 128], F32) for i in range(2)]
            Lt = [k.sbuf("s_Lt%d" % i, [128, 4, 128], F32) for i in range(2)]
            St = [k.sbuf("s_St%d" % i, [128, 4, 128], BF16) for i in range(2)]
            xdt = [k.sbuf("s_xdt%d" % i, [128, 256], BF16) for i in range(2)]
            xde = [k.sbuf("s_xde%d" % i, [128, 256], BF16) for i in range(2)]
            xD = [k.sbuf("s_xD%d" % i, [128, 256], BF16) for i in range(2)]
            htmp = [k.sbuf("s_ht%d" % i, [128, 256], F32) for i in range(2)]
            hstage = [k.sbuf("s_hs%d" % i, [128, 2, 128], F32) for i in range(2)]
            ytmp = [k.sbuf("s_yt%d" % i, [128, 256], F32) for i in range(2)]
            yz = [k.sbuf("s_yz%d" % i, [128, 256], BF16) for i in range(2)]
            for d in range(2):
                hs = hstage[d]
                k.dma("sp", hs[:, :, :], V(h0d.t[d, 4 * g:4 * g + 4].rearrange("(a h) p n -> (h p) a n", a=2), h0d[:].keys))
                bk = k.banks[6]
                for a in range(2):
                    k.transpose(bk[:, a * 128:(a + 1) * 128], hs[:, a, :], k.ident_f[:, :])
                k.copy(hm[d][:, :], bk[:, 0:256], eng="dve")
                k.copy(hb[d][:, :], hm[d][:, :], eng="act")
            for step in range(NCH):
                for d in range(2):
                    c = step if d == 0 else NCH - 1 - step
                    second = (d == 0 and c >= 8) or (d == 1 and c < 8)
                    cs_ = slice(c * 128, (c + 1) * 128)
                    i2 = d
                    bg = k.banks[0]
                    k.mm(bg[:, 0:128], BT[:, cs_], CT[:, cs_])
                    k.copy(Gs[i2][:, :], bg[:, 0:128], eng="act")
                    bl = k.banks[1 + i2]
                    for hh in range(4):
                        acol = V(k.av.t[:, c, d, 4 * g + hh:4 * g + hh + 1], k.av[:].keys)
                        abc = V(acol.ap.to_broadcast([128, 128]), acol.keys)
                        k.mm(bl[:, hh * 128:(hh + 1) * 128], abc, k.tri[:, d, :], start=True, stop=False)
                        k.mm(bl[:, hh * 128:(hh + 1) * 128], k.ident_b[:, :], k.mbias[:, d, :], start=False, stop=True)
                    for hh in range(4):
                        k.act(V(Lt[i2].t[:, hh, :], Lt[i2][:].keys), bl[:, hh * 128:(hh + 1) * 128], AF.Exp,
                              bias=V(k.ncs.t[:, c, d, 4 * g + hh:4 * g + hh + 1], k.ncs[:].keys))
                    k.tt(St[i2][:, :, :], Lt[i2][:, :, :], V(Gs[i2].t[:, :].unsqueeze(1).to_broadcast([128, 4, 128]), Gs[i2][:].keys), ALU.mult)
                    dtb_ = V(k.dt_t.t[:, c, d * 16 + 4 * g:d * 16 + 4 * g + 4].unsqueeze(2).to_broadcast([128, 4, 64]), k.dt_t[:].keys)
                    dte_ = V(k.dte.t[:, c, d, 4 * g:4 * g + 4].unsqueeze(2).to_broadcast([128, 4, 64]), k.dte[:].keys)
                    ecs_ = V(k.ecs.t[:, c, d, 4 * g:4 * g + 4].unsqueeze(2).to_broadcast([128, 4, 64]), k.ecs[:].keys)
                    eto_ = V(k.etot.t[:, c, d, 4 * g:4 * g + 4].unsqueeze(2).to_broadcast([128, 4, 64]), k.etot[:].keys)
                    x3 = V(xtok.t[:, c, :].rearrange("p (h q) -> p h q", h=4), xtok[:, c].keys)
                    v3 = lambda b: V(b.t[:, :].rearrange("p (h q) -> p h q", h=4), b[:].keys)
                    k.tt(v3(xdt[i2]), x3, dtb_, ALU.mult)
                    k.tt(v3(xde[i2]), v3(xdt[i2]), dte_, ALU.mult, eng="pool")
                    yd = k.banks[3 + 2 * d]; yo = SubBuf(k.banks[4 + 2 * d], 0, 256); ps = SubBuf(k.banks[4 + 2 * d], 256, 256)
                    if second:
                        Db = V(Dbc.t[:, 4 * g:4 * g + 4].unsqueeze(2).to_broadcast([128, 4, 64]), Dbc[:].keys)
                        k.tt(v3(xD[i2]), x3, Db, ALU.mult, eng="pool")
                    for hh in range(4):
                        k.mm(yd[:, hh * 64:(hh + 1) * 64], V(St[i2].t[:, hh, :], St[i2][:].keys), xdt[i2][:, hh * 64:(hh + 1) * 64],
                             start=True, stop=(not second))
                        if second:
                            k.mm(yd[:, hh * 64:(hh + 1) * 64], k.ident_b[:, :], xD[i2][:, hh * 64:(hh + 1) * 64], start=False, stop=True)
                    k.mm(yo[:, 0:256], CT[:, cs_], hb[d][:, :])
                    yt = ytmp[i2]
                    k.tt(v3(yt), V(yo.t[:, 0:256].rearrange("p (h q) -> p h q", h=4), yo[:].keys), ecs_, ALU.mult)
                    ya = V(yacc.t[:, c, :], yacc[:, c].keys)
                    if not second:
                        k.tt(ya, yd[:, 0:256], yt[:, :], ALU.add)
                    else:
                        k.tt(yt[:, :], yd[:, 0:256], yt[:, :], ALU.add)
                        k.tt(yt[:, :], yt[:, :], ya, ALU.add, eng="pool")
                        k.tt(yz[i2][:, :], yt[:, :], V(sz.t[:, c, :], sz[:, c].keys), ALU.mult)
                        pb = bfb[i2]
                        for q in range(2):
                            k.transpose(pb[:, q * 128:(q + 1) * 128], yz[i2][:, q * 128:(q + 1) * 128], k.ident_b[:, :])
                        mo = V(k.mixedT.t[:, 2 * g:2 * g + 2, cs_], [(k.mixedT.id, 2 * g), (k.mixedT.id, 2 * g + 1)])
                        k.copy(mo, V(pb.t[:, 0:256].rearrange("p (q t) -> p q t", q=2), pb[:].keys), eng="act")
                    k.mm(ps[:, 0:256], V(Btok.t[:, c, :], Btok[:, c].keys), xde[i2][:, :])
                    ht = htmp[i2]
                    k.tt(v3(ht), v3(hm[d]), eto_, ALU.mult, eng="pool")
                    k.tt(hm[d][:, :], ps[:, 0:256], ht[:, :], ALU.add)
                    last = (c % 2 == 1) if d == 0 else (c % 2 == 0)
                    if last:
                        seg = c // 2
                        bk = k.banks[6]
                        hs2 = hstage[i2]
                        for a in range(2):
                            k.transpose(bk[:, a * 128:(a + 1) * 128], hm[d][:, a * 128:(a + 1) * 128], k.ident_f[:, :])
                        k.copy(V(hs2.t[:, :, :], hs2[:].keys), V(bk.t[:, 0:256].rearrange("p (a n) -> p a n", a=2), bk[:].keys), eng="act")
                        k.dma("sp", V(hout.t[seg, d, 4 * g:4 * g + 4].rearrange("(a h) p n -> (h p) a n", a=2), hout[:].keys), hs2[:, :, :], is_output=True)
                    cn = c + 1 if d == 0 else c - 1
                    if 0 <= cn < NCH:
                        fcol = (16 if d == 0 else 32) + cn
                        k.ts(hm[d][:, :], hm[d][:, :], k.flags[:, fcol:fcol + 1], ALU.mult)
                        k.copy(hb[d][:, :], hm[d][:, :], eng="act")
            scB.__exit__(None, None, None)
        scI.__exit__(None, None, None)
        sq = [k.sbuf("s_sq%d" % i, [128, 512], BF16) for i in range(2)]
        rr = k.sbuf("s_rr", [128, 512], F32); rt = k.sbuf("s_rt", [128, 512], F32)
        for tt in range(NTT):
            sl = slice(tt * 512, (tt + 1) * 512)
            ss = k.banks[tt % 2]
            for fc in range(8):
                s_ = sq[fc % 2]
                k.act(s_[:, :], V(k.mixedT.t[:, fc, sl], k.mixedT[:, fc].keys), AF.Square)
                k.mm(ss[:, :], k.ones_b[:, :], s_[:, :], start=(fc == 0), stop=(fc == 7))
            k.rstd_from(rr[:, :], ss[:, :], 1024.0, rt[:, :])
            for fc in range(8):
                mv_ = V(k.mixedT.t[:, fc, sl], k.mixedT[:, fc].keys)
                k.stt(mv_, mv_, nwc[:, fc:fc + 1], rr[:, :], ALU.mult, ALU.mult)

    def dn(self, win, winv):
        k = self
        cwd = k.din("dn_conv_w", [5, 3072]); nwd = k.din("dn_norm_w", [128])
        s0d = k.din("dn_h0", [2, 8, 128, 128])
        sout = k.dout("dn_out", [NSEG, 2, 8, 128, 128])
        cw = k.sbuf("d_cw", [128, 24, 5], F32)
        for tap in range(5):
            k.load_cols(V(cw.t[:, :, tap], cw[:].keys), cwd.t[tap, :].rearrange("(c p) -> c p", p=128), cwd[:].keys, 24)
        nwb = k.sbuf("d_nwb", [128, 128], F32)
        k.dma("sp", nwb[:, :], V(nwd.t.partition_broadcast(128), nwd[:].keys))
        lnsc = k.sbuf("d_lnsc", [128, 1], F32)
        k.memset(lnsc[:, :], -0.5 * float(np.log(128.0)), eng="pool")
        zero1 = k.sbuf("d_zero1", [128, 1], F32)
        k.memset(zero1[:, :], 0.0, eng="pool")
        qT = k.sbuf("d_qT", [128, T], BF16, gran=128); kT = k.sbuf("d_kT", [128, T], BF16, gran=128); vT = k.sbuf("d_vT", [128, T], BF16, gran=128)
        khtok = k.sbuf("d_khtok", [128, NCH, 128], BF16, gran=1); vtok = k.sbuf("d_vtok", [128, NCH, 128], BF16, gran=1)
        oacc = k.sbuf("d_oacc", [128, NCH, 128], F32, gran=1)
        Wall = k.sbuf("d_Wall", [128, 2 * NCH, 128], BF16, gran=1)
        QKall = k.sbuf("d_QKall", [128, 2 * NCH, 128], BF16, gran=1)
        wg = [k.sbuf("d_wg%d" % i, [128, 8, 128], BF16) for i in range(1)] * 2
        import os
        IDT = BF16 if os.environ.get("DN_INV", "bf16") == "bf16" else F32
        G = 4
        NT = 5
        tmpf = [k.sbuf("d_tf%d" % i, [128, 128], F32) for i in range(NT)]
        tmpb = [k.sbuf("d_tb%d" % i, [128, 128], BF16) for i in range(10)]
        Sm = [k.sbuf("d_S%d" % d, [128, 128], F32) for d in range(2)]
        Sb = [k.sbuf("d_Sb%d" % d, [128, 128], BF16) for d in range(2)]
        fin16 = k.sbuf("d_fin16", [128, 16], F32); fin16b = k.sbuf("d_fin16b", [128, 16], F32)
        pbf = k.psum_bf(1)
        st = {"bank": 0, "tf": 0, "tb": 0, "ev": 0, "lt": 0}
        T_ = k.dtab

        def nbank():
            b = k.banks[st["bank"] % 7]; st["bank"] += 1
            return b

        def tf():
            t = tmpf[st["tf"] % NT]; st["tf"] += 1
            return t

        def tb():
            t = tmpb[st["tb"] % 10]; st["tb"] += 1
            return t

        def evac(dst, src, scale=None):
            st["ev"] += 1
            if scale is not None:
                k.act(dst, src, AF.Copy, scale=scale)
            elif st["ev"] % 3 != 0:
                k.copy(dst, src, eng="act")
            else:
                k.copy(dst, src, eng="dve")

        I_ = k.ident_b if IDT == BF16 else k.ident_f

        def mm1(lhsT, rhs, dst):
            b = nbank()
            k.mm(b[:, 0:128], lhsT, rhs)
            evac(dst, b[:, 0:128])
            return dst

        def mmadd(lhsT, rhs, sb, dst, op=ALU.add, first_sb=False):
            b = nbank()
            k.mm(b[:, 0:128], lhsT, rhs)
            if first_sb:
                k.tt(dst, sb, b[:, 0:128], op)
            else:
                k.tt(dst, b[:, 0:128], sb, op)
            return dst

        F_ = lambda Tq: Tq[:, :, :]
        Q_ = lambda Tq, q: V(Tq.t[:, q, :], Tq[:].keys)
        bc_ = lambda v2: V(v2.ap.unsqueeze(1).to_broadcast([128, 4, 128]), v2.keys)
        bank3 = lambda b: V(b.t[:, :].rearrange("p (q i) -> p q i", q=4), b[:].keys)
        opq = lambda x, q: Q_(x, q) if isinstance(x, Buf) else x

        def mmq(lhsT, rhs, dst):
            b = nbank()
            for q in range(4):
                k.mm(b[:, q * 128:(q + 1) * 128], opq(lhsT, q), opq(rhs, q))
            evac(F_(dst), bank3(b))
            return dst

        def mmaddq(lhsT, rhs, sb, dstv, op=ALU.add, first_sb=False):
            b = nbank()
            for q in range(4):
                k.mm(b[:, q * 128:(q + 1) * 128], opq(lhsT, q), opq(rhs, q))
            if first_sb:
                k.tt(dstv, sb, bank3(b), op)
            else:
                k.tt(dstv, bank3(b), sb, op)

        def quad_gen(h, d, c0, S, LtT):
            u0 = d * NCH + c0
            css = [slice((c0 + q) * 128, (c0 + q + 1) * 128) for q in range(4)]
            U, UT = S[0], S[1]
            s_ = S[2:10]
            Iv = I_[:, :]
            bL = nbank()
            for q in range(4):
                c = c0 + q
                acol = V(T_["av"].t[:, c, d, h:h + 1], T_["av"][:].keys)
                abc = V(acol.ap.to_broadcast([128, 128]), acol.keys)
                k.mm(bL[:, q * 128:(q + 1) * 128], abc, k.tri[:, d, :], start=True, stop=False)
                k.mm(bL[:, q * 128:(q + 1) * 128], k.ident_b[:, :], k.mbias[:, d, :], start=False, stop=True)
            for q in range(4):
                c = c0 + q
                k.act(Q_(LtT, q), bL[:, q * 128:(q + 1) * 128], AF.Exp, bias=V(T_["ncs"].t[:, c, d, h:h + 1], T_["ncs"][:].keys))
            yield
            Ls = s_[7]
            k.tt(F_(Ls), F_(LtT), bc_(k.strict[:, d, :]), ALU.mult, eng="pool")
            bQ = nbank()
            for q in range(4):
                k.mm(bQ[:, q * 128:(q + 1) * 128], kT[:, css[q]], qT[:, css[q]])
            k.tt(V(QKall.t[:, u0:u0 + 4, :], [(QKall.id, u0 + q) for q in range(4)]), bank3(bQ), F_(LtT), ALU.mult)
            yield
            bA = nbank()
            for q in range(4):
                k.mm(bA[:, q * 128:(q + 1) * 128], kT[:, css[q]], kT[:, css[q]])
            for q in range(4):
                c = c0 + q
                beta_ = V(k.beta.t[:, c, d * 8 + h:d * 8 + h + 1], k.beta[:].keys)
                k.stt(Q_(U, q), bA[:, q * 128:(q + 1) * 128], beta_, Q_(Ls, q), ALU.mult, ALU.mult)
            yield
            mmq(U, Iv, UT)
            Ud, UdT, P = s_[0], s_[1], s_[2]
            k.tt(F_(Ud), F_(U), bc_(k.blk_b[:, 0, :]), ALU.mult, eng="pool")
            yield
            k.tt(F_(UdT), F_(UT), bc_(k.blk_b[:, 0, :]), ALU.mult)
            k.tt(F_(P), bc_(Iv), F_(Ud), ALU.subtract)
            yield
            V1 = mmq(UdT, Ud, s_[3]); V1T = mmq(Ud, UdT, s_[4])
            yield
            A1 = s_[5]; k.tt(F_(A1), F_(V1T), bc_(Iv), ALU.add, eng="pool")
            V2 = mmq(V1T, V1, s_[0]); V2T = mmq(V1, V1T, s_[1])
            yield
            P1 = mmq(A1, P, s_[6])
            A3 = s_[3]; mmaddq(V2, V2T, bc_(Iv), F_(A3))
            yield
            A2 = s_[2]; k.tt(F_(A2), F_(V2T), bc_(Iv), ALU.add, eng="pool")
            yield
            P2 = mmq(A2, P1, s_[5])
            yield
            Wd = mmq(A3, P2, s_[4])
            yield
            WdT = s_[6]; mmq(Wd, Iv, WdT)
            slots = {1: (s_[3], s_[7]), 2: (s_[4], s_[6])}
            for lvl in (1, 2, 3):
                B, BT = s_[0], s_[1]
                k.tt(F_(B), F_(U), bc_(k.blk_b[:, lvl, :]), ALU.mult)
                k.tt(F_(BT), F_(UT), bc_(k.blk_b[:, lvl, :]), ALU.mult, eng="pool")
                yield
                Y = mmq(BT, Wd, s_[2])
                if lvl < 3:
                    Yt = mmq(B, WdT, s_[5])
                    yield
                    nW, nWT = slots[lvl]
                    mmaddq(WdT, Y, F_(Wd), F_(nW), op=ALU.subtract, first_sb=True)
                    mmaddq(Wd, Yt, F_(WdT), F_(nWT), op=ALU.subtract, first_sb=True)
                    Wd, WdT = nW, nWT
                    yield
                else:
                    yield
                    mmaddq(WdT, Y, F_(Wd), V(Wall.t[:, u0:u0 + 4, :], [(Wall.id, u0 + q) for q in range(4)]), op=ALU.subtract, first_sb=True)

        def chain_step(h, d, c):
            cs_ = slice(c * 128, (c + 1) * 128)
            u = d * NCH + c
            col = lambda nm: V(T_[nm].t[:, c, d, h:h + 1], T_[nm][:].keys)
            beta_ = V(k.beta.t[:, c, d * 8 + h:d * 8 + h + 1], k.beta[:].keys)
            Wb = V(Wall.t[:, u, :], Wall[:, u].keys); QKm = V(QKall.t[:, u, :], QKall[:, u].keys)
            kend = tb()[:, :]
            k.act(kend, V(khtok.t[:, c, :], khtok[:, c].keys), AF.Copy, scale=col("dte"))
            bK = nbank(); k.mm(bK[:, 0:128], kT[:, cs_], Sb[d][:, :])
            bS = nbank(); k.mm(bS[:, 0:128], qT[:, cs_], Sb[d][:, :])
            R0 = tb()[:, :]
            k.stt(R0, bK[:, 0:128], col("necs"), V(vtok.t[:, c, :], vtok[:, c].keys), ALU.mult, ALU.add)
            t_ = tf()[:, :]
            k.act(t_, bS[:, 0:128], AF.Copy, scale=col("ecs"))
            bV = nbank(); k.mm(bV[:, 0:128], Wb, R0)
            vnew = tb()[:, :]
            k.act(vnew, bV[:, 0:128], AF.Copy, scale=beta_)
            bU = nbank(); k.mm(bU[:, 0:128], kend, vnew)
            bO = nbank(); k.mm(bO[:, 0:128], QKm, vnew)
            k.stt(Sm[d][:, :], Sm[d][:, :], col("etot"), bU[:, 0:128], ALU.mult, ALU.add)
            last = (c % 2 == 1) if d == 0 else (c % 2 == 0)
            if last:
                k.dma("sp", V(sout.t[c // 2, d, h], sout[:].keys), Sm[d][:, :], is_output=True)
            cn = c + 1 if d == 0 else c - 1
            if 0 <= cn < NCH:
                fcol = (16 if d == 0 else 32) + cn
                k.ts(Sm[d][:, :], Sm[d][:, :], k.flags[:, fcol:fcol + 1], ALU.mult)
                k.copy(Sb[d][:, :], Sm[d][:, :], eng="act")
            oa = V(oacc.t[:, c, :], oacc[:, c].keys)
            first = (d == 0 and c < 8) or (d == 1 and c >= 8)
            if first:
                k.tt(oa, bO[:, 0:128], t_, ALU.add)
            else:
                k.tt(t_, bO[:, 0:128], t_, ALU.add)
                k.tt(oa, oa, t_, ALU.add, eng="pool")

        k.marks = getattr(k, "marks", [])
        mk_ = lambda lab: k.marks.append((lab, k.cnt["e_pe"]))
        for h in range(8):
            mk_("dn%d:proj" % h)
            sc1 = k.scope(); sc1.__enter__()
            k.pc_alloc()
            sqb = [k.sbuf("d_sq%d" % i, [128, 512], BF16) for i in range(2)]
            rr = [k.sbuf("d_rr%d" % i, [128, 512], F32) for i in range(2)]
            pw = [k.pc_prefetch(win, winv, coff + h * 128) for coff in (3104, 4128, 5152)]
            for i_, (buf, coff, cc) in enumerate(((qT, 3104, h), (kT, 4128, 8 + h), (vT, 5152, 16 + h))):
                k.proj_conv5(win, winv, coff + h * 128, lambda tap, cc=cc: cw[:, cc, tap:tap + 1], None,
                             lambda seg, buf=buf: buf[:, seg * 256:(seg + 1) * 256], w=pw[i_])
            mk_("dn%d:l2" % h)
            for (buf, isq) in ((qT, True), (kT, False)):
                for tt in range(NTT):
                    sl = slice(tt * 512, (tt + 1) * 512)
                    s2 = sqb[tt % 2]; r_ = rr[tt % 2]
                    k.act(s2[:, :], buf[:, sl], AF.Square)
                    b = nbank()
                    k.mm(b[:, :], k.ones_b[:, :], s2[:, :])
                    k.act(r_[:, :], b[:, :], AF.Ln, bias=k.eps_t[:, 0:1])
                    k.act(r_[:, :], r_[:, :], AF.Exp, scale=-0.5, bias=(lnsc[:, 0:1] if isq else zero1[:, 0:1]))
                    k.tt(buf[:, sl], buf[:, sl], r_[:, :], ALU.mult)
            w = wg[h % 2]
            k.dma("pool", w[:, :, :], V(winv[:, :, 6176 + h * 128:6176 + (h + 1) * 128], win[:].keys))
            for c in range(NCH):
                cs_ = slice(c * 128, (c + 1) * 128)
                k.transpose(pbf[:, 0:128], kT[:, cs_], k.ident_b[:, :])
                k.transpose(pbf[:, 128:256], vT[:, cs_], k.ident_b[:, :])
                k.copy(V(khtok.t[:, c, :], khtok[:, c].keys), pbf[:, 0:128], eng="dve")
                k.copy(V(vtok.t[:, c, :], vtok[:, c].keys), pbf[:, 128:256], eng="dve")
            sc1.__exit__(None, None, None)
            mk_("dn%d:P" % h)
            sc2 = k.scope(); sc2.__enter__()
            scr = [[k.sbuf("d_s%d_%d" % (g_, i), [128, 4, 128], IDT) for i in range(10)] for g_ in range(G)]
            quads = [(d, c0) for d in range(2) for c0 in range(0, NCH, 4)]
            for g0 in range(0, len(quads), G):
                gens = [quad_gen(h, d, c0, scr[i], scr[i][2 + 6]) for i, (d, c0) in enumerate(quads[g0:g0 + G])]
                while gens:
                    for g_ in list(gens):
                        try:
                            next(g_)
                        except StopIteration:
                            gens.remove(g_)
            sc2.__exit__(None, None, None)
            mk_("dn%d:C" % h)
            for d in range(2):
                k.dma("sp", Sm[d][:, :], V(s0d.t[d, h], s0d[:].keys))
                k.copy(Sb[d][:, :], Sm[d][:, :], eng="act")
            for step in range(NCH):
                chain_step(h, 0, step)
                chain_step(h, 1, NCH - 1 - step)
            mk_("dn%d:fin" % h)
            for c in range(NCH):
                oa = V(oacc.t[:, c, :], oacc[:, c].keys)
                junk = tf()[:, :]
                k.act(junk, oa, AF.Square, accum_out=fin16[:, c:c + 1])
            k.act(fin16b[:, :], fin16[:, :], AF.Sqrt, bias=k.eps_t[:, 0:1], scale=1.0 / 128.0)
            k.recip(fin16[:, :], fin16b[:, :])
            for c in range(NCH):
                cs_ = slice(c * 128, (c + 1) * 128)
                b = nbank()
                for kc in range(8):
                    k.mm(b[:, 0:128], k.hT[kc][:, PAD + c * 128:PAD + (c + 1) * 128], w[:, kc, :], start=(kc == 0), stop=(kc == 7))
                sg = tb()[:, :]
                k.act(sg, b[:, 0:128], AF.Silu)
                oa = V(oacc.t[:, c, :], oacc[:, c].keys)
                o1 = tf()[:, :]
                k.stt(o1, oa, fin16[:, c:c + 1], nwb[:, :], ALU.mult, ALU.mult)
                ob = tb()[:, :]
                k.tt(ob, o1, sg, ALU.mult)
                k.transpose(pbf[:, 0:128], ob, k.ident_b[:, :])
                k.copy(V(k.mixedT.t[:, 8 + h, cs_], k.mixedT[:, 8 + h].keys), pbf[:, 0:128], eng="dve")


def core_inputs(inputs, core):
    f32 = np.float32
    d = {}
    flags = np.zeros(128, f32)
    rope = np.zeros((2, 128, T), f32)
    if core < 4:
        rope[0] = 1.0
        d["ctx_k"] = np.zeros((512, 256), f32)
        d["ctx_v"] = np.zeros((512, 256), f32)
        flags[112] = -30000.0
        for c in range(16):
            flags[48 + c] = 0.0; flags[64 + c] = 1.0 if c % 2 == 1 else 0.0
            flags[80 + c] = 0.0; flags[96 + c] = 1.0 if c % 2 == 0 else 0.0
        d["x"] = np.ascontiguousarray(inputs["x_prompt"][core * 8:(core + 1) * 8].reshape(T, D))
        d["cond"] = np.ascontiguousarray(inputs["c_ctx"])
        for c in range(16):
            flags[16 + c] = 0.0 if c % 2 == 0 else 1.0
            flags[32 + c] = 0.0 if c % 2 == 1 else 1.0
    else:
        b = (core - 4) % 2
        d["ctx_k"] = np.ascontiguousarray(inputs["cache_l1_k"][b].reshape(512, 256))
        d["ctx_v"] = np.ascontiguousarray(inputs["cache_l1_v"][b].reshape(512, 256))
        pos = np.arange(T)
        inv = (10000.0 ** (-np.arange(32, dtype=np.float64) / 32.0))
        ang = np.concatenate([(pos // 64)[:, None] * inv[None, :], (pos % 64)[:, None] * inv[None, :]], axis=1)
        ang = ang.astype(f32)
        rope[0] = np.concatenate([np.cos(ang), np.cos(ang)], axis=1).T
        rope[1] = np.concatenate([np.sin(ang), np.sin(ang)], axis=1).T
        for c in range(16):
            flags[48 + c] = 1.0; flags[80 + c] = 1.0
        d["x"] = np.ascontiguousarray(inputs["x_sample"][b])
        d["cond"] = np.ascontiguousarray(inputs["c"][b])
        for s in range(8):
            flags[2 * s] = 0.0 if s == 0 else 1.0
            flags[2 * s + 1] = 0.0 if s == 7 else 1.0
        for c in range(16):
            flags[16 + c] = 0.0 if c == 0 else 1.0
            flags[32 + c] = 0.0 if c == 15 else 1.0
    d["flags"] = flags
    d["rope"] = rope
    d["cst_ident"] = np.eye(128, dtype=f32)
    sq = np.arange(128)
    band = np.zeros((2, 128, 512), f32)
    band[0] = np.tile((sq[:, None] >= sq[None, :]).astype(f32), (1, 4))
    band[1] = np.tile((sq[:, None] <= sq[None, :]).astype(f32), (1, 4))
    d["cst_band"] = band
    rot = np.zeros((128, 128), f32)
    for dp in range(64):
        rot[dp + 64, dp] = -1.0
        rot[dp, dp + 64] = 1.0
    d["cst_rot"] = rot
    tri = np.zeros((2, 128, 128), f32)
    tri[0] = (sq[:, None] <= sq[None, :]).astype(f32)
    tri[1] = (sq[:, None] >= sq[None, :]).astype(f32)
    d["cst_tri"] = tri
    mbias = np.zeros((2, 128, 128), f32)
    mbias[0] = np.where(sq[None, :] >= sq[:, None], 0.0, -30000.0)
    mbias[1] = np.where(sq[None, :] <= sq[:, None], 0.0, -30000.0)
    d["cst_mbias"] = mbias
    strict = np.zeros((2, 128, 128), f32)
    strict[0] = (sq[None, :] > sq[:, None]).astype(f32)
    strict[1] = (sq[None, :] < sq[:, None]).astype(f32)
    d["cst_strict"] = strict
    blk = np.zeros((4, 128, 128), f32)
    bd = lambda b: ((sq[:, None] // b) == (sq[None, :] // b)).astype(f32)
    blk[0] = bd(16); blk[1] = bd(32) - bd(16); blk[2] = bd(64) - bd(32); blk[3] = 1.0 - bd(64)
    d["cst_blk"] = blk
    if core < 4:
        d["ssd_h0"] = np.zeros((2, 16, 64, 128), f32)
        d["dn_h0"] = np.zeros((2, 8, 128, 128), f32)
    else:
        d["ssd_h0"] = np.ascontiguousarray(inputs["state_l0_ssd"][(core - 4) % 2])
        d["dn_h0"] = np.ascontiguousarray(inputs["state_l0_dn"][(core - 4) % 2])
    return d


def build(stages):
    k = Net()
    k.load_consts()
    k.load_x()
    for st in stages:
        if st[0] == "ffn":
            l = st[1]
            cols = k.modulation(l)
            k.ffn(l, cols)
        elif st[0] == "mix0":
            cols = k.modulation(0)
            k.mixer_l0(cols)
        elif st[0] == "mix1":
            cols = k.modulation(1)
            k.mixer_l1(cols)
        elif st[0] == "mod":
            cols = k.modulation(st[1])
        elif st[0] == "pre":
            cols = k.modulation(st[1])
            k.prenorm(cols, 3, 4)
    k.store_y()
    k.finish()
    return k


def run(inputs, stages, n_cores=8):
    from concourse.bass_utils import run_bass_kernel_spmd
    k = build(stages)
    in_maps = []
    for c in range(n_cores):
        d = core_inputs(inputs, c)
        m = {}
        for name in k.inp:
            if name in d:
                m[name] = d[name]
            else:
                m[name] = np.ascontiguousarray(np.asarray(inputs[name], dtype=np.float32))
        in_maps.append(m)
    res = run_bass_kernel_spmd(k.nc, in_maps, core_ids=list(range(n_cores)))
    return k, res


FULL = [("full",)]


def build_full():
    k = Net()
    k.load_consts()
    k.load_x()
    import os
    nsub = int(os.environ.get("FULL_N", 4))
    cols0 = k.modulation(0)
    k.mixer_l0(cols0)
    if nsub >= 2:
        k.ffn(0, cols0)
    if nsub >= 3:
        cols1 = k.modulation(1)
        k.mixer_l1(cols1)
    if nsub >= 4:
        k.ffn(1, cols1)
    k.store_y()
    k.finish()
    return k


def kernel(**inputs):
    from concourse.bass_utils import run_bass_kernel_spmd
    inputs = {n: np.asarray(v) for n, v in inputs.items()}
    used = ("x_prompt", "x_sample", "state_l0_ssd", "state_l0_dn", "cache_l1_k", "cache_l1_v", "c", "c_ctx",
            "mod_w_l0", "mod_b_l0", "norm_mix_pre_l0", "norm_mix_post_l0", "norm_ffn_pre_l0", "norm_ffn_post_l0",
            "ffn_up_l0", "ffn_conv_w_l0", "ffn_conv_b_l0", "ffn_down_l0",
            "mod_w_l1", "mod_b_l1", "norm_mix_pre_l1", "norm_mix_post_l1", "norm_ffn_pre_l1", "norm_ffn_post_l1",
            "ffn_up_l1", "ffn_conv_w_l1", "ffn_conv_b_l1", "ffn_down_l1",
            "mix_in_l0", "mix_out_l0", "ssd_conv_w", "ssd_conv_b", "ssd_dt_bias", "ssd_A_log", "ssd_D", "ssd_norm_w",
            "dn_conv_w", "dn_dt_bias", "dn_A_log", "dn_norm_w",
            "mix_in_l1", "mix_out_l1", "sg_ln_w", "sg_ln_b", "sg_w_s", "sg_b_s", "attn_sink")
    assert all(n in inputs for n in used)
    k = build_full()
    in_maps = []
    for c in range(8):
        d = core_inputs(inputs, c)
        m = {}
        for name in k.inp:
            m[name] = d[name] if name in d else np.ascontiguousarray(np.asarray(inputs[name], dtype=np.float32))
        in_maps.append(m)
    res = run_bass_kernel_spmd(k.nc, in_maps, core_ids=list(range(8)))
    r = res.results
    f32 = np.float32
    y_prompt = np.concatenate([r[c]["y"].reshape(8, 256, D) for c in range(4)], axis=0).astype(f32)
    y_sample = np.stack([r[4]["y"], r[5]["y"]], axis=0).astype(f32)
    new_ssd = np.concatenate([r[c]["ssd_out"] for c in range(4)], axis=0).astype(f32)
    new_dn = np.concatenate([r[c]["dn_out"] for c in range(4)], axis=0).astype(f32)
    new_k = np.concatenate([r[c]["nk"].reshape(8, 256, 2, 128) for c in range(4)], axis=0).astype(f32)
    new_v = np.concatenate([r[c]["nv"].reshape(8, 256, 2, 128) for c in range(4)], axis=0).astype(f32)
    return (y_prompt, y_sample, new_ssd, new_dn, new_k, new_v)
```

```python
import contextlib
import numpy as np
import concourse.bass as bass
import concourse.mybir as mybir

F32 = mybir.dt.float32
BF16 = mybir.dt.bfloat16
AF = mybir.ActivationFunctionType
ALU = mybir.AluOpType
AX = mybir.AxisListType

SAME_ENGINE_SYNC = True


class V:
    __slots__ = ("ap", "keys")

    def __init__(self, ap, keys):
        self.ap = ap
        self.keys = keys


class Buf:
    _n = 0

    def __init__(self, kb, t, shape, gran=None):
        self.kb = kb
        self.t = t
        self.shape = list(shape)
        Buf._n += 1
        self.id = Buf._n
        f0 = self.shape[1] if len(self.shape) > 1 else 1
        self.gran = gran if gran else f0
        self.ng = (f0 + self.gran - 1) // self.gran

    def _keys(self, idx):
        if not isinstance(idx, tuple):
            idx = (idx,)
        lo, hi = 0, self.ng - 1
        if len(idx) > 1:
            i1 = idx[1]
            if isinstance(i1, slice):
                a = 0 if i1.start is None else i1.start
                b = self.shape[1] if i1.stop is None else i1.stop
                lo, hi = a // self.gran, (b - 1) // self.gran
            elif isinstance(i1, int):
                lo = hi = i1 // self.gran
        return [(self.id, g) for g in range(lo, hi + 1)]

    def __getitem__(self, idx):
        return V(self.t[idx], self._keys(idx))

    def v(self, ap, idx=None):
        return V(ap, self._keys(idx) if idx is not None else [(self.id, g) for g in range(self.ng)])


class KB:
    ENG = ("pe", "act", "dve", "pool", "sp")

    def __init__(self):
        self.nc = bass.Bass("TRN2", target_bir_lowering=False)
        self.es = contextlib.ExitStack()
        nc = self.nc
        self.E = {"pe": nc.tensor, "act": nc.scalar, "dve": nc.vector, "pool": nc.gpsimd, "sp": nc.sync}
        self.sems = {}
        self.cnt = {}
        for e in self.ENG:
            self.sems["e_" + e] = self.es.enter_context(nc.semaphore("s_" + e))
            self.cnt["e_" + e] = 0
        self.NRING = 12
        for q in ("hw", "sw"):
            for i in range(self.NRING):
                nm = "d_%s%d" % (q, i)
                self.sems[nm] = self.es.enter_context(nc.semaphore(nm))
                self.cnt[nm] = 0
        self.ring_i = {"hw": 0, "sw": 0}
        self.seen = {e: {} for e in self.ENG}
        self.last_w = {}
        self.readers = {}
        self.n_inst = 0
        self.n_wait = 0
        self.out_deps = []
        self.stack = [self.es]
        self.waits_by = {}
        self.cur_birth = None
        self.birth = {}
        self.birth_done = set()

    @contextlib.contextmanager
    def scope(self):
        es = contextlib.ExitStack()
        self.stack.append(es)
        try:
            yield
        finally:
            self.stack.pop()
            es.close()
            self.cur_birth = tuple((s, v) for s, v in self.cnt.items() if v > 0)

    def sbuf(self, name, shape, dtype=F32, gran=None):
        self._nalloc = getattr(self, "_nalloc", 0) + 1
        t = self.stack[-1].enter_context(self.nc.sbuf_tensor("%s_%d" % (name, self._nalloc), list(shape), dtype))
        b = Buf(self, t, shape, gran)
        if self.cur_birth:
            self.birth[b.id] = self.cur_birth
        return b

    def psum(self, name, shape, dtype=F32, gran=None):
        t = self.stack[-1].enter_context(self.nc.psum_tensor(name, list(shape), dtype))
        return Buf(self, t, shape, gran)

    def dram(self, name, shape, dtype=F32, kind="Internal", gran=None):
        t = self.nc.dram_tensor(name, list(shape), dtype, kind=kind)
        return Buf(self, t.ap() if hasattr(t, "ap") else t, shape, gran)

    def _collect(self, reads, writes, eng=None):
        deps = {}

        def add(d):
            if d is None:
                return
            s, v = d
            if deps.get(s, 0) < v:
                deps[s] = v
        if self.birth:
            for r in list(reads) + list(writes):
                for k in r.keys:
                    bid = k[0]
                    if bid in self.birth and (eng, bid) not in self.birth_done:
                        self.birth_done.add((eng, bid))
                        for d in self.birth[bid]:
                            add(d)
        for r in reads:
            for k in r.keys:
                add(self.last_w.get(k))
        for w in writes:
            for k in w.keys:
                add(self.last_w.get(k))
                for d in self.readers.get(k, ()):
                    add(d)
        return deps

    def _emit_waits(self, eng, deps, war_only_same=None):
        need = []
        own = "e_" + eng
        for s, v in deps.items():
            if s == own and (not SAME_ENGINE_SYNC or eng == "pe"):
                continue
            if self.seen[eng].get(s, 0) >= v:
                continue
            need.append((s, v))
        return need

    def _record(self, mark, reads, writes):
        for r in reads:
            for k in r.keys:
                self.readers.setdefault(k, []).append(mark)
        for w in writes:
            for k in w.keys:
                self.last_w[k] = mark
                self.readers[k] = []

    def op(self, eng, fn, reads, writes):
        deps = self._collect(reads, writes, eng)
        need = self._emit_waits(eng, deps)
        E = self.E[eng]
        for s, v in need[:-1]:
            E.wait_ge(self.sems[s], v)
            self.n_wait += 1
            self.waits_by[eng] = self.waits_by.get(eng, 0) + 1
        inst = fn()
        if need:
            s, v = need[-1]
            inst._wait_ge(self.sems[s], v)
        for s, v in need:
            self.seen[eng][s] = v
        own = "e_" + eng
        self.cnt[own] += 1
        inst.then_inc(self.sems[own], 1)
        mark = (own, self.cnt[own])
        self._record(mark, reads, writes)
        self.n_inst += 1
        return inst

    def dma(self, queue, out, in_, is_output=False, **kw):
        ring = "sw" if queue == "pool" else "hw"
        i = self.ring_i[ring]
        self.ring_i[ring] += 1
        nm = "d_%s%d" % (ring, i % self.NRING)
        deps = self._collect([in_], [out], queue)
        if self.cnt[nm] > 0:
            if deps.get(nm, 0) < self.cnt[nm]:
                deps[nm] = self.cnt[nm]
        need = self._emit_waits(queue, deps)
        E = self.E[queue]
        for s, v in need[:-1]:
            E.wait_ge(self.sems[s], v)
            self.n_wait += 1
        inst = E.dma_start(out=out.ap, in_=in_.ap, **kw)
        if need:
            s, v = need[-1]
            inst._wait_ge(self.sems[s], v)
        for s, v in need:
            self.seen[queue][s] = v
        self.cnt[nm] += 16
        inst.then_inc(self.sems[nm], 16)
        mark = (nm, self.cnt[nm])
        self._record(mark, [in_], [out])
        if is_output:
            self.out_deps.append(mark)
        self.n_inst += 1
        return inst

    def finish(self):
        for s, v in self.cnt.items():
            if v > 0:
                self.E["sp"].wait_ge(self.sems[s], v)

    def mm(self, out, lhsT, rhs, start=True, stop=True, **kw):
        return self.op("pe", lambda: self.nc.tensor.matmul(out.ap, lhsT=lhsT.ap, rhs=rhs.ap, start=start, stop=stop, **kw),
                       [lhsT, rhs] + ([] if start else [out]), [out])

    def transpose(self, out, in_, ident):
        return self.op("pe", lambda: self.nc.tensor.transpose(out.ap, in_.ap, ident.ap), [in_, ident], [out])

    def act(self, out, in_, func, bias=None, scale=None, accum_out=None, eng="act"):
        reads = [in_]
        kw = {}
        if bias is not None:
            if isinstance(bias, V):
                reads.append(bias); kw["bias"] = bias.ap
            else:
                kw["bias"] = bias
        if scale is not None:
            if isinstance(scale, V):
                reads.append(scale); kw["scale"] = scale.ap
            else:
                kw["scale"] = scale
        writes = [out]
        if accum_out is not None:
            writes.append(accum_out); kw["accum_out"] = accum_out.ap
        return self.op("act", lambda: self.nc.scalar.activation(out=out.ap, in_=in_.ap, func=func, **kw), reads, writes)

    def tt(self, out, in0, in1, op, eng="dve"):
        E = self.E[eng]
        return self.op(eng, lambda: E.tensor_tensor(out=out.ap, in0=in0.ap, in1=in1.ap, op=op), [in0, in1], [out])

    def ts(self, out, in0, s1, op0, s2=None, op1=None, eng="dve", accum_out=None):
        E = self.E[eng]
        reads = [in0]
        a1 = s1.ap if isinstance(s1, V) else s1
        a2 = s2.ap if isinstance(s2, V) else s2
        if isinstance(s1, V): reads.append(s1)
        if isinstance(s2, V): reads.append(s2)
        kw = {}
        writes = [out]
        if op1 is not None:
            kw["op1"] = op1
        if accum_out is not None:
            kw["accum_out"] = accum_out.ap; writes.append(accum_out)
        return self.op(eng, lambda: E.tensor_scalar(out=out.ap, in0=in0.ap, scalar1=a1, scalar2=a2, op0=op0, **kw), reads, writes)

    def stt(self, out, in0, scalar, in1, op0, op1, eng="dve"):
        E = self.E[eng]
        reads = [in0, in1]
        a = scalar.ap if isinstance(scalar, V) else scalar
        if isinstance(scalar, V): reads.append(scalar)
        return self.op(eng, lambda: E.scalar_tensor_tensor(out=out.ap, in0=in0.ap, scalar=a, in1=in1.ap, op0=op0, op1=op1), reads, [out])

    def copy(self, out, in_, eng="dve"):
        if eng == "act":
            return self.op("act", lambda: self.nc.scalar.copy(out=out.ap, in_=in_.ap), [in_], [out])
        E = self.E[eng]
        return self.op(eng, lambda: E.tensor_copy(out=out.ap, in_=in_.ap), [in_], [out])

    def memset(self, out, val, eng="dve"):
        E = self.E[eng]
        return self.op(eng, lambda: E.memset(out.ap, val), [], [out])

    def recip(self, out, in_):
        return self.op("dve", lambda: self.nc.vector.reciprocal(out=out.ap, in_=in_.ap), [in_], [out])

T = 2048
D = 1024
NCH = 16
NTT = 4
NSEG = 8
DFF = 2816
NJ = DFF // 128
EPS = 1e-6
PAD = 2
L0_PROJ = 7232
L1_PROJ = 3584


class SubBuf:
    def __init__(self, buf, off, width):
        self.buf = buf; self.off = off; self.width = width
        self.id = buf.id
        self.t = buf.t[:, off:off + width]
        self._k = buf._keys((slice(None), slice(off, off + width)))

    def __getitem__(self, idx):
        return V(self.t[idx], self._k)


class Net(KB):
    def __init__(self, dbg=None):
        super().__init__()
        self.dbg = dbg or {}
        self.inp = {}
        self.banks = [self.psum("bank%d" % i, [128, 512], F32) for i in range(7)]
        self.psum_bf(0)
        self.banks.append(None)

    def psum_bf(self, i):
        if not hasattr(self, "_bfb"):
            big = self.psum("bfbig", [128, 1024], BF16)
            self._bfb = [SubBuf(big, j * 512, 512) for j in range(2)]
        return self._bfb[i]

    def din(self, name, shape, dtype=F32, gran=None):
        b = self.dram(name, shape, dtype, kind="ExternalInput", gran=gran)
        self.inp[name] = b
        return b

    def dout(self, name, shape, dtype=F32, gran=None):
        return self.dram(name, shape, dtype, kind="ExternalOutput", gran=gran)

    def load_cols(self, dst, src_ap, src_keys, n):
        k = self
        st = k.colstage[k.ncst % 2]; k.ncst += 1
        k.dma("sp", st[0:n, :], V(src_ap, src_keys))
        bk = k.banks[6]
        k.transpose(bk[:, 0:n], st[0:n, :], k.ident_f[0:n, 0:n])
        k.copy(dst, bk[:, 0:n], eng="dve")

    def load_consts(self):
        k = self
        c = k.din("cst_ident", [128, 128])
        k.ident_f = k.sbuf("ident_f", [128, 128], F32)
        k.dma("sp", k.ident_f[:, :], c[:, :])
        k.ident_b = k.sbuf("ident_b", [128, 128], BF16)
        k.copy(k.ident_b[:, :], k.ident_f[:, :], eng="dve")
        k.ones_b = k.sbuf("ones_b", [128, 128], BF16)
        k.memset(k.ones_b[:, :], 1.0, eng="pool")
        k.ones_f = k.sbuf("ones_f", [128, 128], F32)
        k.memset(k.ones_f[:, :], 1.0, eng="pool")
        k.colstage = [k.sbuf("colstage%d" % i, [128, 128], F32) for i in range(2)]
        k.ncst = 0
        k.eps_t = k.sbuf("eps_t", [128, 1], F32)
        k.memset(k.eps_t[:, :], EPS, eng="pool")
        fl = k.din("flags", [128])
        k.flags = k.sbuf("flags_t", [128, 128], F32)
        k.dma("sp", k.flags[:, :], V(fl.t.partition_broadcast(128), fl[:].keys))
        k.xs = [k.dram("xs%d" % fc, [128, T], F32, gran=512) for fc in range(8)]
        k.hT = [k.sbuf("hT%d" % fc, [128, T + 2 * PAD], BF16, gran=None) for fc in range(8)]
        for fc in range(8):
            k.memset(k.hT[fc][:, 0:PAD], 0.0, eng="pool")
            k.memset(k.hT[fc][:, T + PAD:T + 2 * PAD], 0.0, eng="pool")

    def load_x(self):
        k = self
        x = k.din("x", [T, D], gran=None)
        sc = k.scope(); sc.__enter__()
        xin = [k.sbuf("xin%d" % i, [128, 4, D], F32) for i in range(2)]
        xt = [k.sbuf("xt%d" % i, [128, 512], F32) for i in range(4)]
        n = 0
        for tt in range(NTT):
            xi = xin[tt % 2]
            k.dma("sp", xi[:, :, :], V(x.t[tt * 512:(tt + 1) * 512, :].rearrange("(c p) d -> p c d", p=128), x[:].keys))
            for fc in range(8):
                bk = k.banks[n % 4]
                for c in range(4):
                    k.transpose(bk[:, c * 128:(c + 1) * 128], xi[:, c, fc * 128:(fc + 1) * 128], k.ident_f[:, :])
                xo = xt[n % 4]
                if n % 2 == 0:
                    k.copy(xo[:, :], bk[:, :], eng="act")
                else:
                    k.copy(xo[:, :], bk[:, :], eng="dve")
                k.dma("sp", k.xs[fc][:, tt * 512:(tt + 1) * 512], xo[:, :])
                n += 1
        sc.__exit__(None, None, None)

    def store_y(self):
        k = self
        y = k.dout("y", [T, D])
        sc = k.scope(); sc.__enter__()
        xl = [k.sbuf("yl%d" % i, [128, 512], F32) for i in range(4)]
        yo = [k.sbuf("yo%d" % i, [128, 4, D], F32) for i in range(2)]
        n = 0
        for tt in range(NTT):
            yt = yo[tt % 2]
            for fc in range(8):
                xi = xl[n % 4]
                k.dma("sp", xi[:, :], k.xs[fc][:, tt * 512:(tt + 1) * 512])
                bk = k.banks[n % 4]
                for c in range(4):
                    k.transpose(bk[:, c * 128:(c + 1) * 128], xi[:, c * 128:(c + 1) * 128], k.ident_f[:, :])
                o = V(yt.t[:, :, fc * 128:(fc + 1) * 128], yt[:].keys)
                src = V(bk.t[:, :].rearrange("p (c f) -> p c f", c=4), bk[:].keys)
                if n % 2 == 0:
                    k.copy(o, src, eng="act")
                else:
                    k.copy(o, src, eng="dve")
                n += 1
            k.dma("sp", V(y.t[tt * 512:(tt + 1) * 512, :].rearrange("(c p) d -> p c d", p=128), y[:].keys), yt[:, :, :], is_output=True)
        sc.__exit__(None, None, None)

    def modulation(self, l):
        k = self
        L = "l%d" % l
        cond = k.inp.get("cond") or k.din("cond", [D])
        mw = k.din("mod_w_" + L, [D, 6 * D])
        mb = k.din("mod_b_" + L, [6 * D])
        nws = [k.din(n + L, [D]) for n in ("norm_mix_pre_", "norm_mix_post_", "norm_ffn_pre_", "norm_ffn_post_")]
        cols = k.sbuf("modcols_" + L, [128, 6, 8], F32)
        sc = k.scope(); sc.__enter__()
        cs = k.sbuf("cond_" + L, [128, 8], F32)
        k.load_cols(cs[:, :], cond.t.rearrange("(c p) -> c p", p=128), cond[:].keys, 8)
        mbt = k.sbuf("modb_" + L, [128, 48], F32)
        k.load_cols(mbt[:, :], mb.t.rearrange("(j p) -> j p", p=128), mb[:].keys, 48)
        nwt = k.sbuf("nw_" + L, [128, 4, 8], F32)
        for i, nw in enumerate(nws):
            k.load_cols(nwt[:, i, :], nw.t.rearrange("(c p) -> c p", p=128), nw[:].keys, 8)
        sb = k.sbuf("scond_" + L, [128, 8], BF16)
        k.act(sb[:, :], cs[:, :], AF.Silu)
        wbuf = [k.sbuf("modw%d_%s" % (i, L), [128, 8, 512], BF16) for i in range(2)]
        mps = k.banks[6]
        mwv = mw.t.rearrange("(c p) n -> p c n", p=128)
        for blk in range(12):
            wb = wbuf[blk % 2]
            k.dma("pool", wb[:, :, :], V(mwv[:, :, blk * 512:(blk + 1) * 512], mw[:].keys))
            for nn in range(4):
                j = blk * 4 + nn
                for kc in range(8):
                    k.mm(mps[:, j:j + 1], wb[:, kc, nn * 128:(nn + 1) * 128], sb[:, kc:kc + 1], start=(kc == 0), stop=(kc == 7))
        mod = k.sbuf("mod_" + L, [128, 48], F32)
        k.tt(mod[:, :], mps[:, 0:48], mbt[:, :], ALU.add)
        k.stt(cols[:, 0, :], mod[:, 8:16], 1.0, nwt[:, 0, :], ALU.add, ALU.mult)
        k.copy(cols[:, 1, :], mod[:, 0:8], eng="dve")
        k.tt(cols[:, 2, :], mod[:, 16:24], nwt[:, 1, :], ALU.mult)
        k.stt(cols[:, 3, :], mod[:, 32:40], 1.0, nwt[:, 2, :], ALU.add, ALU.mult)
        k.copy(cols[:, 4, :], mod[:, 24:32], eng="dve")
        k.tt(cols[:, 5, :], mod[:, 40:48], nwt[:, 3, :], ALU.mult)
        sc.__exit__(None, None, None)
        return cols

    def rstd_from(self, out, ss, n, tmp):
        k = self
        k.act(tmp, ss, AF.Ln, bias=k.eps_t[:, 0:1], scale=1.0 / n)
        k.act(out, tmp, AF.Exp, scale=-0.5)

    def prenorm(self, cols, ia, ib):
        k = self
        sc = k.scope(); sc.__enter__()
        k.pn_x = [k.sbuf("pn_x%d" % i, [128, 512], F32) for i in range(10)]
        k.pn_sq = [k.sbuf("pn_sq%d" % i, [128, 512], BF16) for i in range(3)]
        k.pn_r = [k.sbuf("pn_r%d" % i, [128, 512], F32) for i in range(2)]
        k.pn_t = [k.sbuf("pn_t%d" % i, [128, 512], F32) for i in range(3)]
        n = 0
        for tt in range(NTT):
            sl = slice(tt * 512, (tt + 1) * 512)
            ss = k.banks[tt % 2]
            xs_t = []
            for fc in range(8):
                xb = k.pn_x[(tt * 8 + fc) % 10]
                k.dma("sp", xb[:, :], k.xs[fc][:, sl])
                sq = k.pn_sq[n % 3]; n += 1
                k.act(sq[:, :], xb[:, :], AF.Square)
                k.mm(ss[:, :], k.ones_b[:, :], sq[:, :], start=(fc == 0), stop=(fc == 7))
                xs_t.append(xb)
            r = k.pn_r[tt % 2]
            k.rstd_from(r[:, :], ss[:, :], float(D), k.pn_t[0][:, :])
            for fc in range(8):
                tm = k.pn_t[1 + fc % 2]
                k.stt(tm[:, :], xs_t[fc][:, :], cols[:, ia, fc:fc + 1], r[:, :], ALU.mult, ALU.mult)
                k.act(k.hT[fc][:, PAD + tt * 512:PAD + (tt + 1) * 512], tm[:, :], AF.Identity, bias=cols[:, ib, fc:fc + 1])
        sc.__exit__(None, None, None)

    def postnorm_alloc(self):
        k = self
        k.po_x = [k.sbuf("po_x%d" % i, [128, 512], F32) for i in range(8)]
        k.po_r = k.sbuf("po_r", [128, 512], F32)
        k.po_t = [k.sbuf("po_t%d" % i, [128, 512], F32) for i in range(3)]

    def postnorm_prefetch(self, tt):
        k = self
        sl = slice(tt * 512, (tt + 1) * 512)
        for fc in range(8):
            k.dma("sp", k.po_x[fc][:, :], k.xs[fc][:, sl])

    def postnorm_tile(self, tt, o_tile, ss, cols, ig):
        k = self
        sl = slice(tt * 512, (tt + 1) * 512)
        k.rstd_from(k.po_r[:, :], ss, float(D), k.po_t[0][:, :])
        for fc in range(8):
            xb = k.po_x[fc]
            tm = k.po_t[1 + fc % 2]
            k.tt(tm[:, :], o_tile[fc], k.po_r[:, :], ALU.mult)
            k.stt(xb[:, :], tm[:, :], cols[:, ig, fc:fc + 1], xb[:, :], ALU.mult, ALU.add)
            k.dma("sp", k.xs[fc][:, sl], xb[:, :])

    def ffn(self, l, cols):
        k = self
        L = "l%d" % l
        wup = k.din("ffn_up_" + L, [D, 2 * DFF])
        wcv = k.din("ffn_conv_w_" + L, [3, 2 * DFF])
        bcv = k.din("ffn_conv_b_" + L, [2 * DFF])
        wdn = k.din("ffn_down_" + L, [DFF, D])
        k.prenorm(cols, 3, 4)
        sc = k.scope(); sc.__enter__()
        k.ff_g = k.sbuf("ff_g", [128, NJ, 1024], BF16, gran=1)
        k.ff_wu = [k.sbuf("ff_wu%d" % i, [128, 8, 2, 512], BF16) for i in range(2)]
        k.ff_wd = [k.sbuf("ff_wd%d" % i, [128, 512], BF16) for i in range(4)]
        k.ff_u = [k.sbuf("ff_u%d" % i, [128, 258], BF16) for i in range(4)]
        k.ff_dgall = k.sbuf("ff_dgall", [128, 2 * NJ, 3, 128], BF16, gran=1)
        k.ff_sa = [k.sbuf("ff_sa%d" % i, [128, 256], F32) for i in range(2)]
        k.ff_o = [k.sbuf("ff_o%d" % i, [128, 512], F32) for i in range(8)]
        k.ff_sq = [k.sbuf("ff_sq%d" % i, [128, 512], BF16) for i in range(2)]
        k.postnorm_alloc()
        cw = k.sbuf("ff_cw_" + L, [128, 44, 3], F32)
        cb = k.sbuf("ff_cb_" + L, [128, 44], F32)
        for tap in range(3):
            k.load_cols(V(cw.t[:, :, tap], cw[:].keys), wcv.t[tap, :].rearrange("(c p) -> c p", p=128), wcv[:].keys, 44)
        k.load_cols(cb[:, :], bcv.t.rearrange("(c p) -> c p", p=128), bcv[:].keys, 44)
        for ch in range(2 * NJ):
            for tap in range(3):
                k.act(V(k.ff_dgall.t[:, ch, tap, :], k.ff_dgall[:, ch].keys), k.ident_b[:, :], AF.Copy, scale=cw[:, ch, tap:tap + 1])
        wupv = wup.t.rearrange("(c p) n -> p c n", p=128)
        nwu = 0
        nu = 0
        nd = 0
        import os
        dbg_tt = int(os.environ.get("FFN_TT", NTT)); dbg_jp = int(os.environ.get("FFN_JP", NJ // 2)); dbg_part = int(os.environ.get("FFN_PART", 9))
        for tp in range(dbg_tt // 2):
            pending = None

            def conv_stage(item):
                j, sg, us, nu_ = item
                dgs = [V(k.ff_dgall.t[:, ab * NJ + j, :, :], k.ff_dgall[:, ab * NJ + j].keys) for ab in range(2)]
                pcs = []
                for ab in range(2):
                    pc = k.banks[4 + ab]
                    for tap in range(3):
                        k.mm(pc[:, 0:256], V(dgs[ab].ap[:, tap, :], dgs[ab].keys), us[ab][:, tap:tap + 256], start=(tap == 0), stop=(tap == 2))
                    pcs.append(pc)
                sa = k.ff_sa[nu_ % 2]
                k.act(sa[:, :], pcs[0][:, 0:256], AF.Silu, bias=cb[:, j:j + 1])
                k.stt(V(k.ff_g.t[:, j, sg * 256:(sg + 1) * 256], k.ff_g[:, j].keys), pcs[1][:, 0:256], cb[:, NJ + j:NJ + j + 1], sa[:, :],
                      ALU.add, ALU.mult)

            for jp in range((NJ + 3) // 4):
                wu = k.ff_wu[nwu % 2]; nwu += 1
                nj_here = min(4, NJ - jp * 4)
                for ab in range(2):
                    c0 = ab * DFF + jp * 512
                    k.dma("pool", V(wu.t[:, :, ab, 0:nj_here * 128], wu[:].keys), V(wupv[:, :, c0:c0 + nj_here * 128], wup[:].keys))
                for jj in range(nj_here):
                    j = jp * 4 + jj
                    for sg in range(4):
                        seg = tp * 4 + sg
                        c_lo = PAD + seg * 256 - 1
                        us = []
                        for ab in range(2):
                            pb = k.banks[(nu * 2 + ab) % 4]
                            for kc in range(8):
                                k.mm(pb[:, 0:258], wu[:, kc, ab, jj * 128:(jj + 1) * 128], k.hT[kc][:, c_lo:c_lo + 258],
                                     start=(kc == 0), stop=(kc == 7))
                            u = k.ff_u[(nu * 2 + ab) % 4]
                            if ab == 0:
                                k.copy(u[:, 0:258], pb[:, 0:258], eng="act")
                            else:
                                k.copy(u[:, 0:258], pb[:, 0:258], eng="dve")
                            uv = V(u.t[:, 0:258:257], u[:].keys)
                            k.tt(uv, uv, k.flags[:, seg * 2:seg * 2 + 2], ALU.mult)
                            us.append(u)
                        if pending is not None:
                            conv_stage(pending)
                        pending = (j, sg, us, nu)
                        nu += 1
            if pending is not None:
                conv_stage(pending)
            for st_ in range(2):
                tt = tp * 2 + st_
                k.postnorm_prefetch(tt)
                ss = k.banks[6]
                for half in range(2):
                    for j in range(NJ):
                        wd = k.ff_wd[nd % 4]; nd += 1
                        k.dma("pool", wd[:, :], V(wdn.t[j * 128:(j + 1) * 128, half * 512:(half + 1) * 512], wdn[:].keys))
                        for nn in range(4):
                            k.mm(k.banks[nn][:, :], wd[:, nn * 128:(nn + 1) * 128], V(k.ff_g.t[:, j, st_ * 512:(st_ + 1) * 512], k.ff_g[:, j].keys),
                                 start=(j == 0), stop=(j == NJ - 1))
                    for nn in range(4):
                        n = half * 4 + nn
                        k.copy(k.ff_o[n][:, :], k.banks[nn][:, :], eng="dve")
                        sq = k.ff_sq[n % 2]
                        k.act(sq[:, :], k.ff_o[n][:, :], AF.Square)
                        k.mm(ss[:, :], k.ones_b[:, :], sq[:, :], start=(n == 0), stop=(n == 7))
                k.postnorm_tile(tt, [k.ff_o[n][:, :] for n in range(8)], ss[:, :], cols, 5)
        sc.__exit__(None, None, None)

    def gelu(self, out, src_psum, n, scr):
        k = self
        k.act(out, src_psum, AF.Gelu_apprx_tanh)

    def outproj_post(self, wname, cols):
        k = self
        wo = k.din(wname, [2048, D])
        sc = k.scope(); sc.__enter__()
        wt = k.sbuf("wo", [128, 16, D], BF16)
        wov = wo.t.rearrange("(c p) n -> p c n", p=128)
        for q in range(4):
            k.dma("pool", V(wt.t[:, q * 4:(q + 1) * 4, :], wt[:].keys), V(wov[:, q * 4:(q + 1) * 4, :], wo[:].keys))
        o_t = [k.sbuf("op_o%d" % i, [128, 512], F32) for i in range(8)]
        sqs = [k.sbuf("op_sq%d" % i, [128, 512], BF16) for i in range(2)]
        k.postnorm_alloc()
        nb = 0
        for tt in range(NTT):
            k.postnorm_prefetch(tt)
            ss = k.banks[6]
            for n in range(8):
                bk = k.banks[nb % 4]; nb += 1
                for kc in range(16):
                    k.mm(bk[:, :], wt[:, kc, n * 128:(n + 1) * 128], V(k.mixedT.t[:, kc, tt * 512:(tt + 1) * 512], k.mixedT[:, kc].keys),
                         start=(kc == 0), stop=(kc == 15))
                k.copy(o_t[n][:, :], bk[:, :], eng="dve")
                sq = sqs[n % 2]
                k.act(sq[:, :], o_t[n][:, :], AF.Square)
                k.mm(ss[:, :], k.ones_b[:, :], sq[:, :], start=(n == 0), stop=(n == 7))
            k.postnorm_tile(tt, [o_t[n][:, :] for n in range(8)], ss[:, :], cols, 2)
        sc.__exit__(None, None, None)

    def mixer_l1(self, cols):
        k = self
        win = k.din("mix_in_l1", [D, L1_PROJ])
        winv = win.t.rearrange("(c p) n -> p c n", p=128)
        lnw = k.din("sg_ln_w", [1024]); lnb = k.din("sg_ln_b", [1024])
        sws = k.din("sg_w_s", [4, 128, 128]); sbs = k.din("sg_b_s", [4, 128])
        sink = k.din("attn_sink", [8])
        ck = k.din("ctx_k", [512, 256]); cv = k.din("ctx_v", [512, 256])
        rope = k.din("rope", [2, 128, T])
        cband = k.din("cst_band", [2, 128, 512])
        crot = k.din("cst_rot", [128, 128])
        nk = k.dout("nk", [T, 256]); nv = k.dout("nv", [T, 256])
        k.prenorm(cols, 0, 1)
        sc0 = k.scope(); sc0.__enter__()
        k.mixedT = k.sbuf("mixedT", [128, 16, T], BF16, gran=1)
        sc = k.scope(); sc.__enter__()
        wgv = k.sbuf("wgv", [128, 8, 1024], BF16)
        for q in range(2):
            k.dma("pool", V(wgv.t[:, :, q * 512:(q + 1) * 512], wgv[:].keys), V(winv[:, :, 1024 + q * 512:1024 + (q + 1) * 512], win[:].keys))
        lnw_t = k.sbuf("lnw_t", [128, 1024], F32); lnb_t = k.sbuf("lnb_t", [128, 1024], F32)
        k.dma("sp", lnw_t[:, :], V(lnw.t.partition_broadcast(128), lnw[:].keys))
        k.dma("sp", lnb_t[:, :], V(lnb.t.partition_broadcast(128), lnb[:].keys))
        wsT = k.sbuf("wsT", [128, 4, 128], BF16)
        wstage = k.sbuf("wstage", [128, 4, 128], F32)
        k.dma("sp", wstage[:, :, :], V(sws.t.rearrange("g i j -> i g j"), sws[:].keys))
        for g in range(4):
            bk = k.banks[g % 2]
            k.transpose(bk[:, 0:128], wstage[:, g, :], k.ident_f[:, :])
            k.copy(wsT[:, g, :], bk[:, 0:128], eng="dve")
        bsf = k.sbuf("bsf", [1, 512], F32); bsb = k.sbuf("bsb", [1, 512], BF16)
        k.dma("sp", bsf[:, :], V(sbs.t.rearrange("(o g) i -> o (g i)", o=1), sbs[:].keys))
        k.copy(bsb[:, :], bsf[:, :], eng="dve")
        gsc = [(k.sbuf("g_x%d" % i, [128, 512], F32), k.sbuf("g_t%d" % i, [128, 512], F32), k.sbuf("g_s%d" % i, [128, 512], F32)) for i in range(2)]
        wu = [k.sbuf("wu1_%d" % i, [128, 8, 128], BF16) for i in range(2)]
        ng = 0
        k.dma("pool", wu[0][:, :, :], V(winv[:, :, 0:128], win[:].keys))
        for n in range(8):
            w = wu[n % 2]
            if n + 1 < 8:
                k.dma("pool", wu[(n + 1) % 2][:, :, :], V(winv[:, :, (n + 1) * 128:(n + 2) * 128], win[:].keys))
            for tt in range(NTT):
                bk = k.banks[ng % 2]
                for kc in range(8):
                    k.mm(bk[:, :], w[:, kc, :], k.hT[kc][:, PAD + tt * 512:PAD + (tt + 1) * 512], start=(kc == 0), stop=(kc == 7))
                k.gelu(V(k.mixedT.t[:, n, tt * 512:(tt + 1) * 512], k.mixedT[:, n].keys), bk[:, :], 512, gsc[ng % 2])
                ng += 1
        gv = [k.sbuf("gv%d" % i, [128, 1024], F32) for i in range(2)]
        gvn = [k.sbuf("gvn%d" % i, [128, 1024], BF16) for i in range(2)]
        st6 = k.sbuf("st6", [128, 2, 6], F32); mv = k.sbuf("mv", [128, 2], F32); rs = k.sbuf("rs", [128, 2], F32)
        for c in range(NCH):
            g_ = gv[c % 2]
            for hf in range(2):
                bk = k.banks[2 + hf]
                for kc in range(8):
                    k.mm(bk[:, :], k.hT[kc][:, PAD + c * 128:PAD + (c + 1) * 128], wgv[:, kc, hf * 512:(hf + 1) * 512], start=(kc == 0), stop=(kc == 7))
                k.gelu(g_[:, hf * 512:(hf + 1) * 512], bk[:, :], 512, gsc[ng % 2]); ng += 1
                k.op("dve", lambda g_=g_, hf=hf: k.nc.vector.bn_stats(out=st6.t[:, hf, :], in_=g_.t[:, hf * 512:(hf + 1) * 512]), [g_[:, :]], [st6[:, :, :]])
            k.op("dve", lambda: k.nc.vector.bn_aggr(out=mv.t[:, :], in_=st6.t[:, :, :].rearrange("p a b -> p (a b)")), [st6[:, :, :]], [mv[:, :]])
            k.act(rs[:, 0:1], mv[:, 1:2], AF.Sqrt, bias=k.eps_t[:, 0:1], scale=1.0)
            k.recip(rs[:, 1:2], rs[:, 0:1])
            k.ts(g_[:, :], g_[:, :], mv[:, 0:1], ALU.subtract, rs[:, 1:2], ALU.mult)
            k.tt(g_[:, :], g_[:, :], lnw_t[:, :], ALU.mult, eng="pool")
            gn = gvn[c % 2]
            k.tt(gn[:, :], g_[:, :], lnb_t[:, :], ALU.add)
            for hf in range(2):
                bk = k.banks[4 + hf]
                for q in range(4):
                    dch = hf * 4 + q
                    g = dch // 2
                    k.mm(bk[:, q * 128:(q + 1) * 128], gn[:, dch * 128:(dch + 1) * 128], wsT[:, g, :], start=True, stop=False)
                    k.mm(bk[:, q * 128:(q + 1) * 128], k.ones_b[0:1, :], bsb[0:1, g * 128:(g + 1) * 128], start=False, stop=True)
                mo = V(k.mixedT.t[:, hf * 4:hf * 4 + 4, c * 128:(c + 1) * 128], [(k.mixedT.id, hf * 4 + q) for q in range(4)])
                k.tt(mo, V(bk.t[:, :].rearrange("p (q i) -> p q i", q=4), bk[:].keys), mo, ALU.mult)
        sc.__exit__(None, None, None)
        sc = k.scope(); sc.__enter__()
        qT = k.sbuf("qT", [128, 8, T], BF16, gran=1)
        kT = k.sbuf("kT", [128, 2, T], BF16, gran=1)
        vtok = k.sbuf("vtok", [128, NCH, 256], BF16, gran=1)
        kcT = k.sbuf("kcT", [128, 2, 512], BF16)
        vc = k.sbuf("vc", [128, 4, 256], BF16)
        skr = k.sbuf("skr", [1, 8, 128], BF16)
        band = k.sbuf("band", [128, 2, 512], BF16)
        scA = k.scope(); scA.__enter__()
        ropeC = k.sbuf("ropeC", [128, T], F32); ropeS = k.sbuf("ropeS", [128, T], F32)
        k.dma("sp", ropeC[:, :], V(rope.t[0], rope[:].keys)); k.dma("sp", ropeS[:, :], V(rope.t[1], rope[:].keys))
        rotf = k.sbuf("rotf", [128, 128], F32); rotb = k.sbuf("rotb", [128, 128], BF16)
        k.dma("sp", rotf[:, :], crot[:, :]); k.copy(rotb[:, :], rotf[:, :], eng="dve")
        wq = [k.sbuf("wq%d" % i, [128, 8, 128], BF16) for i in range(2)]
        qs = [k.sbuf("q_s%d" % i, [128, 512], BF16) for i in range(2)]
        t1 = [k.sbuf("q_t1%d" % i, [128, 512], F32) for i in range(2)]
        t2 = [k.sbuf("q_t2%d" % i, [128, 512], F32) for i in range(2)]
        nq = 0
        k.dma("pool", wq[0][:, :, :], V(winv[:, :, 2048:2048 + 128], win[:].keys))
        for hh in range(10):
            w = wq[hh % 2]
            if hh + 1 < 10:
                c1 = 2048 + (hh + 1) * 128
                k.dma("pool", wq[(hh + 1) % 2][:, :, :], V(winv[:, :, c1:c1 + 128], win[:].keys))
            for tt in range(NTT):
                sl = slice(tt * 512, (tt + 1) * 512)
                bk = k.banks[nq % 2]; bk2 = k.banks[2 + nq % 2]
                for kc in range(8):
                    k.mm(bk[:, :], w[:, kc, :], k.hT[kc][:, PAD + tt * 512:PAD + (tt + 1) * 512], start=(kc == 0), stop=(kc == 7))
                q_ = qs[nq % 2]
                k.copy(q_[:, :], bk[:, :], eng="act")
                k.mm(bk2[:, :], rotb[:, :], q_[:, :])
                a = t1[nq % 2]; b = t2[nq % 2]
                k.tt(a[:, :], q_[:, :], ropeC[:, sl], ALU.mult, eng="pool")
                k.tt(b[:, :], bk2[:, :], ropeS[:, sl], ALU.mult)
                dst = V(qT.t[:, hh, sl], qT[:, hh].keys) if hh < 8 else V(kT.t[:, hh - 8, sl], kT[:, hh - 8].keys)
                k.tt(dst, a[:, :], b[:, :], ALU.add)
                nq += 1
        wkv = k.sbuf("wkv", [128, 8, 512], BF16)
        k.dma("pool", wkv[:, :, :], V(winv[:, :, 3072:3584], win[:].keys))
        kvo = [k.sbuf("kvo%d" % i, [128, 512], F32) for i in range(2)]
        for c in range(NCH):
            bk = k.banks[c % 2]
            for kc in range(8):
                k.mm(bk[:, :], k.hT[kc][:, PAD + c * 128:PAD + (c + 1) * 128], wkv[:, kc, :], start=(kc == 0), stop=(kc == 7))
            o = kvo[c % 2]
            k.copy(o[:, :], bk[:, :], eng="act")
            k.copy(V(vtok.t[:, c, :], vtok[:, c].keys), o[:, 256:512], eng="dve")
            k.dma("sp", V(nk.t[c * 128:(c + 1) * 128, :], nk[:].keys), o[:, 0:256], is_output=True)
            k.dma("sp", V(nv.t[c * 128:(c + 1) * 128, :], nv[:].keys), o[:, 256:512], is_output=True)
        kcs = k.sbuf("kcs", [128, 4, 256], F32)
        k.dma("sp", kcs[:, :, :], V(ck.t.rearrange("(c p) d -> p c d", p=128), ck[:].keys))
        for kvh in range(2):
            bk = k.banks[kvh]
            for sc_ in range(4):
                k.transpose(bk[:, sc_ * 128:(sc_ + 1) * 128], kcs[:, sc_, kvh * 128:(kvh + 1) * 128], k.ident_f[:, :])
            k.copy(kcT[:, kvh, :], bk[:, :], eng="dve")
        k.dma("pool", vc[:, :, :], V(cv.t.rearrange("(c p) d -> p c d", p=128), cv[:].keys))
        skf = k.sbuf("skf", [1, 8], F32); ske = k.sbuf("ske", [1, 8], F32)
        k.dma("sp", skf[:, :], V(sink.t.rearrange("(o h) -> o h", o=1), sink[:].keys))
        k.act(ske[:, :], skf[:, :], AF.Exp)
        k.copy(skr[:, :, :], V(ske.t[:, :].unsqueeze(2).to_broadcast([1, 8, 128]), ske[:].keys), eng="dve")
        bandf = k.sbuf("bandf", [128, 2, 512], F32)
        k.dma("sp", bandf[:, :, :], V(cband.t.rearrange("a p n -> p a n"), cband[:].keys))
        k.copy(band[:, :, :], bandf[:, :, :], eng="dve")
        scA.__exit__(None, None, None)
        pT = [k.sbuf("pT%d" % i, [128, 512], BF16) for i in range(3)]
        mk = [k.sbuf("mk%d" % i, [128, 512], BF16) for i in range(2)]
        rden = [k.sbuf("rden%d" % i, [128, 512], F32) for i in range(2)]
        scale = 128.0 ** -0.5
        npb = 0; nmk = 0; nu = 0
        for c in range(NCH):
            for kvh in range(2):
                blocks = []
                if c > 0: blocks.append(("prev", c - 1))
                blocks.append(("same", c))
                if c < NCH - 1: blocks.append(("next", c + 1))
                for s4 in range(4): blocks.append(("ctx", s4))
                po = k.banks[3 + nu % 2]; pd = k.banks[5 + nu % 2]
                rhs_q = V(qT.t[:, kvh * 4:kvh * 4 + 4, c * 128:(c + 1) * 128], [(qT.id, kvh * 4 + i) for i in range(4)])
                for bi, (kind, idx) in enumerate(blocks):
                    ps = k.banks[npb % 3]
                    p_ = pT[npb % 3]; npb += 1
                    if kind == "ctx":
                        k.mm(V(ps.t[:, :].rearrange("p (h q) -> p h q", h=4), ps[:].keys), kcT[:, kvh, idx * 128:(idx + 1) * 128], rhs_q)
                        k.act(p_[:, :], ps[:, :], AF.Exp, bias=k.flags[:, 112:113], scale=scale)
                        lv = vc[:, idx, kvh * 128:(kvh + 1) * 128]
                    else:
                        k.mm(V(ps.t[:, :].rearrange("p (h q) -> p h q", h=4), ps[:].keys), V(kT.t[:, kvh, idx * 128:(idx + 1) * 128], kT[:, kvh].keys), rhs_q)
                        k.act(p_[:, :], ps[:, :], AF.Exp, scale=scale)
                        if kind != "same":
                            m = mk[nmk % 2]; nmk += 1
                            bsel = 0 if kind == "prev" else 1
                            f0 = 48 + (0 if kind == "prev" else 32) + c
                            k.ts(m[:, :], band[:, bsel, :], k.flags[:, f0:f0 + 1], ALU.mult, k.flags[:, f0 + 16:f0 + 17], ALU.add, eng="pool")
                            k.tt(p_[:, :], p_[:, :], m[:, :], ALU.mult)
                        lv = V(vtok.t[:, idx, kvh * 128:(kvh + 1) * 128], vtok[:, idx].keys)
                    k.mm(po[:, :], lv, p_[:, :], start=(bi == 0), stop=(bi == len(blocks) - 1))
                    k.mm(pd[:, :], k.ones_b[:, :], p_[:, :], start=(bi == 0), stop=False)
                k.mm(pd[:, :], k.ones_b[0:1, :], V(skr.t[0:1, kvh * 4:kvh * 4 + 4, :].rearrange("o h q -> o (h q)"), skr[:].keys), start=False, stop=True)
                rd = rden[nu % 2]
                k.act(rd[:, :], pd[:, :], AF.Ln)
                k.act(rd[:, :], rd[:, :], AF.Exp, scale=-1.0)
                mo = V(k.mixedT.t[:, 8 + kvh * 4:8 + kvh * 4 + 4, c * 128:(c + 1) * 128], [(k.mixedT.id, 8 + kvh * 4 + i) for i in range(4)])
                k.tt(mo, V(po.t[:, :].rearrange("p (h q) -> p h q", h=4), po[:].keys), V(rd.t[:, :].rearrange("p (h q) -> p h q", h=4), rd[:].keys), ALU.mult)
                nu += 1
        sc.__exit__(None, None, None)
        k.outproj_post("mix_out_l1", cols)
        sc0.__exit__(None, None, None)

    def pc_prefetch(self, win, winv, col0):
        k = self
        w = k.pc_w[k.pc_nw % len(k.pc_w)]; k.pc_nw += 1
        k.dma("pool", w[:, :, :], V(winv[:, :, col0:col0 + 128], win[:].keys))
        return w

    def proj_conv5(self, win, winv, col0, cw5, cb, dst_fn, w=None):
        k = self
        if w is None:
            w = k.pc_prefetch(win, winv, col0)
        dg = k.pc_dg[k.pc_n % 2]
        for tap in range(5):
            k.act(dg[:, tap, :], k.ident_b[:, :], AF.Copy, scale=cw5(tap))
        k.pc_n += 1
        pending = None

        def conv_stage(item):
            seg, u, pc = item
            for tap in range(5):
                k.mm(pc[:, 0:256], dg[:, tap, :], u[:, tap:tap + 256], start=(tap == 0), stop=(tap == 4))
            if cb is not None:
                k.act(dst_fn(seg), pc[:, 0:256], AF.Silu, bias=cb)
            else:
                k.act(dst_fn(seg), pc[:, 0:256], AF.Silu)

        for seg in range(NSEG):
            pb = k.banks[k.pc_m % 2]; pc = k.banks[2 + k.pc_m % 2]
            u = k.pc_u[k.pc_m % 2]; k.pc_m += 1
            c_lo = seg * 256
            for kc in range(8):
                k.mm(pb[:, 0:260], w[:, kc, :], k.hT[kc][:, c_lo:c_lo + 260], start=(kc == 0), stop=(kc == 7))
            k.copy(u[:, 0:260], pb[:, 0:260], eng="act")
            k.ts(u[:, 0:2], u[:, 0:2], k.flags[:, 2 * seg:2 * seg + 1], ALU.mult)
            k.ts(u[:, 258:260], u[:, 258:260], k.flags[:, 2 * seg + 1:2 * seg + 2], ALU.mult)
            if pending is not None:
                conv_stage(pending)
            pending = (seg, u, pc)
        conv_stage(pending)

    def pc_alloc(self):
        k = self
        k.pc_w = [k.sbuf("pc_w%d" % i, [128, 8, 128], BF16) for i in range(4)]
        k.pc_nw = 0
        k.pc_dg = [k.sbuf("pc_dg%d" % i, [128, 5, 128], BF16) for i in range(2)]
        k.pc_u = [k.sbuf("pc_u%d" % i, [128, 260], BF16) for i in range(2)]
        k.pc_n = 0; k.pc_m = 0

    def mixer_l0(self, cols):
        k = self
        win = k.din("mix_in_l0", [D, L0_PROJ])
        winv = win.t.rearrange("(c p) n -> p c n", p=128)
        k.prenorm(cols, 0, 1)
        sc0 = k.scope(); sc0.__enter__()
        k.mixedT = k.sbuf("mixedT", [128, 16, T], BF16, gran=1)
        k.l0_consts()
        k.dtab = {nm: k.sbuf("dn_" + nm, [128, NCH, 2, 8], F32) for nm in ("av", "ncs", "ecs", "necs", "dte", "etot")}
        k.beta = k.sbuf("beta", [128, NCH, 16], F32)
        tri = k.din("cst_tri", [2, 128, 128])
        k.tri = k.sbuf("tri", [128, 2, 128], F32)
        k.dma("sp", k.tri[:, :, :], V(tri.t.rearrange("a p n -> p a n"), tri[:].keys))
        import os
        part = os.environ.get("L0_PART", "both")
        s1 = k.scope(); s1.__enter__()
        k.l0_small(win, winv)
        if part in ("both", "ssd"):
            sc = k.scope(); sc.__enter__()
            k.ssd(win, winv)
            sc.__exit__(None, None, None)
        else:
            for fc in range(8):
                k.memset(V(k.mixedT.t[:, fc, :], k.mixedT[:, fc].keys), 0.0, eng="pool")
        s1.__exit__(None, None, None)
        if part in ("both", "dn"):
            sc = k.scope(); sc.__enter__()
            k.dn(win, winv)
            sc.__exit__(None, None, None)
        else:
            for fc in range(8, 16):
                k.memset(V(k.mixedT.t[:, fc, :], k.mixedT[:, fc].keys), 0.0, eng="pool")
        k.outproj_post("mix_out_l0", cols)
        sc0.__exit__(None, None, None)

    def l0_small(self, win, winv):
        k = self
        dtb = k.din("ssd_dt_bias", [2, 16]); alog = k.din("ssd_A_log", [2, 16])
        ddtb = k.din("dn_dt_bias", [2, 8]); dalog = k.din("dn_A_log", [2, 8])
        k.dt_t = k.sbuf("dt_t", [128, NCH, 32], F32)
        k.av = k.sbuf("av", [128, NCH, 2, 24], F32)
        k.cs = k.sbuf("cs", [128, NCH, 2, 24], F32)
        k.ncs = k.sbuf("ncs", [128, NCH, 2, 24], F32)
        k.ecs = k.sbuf("ecs", [128, NCH, 2, 24], F32)
        k.necs = k.sbuf("necs", [128, NCH, 2, 24], F32)
        k.dte = k.sbuf("dte", [128, NCH, 2, 24], F32)
        k.etot = k.sbuf("etot", [128, NCH, 2, 24], F32)
        sc = k.scope(); sc.__enter__()
        wsm = k.sbuf("wsm", [128, 8, 64], BF16)
        k.dma("pool", V(wsm.t[:, :, 0:32], wsm[:].keys), V(winv[:, :, 3072:3104], win[:].keys))
        k.dma("pool", V(wsm.t[:, :, 32:64], wsm[:].keys), V(winv[:, :, 7200:7232], win[:].keys))
        sm = k.sbuf("sm", [128, NCH, 64], F32)
        for c in range(NCH):
            bk = k.banks[c % 2]
            for kc in range(8):
                k.mm(bk[:, 0:64], k.hT[kc][:, PAD + c * 128:PAD + (c + 1) * 128], wsm[:, kc, :], start=(kc == 0), stop=(kc == 7))
            k.copy(V(sm.t[:, c, :], sm[:].keys), bk[:, 0:64], eng="act" if c % 2 == 0 else "dve")
        bias48 = k.sbuf("bias48", [128, 48], F32); al48 = k.sbuf("al48", [128, 48], F32)
        k.dma("sp", bias48[:, 0:32], V(dtb.t.rearrange("a h -> (a h)").partition_broadcast(128), dtb[:].keys))
        k.dma("sp", bias48[:, 32:48], V(ddtb.t.rearrange("a h -> (a h)").partition_broadcast(128), ddtb[:].keys))
        k.dma("sp", al48[:, 0:32], V(alog.t.rearrange("a h -> (a h)").partition_broadcast(128), alog[:].keys))
        k.dma("sp", al48[:, 32:48], V(dalog.t.rearrange("a h -> (a h)").partition_broadcast(128), dalog[:].keys))
        nega = k.sbuf("nega", [128, 48], F32)
        k.act(nega[:, :], al48[:, :], AF.Exp)
        k.ts(nega[:, :], nega[:, :], -1.0, ALU.mult)
        sp_ = k.sbuf("sp_", [128, NCH, 48], F32)
        bb = V(bias48.t[:, :].unsqueeze(1).to_broadcast([128, NCH, 48]), bias48[:].keys)
        k.tt(sp_[:, :, :], V(sm.t[:, :, 0:48], sm[:].keys), bb, ALU.add)
        k.act(sp_[:, :, :], sp_[:, :, :], AF.Exp)
        k.ts(sp_[:, :, :], sp_[:, :, :], 1.0, ALU.add)
        k.act(sp_[:, :, :], sp_[:, :, :], AF.Ln)
        k.copy(k.dt_t[:, :, :], V(sp_.t[:, :, 0:32], sp_[:].keys), eng="dve")
        k.act(k.beta[:, :, :], V(sm.t[:, :, 48:64], sm[:].keys), AF.Sigmoid)
        nb = V(nega.t[:, :].unsqueeze(1).to_broadcast([128, NCH, 48]), nega[:].keys)
        k.tt(sp_[:, :, :], sp_[:, :, :], nb, ALU.mult)
        for d in range(2):
            k.copy(V(k.av.t[:, :, d, 0:16], k.av[:].keys), V(sp_.t[:, :, d * 16:(d + 1) * 16], sp_[:].keys), eng="dve")
            k.copy(V(k.av.t[:, :, d, 16:24], k.av[:].keys), V(sp_.t[:, :, 32 + d * 8:32 + (d + 1) * 8], sp_[:].keys), eng="dve")
        tot = k.sbuf("tot", [128, NCH, 2, 24], F32)
        for d in range(2):
            bk = k.banks[d]; bk2 = k.banks[2 + d]
            rhs = V(k.av.t[:, :, d, :], k.av[:].keys)
            k.mm(V(bk.t[:, 0:384].rearrange("p (c n) -> p c n", c=NCH), bk[:].keys), k.tri[:, d, :], rhs)
            k.mm(V(bk2.t[:, 0:384].rearrange("p (c n) -> p c n", c=NCH), bk2[:].keys), k.ones_f[:, :], rhs)
            k.copy(V(k.cs.t[:, :, d, :], k.cs[:].keys), V(bk.t[:, 0:384].rearrange("p (c n) -> p c n", c=NCH), bk[:].keys), eng="dve")
            k.copy(V(tot.t[:, :, d, :], tot[:].keys), V(bk2.t[:, 0:384].rearrange("p (c n) -> p c n", c=NCH), bk2[:].keys), eng="dve")
        k.ts(k.ncs[:, :, :, :], k.cs[:, :, :, :], -1.0, ALU.mult)
        k.act(k.ecs[:, :, :, :], k.cs[:, :, :, :], AF.Exp)
        k.ts(k.necs[:, :, :, :], k.ecs[:, :, :, :], -1.0, ALU.mult)
        k.act(k.etot[:, :, :, :], tot[:, :, :, :], AF.Exp)
        k.tt(tot[:, :, :, :], tot[:, :, :, :], k.cs[:, :, :, :], ALU.subtract)
        k.act(k.dte[:, :, :, :], tot[:, :, :, :], AF.Exp)
        for nm, src in (("av", k.av), ("ncs", k.ncs), ("ecs", k.ecs), ("necs", k.necs), ("dte", k.dte), ("etot", k.etot)):
            k.copy(k.dtab[nm][:, :, :, :], V(src.t[:, :, :, 16:24], src[:].keys), eng="pool")
        sc.__exit__(None, None, None)

    def build_Lt(self, lt, ps, d, acol, ncol):
        k = self
        abc = V(acol.ap.to_broadcast([128, 128]), acol.keys)
        k.mm(ps, abc, k.tri[:, d, :], start=True, stop=False)
        k.mm(ps, k.ident_b[:, :], k.mbias[:, d, :], start=False, stop=True)
        k.act(lt, ps, AF.Exp, bias=ncol)

    def l0_consts(self):
        k = self
        mb = k.din("cst_mbias", [2, 128, 128]); st = k.din("cst_strict", [2, 128, 128]); blk = k.din("cst_blk", [4, 128, 128])
        k.mbias = k.sbuf("mbias", [128, 2, 128], BF16)
        k.strict = k.sbuf("strict", [128, 2, 128], F32)
        k.blk = k.sbuf("blk", [128, 4, 128], F32)
        k.dma("pool", k.mbias[:, :, :], V(mb.t.rearrange("a p n -> p a n"), mb[:].keys))
        k.blk_b = k.sbuf("blk_b", [128, 4, 128], BF16)
        k.dma("pool", k.blk_b[:, :, :], V(blk.t.rearrange("a p n -> p a n"), blk[:].keys))
        k.dma("sp", k.strict[:, :, :], V(st.t.rearrange("a p n -> p a n"), st[:].keys))
        k.dma("sp", k.blk[:, :, :], V(blk.t.rearrange("a p n -> p a n"), blk[:].keys))

    def ssd(self, win, winv):
        k = self
        cwd = k.din("ssd_conv_w", [5, 2048]); cbd = k.din("ssd_conv_b", [2048])
        dD = k.din("ssd_D", [16]); nwd = k.din("ssd_norm_w", [1024])
        h0d = k.din("ssd_h0", [2, 16, 64, 128])
        hout = k.dout("ssd_out", [NSEG, 2, 16, 64, 128])
        k.pc_alloc()
        cw = k.sbuf("s_cw", [128, 16, 5], F32); cb = k.sbuf("s_cb", [128, 16], F32)
        for tap in range(5):
            k.load_cols(V(cw.t[:, :, tap], cw[:].keys), cwd.t[tap, :].rearrange("(c p) -> c p", p=128), cwd[:].keys, 16)
        k.load_cols(cb[:, :], cbd.t.rearrange("(c p) -> c p", p=128), cbd[:].keys, 16)
        Dbc = k.sbuf("Dbc", [128, 16], F32)
        k.dma("sp", Dbc[:, :], V(dD.t.partition_broadcast(128), dD[:].keys))
        nwc = k.sbuf("s_nw", [128, 8], F32)
        k.load_cols(nwc[:, :], nwd.t.rearrange("(c p) -> c p", p=128), nwd[:].keys, 8)
        scI = k.scope(); scI.__enter__()
        BT = k.sbuf("s_BT", [128, T], BF16); CT = k.sbuf("s_CT", [128, T], BF16)
        xtok = k.sbuf("s_xtok", [128, NCH, 256], BF16, gran=1)
        Btok = k.sbuf("s_Btok", [128, NCH, 128], BF16, gran=1)
        sz = k.sbuf("s_sz", [128, NCH, 256], BF16, gran=1)
        yacc = k.sbuf("s_yacc", [128, NCH, 256], BF16, gran=1)
        hm = [k.sbuf("s_hm%d" % d, [128, 256], F32) for d in range(2)]
        hb = [k.sbuf("s_hb%d" % d, [128, 256], BF16) for d in range(2)]
        bfb = [k.psum_bf(i) for i in range(2)]
        n = 0
        for g in range(4):
            scA = k.scope(); scA.__enter__()
            xT = k.sbuf("s_xT", [128, 2, T], BF16, gran=1)
            wz = [k.sbuf("s_wz%d" % i, [128, 8, 256], BF16) for i in range(1)]
            chB = 8 + g; chC = 12 + g
            pw = [k.pc_prefetch(win, winv, 1024 + ch_ * 128) for ch_ in (2 * g, 2 * g + 1, chB, chC)]
            w = wz[0]
            k.dma("pool", w[:, :, :], V(winv[:, :, g * 256:(g + 1) * 256], win[:].keys))
            for q in range(2):
                ch = 2 * g + q
                k.proj_conv5(win, winv, 1024 + ch * 128, lambda tap, ch=ch: cw[:, ch, tap:tap + 1], cb[:, ch:ch + 1],
                             lambda seg, q=q: V(xT.t[:, q, seg * 256:(seg + 1) * 256], xT[:, q].keys), w=pw[q])
            k.proj_conv5(win, winv, 1024 + chB * 128, lambda tap: cw[:, chB, tap:tap + 1], cb[:, chB:chB + 1], lambda seg: BT[:, seg * 256:(seg + 1) * 256], w=pw[2])
            k.proj_conv5(win, winv, 1024 + chC * 128, lambda tap: cw[:, chC, tap:tap + 1], cb[:, chC:chC + 1], lambda seg: CT[:, seg * 256:(seg + 1) * 256], w=pw[3])
            for c in range(NCH):
                cs_ = slice(c * 128, (c + 1) * 128)
                bk = k.banks[4 + c % 2]
                for kc in range(8):
                    k.mm(bk[:, 0:256], k.hT[kc][:, PAD + c * 128:PAD + (c + 1) * 128], w[:, kc, :], start=(kc == 0), stop=(kc == 7))
                k.act(V(sz.t[:, c, :], sz[:, c].keys), bk[:, 0:256], AF.Silu)
                pb = bfb[c % 2]
                for q in range(2):
                    k.transpose(pb[:, q * 128:(q + 1) * 128], V(xT.t[:, q, cs_], xT[:, q].keys), k.ident_b[:, :])
                k.transpose(pb[:, 256:384], BT[:, cs_], k.ident_b[:, :])
                k.copy(V(xtok.t[:, c, :], xtok[:, c].keys), pb[:, 0:256], eng="dve")
                k.copy(V(Btok.t[:, c, :], Btok[:, c].keys), pb[:, 256:384], eng="dve")
            scA.__exit__(None, None, None)
            scB = k.scope(); scB.__enter__()
            Gs = [k.sbuf("s_G%d" % i, [128, 128], F32) for i in range(2)]
            Lt = [k.sbuf("s_Lt%d" % i, [128, 4, 128], F32) for i in range(2)]
            St = [k.sbuf("s_St%d" % i, [128, 4, 128], BF16) for i in range(2)]
            xdt = [k.sbuf("s_xdt%d" % i, [128, 256], BF16) for i in range(2)]
            xde = [k.sbuf("s_xde%d" % i, [128, 256], BF16) for i in range(2)]
            xD = [k.sbuf("s_xD%d" % i, [128, 256], BF16) for i in range(2)]
            htmp = [k.sbuf("s_ht%d" % i, [128, 256], F32) for i in range(2)]
            hstage = [k.sbuf("s_hs%d" % i, [128, 2, 128], F32) for i in range(2)]
            ytmp = [k.sbuf("s_yt%d" % i, [128, 256], F32) for i in range(2)]
            yz = [k.sbuf("s_yz%d" % i, [128, 256], BF16) for i in range(2)]
            for d in range(2):
                hs = hstage[d]
                k.dma("sp", hs[:, :, :], V(h0d.t[d, 4 * g:4 * g + 4].rearrange("(a h) p n -> (h p) a n", a=2), h0d[:].keys))
                bk = k.banks[6]
                for a in range(2):
                    k.transpose(bk[:, a * 128:(a + 1) * 128], hs[:, a, :], k.ident_f[:, :])
                k.copy(hm[d][:, :], bk[:, 0:256], eng="dve")
                k.copy(hb[d][:, :], hm[d][:, :], eng="act")
            for step in range(NCH):
                for d in range(2):
                    c = step if d == 0 else NCH - 1 - step
                    second = (d == 0 and c >= 8) or (d == 1 and c < 8)
                    cs_ = slice(c * 128, (c + 1) * 128)
                    i2 = d
                    bg = k.banks[0]
                    k.mm(bg[:, 0:128], BT[:, cs_], CT[:, cs_])
                    k.copy(Gs[i2][:, :], bg[:, 0:128], eng="act")
                    bl = k.banks[1 + i2]
                    for hh in range(4):
                        acol = V(k.av.t[:, c, d, 4 * g + hh:4 * g + hh + 1], k.av[:].keys)
                        abc = V(acol.ap.to_broadcast([128, 128]), acol.keys)
                        k.mm(bl[:, hh * 128:(hh + 1) * 128], abc, k.tri[:, d, :], start=True, stop=False)
                        k.mm(bl[:, hh * 128:(hh + 1) * 128], k.ident_b[:, :], k.mbias[:, d, :], start=False, stop=True)
                    for hh in range(4):
                        k.act(V(Lt[i2].t[:, hh, :], Lt[i2][:].keys), bl[:, hh * 128:(hh + 1) * 128], AF.Exp,
                              bias=V(k.ncs.t[:, c, d, 4 * g + hh:4 * g + hh + 1], k.ncs[:].keys))
                    k.tt(St[i2][:, :, :], Lt[i2][:, :, :], V(Gs[i2].t[:, :].unsqueeze(1).to_broadcast([128, 4, 128]), Gs[i2][:].keys), ALU.mult)
                    dtb_ = V(k.dt_t.t[:, c, d * 16 + 4 * g:d * 16 + 4 * g + 4].unsqueeze(2).to_broadcast([128, 4, 64]), k.dt_t[:].keys)
                    dte_ = V(k.dte.t[:, c, d, 4 * g:4 * g + 4].unsqueeze(2).to_broadcast([128, 4, 64]), k.dte[:].keys)
                    ecs_ = V(k.ecs.t[:, c, d, 4 * g:4 * g + 4].unsqueeze(2).to_broadcast([128, 4, 64]), k.ecs[:].keys)
                    eto_ = V(k.etot.t[:, c, d, 4 * g:4 * g + 4].unsqueeze(2).to_broadcast([128, 4, 64]), k.etot[:].keys)
                    x3 = V(xtok.t[:, c, :].rearrange("p (h q) -> p h q", h=4), xtok[:, c].keys)
                    v3 = lambda b: V(b.t[:, :].rearrange("p (h q) -> p h q", h=4), b[:].keys)
                    k.tt(v3(xdt[i2]), x3, dtb_, ALU.mult)
                    k.tt(v3(xde[i2]), v3(xdt[i2]), dte_, ALU.mult, eng="pool")
                    yd = k.banks[3 + 2 * d]; yo = SubBuf(k.banks[4 + 2 * d], 0, 256); ps = SubBuf(k.banks[4 + 2 * d], 256, 256)
                    if second:
                        Db = V(Dbc.t[:, 4 * g:4 * g + 4].unsqueeze(2).to_broadcast([128, 4, 64]), Dbc[:].keys)
                        k.tt(v3(xD[i2]), x3, Db, ALU.mult, eng="pool")
                    for hh in range(4):
                        k.mm(yd[:, hh * 64:(hh + 1) * 64], V(St[i2].t[:, hh, :], St[i2][:].keys), xdt[i2][:, hh * 64:(hh + 1) * 64],
                             start=True, stop=(not second))
                        if second:
                            k.mm(yd[:, hh * 64:(hh + 1) * 64], k.ident_b[:, :], xD[i2][:, hh * 64:(hh + 1) * 64], start=False, stop=True)
                    k.mm(yo[:, 0:256], CT[:, cs_], hb[d][:, :])
                    yt = ytmp[i2]
                    k.tt(v3(yt), V(yo.t[:, 0:256].rearrange("p (h q) -> p h q", h=4), yo[:].keys), ecs_, ALU.mult)
                    ya = V(yacc.t[:, c, :], yacc[:, c].keys)
                    if not second:
                        k.tt(ya, yd[:, 0:256], yt[:, :], ALU.add)
                    else:
                        k.tt(yt[:, :], yd[:, 0:256], yt[:, :], ALU.add)
                        k.tt(yt[:, :], yt[:, :], ya, ALU.add, eng="pool")
                        k.tt(yz[i2][:, :], yt[:, :], V(sz.t[:, c, :], sz[:, c].keys), ALU.mult)
                        pb = bfb[i2]
                        for q in range(2):
                            k.transpose(pb[:, q * 128:(q + 1) * 128], yz[i2][:, q * 128:(q + 1) * 128], k.ident_b[:, :])
                        mo = V(k.mixedT.t[:, 2 * g:2 * g + 2, cs_], [(k.mixedT.id, 2 * g), (k.mixedT.id, 2 * g + 1)])
                        k.copy(mo, V(pb.t[:, 0:256].rearrange("p (q t) -> p q t", q=2), pb[:].keys), eng="act")
                    k.mm(ps[:, 0:256], V(Btok.t[:, c, :], Btok[:, c].keys), xde[i2][:, :])
                    ht = htmp[i2]
                    k.tt(v3(ht), v3(hm[d]), eto_, ALU.mult, eng="pool")
                    k.tt(hm[d][:, :], ps[:, 0:256], ht[:, :], ALU.add)
                    last = (c % 2 == 1) if d == 0 else (c % 2 == 0)
                    if last:
                        seg = c // 2
                        bk = k.banks[6]
                        hs2 = hstage[i2]
                        for a in range(2):
                            k.transpose(bk[:, a * 128:(a + 1) * 128], hm[d][:, a * 128:(a + 1) * 128], k.ident_f[:, :])
                        k.copy(V(hs2.t[:, :, :], hs2[:].keys), V(bk.t[:, 0:256].rearrange("p (a n) -> p a n", a=2), bk[:].keys), eng="act")
                        k.dma("sp", V(hout.t[seg, d, 4 * g:4 * g + 4].rearrange("(a h) p n -> (h p) a n", a=2), hout[:].keys), hs2[:, :, :], is_output=True)
                    cn = c + 1 if d == 0 else c - 1
                    if 0 <= cn < NCH:
                        fcol = (16 if d == 0 else 32) + cn
                        k.ts(hm[d][:, :], hm[d][:, :], k.flags[:, fcol:fcol + 1], ALU.mult)
                        k.copy(hb[d][:, :], hm[d][:, :], eng="act")
            scB.__exit__(None, None, None)
        scI.__exit__(None, None, None)
        sq = [k.sbuf("s_sq%d" % i, [128, 512], BF16) for i in range(2)]
        rr = k.sbuf("s_rr", [128, 512], F32); rt = k.sbuf("s_rt", [128, 512], F32)
        for tt in range(NTT):
            sl = slice(tt * 512, (tt + 1) * 512)
            ss = k.banks[tt % 2]
            for fc in range(8):
                s_ = sq[fc % 2]
                k.act(s_[:, :], V(k.mixedT.t[:, fc, sl], k.mixedT[:, fc].keys), AF.Square)
                k.mm(ss[:, :], k.ones_b[:, :], s_[:, :], start=(fc == 0), stop=(fc == 7))
            k.rstd_from(rr[:, :], ss[:, :], 1024.0, rt[:, :])
            for fc in range(8):
                mv_ = V(k.mixedT.t[:, fc, sl], k.mixedT[:, fc].keys)
                k.stt(mv_, mv_, nwc[:, fc:fc + 1], rr[:, :], ALU.mult, ALU.mult)

    def dn(self, win, winv):
        k = self
        cwd = k.din("dn_conv_w", [5, 3072]); nwd = k.din("dn_norm_w", [128])
        s0d = k.din("dn_h0", [2, 8, 128, 128])
        sout = k.dout("dn_out", [NSEG, 2, 8, 128, 128])
        cw = k.sbuf("d_cw", [128, 24, 5], F32)
        for tap in range(5):
            k.load_cols(V(cw.t[:, :, tap], cw[:].keys), cwd.t[tap, :].rearrange("(c p) -> c p", p=128), cwd[:].keys, 24)
        nwb = k.sbuf("d_nwb", [128, 128], F32)
        k.dma("sp", nwb[:, :], V(nwd.t.partition_broadcast(128), nwd[:].keys))
        lnsc = k.sbuf("d_lnsc", [128, 1], F32)
        k.memset(lnsc[:, :], -0.5 * float(np.log(128.0)), eng="pool")
        zero1 = k.sbuf("d_zero1", [128, 1], F32)
        k.memset(zero1[:, :], 0.0, eng="pool")
        qT = k.sbuf("d_qT", [128, T], BF16, gran=128); kT = k.sbuf("d_kT", [128, T], BF16, gran=128); vT = k.sbuf("d_vT", [128, T], BF16, gran=128)
        khtok = k.sbuf("d_khtok", [128, NCH, 128], BF16, gran=1); vtok = k.sbuf("d_vtok", [128, NCH, 128], BF16, gran=1)
        oacc = k.sbuf("d_oacc", [128, NCH, 128], F32, gran=1)
        Wall = k.sbuf("d_Wall", [128, 2 * NCH, 128], BF16, gran=1)
        QKall = k.sbuf("d_QKall", [128, 2 * NCH, 128], BF16, gran=1)
        wg = [k.sbuf("d_wg%d" % i, [128, 8, 128], BF16) for i in range(1)] * 2
        import os
        IDT = BF16 if os.environ.get("DN_INV", "bf16") == "bf16" else F32
        G = 4
        NT = 5
        tmpf = [k.sbuf("d_tf%d" % i, [128, 128], F32) for i in range(NT)]
        tmpb = [k.sbuf("d_tb%d" % i, [128, 128], BF16) for i in range(10)]
        Sm = [k.sbuf("d_S%d" % d, [128, 128], F32) for d in range(2)]
        Sb = [k.sbuf("d_Sb%d" % d, [128, 128], BF16) for d in range(2)]
        fin16 = k.sbuf("d_fin16", [128, 16], F32); fin16b = k.sbuf("d_fin16b", [128, 16], F32)
        pbf = k.psum_bf(1)
        st = {"bank": 0, "tf": 0, "tb": 0, "ev": 0, "lt": 0}
        T_ = k.dtab

        def nbank():
            b = k.banks[st["bank"] % 7]; st["bank"] += 1
            return b

        def tf():
            t = tmpf[st["tf"] % NT]; st["tf"] += 1
            return t

        def tb():
            t = tmpb[st["tb"] % 10]; st["tb"] += 1
            return t

        def evac(dst, src, scale=None):
            st["ev"] += 1
            if scale is not None:
                k.act(dst, src, AF.Copy, scale=scale)
            elif st["ev"] % 3 != 0:
                k.copy(dst, src, eng="act")
            else:
                k.copy(dst, src, eng="dve")

        I_ = k.ident_b if IDT == BF16 else k.ident_f

        def mm1(lhsT, rhs, dst):
            b = nbank()
            k.mm(b[:, 0:128], lhsT, rhs)
            evac(dst, b[:, 0:128])
            return dst

        def mmadd(lhsT, rhs, sb, dst, op=ALU.add, first_sb=False):
            b = nbank()
            k.mm(b[:, 0:128], lhsT, rhs)
            if first_sb:
                k.tt(dst, sb, b[:, 0:128], op)
            else:
                k.tt(dst, b[:, 0:128], sb, op)
            return dst

        F_ = lambda Tq: Tq[:, :, :]
        Q_ = lambda Tq, q: V(Tq.t[:, q, :], Tq[:].keys)
        bc_ = lambda v2: V(v2.ap.unsqueeze(1).to_broadcast([128, 4, 128]), v2.keys)
        bank3 = lambda b: V(b.t[:, :].rearrange("p (q i) -> p q i", q=4), b[:].keys)
        opq = lambda x, q: Q_(x, q) if isinstance(x, Buf) else x

        def mmq(lhsT, rhs, dst):
            b = nbank()
            for q in range(4):
                k.mm(b[:, q * 128:(q + 1) * 128], opq(lhsT, q), opq(rhs, q))
            evac(F_(dst), bank3(b))
            return dst

        def mmaddq(lhsT, rhs, sb, dstv, op=ALU.add, first_sb=False):
            b = nbank()
            for q in range(4):
                k.mm(b[:, q * 128:(q + 1) * 128], opq(lhsT, q), opq(rhs, q))
            if first_sb:
                k.tt(dstv, sb, bank3(b), op)
            else:
                k.tt(dstv, bank3(b), sb, op)

        def quad_gen(h, d, c0, S, LtT):
            u0 = d * NCH + c0
            css = [slice((c0 + q) * 128, (c0 + q + 1) * 128) for q in range(4)]
            U, UT = S[0], S[1]
            s_ = S[2:10]
            Iv = I_[:, :]
            bL = nbank()
            for q in range(4):
                c = c0 + q
                acol = V(T_["av"].t[:, c, d, h:h + 1], T_["av"][:].keys)
                abc = V(acol.ap.to_broadcast([128, 128]), acol.keys)
                k.mm(bL[:, q * 128:(q + 1) * 128], abc, k.tri[:, d, :], start=True, stop=False)
                k.mm(bL[:, q * 128:(q + 1) * 128], k.ident_b[:, :], k.mbias[:, d, :], start=False, stop=True)
            for q in range(4):
                c = c0 + q
                k.act(Q_(LtT, q), bL[:, q * 128:(q + 1) * 128], AF.Exp, bias=V(T_["ncs"].t[:, c, d, h:h + 1], T_["ncs"][:].keys))
            yield
            Ls = s_[7]
            k.tt(F_(Ls), F_(LtT), bc_(k.strict[:, d, :]), ALU.mult, eng="pool")
            bQ = nbank()
            for q in range(4):
                k.mm(bQ[:, q * 128:(q + 1) * 128], kT[:, css[q]], qT[:, css[q]])
            k.tt(V(QKall.t[:, u0:u0 + 4, :], [(QKall.id, u0 + q) for q in range(4)]), bank3(bQ), F_(LtT), ALU.mult)
            yield
            bA = nbank()
            for q in range(4):
                k.mm(bA[:, q * 128:(q + 1) * 128], kT[:, css[q]], kT[:, css[q]])
            for q in range(4):
                c = c0 + q
                beta_ = V(k.beta.t[:, c, d * 8 + h:d * 8 + h + 1], k.beta[:].keys)
                k.stt(Q_(U, q), bA[:, q * 128:(q + 1) * 128], beta_, Q_(Ls, q), ALU.mult, ALU.mult)
            yield
            mmq(U, Iv, UT)
            Ud, UdT, P = s_[0], s_[1], s_[2]
            k.tt(F_(Ud), F_(U), bc_(k.blk_b[:, 0, :]), ALU.mult, eng="pool")
            yield
            k.tt(F_(UdT), F_(UT), bc_(k.blk_b[:, 0, :]), ALU.mult)
            k.tt(F_(P), bc_(Iv), F_(Ud), ALU.subtract)
            yield
            V1 = mmq(UdT, Ud, s_[3]); V1T = mmq(Ud, UdT, s_[4])
            yield
            A1 = s_[5]; k.tt(F_(A1), F_(V1T), bc_(Iv), ALU.add, eng="pool")
            V2 = mmq(V1T, V1, s_[0]); V2T = mmq(V1, V1T, s_[1])
            yield
            P1 = mmq(A1, P, s_[6])
            A3 = s_[3]; mmaddq(V2, V2T, bc_(Iv), F_(A3))
            yield
            A2 = s_[2]; k.tt(F_(A2), F_(V2T), bc_(Iv), ALU.add, eng="pool")
            yield
            P2 = mmq(A2, P1, s_[5])
            yield
            Wd = mmq(A3, P2, s_[4])
            yield
            WdT = s_[6]; mmq(Wd, Iv, WdT)
            slots = {1: (s_[3], s_[7]), 2: (s_[4], s_[6])}
            for lvl in (1, 2, 3):
                B, BT = s_[0], s_[1]
                k.tt(F_(B), F_(U), bc_(k.blk_b[:, lvl, :]), ALU.mult)
                k.tt(F_(BT), F_(UT), bc_(k.blk_b[:, lvl, :]), ALU.mult, eng="pool")
                yield
                Y = mmq(BT, Wd, s_[2])
                if lvl < 3:
                    Yt = mmq(B, WdT, s_[5])
                    yield
                    nW, nWT = slots[lvl]
                    mmaddq(WdT, Y, F_(Wd), F_(nW), op=ALU.subtract, first_sb=True)
                    mmaddq(Wd, Yt, F_(WdT), F_(nWT), op=ALU.subtract, first_sb=True)
                    Wd, WdT = nW, nWT
                    yield
                else:
                    yield
                    mmaddq(WdT, Y, F_(Wd), V(Wall.t[:, u0:u0 + 4, :], [(Wall.id, u0 + q) for q in range(4)]), op=ALU.subtract, first_sb=True)

        def chain_step(h, d, c):
            cs_ = slice(c * 128, (c + 1) * 128)
            u = d * NCH + c
            col = lambda nm: V(T_[nm].t[:, c, d, h:h + 1], T_[nm][:].keys)
            beta_ = V(k.beta.t[:, c, d * 8 + h:d * 8 + h + 1], k.beta[:].keys)
            Wb = V(Wall.t[:, u, :], Wall[:, u].keys); QKm = V(QKall.t[:, u, :], QKall[:, u].keys)
            kend = tb()[:, :]
            k.act(kend, V(khtok.t[:, c, :], khtok[:, c].keys), AF.Copy, scale=col("dte"))
            bK = nbank(); k.mm(bK[:, 0:128], kT[:, cs_], Sb[d][:, :])
            bS = nbank(); k.mm(bS[:, 0:128], qT[:, cs_], Sb[d][:, :])
            R0 = tb()[:, :]
            k.stt(R0, bK[:, 0:128], col("necs"), V(vtok.t[:, c, :], vtok[:, c].keys), ALU.mult, ALU.add)
            t_ = tf()[:, :]
            k.act(t_, bS[:, 0:128], AF.Copy, scale=col("ecs"))
            bV = nbank(); k.mm(bV[:, 0:128], Wb, R0)
            vnew = tb()[:, :]
            k.act(vnew, bV[:, 0:128], AF.Copy, scale=beta_)
            bU = nbank(); k.mm(bU[:, 0:128], kend, vnew)
            bO = nbank(); k.mm(bO[:, 0:128], QKm, vnew)
            k.stt(Sm[d][:, :], Sm[d][:, :], col("etot"), bU[:, 0:128], ALU.mult, ALU.add)
            last = (c % 2 == 1) if d == 0 else (c % 2 == 0)
            if last:
                k.dma("sp", V(sout.t[c // 2, d, h], sout[:].keys), Sm[d][:, :], is_output=True)
            cn = c + 1 if d == 0 else c - 1
            if 0 <= cn < NCH:
                fcol = (16 if d == 0 else 32) + cn
                k.ts(Sm[d][:, :], Sm[d][:, :], k.flags[:, fcol:fcol + 1], ALU.mult)
                k.copy(Sb[d][:, :], Sm[d][:, :], eng="act")
            oa = V(oacc.t[:, c, :], oacc[:, c].keys)
            first = (d == 0 and c < 8) or (d == 1 and c >= 8)
            if first:
                k.tt(oa, bO[:, 0:128], t_, ALU.add)
            else:
                k.tt(t_, bO[:, 0:128], t_, ALU.add)
                k.tt(oa, oa, t_, ALU.add, eng="pool")

        k.marks = getattr(k, "marks", [])
        mk_ = lambda lab: k.marks.append((lab, k.cnt["e_pe"]))
        for h in range(8):
            mk_("dn%d:proj" % h)
            sc1 = k.scope(); sc1.__enter__()
            k.pc_alloc()
            sqb = [k.sbuf("d_sq%d" % i, [128, 512], BF16) for i in range(2)]
            rr = [k.sbuf("d_rr%d" % i, [128, 512], F32) for i in range(2)]
            pw = [k.pc_prefetch(win, winv, coff + h * 128) for coff in (3104, 4128, 5152)]
            for i_, (buf, coff, cc) in enumerate(((qT, 3104, h), (kT, 4128, 8 + h), (vT, 5152, 16 + h))):
                k.proj_conv5(win, winv, coff + h * 128, lambda tap, cc=cc: cw[:, cc, tap:tap + 1], None,
                             lambda seg, buf=buf: buf[:, seg * 256:(seg + 1) * 256], w=pw[i_])
            mk_("dn%d:l2" % h)
            for (buf, isq) in ((qT, True), (kT, False)):
                for tt in range(NTT):
                    sl = slice(tt * 512, (tt + 1) * 512)
                    s2 = sqb[tt % 2]; r_ = rr[tt % 2]
                    k.act(s2[:, :], buf[:, sl], AF.Square)
                    b = nbank()
                    k.mm(b[:, :], k.ones_b[:, :], s2[:, :])
                    k.act(r_[:, :], b[:, :], AF.Ln, bias=k.eps_t[:, 0:1])
                    k.act(r_[:, :], r_[:, :], AF.Exp, scale=-0.5, bias=(lnsc[:, 0:1] if isq else zero1[:, 0:1]))
                    k.tt(buf[:, sl], buf[:, sl], r_[:, :], ALU.mult)
            w = wg[h % 2]
            k.dma("pool", w[:, :, :], V(winv[:, :, 6176 + h * 128:6176 + (h + 1) * 128], win[:].keys))
            for c in range(NCH):
                cs_ = slice(c * 128, (c + 1) * 128)
                k.transpose(pbf[:, 0:128], kT[:, cs_], k.ident_b[:, :])
                k.transpose(pbf[:, 128:256], vT[:, cs_], k.ident_b[:, :])
                k.copy(V(khtok.t[:, c, :], khtok[:, c].keys), pbf[:, 0:128], eng="dve")
                k.copy(V(vtok.t[:, c, :], vtok[:, c].keys), pbf[:, 128:256], eng="dve")
            sc1.__exit__(None, None, None)
            mk_("dn%d:P" % h)
            sc2 = k.scope(); sc2.__enter__()
            scr = [[k.sbuf("d_s%d_%d" % (g_, i), [128, 4, 128], IDT) for i in range(10)] for g_ in range(G)]
            quads = [(d, c0) for d in range(2) for c0 in range(0, NCH, 4)]
            for g0 in range(0, len(quads), G):
                gens = [quad_gen(h, d, c0, scr[i], scr[i][2 + 6]) for i, (d, c0) in enumerate(quads[g0:g0 + G])]
                while gens:
                    for g_ in list(gens):
                        try:
                            next(g_)
                        except StopIteration:
                            gens.remove(g_)
            sc2.__exit__(None, None, None)
            mk_("dn%d:C" % h)
            for d in range(2):
                k.dma("sp", Sm[d][:, :], V(s0d.t[d, h], s0d[:].keys))
                k.copy(Sb[d][:, :], Sm[d][:, :], eng="act")
            for step in range(NCH):
                chain_step(h, 0, step)
                chain_step(h, 1, NCH - 1 - step)
            mk_("dn%d:fin" % h)
            for c in range(NCH):
                oa = V(oacc.t[:, c, :], oacc[:, c].keys)
                junk = tf()[:, :]
                k.act(junk, oa, AF.Square, accum_out=fin16[:, c:c + 1])
            k.act(fin16b[:, :], fin16[:, :], AF.Sqrt, bias=k.eps_t[:, 0:1], scale=1.0 / 128.0)
            k.recip(fin16[:, :], fin16b[:, :])
            for c in range(NCH):
                cs_ = slice(c * 128, (c + 1) * 128)
                b = nbank()
                for kc in range(8):
                    k.mm(b[:, 0:128], k.hT[kc][:, PAD + c * 128:PAD + (c + 1) * 128], w[:, kc, :], start=(kc == 0), stop=(kc == 7))
                sg = tb()[:, :]
                k.act(sg, b[:, 0:128], AF.Silu)
                oa = V(oacc.t[:, c, :], oacc[:, c].keys)
                o1 = tf()[:, :]
                k.stt(o1, oa, fin16[:, c:c + 1], nwb[:, :], ALU.mult, ALU.mult)
                ob = tb()[:, :]
                k.tt(ob, o1, sg, ALU.mult)
                k.transpose(pbf[:, 0:128], ob, k.ident_b[:, :])
                k.copy(V(k.mixedT.t[:, 8 + h, cs_], k.mixedT[:, 8 + h].keys), pbf[:, 0:128], eng="dve")


def core_inputs(inputs, core):
    f32 = np.float32
    d = {}
    flags = np.zeros(128, f32)
    rope = np.zeros((2, 128, T), f32)
    if core < 4:
        rope[0] = 1.0
        d["ctx_k"] = np.zeros((512, 256), f32)
        d["ctx_v"] = np.zeros((512, 256), f32)
        flags[112] = -30000.0
        for c in range(16):
            flags[48 + c] = 0.0; flags[64 + c] = 1.0 if c % 2 == 1 else 0.0
            flags[80 + c] = 0.0; flags[96 + c] = 1.0 if c % 2 == 0 else 0.0
        d["x"] = np.ascontiguousarray(inputs["x_prompt"][core * 8:(core + 1) * 8].reshape(T, D))
        d["cond"] = np.ascontiguousarray(inputs["c_ctx"])
        for c in range(16):
            flags[16 + c] = 0.0 if c % 2 == 0 else 1.0
            flags[32 + c] = 0.0 if c % 2 == 1 else 1.0
    else:
        b = (core - 4) % 2
        d["ctx_k"] = np.ascontiguousarray(inputs["cache_l1_k"][b].reshape(512, 256))
        d["ctx_v"] = np.ascontiguousarray(inputs["cache_l1_v"][b].reshape(512, 256))
        pos = np.arange(T)
        inv = (10000.0 ** (-np.arange(32, dtype=np.float64) / 32.0))
        ang = np.concatenate([(pos // 64)[:, None] * inv[None, :], (pos % 64)[:, None] * inv[None, :]], axis=1)
        ang = ang.astype(f32)
        rope[0] = np.concatenate([np.cos(ang), np.cos(ang)], axis=1).T
        rope[1] = np.concatenate([np.sin(ang), np.sin(ang)], axis=1).T
        for c in range(16):
            flags[48 + c] = 1.0; flags[80 + c] = 1.0
        d["x"] = np.ascontiguousarray(inputs["x_sample"][b])
        d["cond"] = np.ascontiguousarray(inputs["c"][b])
        for s in range(8):
            flags[2 * s] = 0.0 if s == 0 else 1.0
            flags[2 * s + 1] = 0.0 if s == 7 else 1.0
        for c in range(16):
            flags[16 + c] = 0.0 if c == 0 else 1.0
            flags[32 + c] = 0.0 if c == 15 else 1.0
    d["flags"] = flags
    d["rope"] = rope
    d["cst_ident"] = np.eye(128, dtype=f32)
    sq = np.arange(128)
    band = np.zeros((2, 128, 512), f32)
    band[0] = np.tile((sq[:, None] >= sq[None, :]).astype(f32), (1, 4))
    band[1] = np.tile((sq[:, None] <= sq[None, :]).astype(f32), (1, 4))
    d["cst_band"] = band
    rot = np.zeros((128, 128), f32)
    for dp in range(64):
        rot[dp + 64, dp] = -1.0
        rot[dp, dp + 64] = 1.0
    d["cst_rot"] = rot
    tri = np.zeros((2, 128, 128), f32)
    tri[0] = (sq[:, None] <= sq[None, :]).astype(f32)
    tri[1] = (sq[:, None] >= sq[None, :]).astype(f32)
    d["cst_tri"] = tri
    mbias = np.zeros((2, 128, 128), f32)
    mbias[0] = np.where(sq[None, :] >= sq[:, None], 0.0, -30000.0)
    mbias[1] = np.where(sq[None, :] <= sq[:, None], 0.0, -30000.0)
    d["cst_mbias"] = mbias
    strict = np.zeros((2, 128, 128), f32)
    strict[0] = (sq[None, :] > sq[:, None]).astype(f32)
    strict[1] = (sq[None, :] < sq[:, None]).astype(f32)
    d["cst_strict"] = strict
    blk = np.zeros((4, 128, 128), f32)
    bd = lambda b: ((sq[:, None] // b) == (sq[None, :] // b)).astype(f32)
    blk[0] = bd(16); blk[1] = bd(32) - bd(16); blk[2] = bd(64) - bd(32); blk[3] = 1.0 - bd(64)
    d["cst_blk"] = blk
    if core < 4:
        d["ssd_h0"] = np.zeros((2, 16, 64, 128), f32)
        d["dn_h0"] = np.zeros((2, 8, 128, 128), f32)
    else:
        d["ssd_h0"] = np.ascontiguousarray(inputs["state_l0_ssd"][(core - 4) % 2])
        d["dn_h0"] = np.ascontiguousarray(inputs["state_l0_dn"][(core - 4) % 2])
    return d


def build(stages):
    k = Net()
    k.load_consts()
    k.load_x()
    for st in stages:
        if st[0] == "ffn":
            l = st[1]
            cols = k.modulation(l)
            k.ffn(l, cols)
        elif st[0] == "mix0":
            cols = k.modulation(0)
            k.mixer_l0(cols)
        elif st[0] == "mix1":
            cols = k.modulation(1)
            k.mixer_l1(cols)
        elif st[0] == "mod":
            cols = k.modulation(st[1])
        elif st[0] == "pre":
            cols = k.modulation(st[1])
            k.prenorm(cols, 3, 4)
    k.store_y()
    k.finish()
    return k


def run(inputs, stages, n_cores=8):
    from concourse.bass_utils import run_bass_kernel_spmd
    k = build(stages)
    in_maps = []
    for c in range(n_cores):
        d = core_inputs(inputs, c)
        m = {}
        for name in k.inp:
            if name in d:
                m[name] = d[name]
            else:
                m[name] = np.ascontiguousarray(np.asarray(inputs[name], dtype=np.float32))
        in_maps.append(m)
    res = run_bass_kernel_spmd(k.nc, in_maps, core_ids=list(range(n_cores)))
    return k, res


FULL = [("full",)]


def build_full():
    k = Net()
    k.load_consts()
    k.load_x()
    import os
    nsub = int(os.environ.get("FULL_N", 4))
    cols0 = k.modulation(0)
    k.mixer_l0(cols0)
    if nsub >= 2:
        k.ffn(0, cols0)
    if nsub >= 3:
        cols1 = k.modulation(1)
        k.mixer_l1(cols1)
    if nsub >= 4:
        k.ffn(1, cols1)
    k.store_y()
    k.finish()
    return k


def kernel(**inputs):
    from concourse.bass_utils import run_bass_kernel_spmd
    inputs = {n: np.asarray(v) for n, v in inputs.items()}
    used = ("x_prompt", "x_sample", "state_l0_ssd", "state_l0_dn", "cache_l1_k", "cache_l1_v", "c", "c_ctx",
            "mod_w_l0", "mod_b_l0", "norm_mix_pre_l0", "norm_mix_post_l0", "norm_ffn_pre_l0", "norm_ffn_post_l0",
            "ffn_up_l0", "ffn_conv_w_l0", "ffn_conv_b_l0", "ffn_down_l0",
            "mod_w_l1", "mod_b_l1", "norm_mix_pre_l1", "norm_mix_post_l1", "norm_ffn_pre_l1", "norm_ffn_post_l1",
            "ffn_up_l1", "ffn_conv_w_l1", "ffn_conv_b_l1", "ffn_down_l1",
            "mix_in_l0", "mix_out_l0", "ssd_conv_w", "ssd_conv_b", "ssd_dt_bias", "ssd_A_log", "ssd_D", "ssd_norm_w",
            "dn_conv_w", "dn_dt_bias", "dn_A_log", "dn_norm_w",
            "mix_in_l1", "mix_out_l1", "sg_ln_w", "sg_ln_b", "sg_w_s", "sg_b_s", "attn_sink")
    assert all(n in inputs for n in used)
    k = build_full()
    in_maps = []
    for c in range(8):
        d = core_inputs(inputs, c)
        m = {}
        for name in k.inp:
            m[name] = d[name] if name in d else np.ascontiguousarray(np.asarray(inputs[name], dtype=np.float32))
        in_maps.append(m)
    res = run_bass_kernel_spmd(k.nc, in_maps, core_ids=list(range(8)))
    r = res.results
    f32 = np.float32
    y_prompt = np.concatenate([r[c]["y"].reshape(8, 256, D) for c in range(4)], axis=0).astype(f32)
    y_sample = np.stack([r[4]["y"], r[5]["y"]], axis=0).astype(f32)
    new_ssd = np.concatenate([r[c]["ssd_out"] for c in range(4)], axis=0).astype(f32)
    new_dn = np.concatenate([r[c]["dn_out"] for c in range(4)], axis=0).astype(f32)
    new_k = np.concatenate([r[c]["nk"].reshape(8, 256, 2, 128) for c in range(4)], axis=0).astype(f32)
    new_v = np.concatenate([r[c]["nv"].reshape(8, 256, 2, 128) for c in range(4)], axis=0).astype(f32)
    return (y_prompt, y_sample, new_ssd, new_dn, new_k, new_v)
```

```python
import contextlib
import numpy as np
import concourse.bass as bass
import concourse.mybir as mybir

F32 = mybir.dt.float32
BF16 = mybir.dt.bfloat16
AF = mybir.ActivationFunctionType
ALU = mybir.AluOpType
AX = mybir.AxisListType

SAME_ENGINE_SYNC = True


class V:
    __slots__ = ("ap", "keys")

    def __init__(self, ap, keys):
        self.ap = ap
        self.keys = keys


class Buf:
    _n = 0

    def __init__(self, kb, t, shape, gran=None):
        self.kb = kb
        self.t = t
        self.shape = list(shape)
        Buf._n += 1
        self.id = Buf._n
        f0 = self.shape[1] if len(self.shape) > 1 else 1
        self.gran = gran if gran else f0
        self.ng = (f0 + self.gran - 1) // self.gran

    def _keys(self, idx):
        if not isinstance(idx, tuple):
            idx = (idx,)
        lo, hi = 0, self.ng - 1
        if len(idx) > 1:
            i1 = idx[1]
            if isinstance(i1, slice):
                a = 0 if i1.start is None else i1.start
                b = self.shape[1] if i1.stop is None else i1.stop
                lo, hi = a // self.gran, (b - 1) // self.gran
            elif isinstance(i1, int):
                lo = hi = i1 // self.gran
        return [(self.id, g) for g in range(lo, hi + 1)]

    def __getitem__(self, idx):
        return V(self.t[idx], self._keys(idx))

    def v(self, ap, idx=None):
        return V(ap, self._keys(idx) if idx is not None else [(self.id, g) for g in range(self.ng)])


class KB:
    ENG = ("pe", "act", "dve", "pool", "sp")

    def __init__(self):
        self.nc = bass.Bass("TRN2", target_bir_lowering=False)
        self.es = contextlib.ExitStack()
        nc = self.nc
        self.E = {"pe": nc.tensor, "act": nc.scalar, "dve": nc.vector, "pool": nc.gpsimd, "sp": nc.sync}
        self.sems = {}
        self.cnt = {}
        for e in self.ENG:
            self.sems["e_" + e] = self.es.enter_context(nc.semaphore("s_" + e))
            self.cnt["e_" + e] = 0
        self.NRING = 12
        for q in ("hw", "sw"):
            for i in range(self.NRING):
                nm = "d_%s%d" % (q, i)
                self.sems[nm] = self.es.enter_context(nc.semaphore(nm))
                self.cnt[nm] = 0
        self.ring_i = {"hw": 0, "sw": 0}
        self.seen = {e: {} for e in self.ENG}
        self.last_w = {}
        self.readers = {}
        self.n_inst = 0
        self.n_wait = 0
        self.out_deps = []
        self.stack = [self.es]
        self.waits_by = {}
        self.cur_birth = None
        self.birth = {}
        self.birth_done = set()

    @contextlib.contextmanager
    def scope(self):
        es = contextlib.ExitStack()
        self.stack.append(es)
        try:
            yield
        finally:
            self.stack.pop()
            es.close()
            self.cur_birth = tuple((s, v) for s, v in self.cnt.items() if v > 0)

    def sbuf(self, name, shape, dtype=F32, gran=None):
        self._nalloc = getattr(self, "_nalloc", 0) + 1
        t = self.stack[-1].enter_context(self.nc.sbuf_tensor("%s_%d" % (name, self._nalloc), list(shape), dtype))
        b = Buf(self, t, shape, gran)
        if self.cur_birth:
            self.birth[b.id] = self.cur_birth
        return b

    def psum(self, name, shape, dtype=F32, gran=None):
        t = self.stack[-1].enter_context(self.nc.psum_tensor(name, list(shape), dtype))
        return Buf(self, t, shape, gran)

    def dram(self, name, shape, dtype=F32, kind="Internal", gran=None):
        t = self.nc.dram_tensor(name, list(shape), dtype, kind=kind)
        return Buf(self, t.ap() if hasattr(t, "ap") else t, shape, gran)

    def _collect(self, reads, writes, eng=None):
        deps = {}

        def add(d):
            if d is None:
                return
            s, v = d
            if deps.get(s, 0) < v:
                deps[s] = v
        if self.birth:
            for r in list(reads) + list(writes):
                for k in r.keys:
                    bid = k[0]
                    if bid in self.birth and (eng, bid) not in self.birth_done:
                        self.birth_done.add((eng, bid))
                        for d in self.birth[bid]:
                            add(d)
        for r in reads:
            for k in r.keys:
                add(self.last_w.get(k))
        for w in writes:
            for k in w.keys:
                add(self.last_w.get(k))
                for d in self.readers.get(k, ()):
                    add(d)
        return deps

    def _emit_waits(self, eng, deps, war_only_same=None):
        need = []
        own = "e_" + eng
        for s, v in deps.items():
            if s == own and (not SAME_ENGINE_SYNC or eng == "pe"):
                continue
            if self.seen[eng].get(s, 0) >= v:
                continue
            need.append((s, v))
        return need

    def _record(self, mark, reads, writes):
        for r in reads:
            for k in r.keys:
                self.readers.setdefault(k, []).append(mark)
        for w in writes:
            for k in w.keys:
                self.last_w[k] = mark
                self.readers[k] = []

    def op(self, eng, fn, reads, writes):
        deps = self._collect(reads, writes, eng)
        need = self._emit_waits(eng, deps)
        E = self.E[eng]
        for s, v in need[:-1]:
            E.wait_ge(self.sems[s], v)
            self.n_wait += 1
            self.waits_by[eng] = self.waits_by.get(eng, 0) + 1
        inst = fn()
        if need:
            s, v = need[-1]
            inst._wait_ge(self.sems[s], v)
        for s, v in need:
            self.seen[eng][s] = v
        own = "e_" + eng
        self.cnt[own] += 1
        inst.then_inc(self.sems[own], 1)
        mark = (own, self.cnt[own])
        self._record(mark, reads, writes)
        self.n_inst += 1
        return inst

    def dma(self, queue, out, in_, is_output=False, **kw):
        ring = "sw" if queue == "pool" else "hw"
        i = self.ring_i[ring]
        self.ring_i[ring] += 1
        nm = "d_%s%d" % (ring, i % self.NRING)
        deps = self._collect([in_], [out], queue)
        if self.cnt[nm] > 0:
            if deps.get(nm, 0) < self.cnt[nm]:
                deps[nm] = self.cnt[nm]
        need = self._emit_waits(queue, deps)
        E = self.E[queue]
        for s, v in need[:-1]:
            E.wait_ge(self.sems[s], v)
            self.n_wait += 1
        inst = E.dma_start(out=out.ap, in_=in_.ap, **kw)
        if need:
            s, v = need[-1]
            inst._wait_ge(self.sems[s], v)
        for s, v in need:
            self.seen[queue][s] = v
        self.cnt[nm] += 16
        inst.then_inc(self.sems[nm], 16)
        mark = (nm, self.cnt[nm])
        self._record(mark, [in_], [out])
        if is_output:
            self.out_deps.append(mark)
        self.n_inst += 1
        return inst

    def finish(self):
        for s, v in self.cnt.items():
            if v > 0:
                self.E["sp"].wait_ge(self.sems[s], v)

    def mm(self, out, lhsT, rhs, start=True, stop=True, **kw):
        return self.op("pe", lambda: self.nc.tensor.matmul(out.ap, lhsT=lhsT.ap, rhs=rhs.ap, start=start, stop=stop, **kw),
                       [lhsT, rhs] + ([] if start else [out]), [out])

    def transpose(self, out, in_, ident):
        return self.op("pe", lambda: self.nc.tensor.transpose(out.ap, in_.ap, ident.ap), [in_, ident], [out])

    def act(self, out, in_, func, bias=None, scale=None, accum_out=None, eng="act"):
        reads = [in_]
        kw = {}
        if bias is not None:
            if isinstance(bias, V):
                reads.append(bias); kw["bias"] = bias.ap
            else:
                kw["bias"] = bias
        if scale is not None:
            if isinstance(scale, V):
                reads.append(scale); kw["scale"] = scale.ap
            else:
                kw["scale"] = scale
        writes = [out]
        if accum_out is not None:
            writes.append(accum_out); kw["accum_out"] = accum_out.ap
        return self.op("act", lambda: self.nc.scalar.activation(out=out.ap, in_=in_.ap, func=func, **kw), reads, writes)

    def tt(self, out, in0, in1, op, eng="dve"):
        E = self.E[eng]
        return self.op(eng, lambda: E.tensor_tensor(out=out.ap, in0=in0.ap, in1=in1.ap, op=op), [in0, in1], [out])

    def ts(self, out, in0, s1, op0, s2=None, op1=None, eng="dve", accum_out=None):
        E = self.E[eng]
        reads = [in0]
        a1 = s1.ap if isinstance(s1, V) else s1
        a2 = s2.ap if isinstance(s2, V) else s2
        if isinstance(s1, V): reads.append(s1)
        if isinstance(s2, V): reads.append(s2)
        kw = {}
        writes = [out]
        if op1 is not None:
            kw["op1"] = op1
        if accum_out is not None:
            kw["accum_out"] = accum_out.ap; writes.append(accum_out)
        return self.op(eng, lambda: E.tensor_scalar(out=out.ap, in0=in0.ap, scalar1=a1, scalar2=a2, op0=op0, **kw), reads, writes)

    def stt(self, out, in0, scalar, in1, op0, op1, eng="dve"):
        E = self.E[eng]
        reads = [in0, in1]
        a = scalar.ap if isinstance(scalar, V) else scalar
        if isinstance(scalar, V): reads.append(scalar)
        return self.op(eng, lambda: E.scalar_tensor_tensor(out=out.ap, in0=in0.ap, scalar=a, in1=in1.ap, op0=op0, op1=op1), reads, [out])

    def copy(self, out, in_, eng="dve"):
        if eng == "act":
            return self.op("act", lambda: self.nc.scalar.copy(out=out.ap, in_=in_.ap), [in_], [out])
        E = self.E[eng]
        return self.op(eng, lambda: E.tensor_copy(out=out.ap, in_=in_.ap), [in_], [out])

    def memset(self, out, val, eng="dve"):
        E = self.E[eng]
        return self.op(eng, lambda: E.memset(out.ap, val), [], [out])

    def recip(self, out, in_):
        return self.op("dve", lambda: self.nc.vector.reciprocal(out=out.ap, in_=in_.ap), [in_], [out])

T = 2048
D = 1024
NCH = 16
NTT = 4
NSEG = 8
DFF = 2816
NJ = DFF // 128
EPS = 1e-6
PAD = 2
L0_PROJ = 7232
L1_PROJ = 3584


class SubBuf:
    def __init__(self, buf, off, width):
        self.buf = buf; self.off = off; self.width = width
        self.id = buf.id
        self.t = buf.t[:, off:off + width]
        self._k = buf._keys((slice(None), slice(off, off + width)))

    def __getitem__(self, idx):
        return V(self.t[idx], self._k)


class Net(KB):
    def __init__(self, dbg=None):
        super().__init__()
        self.dbg = dbg or {}
        self.inp = {}
        self.banks = [self.psum("bank%d" % i, [128, 512], F32) for i in range(7)]
        self.psum_bf(0)
        self.banks.append(None)

    def psum_bf(self, i):
        if not hasattr(self, "_bfb"):
            big = self.psum("bfbig", [128, 1024], BF16)
            self._bfb = [SubBuf(big, j * 512, 512) for j in range(2)]
        return self._bfb[i]

    def din(self, name, shape, dtype=F32, gran=None):
        b = self.dram(name, shape, dtype, kind="ExternalInput", gran=gran)
        self.inp[name] = b
        return b

    def dout(self, name, shape, dtype=F32, gran=None):
        return self.dram(name, shape, dtype, kind="ExternalOutput", gran=gran)

    def load_cols(self, dst, src_ap, src_keys, n):
        k = self
        st = k.colstage[k.ncst % 2]; k.ncst += 1
        k.dma("sp", st[0:n, :], V(src_ap, src_keys))
        bk = k.banks[6]
        k.transpose(bk[:, 0:n], st[0:n, :], k.ident_f[0:n, 0:n])
        k.copy(dst, bk[:, 0:n], eng="dve")

    def load_consts(self):
        k = self
        c = k.din("cst_ident", [128, 128])
        k.ident_f = k.sbuf("ident_f", [128, 128], F32)
        k.dma("sp", k.ident_f[:, :], c[:, :])
        k.ident_b = k.sbuf("ident_b", [128, 128], BF16)
        k.copy(k.ident_b[:, :], k.ident_f[:, :], eng="dve")
        k.ones_b = k.sbuf("ones_b", [128, 128], BF16)
        k.memset(k.ones_b[:, :], 1.0, eng="pool")
        k.ones_f = k.sbuf("ones_f", [128, 128], F32)
        k.memset(k.ones_f[:, :], 1.0, eng="pool")
        k.colstage = [k.sbuf("colstage%d" % i, [128, 128], F32) for i in range(2)]
        k.ncst = 0
        k.eps_t = k.sbuf("eps_t", [128, 1], F32)
        k.memset(k.eps_t[:, :], EPS, eng="pool")
        fl = k.din("flags", [128])
        k.flags = k.sbuf("flags_t", [128, 128], F32)
        k.dma("sp", k.flags[:, :], V(fl.t.partition_broadcast(128), fl[:].keys))
        k.xs = [k.dram("xs%d" % fc, [128, T], F32, gran=512) for fc in range(8)]
        k.hT = [k.sbuf("hT%d" % fc, [128, T + 2 * PAD], BF16, gran=None) for fc in range(8)]
        for fc in range(8):
            k.memset(k.hT[fc][:, 0:PAD], 0.0, eng="pool")
            k.memset(k.hT[fc][:, T + PAD:T + 2 * PAD], 0.0, eng="pool")

    def load_x(self):
        k = self
        x = k.din("x", [T, D], gran=None)
        sc = k.scope(); sc.__enter__()
        xin = [k.sbuf("xin%d" % i, [128, 4, D], F32) for i in range(2)]
        xt = [k.sbuf("xt%d" % i, [128, 512], F32) for i in range(4)]
        n = 0
        for tt in range(NTT):
            xi = xin[tt % 2]
            k.dma("sp", xi[:, :, :], V(x.t[tt * 512:(tt + 1) * 512, :].rearrange("(c p) d -> p c d", p=128), x[:].keys))
            for fc in range(8):
                bk = k.banks[n % 4]
                for c in range(4):
                    k.transpose(bk[:, c * 128:(c + 1) * 128], xi[:, c, fc * 128:(fc + 1) * 128], k.ident_f[:, :])
                xo = xt[n % 4]
                if n % 2 == 0:
                    k.copy(xo[:, :], bk[:, :], eng="act")
                else:
                    k.copy(xo[:, :], bk[:, :], eng="dve")
                k.dma("sp", k.xs[fc][:, tt * 512:(tt + 1) * 512], xo[:, :])
                n += 1
        sc.__exit__(None, None, None)

    def store_y(self):
        k = self
        y = k.dout("y", [T, D])
        sc = k.scope(); sc.__enter__()
        xl = [k.sbuf("yl%d" % i, [128, 512], F32) for i in range(4)]
        yo = [k.sbuf("yo%d" % i, [128, 4, D], F32) for i in range(2)]
        n = 0
        for tt in range(NTT):
            yt = yo[tt % 2]
            for fc in range(8):
                xi = xl[n % 4]
                k.dma("sp", xi[:, :], k.xs[fc][:, tt * 512:(tt + 1) * 512])
                bk = k.banks[n % 4]
                for c in range(4):
                    k.transpose(bk[:, c * 128:(c + 1) * 128], xi[:, c * 128:(c + 1) * 128], k.ident_f[:, :])
                o = V(yt.t[:, :, fc * 128:(fc + 1) * 128], yt[:].keys)
                src = V(bk.t[:, :].rearrange("p (c f) -> p c f", c=4), bk[:].keys)
                if n % 2 == 0:
                    k.copy(o, src, eng="act")
                else:
                    k.copy(o, src, eng="dve")
                n += 1
            k.dma("sp", V(y.t[tt * 512:(tt + 1) * 512, :].rearrange("(c p) d -> p c d", p=128), y[:].keys), yt[:, :, :], is_output=True)
        sc.__exit__(None, None, None)

    def modulation(self, l):
        k = self
        L = "l%d" % l
        cond = k.inp.get("cond") or k.din("cond", [D])
        mw = k.din("mod_w_" + L, [D, 6 * D])
        mb = k.din("mod_b_" + L, [6 * D])
        nws = [k.din(n + L, [D]) for n in ("norm_mix_pre_", "norm_mix_post_", "norm_ffn_pre_", "norm_ffn_post_")]
        cols = k.sbuf("modcols_" + L, [128, 6, 8], F32)
        sc = k.scope(); sc.__enter__()
        cs = k.sbuf("cond_" + L, [128, 8], F32)
        k.load_cols(cs[:, :], cond.t.rearrange("(c p) -> c p", p=128), cond[:].keys, 8)
        mbt = k.sbuf("modb_" + L, [128, 48], F32)
        k.load_cols(mbt[:, :], mb.t.rearrange("(j p) -> j p", p=128), mb[:].keys, 48)
        nwt = k.sbuf("nw_" + L, [128, 4, 8], F32)
        for i, nw in enumerate(nws):
            k.load_cols(nwt[:, i, :], nw.t.rearrange("(c p) -> c p", p=128), nw[:].keys, 8)
        sb = k.sbuf("scond_" + L, [128, 8], BF16)
        k.act(sb[:, :], cs[:, :], AF.Silu)
        wbuf = [k.sbuf("modw%d_%s" % (i, L), [128, 8, 512], BF16) for i in range(2)]
        mps = k.banks[6]
        mwv = mw.t.rearrange("(c p) n -> p c n", p=128)
        for blk in range(12):
            wb = wbuf[blk % 2]
            k.dma("pool", wb[:, :, :], V(mwv[:, :, blk * 512:(blk + 1) * 512], mw[:].keys))
            for nn in range(4):
                j = blk * 4 + nn
                for kc in range(8):
                    k.mm(mps[:, j:j + 1], wb[:, kc, nn * 128:(nn + 1) * 128], sb[:, kc:kc + 1], start=(kc == 0), stop=(kc == 7))
        mod = k.sbuf("mod_" + L, [128, 48], F32)
        k.tt(mod[:, :], mps[:, 0:48], mbt[:, :], ALU.add)
        k.stt(cols[:, 0, :], mod[:, 8:16], 1.0, nwt[:, 0, :], ALU.add, ALU.mult)
        k.copy(cols[:, 1, :], mod[:, 0:8], eng="dve")
        k.tt(cols[:, 2, :], mod[:, 16:24], nwt[:, 1, :], ALU.mult)
        k.stt(cols[:, 3, :], mod[:, 32:40], 1.0, nwt[:, 2, :], ALU.add, ALU.mult)
        k.copy(cols[:, 4, :], mod[:, 24:32], eng="dve")
        k.tt(cols[:, 5, :], mod[:, 40:48], nwt[:, 3, :], ALU.mult)
        sc.__exit__(None, None, None)
        return cols

    def rstd_from(self, out, ss, n, tmp):
        k = self
        k.act(tmp, ss, AF.Ln, bias=k.eps_t[:, 0:1], scale=1.0 / n)
        k.act(out, tmp, AF.Exp, scale=-0.5)

    def prenorm(self, cols, ia, ib):
        k = self
        sc = k.scope(); sc.__enter__()
        k.pn_x = [k.sbuf("pn_x%d" % i, [128, 512], F32) for i in range(10)]
        k.pn_sq = [k.sbuf("pn_sq%d" % i, [128, 512], BF16) for i in range(3)]
        k.pn_r = [k.sbuf("pn_r%d" % i, [128, 512], F32) for i in range(2)]
        k.pn_t = [k.sbuf("pn_t%d" % i, [128, 512], F32) for i in range(3)]
        n = 0
        for tt in range(NTT):
            sl = slice(tt * 512, (tt + 1) * 512)
            ss = k.banks[tt % 2]
            xs_t = []
            for fc in range(8):
                xb = k.pn_x[(tt * 8 + fc) % 10]
                k.dma("sp", xb[:, :], k.xs[fc][:, sl])
                sq = k.pn_sq[n % 3]; n += 1
                k.act(sq[:, :], xb[:, :], AF.Square)
                k.mm(ss[:, :], k.ones_b[:, :], sq[:, :], start=(fc == 0), stop=(fc == 7))
                xs_t.append(xb)
            r = k.pn_r[tt % 2]
            k.rstd_from(r[:, :], ss[:, :], float(D), k.pn_t[0][:, :])
            for fc in range(8):
                tm = k.pn_t[1 + fc % 2]
                k.stt(tm[:, :], xs_t[fc][:, :], cols[:, ia, fc:fc + 1], r[:, :], ALU.mult, ALU.mult)
                k.act(k.hT[fc][:, PAD + tt * 512:PAD + (tt + 1) * 512], tm[:, :], AF.Identity, bias=cols[:, ib, fc:fc + 1])
        sc.__exit__(None, None, None)

    def postnorm_alloc(self):
        k = self
        k.po_x = [k.sbuf("po_x%d" % i, [128, 512], F32) for i in range(8)]
        k.po_r = k.sbuf("po_r", [128, 512], F32)
        k.po_t = [k.sbuf("po_t%d" % i, [128, 512], F32) for i in range(3)]

    def postnorm_prefetch(self, tt):
        k = self
        sl = slice(tt * 512, (tt + 1) * 512)
        for fc in range(8):
            k.dma("sp", k.po_x[fc][:, :], k.xs[fc][:, sl])

    def postnorm_tile(self, tt, o_tile, ss, cols, ig):
        k = self
        sl = slice(tt * 512, (tt + 1) * 512)
        k.rstd_from(k.po_r[:, :], ss, float(D), k.po_t[0][:, :])
        for fc in range(8):
            xb = k.po_x[fc]
            tm = k.po_t[1 + fc % 2]
            k.tt(tm[:, :], o_tile[fc], k.po_r[:, :], ALU.mult)
            k.stt(xb[:, :], tm[:, :], cols[:, ig, fc:fc + 1], xb[:, :], ALU.mult, ALU.add)
            k.dma("sp", k.xs[fc][:, sl], xb[:, :])

    def ffn(self, l, cols):
        k = self
        L = "l%d" % l
        wup = k.din("ffn_up_" + L, [D, 2 * DFF])
        wcv = k.din("ffn_conv_w_" + L, [3, 2 * DFF])
        bcv = k.din("ffn_conv_b_" + L, [2 * DFF])
        wdn = k.din("ffn_down_" + L, [DFF, D])
        k.prenorm(cols, 3, 4)
        sc = k.scope(); sc.__enter__()
        k.ff_g = k.sbuf("ff_g", [128, NJ, 1024], BF16, gran=1)
        k.ff_wu = [k.sbuf("ff_wu%d" % i, [128, 8, 2, 512], BF16) for i in range(2)]
        k.ff_wd = [k.sbuf("ff_wd%d" % i, [128, 512], BF16) for i in range(4)]
        k.ff_u = [k.sbuf("ff_u%d" % i, [128, 258], BF16) for i in range(4)]
        k.ff_dgall = k.sbuf("ff_dgall", [128, 2 * NJ, 3, 128], BF16, gran=1)
        k.ff_sa = [k.sbuf("ff_sa%d" % i, [128, 256], F32) for i in range(2)]
        k.ff_o = [k.sbuf("ff_o%d" % i, [128, 512], F32) for i in range(8)]
        k.ff_sq = [k.sbuf("ff_sq%d" % i, [128, 512], BF16) for i in range(2)]
        k.postnorm_alloc()
        cw = k.sbuf("ff_cw_" + L, [128, 44, 3], F32)
        cb = k.sbuf("ff_cb_" + L, [128, 44], F32)
        for tap in range(3):
            k.load_cols(V(cw.t[:, :, tap], cw[:].keys), wcv.t[tap, :].rearrange("(c p) -> c p", p=128), wcv[:].keys, 44)
        k.load_cols(cb[:, :], bcv.t.rearrange("(c p) -> c p", p=128), bcv[:].keys, 44)
        for ch in range(2 * NJ):
            for tap in range(3):
                k.act(V(k.ff_dgall.t[:, ch, tap, :], k.ff_dgall[:, ch].keys), k.ident_b[:, :], AF.Copy, scale=cw[:, ch, tap:tap + 1])
        wupv = wup.t.rearrange("(c p) n -> p c n", p=128)
        nwu = 0
        nu = 0
        nd = 0
        import os
        dbg_tt = int(os.environ.get("FFN_TT", NTT)); dbg_jp = int(os.environ.get("FFN_JP", NJ // 2)); dbg_part = int(os.environ.get("FFN_PART", 9))
        for tp in range(dbg_tt // 2):
            pending = None

            def conv_stage(item):
                j, sg, us, nu_ = item
                dgs = [V(k.ff_dgall.t[:, ab * NJ + j, :, :], k.ff_dgall[:, ab * NJ + j].keys) for ab in range(2)]
                pcs = []
                for ab in range(2):
                    pc = k.banks[4 + ab]
                    for tap in range(3):
                        k.mm(pc[:, 0:256], V(dgs[ab].ap[:, tap, :], dgs[ab].keys), us[ab][:, tap:tap + 256], start=(tap == 0), stop=(tap == 2))
                    pcs.append(pc)
                sa = k.ff_sa[nu_ % 2]
                k.act(sa[:, :], pcs[0][:, 0:256], AF.Silu, bias=cb[:, j:j + 1])
                k.stt(V(k.ff_g.t[:, j, sg * 256:(sg + 1) * 256], k.ff_g[:, j].keys), pcs[1][:, 0:256], cb[:, NJ + j:NJ + j + 1], sa[:, :],
                      ALU.add, ALU.mult)

            for jp in range((NJ + 3) // 4):
                wu = k.ff_wu[nwu % 2]; nwu += 1
                nj_here = min(4, NJ - jp * 4)
                for ab in range(2):
                    c0 = ab * DFF + jp * 512
                    k.dma("pool", V(wu.t[:, :, ab, 0:nj_here * 128], wu[:].keys), V(wupv[:, :, c0:c0 + nj_here * 128], wup[:].keys))
                for jj in range(nj_here):
                    j = jp * 4 + jj
                    for sg in range(4):
                        seg = tp * 4 + sg
                        c_lo = PAD + seg * 256 - 1
                        us = []
                        for ab in range(2):
                            pb = k.banks[(nu * 2 + ab) % 4]
                            for kc in range(8):
                                k.mm(pb[:, 0:258], wu[:, kc, ab, jj * 128:(jj + 1) * 128], k.hT[kc][:, c_lo:c_lo + 258],
                                     start=(kc == 0), stop=(kc == 7))
                            u = k.ff_u[(nu * 2 + ab) % 4]
                            if ab == 0:
                                k.copy(u[:, 0:258], pb[:, 0:258], eng="act")
                            else:
                                k.copy(u[:, 0:258], pb[:, 0:258], eng="dve")
                            uv = V(u.t[:, 0:258:257], u[:].keys)
                            k.tt(uv, uv, k.flags[:, seg * 2:seg * 2 + 2], ALU.mult)
                            us.append(u)
                        if pending is not None:
                            conv_stage(pending)
                        pending = (j, sg, us, nu)
                        nu += 1
            if pending is not None:
                conv_stage(pending)
            for st_ in range(2):
                tt = tp * 2 + st_
                k.postnorm_prefetch(tt)
                ss = k.banks[6]
                for half in range(2):
                    for j in range(NJ):
                        wd = k.ff_wd[nd % 4]; nd += 1
                        k.dma("pool", wd[:, :], V(wdn.t[j * 128:(j + 1) * 128, half * 512:(half + 1) * 512], wdn[:].keys))
                        for nn in range(4):
                            k.mm(k.banks[nn][:, :], wd[:, nn * 128:(nn + 1) * 128], V(k.ff_g.t[:, j, st_ * 512:(st_ + 1) * 512], k.ff_g[:, j].keys),
                                 start=(j == 0), stop=(j == NJ - 1))
                    for nn in range(4):
                        n = half * 4 + nn
                        k.copy(k.ff_o[n][:, :], k.banks[nn][:, :], eng="dve")
                        sq = k.ff_sq[n % 2]
                        k.act(sq[:, :], k.ff_o[n][:, :], AF.Square)
                        k.mm(ss[:, :], k.ones_b[:, :], sq[:, :], start=(n == 0), stop=(n == 7))
                k.postnorm_tile(tt, [k.ff_o[n][:, :] for n in range(8)], ss[:, :], cols, 5)
        sc.__exit__(None, None, None)

    def gelu(self, out, src_psum, n, scr):
        k = self
        k.act(out, src_psum, AF.Gelu_apprx_tanh)

    def outproj_post(self, wname, cols):
        k = self
        wo = k.din(wname, [2048, D])
        sc = k.scope(); sc.__enter__()
        wt = k.sbuf("wo", [128, 16, D], BF16)
        wov = wo.t.rearrange("(c p) n -> p c n", p=128)
        for q in range(4):
            k.dma("pool", V(wt.t[:, q * 4:(q + 1) * 4, :], wt[:].keys), V(wov[:, q * 4:(q + 1) * 4, :], wo[:].keys))
        o_t = [k.sbuf("op_o%d" % i, [128, 512], F32) for i in range(8)]
        sqs = [k.sbuf("op_sq%d" % i, [128, 512], BF16) for i in range(2)]
        k.postnorm_alloc()
        nb = 0
        for tt in range(NTT):
            k.postnorm_prefetch(tt)
            ss = k.banks[6]
            for n in range(8):
                bk = k.banks[nb % 4]; nb += 1
                for kc in range(16):
                    k.mm(bk[:, :], wt[:, kc, n * 128:(n + 1) * 128], V(k.mixedT.t[:, kc, tt * 512:(tt + 1) * 512], k.mixedT[:, kc].keys),
                         start=(kc == 0), stop=(kc == 15))
                k.copy(o_t[n][:, :], bk[:, :], eng="dve")
                sq = sqs[n % 2]
                k.act(sq[:, :], o_t[n][:, :], AF.Square)
                k.mm(ss[:, :], k.ones_b[:, :], sq[:, :], start=(n == 0), stop=(n == 7))
            k.postnorm_tile(tt, [o_t[n][:, :] for n in range(8)], ss[:, :], cols, 2)
        sc.__exit__(None, None, None)

    def mixer_l1(self, cols):
        k = self
        win = k.din("mix_in_l1", [D, L1_PROJ])
        winv = win.t.rearrange("(c p) n -> p c n", p=128)
        lnw = k.din("sg_ln_w", [1024]); lnb = k.din("sg_ln_b", [1024])
        sws = k.din("sg_w_s", [4, 128, 128]); sbs = k.din("sg_b_s", [4, 128])
        sink = k.din("attn_sink", [8])
        ck = k.din("ctx_k", [512, 256]); cv = k.din("ctx_v", [512, 256])
        rope = k.din("rope", [2, 128, T])
        cband = k.din("cst_band", [2, 128, 512])
        crot = k.din("cst_rot", [128, 128])
        nk = k.dout("nk", [T, 256]); nv = k.dout("nv", [T, 256])
        k.prenorm(cols, 0, 1)
        sc0 = k.scope(); sc0.__enter__()
        k.mixedT = k.sbuf("mixedT", [128, 16, T], BF16, gran=1)
        sc = k.scope(); sc.__enter__()
        wgv = k.sbuf("wgv", [128, 8, 1024], BF16)
        for q in range(2):
            k.dma("pool", V(wgv.t[:, :, q * 512:(q + 1) * 512], wgv[:].keys), V(winv[:, :, 1024 + q * 512:1024 + (q + 1) * 512], win[:].keys))
        lnw_t = k.sbuf("lnw_t", [128, 1024], F32); lnb_t = k.sbuf("lnb_t", [128, 1024], F32)
        k.dma("sp", lnw_t[:, :], V(lnw.t.partition_broadcast(128), lnw[:].keys))
        k.dma("sp", lnb_t[:, :], V(lnb.t.partition_broadcast(128), lnb[:].keys))
        wsT = k.sbuf("wsT", [128, 4, 128], BF16)
        wstage = k.sbuf("wstage", [128, 4, 128], F32)
        k.dma("sp", wstage[:, :, :], V(sws.t.rearrange("g i j -> i g j"), sws[:].keys))
        for g in range(4):
            bk = k.banks[g % 2]
            k.transpose(bk[:, 0:128], wstage[:, g, :], k.ident_f[:, :])
            k.copy(wsT[:, g, :], bk[:, 0:128], eng="dve")
        bsf = k.sbuf("bsf", [1, 512], F32); bsb = k.sbuf("bsb", [1, 512], BF16)
        k.dma("sp", bsf[:, :], V(sbs.t.rearrange("(o g) i -> o (g i)", o=1), sbs[:].keys))
        k.copy(bsb[:, :], bsf[:, :], eng="dve")
        gsc = [(k.sbuf("g_x%d" % i, [128, 512], F32), k.sbuf("g_t%d" % i, [128, 512], F32), k.sbuf("g_s%d" % i, [128, 512], F32)) for i in range(2)]
        wu = [k.sbuf("wu1_%d" % i, [128, 8, 128], BF16) for i in range(2)]
        ng = 0
        k.dma("pool", wu[0][:, :, :], V(winv[:, :, 0:128], win[:].keys))
        for n in range(8):
            w = wu[n % 2]
            if n + 1 < 8:
                k.dma("pool", wu[(n + 1) % 2][:, :, :], V(winv[:, :, (n + 1) * 128:(n + 2) * 128], win[:].keys))
            for tt in range(NTT):
                bk = k.banks[ng % 2]
                for kc in range(8):
                    k.mm(bk[:, :], w[:, kc, :], k.hT[kc][:, PAD + tt * 512:PAD + (tt + 1) * 512], start=(kc == 0), stop=(kc == 7))
                k.gelu(V(k.mixedT.t[:, n, tt * 512:(tt + 1) * 512], k.mixedT[:, n].keys), bk[:, :], 512, gsc[ng % 2])
                ng += 1
        gv = [k.sbuf("gv%d" % i, [128, 1024], F32) for i in range(2)]
        gvn = [k.sbuf("gvn%d" % i, [128, 1024], BF16) for i in range(2)]
        st6 = k.sbuf("st6", [128, 2, 6], F32); mv = k.sbuf("mv", [128, 2], F32); rs = k.sbuf("rs", [128, 2], F32)
        for c in range(NCH):
            g_ = gv[c % 2]
            for hf in range(2):
                bk = k.banks[2 + hf]
                for kc in range(8):
                    k.mm(bk[:, :], k.hT[kc][:, PAD + c * 128:PAD + (c + 1) * 128], wgv[:, kc, hf * 512:(hf + 1) * 512], start=(kc == 0), stop=(kc == 7))
                k.gelu(g_[:, hf * 512:(hf + 1) * 512], bk[:, :], 512, gsc[ng % 2]); ng += 1
                k.op("dve", lambda g_=g_, hf=hf: k.nc.vector.bn_stats(out=st6.t[:, hf, :], in_=g_.t[:, hf * 512:(hf + 1) * 512]), [g_[:, :]], [st6[:, :, :]])
            k.op("dve", lambda: k.nc.vector.bn_aggr(out=mv.t[:, :], in_=st6.t[:, :, :].rearrange("p a b -> p (a b)")), [st6[:, :, :]], [mv[:, :]])
            k.act(rs[:, 0:1], mv[:, 1:2], AF.Sqrt, bias=k.eps_t[:, 0:1], scale=1.0)
            k.recip(rs[:, 1:2], rs[:, 0:1])
            k.ts(g_[:, :], g_[:, :], mv[:, 0:1], ALU.subtract, rs[:, 1:2], ALU.mult)
            k.tt(g_[:, :], g_[:, :], lnw_t[:, :], ALU.mult, eng="pool")
            gn = gvn[c % 2]
            k.tt(gn[:, :], g_[:, :], lnb_t[:, :], ALU.add)
            for hf in range(2):
                bk = k.banks[4 + hf]
                for q in range(4):
                    dch = hf * 4 + q
                    g = dch // 2
                    k.mm(bk[:, q * 128:(q + 1) * 128], gn[:, dch * 128:(dch + 1) * 128], wsT[:, g, :], start=True, stop=False)
                    k.mm(bk[:, q * 128:(q + 1) * 128], k.ones_b[0:1, :], bsb[0:1, g * 128:(g + 1) * 128], start=False, stop=True)
                mo = V(k.mixedT.t[:, hf * 4:hf * 4 + 4, c * 128:(c + 1) * 128], [(k.mixedT.id, hf * 4 + q) for q in range(4)])
                k.tt(mo, V(bk.t[:, :].rearrange("p (q i) -> p q i", q=4), bk[:].keys), mo, ALU.mult)
        sc.__exit__(None, None, None)
        sc = k.scope(); sc.__enter__()
        qT = k.sbuf("qT", [128, 8, T], BF16, gran=1)
        kT = k.sbuf("kT", [128, 2, T], BF16, gran=1)
        vtok = k.sbuf("vtok", [128, NCH, 256], BF16, gran=1)
        kcT = k.sbuf("kcT", [128, 2, 512], BF16)
        vc = k.sbuf("vc", [128, 4, 256], BF16)
        skr = k.sbuf("skr", [1, 8, 128], BF16)
        band = k.sbuf("band", [128, 2, 512], BF16)
        scA = k.scope(); scA.__enter__()
        ropeC = k.sbuf("ropeC", [128, T], F32); ropeS = k.sbuf("ropeS", [128, T], F32)
        k.dma("sp", ropeC[:, :], V(rope.t[0], rope[:].keys)); k.dma("sp", ropeS[:, :], V(rope.t[1], rope[:].keys))
        rotf = k.sbuf("rotf", [128, 128], F32); rotb = k.sbuf("rotb", [128, 128], BF16)
        k.dma("sp", rotf[:, :], crot[:, :]); k.copy(rotb[:, :], rotf[:, :], eng="dve")
        wq = [k.sbuf("wq%d" % i, [128, 8, 128], BF16) for i in range(2)]
        qs = [k.sbuf("q_s%d" % i, [128, 512], BF16) for i in range(2)]
        t1 = [k.sbuf("q_t1%d" % i, [128, 512], F32) for i in range(2)]
        t2 = [k.sbuf("q_t2%d" % i, [128, 512], F32) for i in range(2)]
        nq = 0
        k.dma("pool", wq[0][:, :, :], V(winv[:, :, 2048:2048 + 128], win[:].keys))
        for hh in range(10):
            w = wq[hh % 2]
            if hh + 1 < 10:
                c1 = 2048 + (hh + 1) * 128
                k.dma("pool", wq[(hh + 1) % 2][:, :, :], V(winv[:, :, c1:c1 + 128], win[:].keys))
            for tt in range(NTT):
                sl = slice(tt * 512, (tt + 1) * 512)
                bk = k.banks[nq % 2]; bk2 = k.banks[2 + nq % 2]
                for kc in range(8):
                    k.mm(bk[:, :], w[:, kc, :], k.hT[kc][:, PAD + tt * 512:PAD + (tt + 1) * 512], start=(kc == 0), stop=(kc == 7))
                q_ = qs[nq % 2]
                k.copy(q_[:, :], bk[:, :], eng="act")
                k.mm(bk2[:, :], rotb[:, :], q_[:, :])
                a = t1[nq % 2]; b = t2[nq % 2]
                k.tt(a[:, :], q_[:, :], ropeC[:, sl], ALU.mult, eng="pool")
                k.tt(b[:, :], bk2[:, :], ropeS[:, sl], ALU.mult)
                dst = V(qT.t[:, hh, sl], qT[:, hh].keys) if hh < 8 else V(kT.t[:, hh - 8, sl], kT[:, hh - 8].keys)
                k.tt(dst, a[:, :], b[:, :], ALU.add)
                nq += 1
        wkv = k.sbuf("wkv", [128, 8, 512], BF16)
        k.dma("pool", wkv[:, :, :], V(winv[:, :, 3072:3584], win[:].keys))
        kvo = [k.sbuf("kvo%d" % i, [128, 512], F32) for i in range(2)]
        for c in range(NCH):
            bk = k.banks[c % 2]
            for kc in range(8):
                k.mm(bk[:, :], k.hT[kc][:, PAD + c * 128:PAD + (c + 1) * 128], wkv[:, kc, :], start=(kc == 0), stop=(kc == 7))
            o = kvo[c % 2]
            k.copy(o[:, :], bk[:, :], eng="act")
            k.copy(V(vtok.t[:, c, :], vtok[:, c].keys), o[:, 256:512], eng="dve")
            k.dma("sp", V(nk.t[c * 128:(c + 1) * 128, :], nk[:].keys), o[:, 0:256], is_output=True)
            k.dma("sp", V(nv.t[c * 128:(c + 1) * 128, :], nv[:].keys), o[:, 256:512], is_output=True)
        kcs = k.sbuf("kcs", [128, 4, 256], F32)
        k.dma("sp", kcs[:, :, :], V(ck.t.rearrange("(c p) d -> p c d", p=128), ck[:].keys))
        for kvh in range(2):
            bk = k.banks[kvh]
            for sc_ in range(4):
                k.transpose(bk[:, sc_ * 128:(sc_ + 1) * 128], kcs[:, sc_, kvh * 128:(kvh + 1) * 128], k.ident_f[:, :])
            k.copy(kcT[:, kvh, :], bk[:, :], eng="dve")
        k.dma("pool", vc[:, :, :], V(cv.t.rearrange("(c p) d -> p c d", p=128), cv[:].keys))
        skf = k.sbuf("skf", [1, 8], F32); ske = k.sbuf("ske", [1, 8], F32)
        k.dma("sp", skf[:, :], V(sink.t.rearrange("(o h) -> o h", o=1), sink[:].keys))
        k.act(ske[:, :], skf[:, :], AF.Exp)
        k.copy(skr[:, :, :], V(ske.t[:, :].unsqueeze(2).to_broadcast([1, 8, 128]), ske[:].keys), eng="dve")
        bandf = k.sbuf("bandf", [128, 2, 512], F32)
        k.dma("sp", bandf[:, :, :], V(cband.t.rearrange("a p n -> p a n"), cband[:].keys))
        k.copy(band[:, :, :], bandf[:, :, :], eng="dve")
        scA.__exit__(None, None, None)
        pT = [k.sbuf("pT%d" % i, [128, 512], BF16) for i in range(3)]
        mk = [k.sbuf("mk%d" % i, [128, 512], BF16) for i in range(2)]
        rden = [k.sbuf("rden%d" % i, [128, 512], F32) for i in range(2)]
        scale = 128.0 ** -0.5
        npb = 0; nmk = 0; nu = 0
        for c in range(NCH):
            for kvh in range(2):
                blocks = []
                if c > 0: blocks.append(("prev", c - 1))
                blocks.append(("same", c))
                if c < NCH - 1: blocks.append(("next", c + 1))
                for s4 in range(4): blocks.append(("ctx", s4))
                po = k.banks[3 + nu % 2]; pd = k.banks[5 + nu % 2]
                rhs_q = V(qT.t[:, kvh * 4:kvh * 4 + 4, c * 128:(c + 1) * 128], [(qT.id, kvh * 4 + i) for i in range(4)])
                for bi, (kind, idx) in enumerate(blocks):
                    ps = k.banks[npb % 3]
                    p_ = pT[npb % 3]; npb += 1
                    if kind == "ctx":
                        k.mm(V(ps.t[:, :].rearrange("p (h q) -> p h q", h=4), ps[:].keys), kcT[:, kvh, idx * 128:(idx + 1) * 128], rhs_q)
                        k.act(p_[:, :], ps[:, :], AF.Exp, bias=k.flags[:, 112:113], scale=scale)
                        lv = vc[:, idx, kvh * 128:(kvh + 1) * 128]
                    else:
                        k.mm(V(ps.t[:, :].rearrange("p (h q) -> p h q", h=4), ps[:].keys), V(kT.t[:, kvh, idx * 128:(idx + 1) * 128], kT[:, kvh].keys), rhs_q)
                        k.act(p_[:, :], ps[:, :], AF.Exp, scale=scale)
                        if kind != "same":
                            m = mk[nmk % 2]; nmk += 1
                            bsel = 0 if kind == "prev" else 1
                            f0 = 48 + (0 if kind == "prev" else 32) + c
                            k.ts(m[:, :], band[:, bsel, :], k.flags[:, f0:f0 + 1], ALU.mult, k.flags[:, f0 + 16:f0 + 17], ALU.add, eng="pool")
                            k.tt(p_[:, :], p_[:, :], m[:, :], ALU.mult)
                        lv = V(vtok.t[:, idx, kvh * 128:(kvh + 1) * 128], vtok[:, idx].keys)
                    k.mm(po[:, :], lv, p_[:, :], start=(bi == 0), stop=(bi == len(blocks) - 1))
                    k.mm(pd[:, :], k.ones_b[:, :], p_[:, :], start=(bi == 0), stop=False)
                k.mm(pd[:, :], k.ones_b[0:1, :], V(skr.t[0:1, kvh * 4:kvh * 4 + 4, :].rearrange("o h q -> o (h q)"), skr[:].keys), start=False, stop=True)
                rd = rden[nu % 2]
                k.act(rd[:, :], pd[:, :], AF.Ln)
                k.act(rd[:, :], rd[:, :], AF.Exp, scale=-1.0)
                mo = V(k.mixedT.t[:, 8 + kvh * 4:8 + kvh * 4 + 4, c * 128:(c + 1) * 128], [(k.mixedT.id, 8 + kvh * 4 + i) for i in range(4)])
                k.tt(mo, V(po.t[:, :].rearrange("p (h q) -> p h q", h=4), po[:].keys), V(rd.t[:, :].rearrange("p (h q) -> p h q", h=4), rd[:].keys), ALU.mult)
                nu += 1
        sc.__exit__(None, None, None)
        k.outproj_post("mix_out_l1", cols)
        sc0.__exit__(None, None, None)

    def pc_prefetch(self, win, winv, col0):
        k = self
        w = k.pc_w[k.pc_nw % len(k.pc_w)]; k.pc_nw += 1
        k.dma("pool", w[:, :, :], V(winv[:, :, col0:col0 + 128], win[:].keys))
        return w

    def proj_conv5(self, win, winv, col0, cw5, cb, dst_fn, w=None):
        k = self
        if w is None:
            w = k.pc_prefetch(win, winv, col0)
        dg = k.pc_dg[k.pc_n % 2]
        for tap in range(5):
            k.act(dg[:, tap, :], k.ident_b[:, :], AF.Copy, scale=cw5(tap))
        k.pc_n += 1
        pending = None

        def conv_stage(item):
            seg, u, pc = item
            for tap in range(5):
                k.mm(pc[:, 0:256], dg[:, tap, :], u[:, tap:tap + 256], start=(tap == 0), stop=(tap == 4))
            if cb is not None:
                k.act(dst_fn(seg), pc[:, 0:256], AF.Silu, bias=cb)
            else:
                k.act(dst_fn(seg), pc[:, 0:256], AF.Silu)

        for seg in range(NSEG):
            pb = k.banks[k.pc_m % 2]; pc = k.banks[2 + k.pc_m % 2]
            u = k.pc_u[k.pc_m % 2]; k.pc_m += 1
            c_lo = seg * 256
            for kc in range(8):
                k.mm(pb[:, 0:260], w[:, kc, :], k.hT[kc][:, c_lo:c_lo + 260], start=(kc == 0), stop=(kc == 7))
            k.copy(u[:, 0:260], pb[:, 0:260], eng="act")
            k.ts(u[:, 0:2], u[:, 0:2], k.flags[:, 2 * seg:2 * seg + 1], ALU.mult)
            k.ts(u[:, 258:260], u[:, 258:260], k.flags[:, 2 * seg + 1:2 * seg + 2], ALU.mult)
            if pending is not None:
                conv_stage(pending)
            pending = (seg, u, pc)
        conv_stage(pending)

    def pc_alloc(self):
        k = self
        k.pc_w = [k.sbuf("pc_w%d" % i, [128, 8, 128], BF16) for i in range(4)]
        k.pc_nw = 0
        k.pc_dg = [k.sbuf("pc_dg%d" % i, [128, 5, 128], BF16) for i in range(2)]
        k.pc_u = [k.sbuf("pc_u%d" % i, [128, 260], BF16) for i in range(2)]
        k.pc_n = 0; k.pc_m = 0

    def mixer_l0(self, cols):
        k = self
        win = k.din("mix_in_l0", [D, L0_PROJ])
        winv = win.t.rearrange("(c p) n -> p c n", p=128)
        k.prenorm(cols, 0, 1)
        sc0 = k.scope(); sc0.__enter__()
        k.mixedT = k.sbuf("mixedT", [128, 16, T], BF16, gran=1)
        k.l0_consts()
        k.dtab = {nm: k.sbuf("dn_" + nm, [128, NCH, 2, 8], F32) for nm in ("av", "ncs", "ecs", "necs", "dte", "etot")}
        k.beta = k.sbuf("beta", [128, NCH, 16], F32)
        tri = k.din("cst_tri", [2, 128, 128])
        k.tri = k.sbuf("tri", [128, 2, 128], F32)
        k.dma("sp", k.tri[:, :, :], V(tri.t.rearrange("a p n -> p a n"), tri[:].keys))
        import os
        part = os.environ.get("L0_PART", "both")
        s1 = k.scope(); s1.__enter__()
        k.l0_small(win, winv)
        if part in ("both", "ssd"):
            sc = k.scope(); sc.__enter__()
            k.ssd(win, winv)
            sc.__exit__(None, None, None)
        else:
            for fc in range(8):
                k.memset(V(k.mixedT.t[:, fc, :], k.mixedT[:, fc].keys), 0.0, eng="pool")
        s1.__exit__(None, None, None)
        if part in ("both", "dn"):
            sc = k.scope(); sc.__enter__()
            k.dn(win, winv)
            sc.__exit__(None, None, None)
        else:
            for fc in range(8, 16):
                k.memset(V(k.mixedT.t[:, fc, :], k.mixedT[:, fc].keys), 0.0, eng="pool")
        k.outproj_post("mix_out_l0", cols)
        sc0.__exit__(None, None, None)

    def l0_small(self, win, winv):
        k = self
        dtb = k.din("ssd_dt_bias", [2, 16]); alog = k.din("ssd_A_log", [2, 16])
        ddtb = k.din("dn_dt_bias", [2, 8]); dalog = k.din("dn_A_log", [2, 8])
        k.dt_t = k.sbuf("dt_t", [128, NCH, 32], F32)
        k.av = k.sbuf("av", [128, NCH, 2, 24], F32)
        k.cs = k.sbuf("cs", [128, NCH, 2, 24], F32)
        k.ncs = k.sbuf("ncs", [128, NCH, 2, 24], F32)
        k.ecs = k.sbuf("ecs", [128, NCH, 2, 24], F32)
        k.necs = k.sbuf("necs", [128, NCH, 2, 24], F32)
        k.dte = k.sbuf("dte", [128, NCH, 2, 24], F32)
        k.etot = k.sbuf("etot", [128, NCH, 2, 24], F32)
        sc = k.scope(); sc.__enter__()
        wsm = k.sbuf("wsm", [128, 8, 64], BF16)
        k.dma("pool", V(wsm.t[:, :, 0:32], wsm[:].keys), V(winv[:, :, 3072:3104], win[:].keys))
        k.dma("pool", V(wsm.t[:, :, 32:64], wsm[:].keys), V(winv[:, :, 7200:7232], win[:].keys))
        sm = k.sbuf("sm", [128, NCH, 64], F32)
        for c in range(NCH):
            bk = k.banks[c % 2]
            for kc in range(8):
                k.mm(bk[:, 0:64], k.hT[kc][:, PAD + c * 128:PAD + (c + 1) * 128], wsm[:, kc, :], start=(kc == 0), stop=(kc == 7))
            k.copy(V(sm.t[:, c, :], sm[:].keys), bk[:, 0:64], eng="act" if c % 2 == 0 else "dve")
        bias48 = k.sbuf("bias48", [128, 48], F32); al48 = k.sbuf("al48", [128, 48], F32)
        k.dma("sp", bias48[:, 0:32], V(dtb.t.rearrange("a h -> (a h)").partition_broadcast(128), dtb[:].keys))
        k.dma("sp", bias48[:, 32:48], V(ddtb.t.rearrange("a h -> (a h)").partition_broadcast(128), ddtb[:].keys))
        k.dma("sp", al48[:, 0:32], V(alog.t.rearrange("a h -> (a h)").partition_broadcast(128), alog[:].keys))
        k.dma("sp", al48[:, 32:48], V(dalog.t.rearrange("a h -> (a h)").partition_broadcast(128), dalog[:].keys))
        nega = k.sbuf("nega", [128, 48], F32)
        k.act(nega[:, :], al48[:, :], AF.Exp)
        k.ts(nega[:, :], nega[:, :], -1.0, ALU.mult)
        sp_ = k.sbuf("sp_", [128, NCH, 48], F32)
        bb = V(bias48.t[:, :].unsqueeze(1).to_broadcast([128, NCH, 48]), bias48[:].keys)
        k.tt(sp_[:, :, :], V(sm.t[:, :, 0:48], sm[:].keys), bb, ALU.add)
        k.act(sp_[:, :, :], sp_[:, :, :], AF.Exp)
        k.ts(sp_[:, :, :], sp_[:, :, :], 1.0, ALU.add)
        k.act(sp_[:, :, :], sp_[:, :, :], AF.Ln)
        k.copy(k.dt_t[:, :, :], V(sp_.t[:, :, 0:32], sp_[:].keys), eng="dve")
        k.act(k.beta[:, :, :], V(sm.t[:, :, 48:64], sm[:].keys), AF.Sigmoid)
        nb = V(nega.t[:, :].unsqueeze(1).to_broadcast([128, NCH, 48]), nega[:].keys)
        k.tt(sp_[:, :, :], sp_[:, :, :], nb, ALU.mult)
        for d in range(2):
            k.copy(V(k.av.t[:, :, d, 0:16], k.av[:].keys), V(sp_.t[:, :, d * 16:(d + 1) * 16], sp_[:].keys), eng="dve")
            k.copy(V(k.av.t[:, :, d, 16:24], k.av[:].keys), V(sp_.t[:, :, 32 + d * 8:32 + (d + 1) * 8], sp_[:].keys), eng="dve")
        tot = k.sbuf("tot", [128, NCH, 2, 24], F32)
        for d in range(2):
            bk = k.banks[d]; bk2 = k.banks[2 + d]
            rhs = V(k.av.t[:, :, d, :], k.av[:].keys)
            k.mm(V(bk.t[:, 0:384].rearrange("p (c n) -> p c n", c=NCH), bk[:].keys), k.tri[:, d, :], rhs)
            k.mm(V(bk2.t[:, 0:384].rearrange("p (c n) -> p c n", c=NCH), bk2[:].keys), k.ones_f[:, :], rhs)
            k.copy(V(k.cs.t[:, :, d, :], k.cs[:].keys), V(bk.t[:, 0:384].rearrange("p (c n) -> p c n", c=NCH), bk[:].keys), eng="dve")
            k.copy(V(tot.t[:, :, d, :], tot[:].keys), V(bk2.t[:, 0:384].rearrange("p (c n) -> p c n", c=NCH), bk2[:].keys), eng="dve")
        k.ts(k.ncs[:, :, :, :], k.cs[:, :, :, :], -1.0, ALU.mult)
        k.act(k.ecs[:, :, :, :], k.cs[:, :, :, :], AF.Exp)
        k.ts(k.necs[:, :, :, :], k.ecs[:, :, :, :], -1.0, ALU.mult)
        k.act(k.etot[:, :, :, :], tot[:, :, :, :], AF.Exp)
        k.tt(tot[:, :, :, :], tot[:, :, :, :], k.cs[:, :, :, :], ALU.subtract)
        k.act(k.dte[:, :, :, :], tot[:, :, :, :], AF.Exp)
        for nm, src in (("av", k.av), ("ncs", k.ncs), ("ecs", k.ecs), ("necs", k.necs), ("dte", k.dte), ("etot", k.etot)):
            k.copy(k.dtab[nm][:, :, :, :], V(src.t[:, :, :, 16:24], src[:].keys), eng="pool")
        sc.__exit__(None, None, None)

    def build_Lt(self, lt, ps, d, acol, ncol):
        k = self
        abc = V(acol.ap.to_broadcast([128, 128]), acol.keys)
        k.mm(ps, abc, k.tri[:, d, :], start=True, stop=False)
        k.mm(ps, k.ident_b[:, :], k.mbias[:, d, :], start=False, stop=True)
        k.act(lt, ps, AF.Exp, bias=ncol)

    def l0_consts(self):
        k = self
        mb = k.din("cst_mbias", [2, 128, 128]); st = k.din("cst_strict", [2, 128, 128]); blk = k.din("cst_blk", [4, 128, 128])
        k.mbias = k.sbuf("mbias", [128, 2, 128], BF16)
        k.strict = k.sbuf("strict", [128, 2, 128], F32)
        k.blk = k.sbuf("blk", [128, 4, 128], F32)
        k.dma("pool", k.mbias[:, :, :], V(mb.t.rearrange("a p n -> p a n"), mb[:].keys))
        k.blk_b = k.sbuf("blk_b", [128, 4, 128], BF16)
        k.dma("pool", k.blk_b[:, :, :], V(blk.t.rearrange("a p n -> p a n"), blk[:].keys))
        k.dma("sp", k.strict[:, :, :], V(st.t.rearrange("a p n -> p a n"), st[:].keys))
        k.dma("sp", k.blk[:, :, :], V(blk.t.rearrange("a p n -> p a n"), blk[:].keys))

    def ssd(self, win, winv):
        k = self
        cwd = k.din("ssd_conv_w", [5, 2048]); cbd = k.din("ssd_conv_b", [2048])
        dD = k.din("ssd_D", [16]); nwd = k.din("ssd_norm_w", [1024])
        h0d = k.din("ssd_h0", [2, 16, 64, 128])
        hout = k.dout("ssd_out", [NSEG, 2, 16, 64, 128])
        k.pc_alloc()
        cw = k.sbuf("s_cw", [128, 16, 5], F32); cb = k.sbuf("s_cb", [128, 16], F32)
        for tap in range(5):
            k.load_cols(V(cw.t[:, :, tap], cw[:].keys), cwd.t[tap, :].rearrange("(c p) -> c p", p=128), cwd[:].keys, 16)
        k.load_cols(cb[:, :], cbd.t.rearrange("(c p) -> c p", p=128), cbd[:].keys, 16)
        Dbc = k.sbuf("Dbc", [128, 16], F32)
        k.dma("sp", Dbc[:, :], V(dD.t.partition_broadcast(128), dD[:].keys))
        nwc = k.sbuf("s_nw", [128, 8], F32)
        k.load_cols(nwc[:, :], nwd.t.rearrange("(c p) -> c p", p=128), nwd[:].keys, 8)
        scI = k.scope(); scI.__enter__()
        BT = k.sbuf("s_BT", [128, T], BF16); CT = k.sbuf("s_CT", [128, T], BF16)
        xtok = k.sbuf("s_xtok", [128, NCH, 256], BF16, gran=1)
        Btok = k.sbuf("s_Btok", [128, NCH, 128], BF16, gran=1)
        sz = k.sbuf("s_sz", [128, NCH, 256], BF16, gran=1)
        yacc = k.sbuf("s_yacc", [128, NCH, 256], BF16, gran=1)
        hm = [k.sbuf("s_hm%d" % d, [128, 256], F32) for d in range(2)]
        hb = [k.sbuf("s_hb%d" % d, [128, 256], BF16) for d in range(2)]
        bfb = [k.psum_bf(i) for i in range(2)]
        n = 0
        for g in range(4):
            scA = k.scope(); scA.__enter__()
            xT = k.sbuf("s_xT", [128, 2, T], BF16, gran=1)
            wz = [k.sbuf("s_wz%d" % i, [128, 8, 256], BF16) for i in range(1)]
            chB = 8 + g; chC = 12 + g
            pw = [k.pc_prefetch(win, winv, 1024 + ch_ * 128) for ch_ in (2 * g, 2 * g + 1, chB, chC)]
            w = wz[0]
            k.dma("pool", w[:, :, :], V(winv[:, :, g * 256:(g + 1) * 256], win[:].keys))
            for q in range(2):
                ch = 2 * g + q
                k.proj_conv5(win, winv, 1024 + ch * 128, lambda tap, ch=ch: cw[:, ch, tap:tap + 1], cb[:, ch:ch + 1],
                             lambda seg, q=q: V(xT.t[:, q, seg * 256:(seg + 1) * 256], xT[:, q].keys), w=pw[q])
            k.proj_conv5(win, winv, 1024 + chB * 128, lambda tap: cw[:, chB, tap:tap + 1], cb[:, chB:chB + 1], lambda seg: BT[:, seg * 256:(seg + 1) * 256], w=pw[2])
            k.proj_conv5(win, winv, 1024 + chC * 128, lambda tap: cw[:, chC, tap:tap + 1], cb[:, chC:chC + 1], lambda seg: CT[:, seg * 256:(seg + 1) * 256], w=pw[3])
            for c in range(NCH):
                cs_ = slice(c * 128, (c + 1) * 128)
                bk = k.banks[4 + c % 2]
                for kc in range(8):
                    k.mm(bk[:, 0:256], k.hT[kc][:, PAD + c * 128:PAD + (c + 1) * 128], w[:, kc, :], start=(kc == 0), stop=(kc == 7))
                k.act(V(sz.t[:, c, :], sz[:, c].keys), bk[:, 0:256], AF.Silu)
                pb = bfb[c % 2]
                for q in range(2):
                    k.transpose(pb[:, q * 128:(q + 1) * 128], V(xT.t[:, q, cs_], xT[:, q].keys), k.ident_b[:, :])
                k.transpose(pb[:, 256:384], BT[:, cs_], k.ident_b[:, :])
                k.copy(V(xtok.t[:, c, :], xtok[:, c].keys), pb[:, 0:256], eng="dve")
                k.copy(V(Btok.t[:, c, :], Btok[:, c].keys), pb[:, 256:384], eng="dve")
            scA.__exit__(None, None, None)
            scB = k.scope(); scB.__enter__()
            Gs = [k.sbuf("s_G%d" % i, [128, 128], F32) for i in range(2)]
            Lt = [k.sbuf("s_Lt%d" % i, [128, 4, 128], F32) for i in range(2)]
            St = [k.sbuf("s_St%d" % i, [128, 4, 128], BF16) for i in range(2)]
            xdt = [k.sbuf("s_xdt%d" % i, [128, 256], BF16) for i in range(2)]
            xde = [k.sbuf("s_xde%d" % i, [128, 256], BF16) for i in range(2)]
            xD = [k.sbuf("s_xD%d" % i, [128, 256], BF16) for i in range(2)]
            htmp = [k.sbuf("s_ht%d" % i, [128, 256], F32) for i in range(2)]
            hstage = [k.sbuf("s_hs%d" % i, [128, 2, 128], F32) for i in range(2)]
            ytmp = [k.sbuf("s_yt%d" % i, [128, 256], F32) for i in range(2)]
            yz = [k.sbuf("s_yz%d" % i, [128, 256], BF16) for i in range(2)]
            for d in range(2):
                hs = hstage[d]
                k.dma("sp", hs[:, :, :], V(h0d.t[d, 4 * g:4 * g + 4].rearrange("(a h) p n -> (h p) a n", a=2), h0d[:].keys))
                bk = k.banks[6]
                for a in range(2):
                    k.transpose(bk[:, a * 128:(a + 1) * 128], hs[:, a, :], k.ident_f[:, :])
                k.copy(hm[d][:, :], bk[:, 0:256], eng="dve")
                k.copy(hb[d][:, :], hm[d][:, :], eng="act")
            for step in range(NCH):
                for d in range(2):
                    c = step if d == 0 else NCH - 1 - step
                    second = (d == 0 and c >= 8) or (d == 1 and c < 8)
                    cs_ = slice(c * 128, (c + 1) * 128)
                    i2 = d
                    bg = k.banks[0]
                    k.mm(bg[:, 0:128], BT[:, cs_], CT[:, cs_])
                    k.copy(Gs[i2][:, :], bg[:, 0:128], eng="act")
                    bl = k.banks[1 + i2]
                    for hh in range(4):
                        acol = V(k.av.t[:, c, d, 4 * g + hh:4 * g + hh + 1], k.av[:].keys)
                        abc = V(acol.ap.to_broadcast([128, 128]), acol.keys)
                        k.mm(bl[:, hh * 128:(hh + 1) * 128], abc, k.tri[:, d, :], start=True, stop=False)
                        k.mm(bl[:, hh * 128:(hh + 1) * 128], k.ident_b[:, :], k.mbias[:, d, :], start=False, stop=True)
                    for hh in range(4):
                        k.act(V(Lt[i2].t[:, hh, :], Lt[i2][:].keys), bl[:, hh * 128:(hh + 1) * 128], AF.Exp,
                              bias=V(k.ncs.t[:, c, d, 4 * g + hh:4 * g + hh + 1], k.ncs[:].keys))
                    k.tt(St[i2][:, :, :], Lt[i2][:, :, :], V(Gs[i2].t[:, :].unsqueeze(1).to_broadcast([128, 4, 128]), Gs[i2][:].keys), ALU.mult)
                    dtb_ = V(k.dt_t.t[:, c, d * 16 + 4 * g:d * 16 + 4 * g + 4].unsqueeze(2).to_broadcast([128, 4, 64]), k.dt_t[:].keys)
                    dte_ = V(k.dte.t[:, c, d, 4 * g:4 * g + 4].unsqueeze(2).to_broadcast([128, 4, 64]), k.dte[:].keys)
                    ecs_ = V(k.ecs.t[:, c, d, 4 * g:4 * g + 4].unsqueeze(2).to_broadcast([128, 4, 64]), k.ecs[:].keys)
                    eto_ = V(k.etot.t[:, c, d, 4 * g:4 * g + 4].unsqueeze(2).to_broadcast([128, 4, 64]), k.etot[:].keys)
                    x3 = V(xtok.t[:, c, :].rearrange("p (h q) -> p h q", h=4), xtok[:, c].keys)
                    v3 = lambda b: V(b.t[:, :].rearrange("p (h q) -> p h q", h=4), b[:].keys)
                    k.tt(v3(xdt[i2]), x3, dtb_, ALU.mult)
                    k.tt(v3(xde[i2]), v3(xdt[i2]), dte_, ALU.mult, eng="pool")
                    yd = k.banks[3 + 2 * d]; yo = SubBuf(k.banks[4 + 2 * d], 0, 256); ps = SubBuf(k.banks[4 + 2 * d], 256, 256)
                    if second:
                        Db = V(Dbc.t[:, 4 * g:4 * g + 4].unsqueeze(2).to_broadcast([128, 4, 64]), Dbc[:].keys)
                        k.tt(v3(xD[i2]), x3, Db, ALU.mult, eng="pool")
                    for hh in range(4):
                        k.mm(yd[:, hh * 64:(hh + 1) * 64], V(St[i2].t[:, hh, :], St[i2][:].keys), xdt[i2][:, hh * 64:(hh + 1) * 64],
                             start=True, stop=(not second))
                        if second:
                            k.mm(yd[:, hh * 64:(hh + 1) * 64], k.ident_b[:, :], xD[i2][:, hh * 64:(hh + 1) * 64], start=False, stop=True)
                    k.mm(yo[:, 0:256], CT[:, cs_], hb[d][:, :])
                    yt = ytmp[i2]
                    k.tt(v3(yt), V(yo.t[:, 0:256].rearrange("p (h q) -> p h q", h=4), yo[:].keys), ecs_, ALU.mult)
                    ya = V(yacc.t[:, c, :], yacc[:, c].keys)
                    if not second:
                        k.tt(ya, yd[:, 0:256], yt[:, :], ALU.add)
                    else:
                        k.tt(yt[:, :], yd[:, 0:256], yt[:, :], ALU.add)
                        k.tt(yt[:, :], yt[:, :], ya, ALU.add, eng="pool")
                        k.tt(yz[i2][:, :], yt[:, :], V(sz.t[:, c, :], sz[:, c].keys), ALU.mult)
                        pb = bfb[i2]
                        for q in range(2):
                            k.transpose(pb[:, q * 128:(q + 1) * 128], yz[i2][:, q * 128:(q + 1) * 128], k.ident_b[:, :])
                        mo = V(k.mixedT.t[:, 2 * g:2 * g + 2, cs_], [(k.mixedT.id, 2 * g), (k.mixedT.id, 2 * g + 1)])
                        k.copy(mo, V(pb.t[:, 0:256].rearrange("p (q t) -> p q t", q=2), pb[:].keys), eng="act")
                    k.mm(ps[:, 0:256], V(Btok.t[:, c, :], Btok[:, c].keys), xde[i2][:, :])
                    ht = htmp[i2]
                    k.tt(v3(ht), v3(hm[d]), eto_, ALU.mult, eng="pool")
                    k.tt(hm[d][:, :], ps[:, 0:256], ht[:, :], ALU.add)
                    last = (c % 2 == 1) if d == 0 else (c % 2 == 0)
                    if last:
                        seg = c // 2
                        bk = k.banks[6]
                        hs2 = hstage[i2]
                        for a in range(2):
                            k.transpose(bk[:, a * 128:(a + 1) * 128], hm[d][:, a * 128:(a + 1) * 128], k.ident_f[:, :])
                        k.copy(V(hs2.t[:, :, :], hs2[:].keys), V(bk.t[:, 0:256].rearrange("p (a n) -> p a n", a=2), bk[:].keys), eng="act")
                        k.dma("sp", V(hout.t[seg, d, 4 * g:4 * g + 4].rearrange("(a h) p n -> (h p) a n", a=2), hout[:].keys), hs2[:, :, :], is_output=True)
                    cn = c + 1 if d == 0 else c - 1
                    if 0 <= cn < NCH:
                        fcol = (16 if d == 0 else 32) + cn
                        k.ts(hm[d][:, :], hm[d][:, :], k.flags[:, fcol:fcol + 1], ALU.mult)
                        k.copy(hb[d][:, :], hm[d][:, :], eng="act")
            scB.__exit__(None, None, None)
        scI.__exit__(None, None, None)
        sq = [k.sbuf("s_sq%d" % i, [128, 512], BF16) for i in range(2)]
        rr = k.sbuf("s_rr", [128, 512], F32); rt = k.sbuf("s_rt", [128, 512], F32)
        for tt in range(NTT):
            sl = slice(tt * 512, (tt + 1) * 512)
            ss = k.banks[tt % 2]
            for fc in range(8):
                s_ = sq[fc % 2]
                k.act(s_[:, :], V(k.mixedT.t[:, fc, sl], k.mixedT[:, fc].keys), AF.Square)
                k.mm(ss[:, :], k.ones_b[:, :], s_[:, :], start=(fc == 0), stop=(fc == 7))
            k.rstd_from(rr[:, :], ss[:, :], 1024.0, rt[:, :])
            for fc in range(8):
                mv_ = V(k.mixedT.t[:, fc, sl], k.mixedT[:, fc].keys)
                k.stt(mv_, mv_, nwc[:, fc:fc + 1], rr[:, :], ALU.mult, ALU.mult)

    def dn(self, win, winv):
        k = self
        cwd = k.din("dn_conv_w", [5, 3072]); nwd = k.din("dn_norm_w", [128])
        s0d = k.din("dn_h0", [2, 8, 128, 128])
        sout = k.dout("dn_out", [NSEG, 2, 8, 128, 128])
        cw = k.sbuf("d_cw", [128, 24, 5], F32)
        for tap in range(5):
            k.load_cols(V(cw.t[:, :, tap], cw[:].keys), cwd.t[tap, :].rearrange("(c p) -> c p", p=128), cwd[:].keys, 24)
        nwb = k.sbuf("d_nwb", [128, 128], F32)
        k.dma("sp", nwb[:, :], V(nwd.t.partition_broadcast(128), nwd[:].keys))
        lnsc = k.sbuf("d_lnsc", [128, 1], F32)
        k.memset(lnsc[:, :], -0.5 * float(np.log(128.0)), eng="pool")
        zero1 = k.sbuf("d_zero1", [128, 1], F32)
        k.memset(zero1[:, :], 0.0, eng="pool")
        qT = k.sbuf("d_qT", [128, T], BF16, gran=128); kT = k.sbuf("d_kT", [128, T], BF16, gran=128); vT = k.sbuf("d_vT", [128, T], BF16, gran=128)
        khtok = k.sbuf("d_khtok", [128, NCH, 128], BF16, gran=1); vtok = k.sbuf("d_vtok", [128, NCH, 128], BF16, gran=1)
        oacc = k.sbuf("d_oacc", [128, NCH, 128], F32, gran=1)
        Wall = k.sbuf("d_Wall", [128, 2 * NCH, 128], BF16, gran=1)
        QKall = k.sbuf("d_QKall", [128, 2 * NCH, 128], BF16, gran=1)
        wg = [k.sbuf("d_wg%d" % i, [128, 8, 128], BF16) for i in range(1)] * 2
        import os
        IDT = BF16 if os.environ.get("DN_INV", "bf16") == "bf16" else F32
        G = 4
        NT = 5
        tmpf = [k.sbuf("d_tf%d" % i, [128, 128], F32) for i in range(NT)]
        tmpb = [k.sbuf("d_tb%d" % i, [128, 128], BF16) for i in range(10)]
        Sm = [k.sbuf("d_S%d" % d, [128, 128], F32) for d in range(2)]
        Sb = [k.sbuf("d_Sb%d" % d, [128, 128], BF16) for d in range(2)]
        fin16 = k.sbuf("d_fin16", [128, 16], F32); fin16b = k.sbuf("d_fin16b", [128, 16], F32)
        pbf = k.psum_bf(1)
        st = {"bank": 0, "tf": 0, "tb": 0, "ev": 0, "lt": 0}
        T_ = k.dtab

        def nbank():
            b = k.banks[st["bank"] % 7]; st["bank"] += 1
            return b

        def tf():
            t = tmpf[st["tf"] % NT]; st["tf"] += 1
            return t

        def tb():
            t = tmpb[st["tb"] % 10]; st["tb"] += 1
            return t

        def evac(dst, src, scale=None):
            st["ev"] += 1
            if scale is not None:
                k.act(dst, src, AF.Copy, scale=scale)
            elif st["ev"] % 5 != 0:
                k.copy(dst, src, eng="act")
            else:
                k.copy(dst, src, eng="dve")

        I_ = k.ident_b if IDT == BF16 else k.ident_f

        def mm1(lhsT, rhs, dst):
            b = nbank()
            k.mm(b[:, 0:128], lhsT, rhs)
            evac(dst, b[:, 0:128])
            return dst

        def mmadd(lhsT, rhs, sb, dst, op=ALU.add, first_sb=False):
            b = nbank()
            k.mm(b[:, 0:128], lhsT, rhs)
            if first_sb:
                k.tt(dst, sb, b[:, 0:128], op)
            else:
                k.tt(dst, b[:, 0:128], sb, op)
            return dst

        F_ = lambda Tq: Tq[:, :, :]
        Q_ = lambda Tq, q: V(Tq.t[:, q, :], Tq[:].keys)
        bc_ = lambda v2: V(v2.ap.unsqueeze(1).to_broadcast([128, 4, 128]), v2.keys)
        bank3 = lambda b: V(b.t[:, :].rearrange("p (q i) -> p q i", q=4), b[:].keys)
        opq = lambda x, q: Q_(x, q) if isinstance(x, Buf) else x

        def mmq(lhsT, rhs, dst):
            b = nbank()
            for q in range(4):
                k.mm(b[:, q * 128:(q + 1) * 128], opq(lhsT, q), opq(rhs, q))
            evac(F_(dst), bank3(b))
            return dst

        def mmaddq(lhsT, rhs, sb, dstv, op=ALU.add, first_sb=False):
            b = nbank()
            for q in range(4):
                k.mm(b[:, q * 128:(q + 1) * 128], opq(lhsT, q), opq(rhs, q))
            if first_sb:
                k.tt(dstv, sb, bank3(b), op)
            else:
                k.tt(dstv, bank3(b), sb, op)

        def quad_gen(h, d, c0, S, LtT):
            u0 = d * NCH + c0
            css = [slice((c0 + q) * 128, (c0 + q + 1) * 128) for q in range(4)]
            U, UT = S[0], S[1]
            s_ = S[2:10]
            Iv = I_[:, :]
            bL = nbank()
            for q in range(4):
                c = c0 + q
                acol = V(T_["av"].t[:, c, d, h:h + 1], T_["av"][:].keys)
                abc = V(acol.ap.to_broadcast([128, 128]), acol.keys)
                k.mm(bL[:, q * 128:(q + 1) * 128], abc, k.tri[:, d, :], start=True, stop=False)
                k.mm(bL[:, q * 128:(q + 1) * 128], k.ident_b[:, :], k.mbias[:, d, :], start=False, stop=True)
            for q in range(4):
                c = c0 + q
                k.act(Q_(LtT, q), bL[:, q * 128:(q + 1) * 128], AF.Exp, bias=V(T_["ncs"].t[:, c, d, h:h + 1], T_["ncs"][:].keys))
            yield
            Ls = s_[7]
            k.tt(F_(Ls), F_(LtT), bc_(k.strict[:, d, :]), ALU.mult, eng="pool")
            bQ = nbank()
            for q in range(4):
                k.mm(bQ[:, q * 128:(q + 1) * 128], kT[:, css[q]], qT[:, css[q]])
            k.tt(V(QKall.t[:, u0:u0 + 4, :], [(QKall.id, u0 + q) for q in range(4)]), bank3(bQ), F_(LtT), ALU.mult)
            yield
            bA = nbank()
            for q in range(4):
                k.mm(bA[:, q * 128:(q + 1) * 128], kT[:, css[q]], kT[:, css[q]])
            for q in range(4):
                c = c0 + q
                beta_ = V(k.beta.t[:, c, d * 8 + h:d * 8 + h + 1], k.beta[:].keys)
                k.stt(Q_(U, q), bA[:, q * 128:(q + 1) * 128], beta_, Q_(Ls, q), ALU.mult, ALU.mult)
            yield
            mmq(U, Iv, UT)
            Ud, UdT, P = s_[0], s_[1], s_[2]
            k.tt(F_(Ud), F_(U), bc_(k.blk_b[:, 0, :]), ALU.mult, eng="pool")
            yield
            k.tt(F_(UdT), F_(UT), bc_(k.blk_b[:, 0, :]), ALU.mult)
            k.tt(F_(P), bc_(Iv), F_(Ud), ALU.subtract)
            yield
            V1 = mmq(UdT, Ud, s_[3]); V1T = mmq(Ud, UdT, s_[4])
            yield
            A1 = s_[5]; k.tt(F_(A1), F_(V1T), bc_(Iv), ALU.add, eng="pool")
            V2 = mmq(V1T, V1, s_[0]); V2T = mmq(V1, V1T, s_[1])
            yield
            P1 = mmq(A1, P, s_[6])
            A3 = s_[3]; mmaddq(V2, V2T, bc_(Iv), F_(A3))
            yield
            A2 = s_[2]; k.tt(F_(A2), F_(V2T), bc_(Iv), ALU.add, eng="pool")
            yield
            P2 = mmq(A2, P1, s_[5])
            yield
            Wd = mmq(A3, P2, s_[4])
            yield
            WdT = s_[6]; mmq(Wd, Iv, WdT)
            slots = {1: (s_[3], s_[7]), 2: (s_[4], s_[6])}
            for lvl in (1, 2, 3):
                B, BT = s_[0], s_[1]
                k.tt(F_(B), F_(U), bc_(k.blk_b[:, lvl, :]), ALU.mult)
                k.tt(F_(BT), F_(UT), bc_(k.blk_b[:, lvl, :]), ALU.mult, eng="pool")
                yield
                Y = mmq(BT, Wd, s_[2])
                if lvl < 3:
                    Yt = mmq(B, WdT, s_[5])
                    yield
                    nW, nWT = slots[lvl]
                    mmaddq(WdT, Y, F_(Wd), F_(nW), op=ALU.subtract, first_sb=True)
                    mmaddq(Wd, Yt, F_(WdT), F_(nWT), op=ALU.subtract, first_sb=True)
                    Wd, WdT = nW, nWT
                    yield
                else:
                    yield
                    mmaddq(WdT, Y, F_(Wd), V(Wall.t[:, u0:u0 + 4, :], [(Wall.id, u0 + q) for q in range(4)]), op=ALU.subtract, first_sb=True)

        def chain_step(h, d, c):
            cs_ = slice(c * 128, (c + 1) * 128)
            u = d * NCH + c
            col = lambda nm: V(T_[nm].t[:, c, d, h:h + 1], T_[nm][:].keys)
            beta_ = V(k.beta.t[:, c, d * 8 + h:d * 8 + h + 1], k.beta[:].keys)
            Wb = V(Wall.t[:, u, :], Wall[:, u].keys); QKm = V(QKall.t[:, u, :], QKall[:, u].keys)
            kend = tb()[:, :]
            k.act(kend, V(khtok.t[:, c, :], khtok[:, c].keys), AF.Copy, scale=col("dte"))
            bK = nbank(); k.mm(bK[:, 0:128], kT[:, cs_], Sb[d][:, :])
            bS = nbank(); k.mm(bS[:, 0:128], qT[:, cs_], Sb[d][:, :])
            R0 = tb()[:, :]
            k.stt(R0, bK[:, 0:128], col("necs"), V(vtok.t[:, c, :], vtok[:, c].keys), ALU.mult, ALU.add)
            t_ = tf()[:, :]
            k.act(t_, bS[:, 0:128], AF.Copy, scale=col("ecs"))
            bV = nbank(); k.mm(bV[:, 0:128], Wb, R0)
            vnew = tb()[:, :]
            k.act(vnew, bV[:, 0:128], AF.Copy, scale=beta_)
            bU = nbank(); k.mm(bU[:, 0:128], kend, vnew)
            bO = nbank(); k.mm(bO[:, 0:128], QKm, vnew)
            k.stt(Sm[d][:, :], Sm[d][:, :], col("etot"), bU[:, 0:128], ALU.mult, ALU.add)
            last = (c % 2 == 1) if d == 0 else (c % 2 == 0)
            if last:
                k.dma("sp", V(sout.t[c // 2, d, h], sout[:].keys), Sm[d][:, :], is_output=True)
            cn = c + 1 if d == 0 else c - 1
            if 0 <= cn < NCH:
                fcol = (16 if d == 0 else 32) + cn
                k.ts(Sm[d][:, :], Sm[d][:, :], k.flags[:, fcol:fcol + 1], ALU.mult)
                k.copy(Sb[d][:, :], Sm[d][:, :], eng="act")
            oa = V(oacc.t[:, c, :], oacc[:, c].keys)
            first = (d == 0 and c < 8) or (d == 1 and c >= 8)
            if first:
                k.tt(oa, bO[:, 0:128], t_, ALU.add)
            else:
                k.tt(t_, bO[:, 0:128], t_, ALU.add)
                k.tt(oa, oa, t_, ALU.add, eng="pool")

        k.marks = getattr(k, "marks", [])
        mk_ = lambda lab: k.marks.append((lab, k.cnt["e_pe"]))
        for h in range(8):
            mk_("dn%d:proj" % h)
            sc1 = k.scope(); sc1.__enter__()
            k.pc_alloc()
            sqb = [k.sbuf("d_sq%d" % i, [128, 512], BF16) for i in range(2)]
            rr = [k.sbuf("d_rr%d" % i, [128, 512], F32) for i in range(2)]
            pw = [k.pc_prefetch(win, winv, coff + h * 128) for coff in (3104, 4128, 5152)]
            for i_, (buf, coff, cc) in enumerate(((qT, 3104, h), (kT, 4128, 8 + h), (vT, 5152, 16 + h))):
                k.proj_conv5(win, winv, coff + h * 128, lambda tap, cc=cc: cw[:, cc, tap:tap + 1], None,
                             lambda seg, buf=buf: buf[:, seg * 256:(seg + 1) * 256], w=pw[i_])
            mk_("dn%d:l2" % h)
            for (buf, isq) in ((qT, True), (kT, False)):
                for tt in range(NTT):
                    sl = slice(tt * 512, (tt + 1) * 512)
                    s2 = sqb[tt % 2]; r_ = rr[tt % 2]
                    k.act(s2[:, :], buf[:, sl], AF.Square)
                    b = nbank()
                    k.mm(b[:, :], k.ones_b[:, :], s2[:, :])
                    k.act(r_[:, :], b[:, :], AF.Ln, bias=k.eps_t[:, 0:1])
                    k.act(r_[:, :], r_[:, :], AF.Exp, scale=-0.5, bias=(lnsc[:, 0:1] if isq else zero1[:, 0:1]))
                    k.tt(buf[:, sl], buf[:, sl], r_[:, :], ALU.mult)
            w = wg[h % 2]
            k.dma("pool", w[:, :, :], V(winv[:, :, 6176 + h * 128:6176 + (h + 1) * 128], win[:].keys))
            for c in range(NCH):
                cs_ = slice(c * 128, (c + 1) * 128)
                k.transpose(pbf[:, 0:128], kT[:, cs_], k.ident_b[:, :])
                k.transpose(pbf[:, 128:256], vT[:, cs_], k.ident_b[:, :])
                k.copy(V(khtok.t[:, c, :], khtok[:, c].keys), pbf[:, 0:128], eng="dve")
                k.copy(V(vtok.t[:, c, :], vtok[:, c].keys), pbf[:, 128:256], eng="dve")
            sc1.__exit__(None, None, None)
            mk_("dn%d:P" % h)
            sc2 = k.scope(); sc2.__enter__()
            scr = [[k.sbuf("d_s%d_%d" % (g_, i), [128, 4, 128], IDT) for i in range(10)] for g_ in range(G)]
            quads = [(d, c0) for d in range(2) for c0 in range(0, NCH, 4)]
            for g0 in range(0, len(quads), G):
                gens = [quad_gen(h, d, c0, scr[i], scr[i][2 + 6]) for i, (d, c0) in enumerate(quads[g0:g0 + G])]
                while gens:
                    for g_ in list(gens):
                        try:
                            next(g_)
                        except StopIteration:
                            gens.remove(g_)
            sc2.__exit__(None, None, None)
            mk_("dn%d:C" % h)
            for d in range(2):
                k.dma("sp", Sm[d][:, :], V(s0d.t[d, h], s0d[:].keys))
                k.copy(Sb[d][:, :], Sm[d][:, :], eng="act")
            for step in range(NCH):
                chain_step(h, 0, step)
                chain_step(h, 1, NCH - 1 - step)
            mk_("dn%d:fin" % h)
            for c in range(NCH):
                oa = V(oacc.t[:, c, :], oacc[:, c].keys)
                junk = tf()[:, :]
                k.act(junk, oa, AF.Square, accum_out=fin16[:, c:c + 1])
            k.act(fin16b[:, :], fin16[:, :], AF.Sqrt, bias=k.eps_t[:, 0:1], scale=1.0 / 128.0)
            k.recip(fin16[:, :], fin16b[:, :])
            for c in range(NCH):
                cs_ = slice(c * 128, (c + 1) * 128)
                b = nbank()
                for kc in range(8):
                    k.mm(b[:, 0:128], k.hT[kc][:, PAD + c * 128:PAD + (c + 1) * 128], w[:, kc, :], start=(kc == 0), stop=(kc == 7))
                sg = tb()[:, :]
                k.act(sg, b[:, 0:128], AF.Silu)
                oa = V(oacc.t[:, c, :], oacc[:, c].keys)
                o1 = tf()[:, :]
                k.stt(o1, oa, fin16[:, c:c + 1], nwb[:, :], ALU.mult, ALU.mult)
                ob = tb()[:, :]
                k.tt(ob, o1, sg, ALU.mult)
                k.transpose(pbf[:, 0:128], ob, k.ident_b[:, :])
                k.copy(V(k.mixedT.t[:, 8 + h, cs_], k.mixedT[:, 8 + h].keys), pbf[:, 0:128], eng="dve")


def core_inputs(inputs, core):
    f32 = np.float32
    d = {}
    flags = np.zeros(128, f32)
    rope = np.zeros((2, 128, T), f32)
    if core < 4:
        rope[0] = 1.0
        d["ctx_k"] = np.zeros((512, 256), f32)
        d["ctx_v"] = np.zeros((512, 256), f32)
        flags[112] = -30000.0
        for c in range(16):
            flags[48 + c] = 0.0; flags[64 + c] = 1.0 if c % 2 == 1 else 0.0
            flags[80 + c] = 0.0; flags[96 + c] = 1.0 if c % 2 == 0 else 0.0
        d["x"] = np.ascontiguousarray(inputs["x_prompt"][core * 8:(core + 1) * 8].reshape(T, D))
        d["cond"] = np.ascontiguousarray(inputs["c_ctx"])
        for c in range(16):
            flags[16 + c] = 0.0 if c % 2 == 0 else 1.0
            flags[32 + c] = 0.0 if c % 2 == 1 else 1.0
    else:
        b = (core - 4) % 2
        d["ctx_k"] = np.ascontiguousarray(inputs["cache_l1_k"][b].reshape(512, 256))
        d["ctx_v"] = np.ascontiguousarray(inputs["cache_l1_v"][b].reshape(512, 256))
        pos = np.arange(T)
        inv = (10000.0 ** (-np.arange(32, dtype=np.float64) / 32.0))
        ang = np.concatenate([(pos // 64)[:, None] * inv[None, :], (pos % 64)[:, None] * inv[None, :]], axis=1)
        ang = ang.astype(f32)
        rope[0] = np.concatenate([np.cos(ang), np.cos(ang)], axis=1).T
        rope[1] = np.concatenate([np.sin(ang), np.sin(ang)], axis=1).T
        for c in range(16):
            flags[48 + c] = 1.0; flags[80 + c] = 1.0
        d["x"] = np.ascontiguousarray(inputs["x_sample"][b])
        d["cond"] = np.ascontiguousarray(inputs["c"][b])
        for s in range(8):
            flags[2 * s] = 0.0 if s == 0 else 1.0
            flags[2 * s + 1] = 0.0 if s == 7 else 1.0
        for c in range(16):
            flags[16 + c] = 0.0 if c == 0 else 1.0
            flags[32 + c] = 0.0 if c == 15 else 1.0
    d["flags"] = flags
    d["rope"] = rope
    d["cst_ident"] = np.eye(128, dtype=f32)
    sq = np.arange(128)
    band = np.zeros((2, 128, 512), f32)
    band[0] = np.tile((sq[:, None] >= sq[None, :]).astype(f32), (1, 4))
    band[1] = np.tile((sq[:, None] <= sq[None, :]).astype(f32), (1, 4))
    d["cst_band"] = band
    rot = np.zeros((128, 128), f32)
    for dp in range(64):
        rot[dp + 64, dp] = -1.0
        rot[dp, dp + 64] = 1.0
    d["cst_rot"] = rot
    tri = np.zeros((2, 128, 128), f32)
    tri[0] = (sq[:, None] <= sq[None, :]).astype(f32)
    tri[1] = (sq[:, None] >= sq[None, :]).astype(f32)
    d["cst_tri"] = tri
    mbias = np.zeros((2, 128, 128), f32)
    mbias[0] = np.where(sq[None, :] >= sq[:, None], 0.0, -30000.0)
    mbias[1] = np.where(sq[None, :] <= sq[:, None], 0.0, -30000.0)
    d["cst_mbias"] = mbias
    strict = np.zeros((2, 128, 128), f32)
    strict[0] = (sq[None, :] > sq[:, None]).astype(f32)
    strict[1] = (sq[None, :] < sq[:, None]).astype(f32)
    d["cst_strict"] = strict
    blk = np.zeros((4, 128, 128), f32)
    bd = lambda b: ((sq[:, None] // b) == (sq[None, :] // b)).astype(f32)
    blk[0] = bd(16); blk[1] = bd(32) - bd(16); blk[2] = bd(64) - bd(32); blk[3] = 1.0 - bd(64)
    d["cst_blk"] = blk
    if core < 4:
        d["ssd_h0"] = np.zeros((2, 16, 64, 128), f32)
        d["dn_h0"] = np.zeros((2, 8, 128, 128), f32)
    else:
        d["ssd_h0"] = np.ascontiguousarray(inputs["state_l0_ssd"][(core - 4) % 2])
        d["dn_h0"] = np.ascontiguousarray(inputs["state_l0_dn"][(core - 4) % 2])
    return d


def build(stages):
    k = Net()
    k.load_consts()
    k.load_x()
    for st in stages:
        if st[0] == "ffn":
            l = st[1]
            cols = k.modulation(l)
            k.ffn(l, cols)
        elif st[0] == "mix0":
            cols = k.modulation(0)
            k.mixer_l0(cols)
        elif st[0] == "mix1":
            cols = k.modulation(1)
            k.mixer_l1(cols)
        elif st[0] == "mod":
            cols = k.modulation(st[1])
        elif st[0] == "pre":
            cols = k.modulation(st[1])
            k.prenorm(cols, 3, 4)
    k.store_y()
    k.finish()
    return k


def run(inputs, stages, n_cores=8):
    from concourse.bass_utils import run_bass_kernel_spmd
    k = build(stages)
    in_maps = []
    for c in range(n_cores):
        d = core_inputs(inputs, c)
        m = {}
        for name in k.inp:
            if name in d:
                m[name] = d[name]
            else:
                m[name] = np.ascontiguousarray(np.asarray(inputs[name], dtype=np.float32))
        in_maps.append(m)
    res = run_bass_kernel_spmd(k.nc, in_maps, core_ids=list(range(n_cores)))
    return k, res


FULL = [("full",)]


def build_full():
    k = Net()
    k.load_consts()
    k.load_x()
    import os
    nsub = int(os.environ.get("FULL_N", 4))
    cols0 = k.modulation(0)
    k.mixer_l0(cols0)
    if nsub >= 2:
        k.ffn(0, cols0)
    if nsub >= 3:
        cols1 = k.modulation(1)
        k.mixer_l1(cols1)
    if nsub >= 4:
        k.ffn(1, cols1)
    k.store_y()
    k.finish()
    return k


def kernel(**inputs):
    from concourse.bass_utils import run_bass_kernel_spmd
    inputs = {n: np.asarray(v) for n, v in inputs.items()}
    used = ("x_prompt", "x_sample", "state_l0_ssd", "state_l0_dn", "cache_l1_k", "cache_l1_v", "c", "c_ctx",
            "mod_w_l0", "mod_b_l0", "norm_mix_pre_l0", "norm_mix_post_l0", "norm_ffn_pre_l0", "norm_ffn_post_l0",
            "ffn_up_l0", "ffn_conv_w_l0", "ffn_conv_b_l0", "ffn_down_l0",
            "mod_w_l1", "mod_b_l1", "norm_mix_pre_l1", "norm_mix_post_l1", "norm_ffn_pre_l1", "norm_ffn_post_l1",
            "ffn_up_l1", "ffn_conv_w_l1", "ffn_conv_b_l1", "ffn_down_l1",
            "mix_in_l0", "mix_out_l0", "ssd_conv_w", "ssd_conv_b", "ssd_dt_bias", "ssd_A_log", "ssd_D", "ssd_norm_w",
            "dn_conv_w", "dn_dt_bias", "dn_A_log", "dn_norm_w",
            "mix_in_l1", "mix_out_l1", "sg_ln_w", "sg_ln_b", "sg_w_s", "sg_b_s", "attn_sink")
    assert all(n in inputs for n in used)
    k = build_full()
    in_maps = []
    for c in range(8):
        d = core_inputs(inputs, c)
        m = {}
        for name in k.inp:
            m[name] = d[name] if name in d else np.ascontiguousarray(np.asarray(inputs[name], dtype=np.float32))
        in_maps.append(m)
    res = run_bass_kernel_spmd(k.nc, in_maps, core_ids=list(range(8)))
    r = res.results
    f32 = np.float32
    y_prompt = np.concatenate([r[c]["y"].reshape(8, 256, D) for c in range(4)], axis=0).astype(f32)
    y_sample = np.stack([r[4]["y"], r[5]["y"]], axis=0).astype(f32)
    new_ssd = np.concatenate([r[c]["ssd_out"] for c in range(4)], axis=0).astype(f32)
    new_dn = np.concatenate([r[c]["dn_out"] for c in range(4)], axis=0).astype(f32)
    new_k = np.concatenate([r[c]["nk"].reshape(8, 256, 2, 128) for c in range(4)], axis=0).astype(f32)
    new_v = np.concatenate([r[c]["nv"].reshape(8, 256, 2, 128) for c in range(4)], axis=0).astype(f32)
    return (y_prompt, y_sample, new_ssd, new_dn, new_k, new_v)
```

```python
import contextlib
import numpy as np
import concourse.bass as bass
import concourse.mybir as mybir

F32 = mybir.dt.float32
BF16 = mybir.dt.bfloat16
AF = mybir.ActivationFunctionType
ALU = mybir.AluOpType
AX = mybir.AxisListType

SAME_ENGINE_SYNC = True


class V:
    __slots__ = ("ap", "keys")

    def __init__(self, ap, keys):
        self.ap = ap
        self.keys = keys


class Buf:
    _n = 0

    def __init__(self, kb, t, shape, gran=None):
        self.kb = kb
        self.t = t
        self.shape = list(shape)
        Buf._n += 1
        self.id = Buf._n
        f0 = self.shape[1] if len(self.shape) > 1 else 1
        self.gran = gran if gran else f0
        self.ng = (f0 + self.gran - 1) // self.gran

    def _keys(self, idx):
        if not isinstance(idx, tuple):
            idx = (idx,)
        lo, hi = 0, self.ng - 1
        if len(idx) > 1:
            i1 = idx[1]
            if isinstance(i1, slice):
                a = 0 if i1.start is None else i1.start
                b = self.shape[1] if i1.stop is None else i1.stop
                lo, hi = a // self.gran, (b - 1) // self.gran
            elif isinstance(i1, int):
                lo = hi = i1 // self.gran
        return [(self.id, g) for g in range(lo, hi + 1)]

    def __getitem__(self, idx):
        return V(self.t[idx], self._keys(idx))

    def v(self, ap, idx=None):
        return V(ap, self._keys(idx) if idx is not None else [(self.id, g) for g in range(self.ng)])


class KB:
    ENG = ("pe", "act", "dve", "pool", "sp")

    def __init__(self):
        self.nc = bass.Bass("TRN2", target_bir_lowering=False)
        self.es = contextlib.ExitStack()
        nc = self.nc
        self.E = {"pe": nc.tensor, "act": nc.scalar, "dve": nc.vector, "pool": nc.gpsimd, "sp": nc.sync}
        self.sems = {}
        self.cnt = {}
        for e in self.ENG:
            self.sems["e_" + e] = self.es.enter_context(nc.semaphore("s_" + e))
            self.cnt["e_" + e] = 0
        self.NRING = 12
        for q in ("hw", "sw"):
            for i in range(self.NRING):
                nm = "d_%s%d" % (q, i)
                self.sems[nm] = self.es.enter_context(nc.semaphore(nm))
                self.cnt[nm] = 0
        self.ring_i = {"hw": 0, "sw": 0}
        self.seen = {e: {} for e in self.ENG}
        self.last_w = {}
        self.readers = {}
        self.n_inst = 0
        self.n_wait = 0
        self.out_deps = []
        self.stack = [self.es]
        self.waits_by = {}
        self.cur_birth = None
        self.birth = {}
        self.birth_done = set()

    @contextlib.contextmanager
    def scope(self):
        es = contextlib.ExitStack()
        self.stack.append(es)
        try:
            yield
        finally:
            self.stack.pop()
            es.close()
            self.cur_birth = tuple((s, v) for s, v in self.cnt.items() if v > 0)

    def sbuf(self, name, shape, dtype=F32, gran=None):
        self._nalloc = getattr(self, "_nalloc", 0) + 1
        t = self.stack[-1].enter_context(self.nc.sbuf_tensor("%s_%d" % (name, self._nalloc), list(shape), dtype))
        b = Buf(self, t, shape, gran)
        if self.cur_birth:
            self.birth[b.id] = self.cur_birth
        return b

    def psum(self, name, shape, dtype=F32, gran=None):
        t = self.stack[-1].enter_context(self.nc.psum_tensor(name, list(shape), dtype))
        return Buf(self, t, shape, gran)

    def dram(self, name, shape, dtype=F32, kind="Internal", gran=None):
        t = self.nc.dram_tensor(name, list(shape), dtype, kind=kind)
        return Buf(self, t.ap() if hasattr(t, "ap") else t, shape, gran)

    def _collect(self, reads, writes, eng=None):
        deps = {}

        def add(d):
            if d is None:
                return
            s, v = d
            if deps.get(s, 0) < v:
                deps[s] = v
        if self.birth:
            for r in list(reads) + list(writes):
                for k in r.keys:
                    bid = k[0]
                    if bid in self.birth and (eng, bid) not in self.birth_done:
                        self.birth_done.add((eng, bid))
                        for d in self.birth[bid]:
                            add(d)
        for r in reads:
            for k in r.keys:
                add(self.last_w.get(k))
        for w in writes:
            for k in w.keys:
                add(self.last_w.get(k))
                for d in self.readers.get(k, ()):
                    add(d)
        return deps

    def _emit_waits(self, eng, deps, war_only_same=None):
        need = []
        own = "e_" + eng
        for s, v in deps.items():
            if s == own and (not SAME_ENGINE_SYNC or eng == "pe"):
                continue
            if self.seen[eng].get(s, 0) >= v:
                continue
            need.append((s, v))
        return need

    def _record(self, mark, reads, writes):
        for r in reads:
            for k in r.keys:
                self.readers.setdefault(k, []).append(mark)
        for w in writes:
            for k in w.keys:
                self.last_w[k] = mark
                self.readers[k] = []

    def op(self, eng, fn, reads, writes):
        deps = self._collect(reads, writes, eng)
        need = self._emit_waits(eng, deps)
        E = self.E[eng]
        for s, v in need[:-1]:
            E.wait_ge(self.sems[s], v)
            self.n_wait += 1
            self.waits_by[eng] = self.waits_by.get(eng, 0) + 1
        inst = fn()
        if need:
            s, v = need[-1]
            inst._wait_ge(self.sems[s], v)
        for s, v in need:
            self.seen[eng][s] = v
        own = "e_" + eng
        self.cnt[own] += 1
        inst.then_inc(self.sems[own], 1)
        mark = (own, self.cnt[own])
        self._record(mark, reads, writes)
        self.n_inst += 1
        return inst

    def dma(self, queue, out, in_, is_output=False, **kw):
        ring = "sw" if queue == "pool" else "hw"
        i = self.ring_i[ring]
        self.ring_i[ring] += 1
        nm = "d_%s%d" % (ring, i % self.NRING)
        deps = self._collect([in_], [out], queue)
        if self.cnt[nm] > 0:
            if deps.get(nm, 0) < self.cnt[nm]:
                deps[nm] = self.cnt[nm]
        need = self._emit_waits(queue, deps)
        E = self.E[queue]
        for s, v in need[:-1]:
            E.wait_ge(self.sems[s], v)
            self.n_wait += 1
        inst = E.dma_start(out=out.ap, in_=in_.ap, **kw)
        if need:
            s, v = need[-1]
            inst._wait_ge(self.sems[s], v)
        for s, v in need:
            self.seen[queue][s] = v
        self.cnt[nm] += 16
        inst.then_inc(self.sems[nm], 16)
        mark = (nm, self.cnt[nm])
        self._record(mark, [in_], [out])
        if is_output:
            self.out_deps.append(mark)
        self.n_inst += 1
        return inst

    def finish(self):
        for s, v in self.cnt.items():
            if v > 0:
                self.E["sp"].wait_ge(self.sems[s], v)

    def mm(self, out, lhsT, rhs, start=True, stop=True, **kw):
        return self.op("pe", lambda: self.nc.tensor.matmul(out.ap, lhsT=lhsT.ap, rhs=rhs.ap, start=start, stop=stop, **kw),
                       [lhsT, rhs] + ([] if start else [out]), [out])

    def transpose(self, out, in_, ident):
        return self.op("pe", lambda: self.nc.tensor.transpose(out.ap, in_.ap, ident.ap), [in_, ident], [out])

    def act(self, out, in_, func, bias=None, scale=None, accum_out=None, eng="act"):
        reads = [in_]
        kw = {}
        if bias is not None:
            if isinstance(bias, V):
                reads.append(bias); kw["bias"] = bias.ap
            else:
                kw["bias"] = bias
        if scale is not None:
            if isinstance(scale, V):
                reads.append(scale); kw["scale"] = scale.ap
            else:
                kw["scale"] = scale
        writes = [out]
        if accum_out is not None:
            writes.append(accum_out); kw["accum_out"] = accum_out.ap
        return self.op("act", lambda: self.nc.scalar.activation(out=out.ap, in_=in_.ap, func=func, **kw), reads, writes)

    def tt(self, out, in0, in1, op, eng="dve"):
        E = self.E[eng]
        return self.op(eng, lambda: E.tensor_tensor(out=out.ap, in0=in0.ap, in1=in1.ap, op=op), [in0, in1], [out])

    def ts(self, out, in0, s1, op0, s2=None, op1=None, eng="dve", accum_out=None):
        E = self.E[eng]
        reads = [in0]
        a1 = s1.ap if isinstance(s1, V) else s1
        a2 = s2.ap if isinstance(s2, V) else s2
        if isinstance(s1, V): reads.append(s1)
        if isinstance(s2, V): reads.append(s2)
        kw = {}
        writes = [out]
        if op1 is not None:
            kw["op1"] = op1
        if accum_out is not None:
            kw["accum_out"] = accum_out.ap; writes.append(accum_out)
        return self.op(eng, lambda: E.tensor_scalar(out=out.ap, in0=in0.ap, scalar1=a1, scalar2=a2, op0=op0, **kw), reads, writes)

    def stt(self, out, in0, scalar, in1, op0, op1, eng="dve"):
        E = self.E[eng]
        reads = [in0, in1]
        a = scalar.ap if isinstance(scalar, V) else scalar
        if isinstance(scalar, V): reads.append(scalar)
        return self.op(eng, lambda: E.scalar_tensor_tensor(out=out.ap, in0=in0.ap, scalar=a, in1=in1.ap, op0=op0, op1=op1), reads, [out])

    def copy(self, out, in_, eng="dve"):
        if eng == "act":
            return self.op("act", lambda: self.nc.scalar.copy(out=out.ap, in_=in_.ap), [in_], [out])
        E = self.E[eng]
        return self.op(eng, lambda: E.tensor_copy(out=out.ap, in_=in_.ap), [in_], [out])

    def memset(self, out, val, eng="dve"):
        E = self.E[eng]
        return self.op(eng, lambda: E.memset(out.ap, val), [], [out])

    def recip(self, out, in_):
        return self.op("dve", lambda: self.nc.vector.reciprocal(out=out.ap, in_=in_.ap), [in_], [out])

T = 2048
D = 1024
NCH = 16
NTT = 4
NSEG = 8
DFF = 2816
NJ = DFF // 128
EPS = 1e-6
PAD = 2
L0_PROJ = 7232
L1_PROJ = 3584


class SubBuf:
    def __init__(self, buf, off, width):
        self.buf = buf; self.off = off; self.width = width
        self.id = buf.id
        self.t = buf.t[:, off:off + width]
        self._k = buf._keys((slice(None), slice(off, off + width)))

    def __getitem__(self, idx):
        return V(self.t[idx], self._k)


class Net(KB):
    def __init__(self, dbg=None):
        super().__init__()
        self.dbg = dbg or {}
        self.inp = {}
        self.banks = [self.psum("bank%d" % i, [128, 512], F32) for i in range(7)]
        self.psum_bf(0)
        self.banks.append(None)

    def psum_bf(self, i):
        if not hasattr(self, "_bfb"):
            big = self.psum("bfbig", [128, 1024], BF16)
            self._bfb = [SubBuf(big, j * 512, 512) for j in range(2)]
        return self._bfb[i]

    def din(self, name, shape, dtype=F32, gran=None):
        b = self.dram(name, shape, dtype, kind="ExternalInput", gran=gran)
        self.inp[name] = b
        return b

    def dout(self, name, shape, dtype=F32, gran=None):
        return self.dram(name, shape, dtype, kind="ExternalOutput", gran=gran)

    def load_cols(self, dst, src_ap, src_keys, n):
        k = self
        st = k.colstage[k.ncst % 2]; k.ncst += 1
        k.dma("sp", st[0:n, :], V(src_ap, src_keys))
        bk = k.banks[6]
        k.transpose(bk[:, 0:n], st[0:n, :], k.ident_f[0:n, 0:n])
        k.copy(dst, bk[:, 0:n], eng="dve")

    def load_consts(self):
        k = self
        c = k.din("cst_ident", [128, 128])
        k.ident_f = k.sbuf("ident_f", [128, 128], F32)
        k.dma("sp", k.ident_f[:, :], c[:, :])
        k.ident_b = k.sbuf("ident_b", [128, 128], BF16)
        k.copy(k.ident_b[:, :], k.ident_f[:, :], eng="dve")
        k.ones_b = k.sbuf("ones_b", [128, 128], BF16)
        k.memset(k.ones_b[:, :], 1.0, eng="pool")
        k.ones_f = k.sbuf("ones_f", [128, 128], F32)
        k.memset(k.ones_f[:, :], 1.0, eng="pool")
        k.colstage = [k.sbuf("colstage%d" % i, [128, 128], F32) for i in range(2)]
        k.ncst = 0
        k.eps_t = k.sbuf("eps_t", [128, 1], F32)
        k.memset(k.eps_t[:, :], EPS, eng="pool")
        fl = k.din("flags", [128])
        k.flags = k.sbuf("flags_t", [128, 128], F32)
        k.dma("sp", k.flags[:, :], V(fl.t.partition_broadcast(128), fl[:].keys))
        k.xs = [k.dram("xs%d" % fc, [128, T], F32, gran=512) for fc in range(8)]
        k.hT = [k.sbuf("hT%d" % fc, [128, T + 2 * PAD], BF16, gran=None) for fc in range(8)]
        for fc in range(8):
            k.memset(k.hT[fc][:, 0:PAD], 0.0, eng="pool")
            k.memset(k.hT[fc][:, T + PAD:T + 2 * PAD], 0.0, eng="pool")

    def load_x(self):
        k = self
        x = k.din("x", [T, D], gran=None)
        sc = k.scope(); sc.__enter__()
        xin = [k.sbuf("xin%d" % i, [128, 4, D], F32) for i in range(2)]
        xt = [k.sbuf("xt%d" % i, [128, 512], F32) for i in range(4)]
        n = 0
        for tt in range(NTT):
            xi = xin[tt % 2]
            k.dma("sp", xi[:, :, :], V(x.t[tt * 512:(tt + 1) * 512, :].rearrange("(c p) d -> p c d", p=128), x[:].keys))
            for fc in range(8):
                bk = k.banks[n % 4]
                for c in range(4):
                    k.transpose(bk[:, c * 128:(c + 1) * 128], xi[:, c, fc * 128:(fc + 1) * 128], k.ident_f[:, :])
                xo = xt[n % 4]
                if n % 2 == 0:
                    k.copy(xo[:, :], bk[:, :], eng="act")
                else:
                    k.copy(xo[:, :], bk[:, :], eng="dve")
                k.dma("sp", k.xs[fc][:, tt * 512:(tt + 1) * 512], xo[:, :])
                n += 1
        sc.__exit__(None, None, None)

    def store_y(self):
        k = self
        y = k.dout("y", [T, D])
        sc = k.scope(); sc.__enter__()
        xl = [k.sbuf("yl%d" % i, [128, 512], F32) for i in range(4)]
        yo = [k.sbuf("yo%d" % i, [128, 4, D], F32) for i in range(2)]
        n = 0
        for tt in range(NTT):
            yt = yo[tt % 2]
            for fc in range(8):
                xi = xl[n % 4]
                k.dma("sp", xi[:, :], k.xs[fc][:, tt * 512:(tt + 1) * 512])
                bk = k.banks[n % 4]
                for c in range(4):
                    k.transpose(bk[:, c * 128:(c + 1) * 128], xi[:, c * 128:(c + 1) * 128], k.ident_f[:, :])
                o = V(yt.t[:, :, fc * 128:(fc + 1) * 128], yt[:].keys)
                src = V(bk.t[:, :].rearrange("p (c f) -> p c f", c=4), bk[:].keys)
                if n % 2 == 0:
                    k.copy(o, src, eng="act")
                else:
                    k.copy(o, src, eng="dve")
                n += 1
            k.dma("sp", V(y.t[tt * 512:(tt + 1) * 512, :].rearrange("(c p) d -> p c d", p=128), y[:].keys), yt[:, :, :], is_output=True)
        sc.__exit__(None, None, None)

    def modulation(self, l):
        k = self
        L = "l%d" % l
        cond = k.inp.get("cond") or k.din("cond", [D])
        mw = k.din("mod_w_" + L, [D, 6 * D])
        mb = k.din("mod_b_" + L, [6 * D])
        nws = [k.din(n + L, [D]) for n in ("norm_mix_pre_", "norm_mix_post_", "norm_ffn_pre_", "norm_ffn_post_")]
        cols = k.sbuf("modcols_" + L, [128, 6, 8], F32)
        sc = k.scope(); sc.__enter__()
        cs = k.sbuf("cond_" + L, [128, 8], F32)
        k.load_cols(cs[:, :], cond.t.rearrange("(c p) -> c p", p=128), cond[:].keys, 8)
        mbt = k.sbuf("modb_" + L, [128, 48], F32)
        k.load_cols(mbt[:, :], mb.t.rearrange("(j p) -> j p", p=128), mb[:].keys, 48)
        nwt = k.sbuf("nw_" + L, [128, 4, 8], F32)
        for i, nw in enumerate(nws):
            k.load_cols(nwt[:, i, :], nw.t.rearrange("(c p) -> c p", p=128), nw[:].keys, 8)
        sb = k.sbuf("scond_" + L, [128, 8], BF16)
        k.act(sb[:, :], cs[:, :], AF.Silu)
        wbuf = [k.sbuf("modw%d_%s" % (i, L), [128, 8, 512], BF16) for i in range(2)]
        mps = k.banks[6]
        mwv = mw.t.rearrange("(c p) n -> p c n", p=128)
        for blk in range(12):
            wb = wbuf[blk % 2]
            k.dma("pool", wb[:, :, :], V(mwv[:, :, blk * 512:(blk + 1) * 512], mw[:].keys))
            for nn in range(4):
                j = blk * 4 + nn
                for kc in range(8):
                    k.mm(mps[:, j:j + 1], wb[:, kc, nn * 128:(nn + 1) * 128], sb[:, kc:kc + 1], start=(kc == 0), stop=(kc == 7))
        mod = k.sbuf("mod_" + L, [128, 48], F32)
        k.tt(mod[:, :], mps[:, 0:48], mbt[:, :], ALU.add)
        k.stt(cols[:, 0, :], mod[:, 8:16], 1.0, nwt[:, 0, :], ALU.add, ALU.mult)
        k.copy(cols[:, 1, :], mod[:, 0:8], eng="dve")
        k.tt(cols[:, 2, :], mod[:, 16:24], nwt[:, 1, :], ALU.mult)
        k.stt(cols[:, 3, :], mod[:, 32:40], 1.0, nwt[:, 2, :], ALU.add, ALU.mult)
        k.copy(cols[:, 4, :], mod[:, 24:32], eng="dve")
        k.tt(cols[:, 5, :], mod[:, 40:48], nwt[:, 3, :], ALU.mult)
        sc.__exit__(None, None, None)
        return cols

    def rstd_from(self, out, ss, n, tmp):
        k = self
        k.act(tmp, ss, AF.Ln, bias=k.eps_t[:, 0:1], scale=1.0 / n)
        k.act(out, tmp, AF.Exp, scale=-0.5)

    def prenorm(self, cols, ia, ib):
        k = self
        sc = k.scope(); sc.__enter__()
        k.pn_x = [k.sbuf("pn_x%d" % i, [128, 512], F32) for i in range(10)]
        k.pn_sq = [k.sbuf("pn_sq%d" % i, [128, 512], BF16) for i in range(3)]
        k.pn_r = [k.sbuf("pn_r%d" % i, [128, 512], F32) for i in range(2)]
        k.pn_t = [k.sbuf("pn_t%d" % i, [128, 512], F32) for i in range(3)]
        n = 0
        for tt in range(NTT):
            sl = slice(tt * 512, (tt + 1) * 512)
            ss = k.banks[tt % 2]
            xs_t = []
            for fc in range(8):
                xb = k.pn_x[(tt * 8 + fc) % 10]
                k.dma("sp", xb[:, :], k.xs[fc][:, sl])
                sq = k.pn_sq[n % 3]; n += 1
                k.act(sq[:, :], xb[:, :], AF.Square)
                k.mm(ss[:, :], k.ones_b[:, :], sq[:, :], start=(fc == 0), stop=(fc == 7))
                xs_t.append(xb)
            r = k.pn_r[tt % 2]
            k.rstd_from(r[:, :], ss[:, :], float(D), k.pn_t[0][:, :])
            for fc in range(8):
                tm = k.pn_t[1 + fc % 2]
                k.stt(tm[:, :], xs_t[fc][:, :], cols[:, ia, fc:fc + 1], r[:, :], ALU.mult, ALU.mult)
                k.act(k.hT[fc][:, PAD + tt * 512:PAD + (tt + 1) * 512], tm[:, :], AF.Identity, bias=cols[:, ib, fc:fc + 1])
        sc.__exit__(None, None, None)

    def postnorm_alloc(self):
        k = self
        k.po_x = [k.sbuf("po_x%d" % i, [128, 512], F32) for i in range(8)]
        k.po_r = k.sbuf("po_r", [128, 512], F32)
        k.po_t = [k.sbuf("po_t%d" % i, [128, 512], F32) for i in range(3)]

    def postnorm_prefetch(self, tt):
        k = self
        sl = slice(tt * 512, (tt + 1) * 512)
        for fc in range(8):
            k.dma("sp", k.po_x[fc][:, :], k.xs[fc][:, sl])

    def postnorm_tile(self, tt, o_tile, ss, cols, ig):
        k = self
        sl = slice(tt * 512, (tt + 1) * 512)
        k.rstd_from(k.po_r[:, :], ss, float(D), k.po_t[0][:, :])
        for fc in range(8):
            xb = k.po_x[fc]
            tm = k.po_t[1 + fc % 2]
            k.tt(tm[:, :], o_tile[fc], k.po_r[:, :], ALU.mult)
            k.stt(xb[:, :], tm[:, :], cols[:, ig, fc:fc + 1], xb[:, :], ALU.mult, ALU.add)
            k.dma("sp", k.xs[fc][:, sl], xb[:, :])

    def ffn(self, l, cols):
        k = self
        L = "l%d" % l
        wup = k.din("ffn_up_" + L, [D, 2 * DFF])
        wcv = k.din("ffn_conv_w_" + L, [3, 2 * DFF])
        bcv = k.din("ffn_conv_b_" + L, [2 * DFF])
        wdn = k.din("ffn_down_" + L, [DFF, D])
        k.prenorm(cols, 3, 4)
        sc = k.scope(); sc.__enter__()
        k.ff_g = k.sbuf("ff_g", [128, NJ, 1024], BF16, gran=1)
        k.ff_wu = [k.sbuf("ff_wu%d" % i, [128, 8, 2, 512], BF16) for i in range(2)]
        k.ff_wd = [k.sbuf("ff_wd%d" % i, [128, 512], BF16) for i in range(4)]
        k.ff_u = [k.sbuf("ff_u%d" % i, [128, 258], BF16) for i in range(4)]
        k.ff_dgall = k.sbuf("ff_dgall", [128, 2 * NJ, 3, 128], BF16, gran=1)
        k.ff_sa = [k.sbuf("ff_sa%d" % i, [128, 256], F32) for i in range(2)]
        k.ff_o = [k.sbuf("ff_o%d" % i, [128, 512], F32) for i in range(8)]
        k.ff_sq = [k.sbuf("ff_sq%d" % i, [128, 512], BF16) for i in range(2)]
        k.postnorm_alloc()
        cw = k.sbuf("ff_cw_" + L, [128, 44, 3], F32)
        cb = k.sbuf("ff_cb_" + L, [128, 44], F32)
        for tap in range(3):
            k.load_cols(V(cw.t[:, :, tap], cw[:].keys), wcv.t[tap, :].rearrange("(c p) -> c p", p=128), wcv[:].keys, 44)
        k.load_cols(cb[:, :], bcv.t.rearrange("(c p) -> c p", p=128), bcv[:].keys, 44)
        for ch in range(2 * NJ):
            for tap in range(3):
                k.act(V(k.ff_dgall.t[:, ch, tap, :], k.ff_dgall[:, ch].keys), k.ident_b[:, :], AF.Copy, scale=cw[:, ch, tap:tap + 1])
        wupv = wup.t.rearrange("(c p) n -> p c n", p=128)
        nwu = 0
        nu = 0
        nd = 0
        import os
        dbg_tt = int(os.environ.get("FFN_TT", NTT)); dbg_jp = int(os.environ.get("FFN_JP", NJ // 2)); dbg_part = int(os.environ.get("FFN_PART", 9))
        for tp in range(dbg_tt // 2):
            pending = None

            def conv_stage(item):
                j, sg, us, nu_ = item
                dgs = [V(k.ff_dgall.t[:, ab * NJ + j, :, :], k.ff_dgall[:, ab * NJ + j].keys) for ab in range(2)]
                pcs = []
                for ab in range(2):
                    pc = k.banks[4 + ab]
                    for tap in range(3):
                        k.mm(pc[:, 0:256], V(dgs[ab].ap[:, tap, :], dgs[ab].keys), us[ab][:, tap:tap + 256], start=(tap == 0), stop=(tap == 2))
                    pcs.append(pc)
                sa = k.ff_sa[nu_ % 2]
                k.act(sa[:, :], pcs[0][:, 0:256], AF.Silu, bias=cb[:, j:j + 1])
                k.stt(V(k.ff_g.t[:, j, sg * 256:(sg + 1) * 256], k.ff_g[:, j].keys), pcs[1][:, 0:256], cb[:, NJ + j:NJ + j + 1], sa[:, :],
                      ALU.add, ALU.mult)

            for jp in range((NJ + 3) // 4):
                wu = k.ff_wu[nwu % 2]; nwu += 1
                nj_here = min(4, NJ - jp * 4)
                for ab in range(2):
                    c0 = ab * DFF + jp * 512
                    k.dma("pool", V(wu.t[:, :, ab, 0:nj_here * 128], wu[:].keys), V(wupv[:, :, c0:c0 + nj_here * 128], wup[:].keys))
                for jj in range(nj_here):
                    j = jp * 4 + jj
                    for sg in range(4):
                        seg = tp * 4 + sg
                        c_lo = PAD + seg * 256 - 1
                        us = []
                        for ab in range(2):
                            pb = k.banks[(nu * 2 + ab) % 4]
                            for kc in range(8):
                                k.mm(pb[:, 0:258], wu[:, kc, ab, jj * 128:(jj + 1) * 128], k.hT[kc][:, c_lo:c_lo + 258],
                                     start=(kc == 0), stop=(kc == 7))
                            u = k.ff_u[(nu * 2 + ab) % 4]
                            if ab == 0:
                                k.copy(u[:, 0:258], pb[:, 0:258], eng="act")
                            else:
                                k.copy(u[:, 0:258], pb[:, 0:258], eng="dve")
                            uv = V(u.t[:, 0:258:257], u[:].keys)
                            k.tt(uv, uv, k.flags[:, seg * 2:seg * 2 + 2], ALU.mult)
                            us.append(u)
                        if pending is not None:
                            conv_stage(pending)
                        pending = (j, sg, us, nu)
                        nu += 1
            if pending is not None:
                conv_stage(pending)
            for st_ in range(2):
                tt = tp * 2 + st_
                k.postnorm_prefetch(tt)
                ss = k.banks[6]
                for half in range(2):
                    for j in range(NJ):
                        wd = k.ff_wd[nd % 4]; nd += 1
                        k.dma("pool", wd[:, :], V(wdn.t[j * 128:(j + 1) * 128, half * 512:(half + 1) * 512], wdn[:].keys))
                        for nn in range(4):
                            k.mm(k.banks[nn][:, :], wd[:, nn * 128:(nn + 1) * 128], V(k.ff_g.t[:, j, st_ * 512:(st_ + 1) * 512], k.ff_g[:, j].keys),
                                 start=(j == 0), stop=(j == NJ - 1))
                    for nn in range(4):
                        n = half * 4 + nn
                        k.copy(k.ff_o[n][:, :], k.banks[nn][:, :], eng="dve")
                        sq = k.ff_sq[n % 2]
                        k.act(sq[:, :], k.ff_o[n][:, :], AF.Square)
                        k.mm(ss[:, :], k.ones_b[:, :], sq[:, :], start=(n == 0), stop=(n == 7))
                k.postnorm_tile(tt, [k.ff_o[n][:, :] for n in range(8)], ss[:, :], cols, 5)
        sc.__exit__(None, None, None)

    def gelu(self, out, src_psum, n, scr):
        k = self
        k.act(out, src_psum, AF.Gelu_apprx_tanh)

    def outproj_post(self, wname, cols):
        k = self
        wo = k.din(wname, [2048, D])
        sc = k.scope(); sc.__enter__()
        wt = k.sbuf("wo", [128, 16, D], BF16)
        wov = wo.t.rearrange("(c p) n -> p c n", p=128)
        for q in range(4):
            k.dma("pool", V(wt.t[:, q * 4:(q + 1) * 4, :], wt[:].keys), V(wov[:, q * 4:(q + 1) * 4, :], wo[:].keys))
        o_t = [k.sbuf("op_o%d" % i, [128, 512], F32) for i in range(8)]
        sqs = [k.sbuf("op_sq%d" % i, [128, 512], BF16) for i in range(2)]
        k.postnorm_alloc()
        nb = 0
        for tt in range(NTT):
            k.postnorm_prefetch(tt)
            ss = k.banks[6]
            for n in range(8):
                bk = k.banks[nb % 4]; nb += 1
                for kc in range(16):
                    k.mm(bk[:, :], wt[:, kc, n * 128:(n + 1) * 128], V(k.mixedT.t[:, kc, tt * 512:(tt + 1) * 512], k.mixedT[:, kc].keys),
                         start=(kc == 0), stop=(kc == 15))
                k.copy(o_t[n][:, :], bk[:, :], eng="dve")
                sq = sqs[n % 2]
                k.act(sq[:, :], o_t[n][:, :], AF.Square)
                k.mm(ss[:, :], k.ones_b[:, :], sq[:, :], start=(n == 0), stop=(n == 7))
            k.postnorm_tile(tt, [o_t[n][:, :] for n in range(8)], ss[:, :], cols, 2)
        sc.__exit__(None, None, None)

    def mixer_l1(self, cols):
        k = self
        win = k.din("mix_in_l1", [D, L1_PROJ])
        winv = win.t.rearrange("(c p) n -> p c n", p=128)
        lnw = k.din("sg_ln_w", [1024]); lnb = k.din("sg_ln_b", [1024])
        sws = k.din("sg_w_s", [4, 128, 128]); sbs = k.din("sg_b_s", [4, 128])
        sink = k.din("attn_sink", [8])
        ck = k.din("ctx_k", [512, 256]); cv = k.din("ctx_v", [512, 256])
        rope = k.din("rope", [2, 128, T])
        cband = k.din("cst_band", [2, 128, 512])
        crot = k.din("cst_rot", [128, 128])
        nk = k.dout("nk", [T, 256]); nv = k.dout("nv", [T, 256])
        k.prenorm(cols, 0, 1)
        sc0 = k.scope(); sc0.__enter__()
        k.mixedT = k.sbuf("mixedT", [128, 16, T], BF16, gran=1)
        sc = k.scope(); sc.__enter__()
        wgv = k.sbuf("wgv", [128, 8, 1024], BF16)
        for q in range(2):
            k.dma("pool", V(wgv.t[:, :, q * 512:(q + 1) * 512], wgv[:].keys), V(winv[:, :, 1024 + q * 512:1024 + (q + 1) * 512], win[:].keys))
        lnw_t = k.sbuf("lnw_t", [128, 1024], F32); lnb_t = k.sbuf("lnb_t", [128, 1024], F32)
        k.dma("sp", lnw_t[:, :], V(lnw.t.partition_broadcast(128), lnw[:].keys))
        k.dma("sp", lnb_t[:, :], V(lnb.t.partition_broadcast(128), lnb[:].keys))
        wsT = k.sbuf("wsT", [128, 4, 128], BF16)
        wstage = k.sbuf("wstage", [128, 4, 128], F32)
        k.dma("sp", wstage[:, :, :], V(sws.t.rearrange("g i j -> i g j"), sws[:].keys))
        for g in range(4):
            bk = k.banks[g % 2]
            k.transpose(bk[:, 0:128], wstage[:, g, :], k.ident_f[:, :])
            k.copy(wsT[:, g, :], bk[:, 0:128], eng="dve")
        bsf = k.sbuf("bsf", [1, 512], F32); bsb = k.sbuf("bsb", [1, 512], BF16)
        k.dma("sp", bsf[:, :], V(sbs.t.rearrange("(o g) i -> o (g i)", o=1), sbs[:].keys))
        k.copy(bsb[:, :], bsf[:, :], eng="dve")
        gsc = [(k.sbuf("g_x%d" % i, [128, 512], F32), k.sbuf("g_t%d" % i, [128, 512], F32), k.sbuf("g_s%d" % i, [128, 512], F32)) for i in range(2)]
        wu = [k.sbuf("wu1_%d" % i, [128, 8, 128], BF16) for i in range(2)]
        ng = 0
        k.dma("pool", wu[0][:, :, :], V(winv[:, :, 0:128], win[:].keys))
        for n in range(8):
            w = wu[n % 2]
            if n + 1 < 8:
                k.dma("pool", wu[(n + 1) % 2][:, :, :], V(winv[:, :, (n + 1) * 128:(n + 2) * 128], win[:].keys))
            for tt in range(NTT):
                bk = k.banks[ng % 2]
                for kc in range(8):
                    k.mm(bk[:, :], w[:, kc, :], k.hT[kc][:, PAD + tt * 512:PAD + (tt + 1) * 512], start=(kc == 0), stop=(kc == 7))
                k.gelu(V(k.mixedT.t[:, n, tt * 512:(tt + 1) * 512], k.mixedT[:, n].keys), bk[:, :], 512, gsc[ng % 2])
                ng += 1
        gv = [k.sbuf("gv%d" % i, [128, 1024], F32) for i in range(2)]
        gvn = [k.sbuf("gvn%d" % i, [128, 1024], BF16) for i in range(2)]
        st6 = k.sbuf("st6", [128, 2, 6], F32); mv = k.sbuf("mv", [128, 2], F32); rs = k.sbuf("rs", [128, 2], F32)
        for c in range(NCH):
            g_ = gv[c % 2]
            for hf in range(2):
                bk = k.banks[2 + hf]
                for kc in range(8):
                    k.mm(bk[:, :], k.hT[kc][:, PAD + c * 128:PAD + (c + 1) * 128], wgv[:, kc, hf * 512:(hf + 1) * 512], start=(kc == 0), stop=(kc == 7))
                k.gelu(g_[:, hf * 512:(hf + 1) * 512], bk[:, :], 512, gsc[ng % 2]); ng += 1
                k.op("dve", lambda g_=g_, hf=hf: k.nc.vector.bn_stats(out=st6.t[:, hf, :], in_=g_.t[:, hf * 512:(hf + 1) * 512]), [g_[:, :]], [st6[:, :, :]])
            k.op("dve", lambda: k.nc.vector.bn_aggr(out=mv.t[:, :], in_=st6.t[:, :, :].rearrange("p a b -> p (a b)")), [st6[:, :, :]], [mv[:, :]])
            k.act(rs[:, 0:1], mv[:, 1:2], AF.Sqrt, bias=k.eps_t[:, 0:1], scale=1.0)
            k.recip(rs[:, 1:2], rs[:, 0:1])
            k.ts(g_[:, :], g_[:, :], mv[:, 0:1], ALU.subtract, rs[:, 1:2], ALU.mult)
            k.tt(g_[:, :], g_[:, :], lnw_t[:, :], ALU.mult, eng="pool")
            gn = gvn[c % 2]
            k.tt(gn[:, :], g_[:, :], lnb_t[:, :], ALU.add)
            for hf in range(2):
                bk = k.banks[4 + hf]
                for q in range(4):
                    dch = hf * 4 + q
                    g = dch // 2
                    k.mm(bk[:, q * 128:(q + 1) * 128], gn[:, dch * 128:(dch + 1) * 128], wsT[:, g, :], start=True, stop=False)
                    k.mm(bk[:, q * 128:(q + 1) * 128], k.ones_b[0:1, :], bsb[0:1, g * 128:(g + 1) * 128], start=False, stop=True)
                mo = V(k.mixedT.t[:, hf * 4:hf * 4 + 4, c * 128:(c + 1) * 128], [(k.mixedT.id, hf * 4 + q) for q in range(4)])
                k.tt(mo, V(bk.t[:, :].rearrange("p (q i) -> p q i", q=4), bk[:].keys), mo, ALU.mult)
        sc.__exit__(None, None, None)
        sc = k.scope(); sc.__enter__()
        qT = k.sbuf("qT", [128, 8, T], BF16, gran=1)
        kT = k.sbuf("kT", [128, 2, T], BF16, gran=1)
        vtok = k.sbuf("vtok", [128, NCH, 256], BF16, gran=1)
        kcT = k.sbuf("kcT", [128, 2, 512], BF16)
        vc = k.sbuf("vc", [128, 4, 256], BF16)
        skr = k.sbuf("skr", [1, 8, 128], BF16)
        band = k.sbuf("band", [128, 2, 512], BF16)
        scA = k.scope(); scA.__enter__()
        ropeC = k.sbuf("ropeC", [128, T], F32); ropeS = k.sbuf("ropeS", [128, T], F32)
        k.dma("sp", ropeC[:, :], V(rope.t[0], rope[:].keys)); k.dma("sp", ropeS[:, :], V(rope.t[1], rope[:].keys))
        rotf = k.sbuf("rotf", [128, 128], F32); rotb = k.sbuf("rotb", [128, 128], BF16)
        k.dma("sp", rotf[:, :], crot[:, :]); k.copy(rotb[:, :], rotf[:, :], eng="dve")
        wq = [k.sbuf("wq%d" % i, [128, 8, 128], BF16) for i in range(2)]
        qs = [k.sbuf("q_s%d" % i, [128, 512], BF16) for i in range(2)]
        t1 = [k.sbuf("q_t1%d" % i, [128, 512], F32) for i in range(2)]
        t2 = [k.sbuf("q_t2%d" % i, [128, 512], F32) for i in range(2)]
        nq = 0
        k.dma("pool", wq[0][:, :, :], V(winv[:, :, 2048:2048 + 128], win[:].keys))
        for hh in range(10):
            w = wq[hh % 2]
            if hh + 1 < 10:
                c1 = 2048 + (hh + 1) * 128
                k.dma("pool", wq[(hh + 1) % 2][:, :, :], V(winv[:, :, c1:c1 + 128], win[:].keys))
            for tt in range(NTT):
                sl = slice(tt * 512, (tt + 1) * 512)
                bk = k.banks[nq % 2]; bk2 = k.banks[2 + nq % 2]
                for kc in range(8):
                    k.mm(bk[:, :], w[:, kc, :], k.hT[kc][:, PAD + tt * 512:PAD + (tt + 1) * 512], start=(kc == 0), stop=(kc == 7))
                q_ = qs[nq % 2]
                k.copy(q_[:, :], bk[:, :], eng="act")
                k.mm(bk2[:, :], rotb[:, :], q_[:, :])
                a = t1[nq % 2]; b = t2[nq % 2]
                k.tt(a[:, :], q_[:, :], ropeC[:, sl], ALU.mult, eng="pool")
                k.tt(b[:, :], bk2[:, :], ropeS[:, sl], ALU.mult)
                dst = V(qT.t[:, hh, sl], qT[:, hh].keys) if hh < 8 else V(kT.t[:, hh - 8, sl], kT[:, hh - 8].keys)
                k.tt(dst, a[:, :], b[:, :], ALU.add)
                nq += 1
        wkv = k.sbuf("wkv", [128, 8, 512], BF16)
        k.dma("pool", wkv[:, :, :], V(winv[:, :, 3072:3584], win[:].keys))
        kvo = [k.sbuf("kvo%d" % i, [128, 512], F32) for i in range(2)]
        for c in range(NCH):
            bk = k.banks[c % 2]
            for kc in range(8):
                k.mm(bk[:, :], k.hT[kc][:, PAD + c * 128:PAD + (c + 1) * 128], wkv[:, kc, :], start=(kc == 0), stop=(kc == 7))
            o = kvo[c % 2]
            k.copy(o[:, :], bk[:, :], eng="act")
            k.copy(V(vtok.t[:, c, :], vtok[:, c].keys), o[:, 256:512], eng="dve")
            k.dma("sp", V(nk.t[c * 128:(c + 1) * 128, :], nk[:].keys), o[:, 0:256], is_output=True)
            k.dma("sp", V(nv.t[c * 128:(c + 1) * 128, :], nv[:].keys), o[:, 256:512], is_output=True)
        kcs = k.sbuf("kcs", [128, 4, 256], F32)
        k.dma("sp", kcs[:, :, :], V(ck.t.rearrange("(c p) d -> p c d", p=128), ck[:].keys))
        for kvh in range(2):
            bk = k.banks[kvh]
            for sc_ in range(4):
                k.transpose(bk[:, sc_ * 128:(sc_ + 1) * 128], kcs[:, sc_, kvh * 128:(kvh + 1) * 128], k.ident_f[:, :])
            k.copy(kcT[:, kvh, :], bk[:, :], eng="dve")
        k.dma("pool", vc[:, :, :], V(cv.t.rearrange("(c p) d -> p c d", p=128), cv[:].keys))
        skf = k.sbuf("skf", [1, 8], F32); ske = k.sbuf("ske", [1, 8], F32)
        k.dma("sp", skf[:, :], V(sink.t.rearrange("(o h) -> o h", o=1), sink[:].keys))
        k.act(ske[:, :], skf[:, :], AF.Exp)
        k.copy(skr[:, :, :], V(ske.t[:, :].unsqueeze(2).to_broadcast([1, 8, 128]), ske[:].keys), eng="dve")
        bandf = k.sbuf("bandf", [128, 2, 512], F32)
        k.dma("sp", bandf[:, :, :], V(cband.t.rearrange("a p n -> p a n"), cband[:].keys))
        k.copy(band[:, :, :], bandf[:, :, :], eng="dve")
        scA.__exit__(None, None, None)
        pT = [k.sbuf("pT%d" % i, [128, 512], BF16) for i in range(3)]
        mk = [k.sbuf("mk%d" % i, [128, 512], BF16) for i in range(2)]
        rden = [k.sbuf("rden%d" % i, [128, 512], F32) for i in range(2)]
        scale = 128.0 ** -0.5
        npb = 0; nmk = 0; nu = 0
        for c in range(NCH):
            for kvh in range(2):
                blocks = []
                if c > 0: blocks.append(("prev", c - 1))
                blocks.append(("same", c))
                if c < NCH - 1: blocks.append(("next", c + 1))
                for s4 in range(4): blocks.append(("ctx", s4))
                po = k.banks[3 + nu % 2]; pd = k.banks[5 + nu % 2]
                rhs_q = V(qT.t[:, kvh * 4:kvh * 4 + 4, c * 128:(c + 1) * 128], [(qT.id, kvh * 4 + i) for i in range(4)])
                for bi, (kind, idx) in enumerate(blocks):
                    ps = k.banks[npb % 3]
                    p_ = pT[npb % 3]; npb += 1
                    if kind == "ctx":
                        k.mm(V(ps.t[:, :].rearrange("p (h q) -> p h q", h=4), ps[:].keys), kcT[:, kvh, idx * 128:(idx + 1) * 128], rhs_q)
                        k.act(p_[:, :], ps[:, :], AF.Exp, bias=k.flags[:, 112:113], scale=scale)
                        lv = vc[:, idx, kvh * 128:(kvh + 1) * 128]
                    else:
                        k.mm(V(ps.t[:, :].rearrange("p (h q) -> p h q", h=4), ps[:].keys), V(kT.t[:, kvh, idx * 128:(idx + 1) * 128], kT[:, kvh].keys), rhs_q)
                        k.act(p_[:, :], ps[:, :], AF.Exp, scale=scale)
                        if kind != "same":
                            m = mk[nmk % 2]; nmk += 1
                            bsel = 0 if kind == "prev" else 1
                            f0 = 48 + (0 if kind == "prev" else 32) + c
                            k.ts(m[:, :], band[:, bsel, :], k.flags[:, f0:f0 + 1], ALU.mult, k.flags[:, f0 + 16:f0 + 17], ALU.add, eng="pool")
                            k.tt(p_[:, :], p_[:, :], m[:, :], ALU.mult)
                        lv = V(vtok.t[:, idx, kvh * 128:(kvh + 1) * 128], vtok[:, idx].keys)
                    k.mm(po[:, :], lv, p_[:, :], start=(bi == 0), stop=(bi == len(blocks) - 1))
                    k.mm(pd[:, :], k.ones_b[:, :], p_[:, :], start=(bi == 0), stop=False)
                k.mm(pd[:, :], k.ones_b[0:1, :], V(skr.t[0:1, kvh * 4:kvh * 4 + 4, :].rearrange("o h q -> o (h q)"), skr[:].keys), start=False, stop=True)
                rd = rden[nu % 2]
                k.act(rd[:, :], pd[:, :], AF.Ln)
                k.act(rd[:, :], rd[:, :], AF.Exp, scale=-1.0)
                mo = V(k.mixedT.t[:, 8 + kvh * 4:8 + kvh * 4 + 4, c * 128:(c + 1) * 128], [(k.mixedT.id, 8 + kvh * 4 + i) for i in range(4)])
                k.tt(mo, V(po.t[:, :].rearrange("p (h q) -> p h q", h=4), po[:].keys), V(rd.t[:, :].rearrange("p (h q) -> p h q", h=4), rd[:].keys), ALU.mult)
                nu += 1
        sc.__exit__(None, None, None)
        k.outproj_post("mix_out_l1", cols)
        sc0.__exit__(None, None, None)

    def pc_prefetch(self, win, winv, col0):
        k = self
        w = k.pc_w[k.pc_nw % len(k.pc_w)]; k.pc_nw += 1
        k.dma("pool", w[:, :, :], V(winv[:, :, col0:col0 + 128], win[:].keys))
        return w

    def proj_conv5(self, win, winv, col0, cw5, cb, dst_fn, w=None):
        k = self
        if w is None:
            w = k.pc_prefetch(win, winv, col0)
        dg = k.pc_dg[k.pc_n % 2]
        for tap in range(5):
            k.act(dg[:, tap, :], k.ident_b[:, :], AF.Copy, scale=cw5(tap))
        k.pc_n += 1
        pending = None

        def conv_stage(item):
            seg, u, pc = item
            for tap in range(5):
                k.mm(pc[:, 0:256], dg[:, tap, :], u[:, tap:tap + 256], start=(tap == 0), stop=(tap == 4))
            if cb is not None:
                k.act(dst_fn(seg), pc[:, 0:256], AF.Silu, bias=cb)
            else:
                k.act(dst_fn(seg), pc[:, 0:256], AF.Silu)

        for seg in range(NSEG):
            pb = k.banks[k.pc_m % 2]; pc = k.banks[2 + k.pc_m % 2]
            u = k.pc_u[k.pc_m % 2]; k.pc_m += 1
            c_lo = seg * 256
            for kc in range(8):
                k.mm(pb[:, 0:260], w[:, kc, :], k.hT[kc][:, c_lo:c_lo + 260], start=(kc == 0), stop=(kc == 7))
            k.copy(u[:, 0:260], pb[:, 0:260], eng="act")
            k.ts(u[:, 0:2], u[:, 0:2], k.flags[:, 2 * seg:2 * seg + 1], ALU.mult)
            k.ts(u[:, 258:260], u[:, 258:260], k.flags[:, 2 * seg + 1:2 * seg + 2], ALU.mult)
            if pending is not None:
                conv_stage(pending)
            pending = (seg, u, pc)
        conv_stage(pending)

    def pc_alloc(self):
        k = self
        k.pc_w = [k.sbuf("pc_w%d" % i, [128, 8, 128], BF16) for i in range(4)]
        k.pc_nw = 0
        k.pc_dg = [k.sbuf("pc_dg%d" % i, [128, 5, 128], BF16) for i in range(2)]
        k.pc_u = [k.sbuf("pc_u%d" % i, [128, 260], BF16) for i in range(2)]
        k.pc_n = 0; k.pc_m = 0

    def mixer_l0(self, cols):
        k = self
        win = k.din("mix_in_l0", [D, L0_PROJ])
        winv = win.t.rearrange("(c p) n -> p c n", p=128)
        k.prenorm(cols, 0, 1)
        sc0 = k.scope(); sc0.__enter__()
        k.mixedT = k.sbuf("mixedT", [128, 16, T], BF16, gran=1)
        k.l0_consts()
        k.dtab = {nm: k.sbuf("dn_" + nm, [128, NCH, 2, 8], F32) for nm in ("av", "ncs", "ecs", "necs", "dte", "etot")}
        k.beta = k.sbuf("beta", [128, NCH, 16], F32)
        tri = k.din("cst_tri", [2, 128, 128])
        k.tri = k.sbuf("tri", [128, 2, 128], F32)
        k.dma("sp", k.tri[:, :, :], V(tri.t.rearrange("a p n -> p a n"), tri[:].keys))
        import os
        part = os.environ.get("L0_PART", "both")
        s1 = k.scope(); s1.__enter__()
        k.l0_small(win, winv)
        if part in ("both", "ssd"):
            sc = k.scope(); sc.__enter__()
            k.ssd(win, winv)
            sc.__exit__(None, None, None)
        else:
            for fc in range(8):
                k.memset(V(k.mixedT.t[:, fc, :], k.mixedT[:, fc].keys), 0.0, eng="pool")
        s1.__exit__(None, None, None)
        if part in ("both", "dn"):
            sc = k.scope(); sc.__enter__()
            k.dn(win, winv)
            sc.__exit__(None, None, None)
        else:
            for fc in range(8, 16):
                k.memset(V(k.mixedT.t[:, fc, :], k.mixedT[:, fc].keys), 0.0, eng="pool")
        k.outproj_post("mix_out_l0", cols)
        sc0.__exit__(None, None, None)

    def l0_small(self, win, winv):
        k = self
        dtb = k.din("ssd_dt_bias", [2, 16]); alog = k.din("ssd_A_log", [2, 16])
        ddtb = k.din("dn_dt_bias", [2, 8]); dalog = k.din("dn_A_log", [2, 8])
        k.dt_t = k.sbuf("dt_t", [128, NCH, 32], F32)
        k.av = k.sbuf("av", [128, NCH, 2, 24], F32)
        k.cs = k.sbuf("cs", [128, NCH, 2, 24], F32)
        k.ncs = k.sbuf("ncs", [128, NCH, 2, 24], F32)
        k.ecs = k.sbuf("ecs", [128, NCH, 2, 24], F32)
        k.necs = k.sbuf("necs", [128, NCH, 2, 24], F32)
        k.dte = k.sbuf("dte", [128, NCH, 2, 24], F32)
        k.etot = k.sbuf("etot", [128, NCH, 2, 24], F32)
        sc = k.scope(); sc.__enter__()
        wsm = k.sbuf("wsm", [128, 8, 64], BF16)
        k.dma("pool", V(wsm.t[:, :, 0:32], wsm[:].keys), V(winv[:, :, 3072:3104], win[:].keys))
        k.dma("pool", V(wsm.t[:, :, 32:64], wsm[:].keys), V(winv[:, :, 7200:7232], win[:].keys))
        sm = k.sbuf("sm", [128, NCH, 64], F32)
        for c in range(NCH):
            bk = k.banks[c % 2]
            for kc in range(8):
                k.mm(bk[:, 0:64], k.hT[kc][:, PAD + c * 128:PAD + (c + 1) * 128], wsm[:, kc, :], start=(kc == 0), stop=(kc == 7))
            k.copy(V(sm.t[:, c, :], sm[:].keys), bk[:, 0:64], eng="act" if c % 2 == 0 else "dve")
        bias48 = k.sbuf("bias48", [128, 48], F32); al48 = k.sbuf("al48", [128, 48], F32)
        k.dma("sp", bias48[:, 0:32], V(dtb.t.rearrange("a h -> (a h)").partition_broadcast(128), dtb[:].keys))
        k.dma("sp", bias48[:, 32:48], V(ddtb.t.rearrange("a h -> (a h)").partition_broadcast(128), ddtb[:].keys))
        k.dma("sp", al48[:, 0:32], V(alog.t.rearrange("a h -> (a h)").partition_broadcast(128), alog[:].keys))
        k.dma("sp", al48[:, 32:48], V(dalog.t.rearrange("a h -> (a h)").partition_broadcast(128), dalog[:].keys))
        nega = k.sbuf("nega", [128, 48], F32)
        k.act(nega[:, :], al48[:, :], AF.Exp)
        k.ts(nega[:, :], nega[:, :], -1.0, ALU.mult)
        sp_ = k.sbuf("sp_", [128, NCH, 48], F32)
        bb = V(bias48.t[:, :].unsqueeze(1).to_broadcast([128, NCH, 48]), bias48[:].keys)
        k.tt(sp_[:, :, :], V(sm.t[:, :, 0:48], sm[:].keys), bb, ALU.add)
        k.act(sp_[:, :, :], sp_[:, :, :], AF.Exp)
        k.ts(sp_[:, :, :], sp_[:, :, :], 1.0, ALU.add)
        k.act(sp_[:, :, :], sp_[:, :, :], AF.Ln)
        k.copy(k.dt_t[:, :, :], V(sp_.t[:, :, 0:32], sp_[:].keys), eng="dve")
        k.act(k.beta[:, :, :], V(sm.t[:, :, 48:64], sm[:].keys), AF.Sigmoid)
        nb = V(nega.t[:, :].unsqueeze(1).to_broadcast([128, NCH, 48]), nega[:].keys)
        k.tt(sp_[:, :, :], sp_[:, :, :], nb, ALU.mult)
        for d in range(2):
            k.copy(V(k.av.t[:, :, d, 0:16], k.av[:].keys), V(sp_.t[:, :, d * 16:(d + 1) * 16], sp_[:].keys), eng="dve")
            k.copy(V(k.av.t[:, :, d, 16:24], k.av[:].keys), V(sp_.t[:, :, 32 + d * 8:32 + (d + 1) * 8], sp_[:].keys), eng="dve")
        tot = k.sbuf("tot", [128, NCH, 2, 24], F32)
        for d in range(2):
            bk = k.banks[d]; bk2 = k.banks[2 + d]
            rhs = V(k.av.t[:, :, d, :], k.av[:].keys)
            k.mm(V(bk.t[:, 0:384].rearrange("p (c n) -> p c n", c=NCH), bk[:].keys), k.tri[:, d, :], rhs)
            k.mm(V(bk2.t[:, 0:384].rearrange("p (c n) -> p c n", c=NCH), bk2[:].keys), k.ones_f[:, :], rhs)
            k.copy(V(k.cs.t[:, :, d, :], k.cs[:].keys), V(bk.t[:, 0:384].rearrange("p (c n) -> p c n", c=NCH), bk[:].keys), eng="dve")
            k.copy(V(tot.t[:, :, d, :], tot[:].keys), V(bk2.t[:, 0:384].rearrange("p (c n) -> p c n", c=NCH), bk2[:].keys), eng="dve")
        k.ts(k.ncs[:, :, :, :], k.cs[:, :, :, :], -1.0, ALU.mult)
        k.act(k.ecs[:, :, :, :], k.cs[:, :, :, :], AF.Exp)
        k.ts(k.necs[:, :, :, :], k.ecs[:, :, :, :], -1.0, ALU.mult)
        k.act(k.etot[:, :, :, :], tot[:, :, :, :], AF.Exp)
        k.tt(tot[:, :, :, :], tot[:, :, :, :], k.cs[:, :, :, :], ALU.subtract)
        k.act(k.dte[:, :, :, :], tot[:, :, :, :], AF.Exp)
        for nm, src in (("av", k.av), ("ncs", k.ncs), ("ecs", k.ecs), ("necs", k.necs), ("dte", k.dte), ("etot", k.etot)):
            k.copy(k.dtab[nm][:, :, :, :], V(src.t[:, :, :, 16:24], src[:].keys), eng="pool")
        sc.__exit__(None, None, None)

    def build_Lt(self, lt, ps, d, acol, ncol):
        k = self
        abc = V(acol.ap.to_broadcast([128, 128]), acol.keys)
        k.mm(ps, abc, k.tri[:, d, :], start=True, stop=False)
        k.mm(ps, k.ident_b[:, :], k.mbias[:, d, :], start=False, stop=True)
        k.act(lt, ps, AF.Exp, bias=ncol)

    def l0_consts(self):
        k = self
        mb = k.din("cst_mbias", [2, 128, 128]); st = k.din("cst_strict", [2, 128, 128]); blk = k.din("cst_blk", [4, 128, 128])
        k.mbias = k.sbuf("mbias", [128, 2, 128], BF16)
        k.strict = k.sbuf("strict", [128, 2, 128], F32)
        k.blk = k.sbuf("blk", [128, 4, 128], F32)
        k.dma("pool", k.mbias[:, :, :], V(mb.t.rearrange("a p n -> p a n"), mb[:].keys))
        k.blk_b = k.sbuf("blk_b", [128, 4, 128], BF16)
        k.dma("pool", k.blk_b[:, :, :], V(blk.t.rearrange("a p n -> p a n"), blk[:].keys))
        k.dma("sp", k.strict[:, :, :], V(st.t.rearrange("a p n -> p a n"), st[:].keys))
        k.dma("sp", k.blk[:, :, :], V(blk.t.rearrange("a p n -> p a n"), blk[:].keys))

    def ssd(self, win, winv):
        k = self
        cwd = k.din("ssd_conv_w", [5, 2048]); cbd = k.din("ssd_conv_b", [2048])
        dD = k.din("ssd_D", [16]); nwd = k.din("ssd_norm_w", [1024])
        h0d = k.din("ssd_h0", [2, 16, 64, 128])
        hout = k.dout("ssd_out", [NSEG, 2, 16, 64, 128])
        k.pc_alloc()
        cw = k.sbuf("s_cw", [128, 16, 5], F32); cb = k.sbuf("s_cb", [128, 16], F32)
        for tap in range(5):
            k.load_cols(V(cw.t[:, :, tap], cw[:].keys), cwd.t[tap, :].rearrange("(c p) -> c p", p=128), cwd[:].keys, 16)
        k.load_cols(cb[:, :], cbd.t.rearrange("(c p) -> c p", p=128), cbd[:].keys, 16)
        Dbc = k.sbuf("Dbc", [128, 16], F32)
        k.dma("sp", Dbc[:, :], V(dD.t.partition_broadcast(128), dD[:].keys))
        nwc = k.sbuf("s_nw", [128, 8], F32)
        k.load_cols(nwc[:, :], nwd.t.rearrange("(c p) -> c p", p=128), nwd[:].keys, 8)
        scI = k.scope(); scI.__enter__()
        BT = k.sbuf("s_BT", [128, T], BF16); CT = k.sbuf("s_CT", [128, T], BF16)
        xtok = k.sbuf("s_xtok", [128, NCH, 256], BF16, gran=1)
        Btok = k.sbuf("s_Btok", [128, NCH, 128], BF16, gran=1)
        sz = k.sbuf("s_sz", [128, NCH, 256], BF16, gran=1)
        yacc = k.sbuf("s_yacc", [128, NCH, 256], BF16, gran=1)
        hm = [k.sbuf("s_hm%d" % d, [128, 256], F32) for d in range(2)]
        hb = [k.sbuf("s_hb%d" % d, [128, 256], BF16) for d in range(2)]
        bfb = [k.psum_bf(i) for i in range(2)]
        n = 0
        for g in range(4):
            scA = k.scope(); scA.__enter__()
            xT = k.sbuf("s_xT", [128, 2, T], BF16, gran=1)
            wz = [k.sbuf("s_wz%d" % i, [128, 8, 256], BF16) for i in range(1)]
            chB = 8 + g; chC = 12 + g
            pw = [k.pc_prefetch(win, winv, 1024 + ch_ * 128) for ch_ in (2 * g, 2 * g + 1, chB, chC)]
            w = wz[0]
            k.dma("pool", w[:, :, :], V(winv[:, :, g * 256:(g + 1) * 256], win[:].keys))
            for q in range(2):
                ch = 2 * g + q
                k.proj_conv5(win, winv, 1024 + ch * 128, lambda tap, ch=ch: cw[:, ch, tap:tap + 1], cb[:, ch:ch + 1],
                             lambda seg, q=q: V(xT.t[:, q, seg * 256:(seg + 1) * 256], xT[:, q].keys), w=pw[q])
            k.proj_conv5(win, winv, 1024 + chB * 128, lambda tap: cw[:, chB, tap:tap + 1], cb[:, chB:chB + 1], lambda seg: BT[:, seg * 256:(seg + 1) * 256], w=pw[2])
            k.proj_conv5(win, winv, 1024 + chC * 128, lambda tap: cw[:, chC, tap:tap + 1], cb[:, chC:chC + 1], lambda seg: CT[:, seg * 256:(seg + 1) * 256], w=pw[3])
            for c in range(NCH):
                cs_ = slice(c * 128, (c + 1) * 128)
                bk = k.banks[4 + c % 2]
                for kc in range(8):
                    k.mm(bk[:, 0:256], k.hT[kc][:, PAD + c * 128:PAD + (c + 1) * 128], w[:, kc, :], start=(kc == 0), stop=(kc == 7))
                k.act(V(sz.t[:, c, :], sz[:, c].keys), bk[:, 0:256], AF.Silu)
                pb = bfb[c % 2]
                for q in range(2):
                    k.transpose(pb[:, q * 128:(q + 1) * 128], V(xT.t[:, q, cs_], xT[:, q].keys), k.ident_b[:, :])
                k.transpose(pb[:, 256:384], BT[:, cs_], k.ident_b[:, :])
                k.copy(V(xtok.t[:, c, :], xtok[:, c].keys), pb[:, 0:256], eng="dve")
                k.copy(V(Btok.t[:, c, :], Btok[:, c].keys), pb[:, 256:384], eng="dve")
            scA.__exit__(None, None, None)
            scB = k.scope(); scB.__enter__()
            Gs = [k.sbuf("s_G%d" % i, [128, 128], BF16) for i in range(2)]
            Lt = [k.sbuf("s_Lt%d" % i, [128, 4, 128], BF16) for i in range(2)]
            St = [k.sbuf("s_St%d" % i, [128, 4, 128], BF16) for i in range(2)]
            xdt = [k.sbuf("s_xdt%d" % i, [128, 256], BF16) for i in range(2)]
            xde = [k.sbuf("s_xde%d" % i, [128, 256], BF16) for i in range(2)]
            xD = [k.sbuf("s_xD%d" % i, [128, 256], BF16) for i in range(2)]
            htmp = [k.sbuf("s_ht%d" % i, [128, 256], F32) for i in range(2)]
            hstage = [k.sbuf("s_hs%d" % i, [128, 2, 128], F32) for i in range(2)]
            ytmp = [k.sbuf("s_yt%d" % i, [128, 256], F32) for i in range(2)]
            yz = [k.sbuf("s_yz%d" % i, [128, 256], BF16) for i in range(2)]
            for d in range(2):
                hs = hstage[d]
                k.dma("sp", hs[:, :, :], V(h0d.t[d, 4 * g:4 * g + 4].rearrange("(a h) p n -> (h p) a n", a=2), h0d[:].keys))
                bk = k.banks[6]
                for a in range(2):
                    k.transpose(bk[:, a * 128:(a + 1) * 128], hs[:, a, :], k.ident_f[:, :])
                k.copy(hm[d][:, :], bk[:, 0:256], eng="dve")
                k.copy(hb[d][:, :], hm[d][:, :], eng="act")
            for step in range(NCH):
                for d in range(2):
                    c = step if d == 0 else NCH - 1 - step
                    second = (d == 0 and c >= 8) or (d == 1 and c < 8)
                    cs_ = slice(c * 128, (c + 1) * 128)
                    i2 = d
                    bg = k.banks[0]
                    k.mm(bg[:, 0:128], BT[:, cs_], CT[:, cs_])
                    k.copy(Gs[i2][:, :], bg[:, 0:128], eng="act")
                    bl = k.banks[1 + i2]
                    for hh in range(4):
                        acol = V(k.av.t[:, c, d, 4 * g + hh:4 * g + hh + 1], k.av[:].keys)
                        abc = V(acol.ap.to_broadcast([128, 128]), acol.keys)
                        k.mm(bl[:, hh * 128:(hh + 1) * 128], abc, k.tri[:, d, :], start=True, stop=False)
                        k.mm(bl[:, hh * 128:(hh + 1) * 128], k.ident_b[:, :], k.mbias[:, d, :], start=False, stop=True)
                    for hh in range(4):
                        k.act(V(Lt[i2].t[:, hh, :], Lt[i2][:].keys), bl[:, hh * 128:(hh + 1) * 128], AF.Exp,
                              bias=V(k.ncs.t[:, c, d, 4 * g + hh:4 * g + hh + 1], k.ncs[:].keys))
                    k.tt(St[i2][:, :, :], Lt[i2][:, :, :], V(Gs[i2].t[:, :].unsqueeze(1).to_broadcast([128, 4, 128]), Gs[i2][:].keys), ALU.mult)
                    dtb_ = V(k.dt_t.t[:, c, d * 16 + 4 * g:d * 16 + 4 * g + 4].unsqueeze(2).to_broadcast([128, 4, 64]), k.dt_t[:].keys)
                    dte_ = V(k.dte.t[:, c, d, 4 * g:4 * g + 4].unsqueeze(2).to_broadcast([128, 4, 64]), k.dte[:].keys)
                    ecs_ = V(k.ecs.t[:, c, d, 4 * g:4 * g + 4].unsqueeze(2).to_broadcast([128, 4, 64]), k.ecs[:].keys)
                    eto_ = V(k.etot.t[:, c, d, 4 * g:4 * g + 4].unsqueeze(2).to_broadcast([128, 4, 64]), k.etot[:].keys)
                    x3 = V(xtok.t[:, c, :].rearrange("p (h q) -> p h q", h=4), xtok[:, c].keys)
                    v3 = lambda b: V(b.t[:, :].rearrange("p (h q) -> p h q", h=4), b[:].keys)
                    k.tt(v3(xdt[i2]), x3, dtb_, ALU.mult)
                    k.tt(v3(xde[i2]), v3(xdt[i2]), dte_, ALU.mult, eng="pool")
                    yd = k.banks[3 + 2 * d]; yo = SubBuf(k.banks[4 + 2 * d], 0, 256); ps = SubBuf(k.banks[4 + 2 * d], 256, 256)
                    if second:
                        Db = V(Dbc.t[:, 4 * g:4 * g + 4].unsqueeze(2).to_broadcast([128, 4, 64]), Dbc[:].keys)
                        k.tt(v3(xD[i2]), x3, Db, ALU.mult, eng="pool")
                    for hh in range(4):
                        k.mm(yd[:, hh * 64:(hh + 1) * 64], V(St[i2].t[:, hh, :], St[i2][:].keys), xdt[i2][:, hh * 64:(hh + 1) * 64],
                             start=True, stop=(not second))
                        if second:
                            k.mm(yd[:, hh * 64:(hh + 1) * 64], k.ident_b[:, :], xD[i2][:, hh * 64:(hh + 1) * 64], start=False, stop=True)
                    k.mm(yo[:, 0:256], CT[:, cs_], hb[d][:, :])
                    yt = ytmp[i2]
                    k.tt(v3(yt), V(yo.t[:, 0:256].rearrange("p (h q) -> p h q", h=4), yo[:].keys), ecs_, ALU.mult)
                    ya = V(yacc.t[:, c, :], yacc[:, c].keys)
                    if not second:
                        k.tt(ya, yd[:, 0:256], yt[:, :], ALU.add)
                    else:
                        k.tt(yt[:, :], yd[:, 0:256], yt[:, :], ALU.add)
                        k.tt(yt[:, :], yt[:, :], ya, ALU.add, eng="pool")
                        k.tt(yz[i2][:, :], yt[:, :], V(sz.t[:, c, :], sz[:, c].keys), ALU.mult)
                        pb = bfb[i2]
                        for q in range(2):
                            k.transpose(pb[:, q * 128:(q + 1) * 128], yz[i2][:, q * 128:(q + 1) * 128], k.ident_b[:, :])
                        mo = V(k.mixedT.t[:, 2 * g:2 * g + 2, cs_], [(k.mixedT.id, 2 * g), (k.mixedT.id, 2 * g + 1)])
                        k.copy(mo, V(pb.t[:, 0:256].rearrange("p (q t) -> p q t", q=2), pb[:].keys), eng="act")
                    k.mm(ps[:, 0:256], V(Btok.t[:, c, :], Btok[:, c].keys), xde[i2][:, :])
                    ht = htmp[i2]
                    k.tt(v3(ht), v3(hm[d]), eto_, ALU.mult, eng="pool")
                    k.tt(hm[d][:, :], ps[:, 0:256], ht[:, :], ALU.add)
                    last = (c % 2 == 1) if d == 0 else (c % 2 == 0)
                    if last:
                        seg = c // 2
                        bk = k.banks[6]
                        hs2 = hstage[i2]
                        for a in range(2):
                            k.transpose(bk[:, a * 128:(a + 1) * 128], hm[d][:, a * 128:(a + 1) * 128], k.ident_f[:, :])
                        k.copy(V(hs2.t[:, :, :], hs2[:].keys), V(bk.t[:, 0:256].rearrange("p (a n) -> p a n", a=2), bk[:].keys), eng="act")
                        k.dma("sp", V(hout.t[seg, d, 4 * g:4 * g + 4].rearrange("(a h) p n -> (h p) a n", a=2), hout[:].keys), hs2[:, :, :], is_output=True)
                    cn = c + 1 if d == 0 else c - 1
                    if 0 <= cn < NCH:
                        fcol = (16 if d == 0 else 32) + cn
                        k.ts(hm[d][:, :], hm[d][:, :], k.flags[:, fcol:fcol + 1], ALU.mult)
                        k.copy(hb[d][:, :], hm[d][:, :], eng="act")
            scB.__exit__(None, None, None)
        scI.__exit__(None, None, None)
        sq = [k.sbuf("s_sq%d" % i, [128, 512], BF16) for i in range(2)]
        rr = k.sbuf("s_rr", [128, 512], F32); rt = k.sbuf("s_rt", [128, 512], F32)
        for tt in range(NTT):
            sl = slice(tt * 512, (tt + 1) * 512)
            ss = k.banks[tt % 2]
            for fc in range(8):
                s_ = sq[fc % 2]
                k.act(s_[:, :], V(k.mixedT.t[:, fc, sl], k.mixedT[:, fc].keys), AF.Square)
                k.mm(ss[:, :], k.ones_b[:, :], s_[:, :], start=(fc == 0), stop=(fc == 7))
            k.rstd_from(rr[:, :], ss[:, :], 1024.0, rt[:, :])
            for fc in range(8):
                mv_ = V(k.mixedT.t[:, fc, sl], k.mixedT[:, fc].keys)
                k.stt(mv_, mv_, nwc[:, fc:fc + 1], rr[:, :], ALU.mult, ALU.mult)

    def dn(self, win, winv):
        k = self
        cwd = k.din("dn_conv_w", [5, 3072]); nwd = k.din("dn_norm_w", [128])
        s0d = k.din("dn_h0", [2, 8, 128, 128])
        sout = k.dout("dn_out", [NSEG, 2, 8, 128, 128])
        cw = k.sbuf("d_cw", [128, 24, 5], F32)
        for tap in range(5):
            k.load_cols(V(cw.t[:, :, tap], cw[:].keys), cwd.t[tap, :].rearrange("(c p) -> c p", p=128), cwd[:].keys, 24)
        nwb = k.sbuf("d_nwb", [128, 128], F32)
        k.dma("sp", nwb[:, :], V(nwd.t.partition_broadcast(128), nwd[:].keys))
        lnsc = k.sbuf("d_lnsc", [128, 1], F32)
        k.memset(lnsc[:, :], -0.5 * float(np.log(128.0)), eng="pool")
        zero1 = k.sbuf("d_zero1", [128, 1], F32)
        k.memset(zero1[:, :], 0.0, eng="pool")
        qT = k.sbuf("d_qT", [128, T], BF16, gran=128); kT = k.sbuf("d_kT", [128, T], BF16, gran=128); vT = k.sbuf("d_vT", [128, T], BF16, gran=128)
        khtok = k.sbuf("d_khtok", [128, NCH, 128], BF16, gran=1); vtok = k.sbuf("d_vtok", [128, NCH, 128], BF16, gran=1)
        oacc = k.sbuf("d_oacc", [128, NCH, 128], F32, gran=1)
        Wall = k.sbuf("d_Wall", [128, 2 * NCH, 128], BF16, gran=1)
        QKall = k.sbuf("d_QKall", [128, 2 * NCH, 128], BF16, gran=1)
        wg = [k.sbuf("d_wg%d" % i, [128, 8, 128], BF16) for i in range(1)] * 2
        import os
        IDT = BF16 if os.environ.get("DN_INV", "bf16") == "bf16" else F32
        G = 4
        NT = 5
        tmpf = [k.sbuf("d_tf%d" % i, [128, 128], F32) for i in range(NT)]
        tmpb = [k.sbuf("d_tb%d" % i, [128, 128], BF16) for i in range(10)]
        Sm = [k.sbuf("d_S%d" % d, [128, 128], F32) for d in range(2)]
        Sb = [k.sbuf("d_Sb%d" % d, [128, 128], BF16) for d in range(2)]
        fin16 = k.sbuf("d_fin16", [128, 16], F32); fin16b = k.sbuf("d_fin16b", [128, 16], F32)
        pbf = k.psum_bf(1)
        st = {"bank": 0, "tf": 0, "tb": 0, "ev": 0, "lt": 0}
        T_ = k.dtab

        def nbank():
            b = k.banks[st["bank"] % 7]; st["bank"] += 1
            return b

        def tf():
            t = tmpf[st["tf"] % NT]; st["tf"] += 1
            return t

        def tb():
            t = tmpb[st["tb"] % 10]; st["tb"] += 1
            return t

        def evac(dst, src, scale=None):
            st["ev"] += 1
            if scale is not None:
                k.act(dst, src, AF.Copy, scale=scale)
            elif st["ev"] % 5 != 0:
                k.copy(dst, src, eng="act")
            else:
                k.copy(dst, src, eng="dve")

        I_ = k.ident_b if IDT == BF16 else k.ident_f

        def mm1(lhsT, rhs, dst):
            b = nbank()
            k.mm(b[:, 0:128], lhsT, rhs)
            evac(dst, b[:, 0:128])
            return dst

        def mmadd(lhsT, rhs, sb, dst, op=ALU.add, first_sb=False):
            b = nbank()
            k.mm(b[:, 0:128], lhsT, rhs)
            if first_sb:
                k.tt(dst, sb, b[:, 0:128], op)
            else:
                k.tt(dst, b[:, 0:128], sb, op)
            return dst

        F_ = lambda Tq: Tq[:, :, :]
        Q_ = lambda Tq, q: V(Tq.t[:, q, :], Tq[:].keys)
        bc_ = lambda v2: V(v2.ap.unsqueeze(1).to_broadcast([128, 4, 128]), v2.keys)
        bank3 = lambda b: V(b.t[:, :].rearrange("p (q i) -> p q i", q=4), b[:].keys)
        opq = lambda x, q: Q_(x, q) if isinstance(x, Buf) else x

        def mmq(lhsT, rhs, dst):
            b = nbank()
            for q in range(4):
                k.mm(b[:, q * 128:(q + 1) * 128], opq(lhsT, q), opq(rhs, q))
            evac(F_(dst), bank3(b))
            return dst

        def mmaddq(lhsT, rhs, sb, dstv, op=ALU.add, first_sb=False):
            b = nbank()
            for q in range(4):
                k.mm(b[:, q * 128:(q + 1) * 128], opq(lhsT, q), opq(rhs, q))
            if first_sb:
                k.tt(dstv, sb, bank3(b), op)
            else:
                k.tt(dstv, bank3(b), sb, op)

        def quad_gen(h, d, c0, S, LtT):
            u0 = d * NCH + c0
            css = [slice((c0 + q) * 128, (c0 + q + 1) * 128) for q in range(4)]
            U, UT = S[0], S[1]
            s_ = S[2:10]
            Iv = I_[:, :]
            bL = nbank()
            for q in range(4):
                c = c0 + q
                acol = V(T_["av"].t[:, c, d, h:h + 1], T_["av"][:].keys)
                abc = V(acol.ap.to_broadcast([128, 128]), acol.keys)
                k.mm(bL[:, q * 128:(q + 1) * 128], abc, k.tri[:, d, :], start=True, stop=False)
                k.mm(bL[:, q * 128:(q + 1) * 128], k.ident_b[:, :], k.mbias[:, d, :], start=False, stop=True)
            for q in range(4):
                c = c0 + q
                k.act(Q_(LtT, q), bL[:, q * 128:(q + 1) * 128], AF.Exp, bias=V(T_["ncs"].t[:, c, d, h:h + 1], T_["ncs"][:].keys))
            yield
            Ls = s_[7]
            k.tt(F_(Ls), F_(LtT), bc_(k.strict[:, d, :]), ALU.mult, eng="pool")
            bQ = nbank()
            for q in range(4):
                k.mm(bQ[:, q * 128:(q + 1) * 128], kT[:, css[q]], qT[:, css[q]])
            k.tt(V(QKall.t[:, u0:u0 + 4, :], [(QKall.id, u0 + q) for q in range(4)]), bank3(bQ), F_(LtT), ALU.mult)
            yield
            bA = nbank()
            for q in range(4):
                k.mm(bA[:, q * 128:(q + 1) * 128], kT[:, css[q]], kT[:, css[q]])
            for q in range(4):
                c = c0 + q
                beta_ = V(k.beta.t[:, c, d * 8 + h:d * 8 + h + 1], k.beta[:].keys)
                k.stt(Q_(U, q), bA[:, q * 128:(q + 1) * 128], beta_, Q_(Ls, q), ALU.mult, ALU.mult)
            yield
            mmq(U, Iv, UT)
            Ud, UdT, P = s_[0], s_[1], s_[2]
            k.tt(F_(Ud), F_(U), bc_(k.blk_b[:, 0, :]), ALU.mult, eng="pool")
            yield
            k.tt(F_(UdT), F_(UT), bc_(k.blk_b[:, 0, :]), ALU.mult)
            k.tt(F_(P), bc_(Iv), F_(Ud), ALU.subtract)
            yield
            V1 = mmq(UdT, Ud, s_[3]); V1T = mmq(Ud, UdT, s_[4])
            yield
            A1 = s_[5]; k.tt(F_(A1), F_(V1T), bc_(Iv), ALU.add, eng="pool")
            V2 = mmq(V1T, V1, s_[0]); V2T = mmq(V1, V1T, s_[1])
            yield
            P1 = mmq(A1, P, s_[6])
            A3 = s_[3]; mmaddq(V2, V2T, bc_(Iv), F_(A3))
            yield
            A2 = s_[2]; k.tt(F_(A2), F_(V2T), bc_(Iv), ALU.add, eng="pool")
            yield
            P2 = mmq(A2, P1, s_[5])
            yield
            Wd = mmq(A3, P2, s_[4])
            yield
            WdT = s_[6]; mmq(Wd, Iv, WdT)
            slots = {1: (s_[3], s_[7]), 2: (s_[4], s_[6])}
            for lvl in (1, 2, 3):
                B, BT = s_[0], s_[1]
                k.tt(F_(B), F_(U), bc_(k.blk_b[:, lvl, :]), ALU.mult)
                k.tt(F_(BT), F_(UT), bc_(k.blk_b[:, lvl, :]), ALU.mult, eng="pool")
                yield
                Y = mmq(BT, Wd, s_[2])
                if lvl < 3:
                    Yt = mmq(B, WdT, s_[5])
                    yield
                    nW, nWT = slots[lvl]
                    mmaddq(WdT, Y, F_(Wd), F_(nW), op=ALU.subtract, first_sb=True)
                    mmaddq(Wd, Yt, F_(WdT), F_(nWT), op=ALU.subtract, first_sb=True)
                    Wd, WdT = nW, nWT
                    yield
                else:
                    yield
                    mmaddq(WdT, Y, F_(Wd), V(Wall.t[:, u0:u0 + 4, :], [(Wall.id, u0 + q) for q in range(4)]), op=ALU.subtract, first_sb=True)

        def chain_step(h, d, c):
            cs_ = slice(c * 128, (c + 1) * 128)
            u = d * NCH + c
            col = lambda nm: V(T_[nm].t[:, c, d, h:h + 1], T_[nm][:].keys)
            beta_ = V(k.beta.t[:, c, d * 8 + h:d * 8 + h + 1], k.beta[:].keys)
            Wb = V(Wall.t[:, u, :], Wall[:, u].keys); QKm = V(QKall.t[:, u, :], QKall[:, u].keys)
            kend = tb()[:, :]
            k.act(kend, V(khtok.t[:, c, :], khtok[:, c].keys), AF.Copy, scale=col("dte"))
            bK = nbank(); k.mm(bK[:, 0:128], kT[:, cs_], Sb[d][:, :])
            bS = nbank(); k.mm(bS[:, 0:128], qT[:, cs_], Sb[d][:, :])
            R0 = tb()[:, :]
            k.stt(R0, bK[:, 0:128], col("necs"), V(vtok.t[:, c, :], vtok[:, c].keys), ALU.mult, ALU.add)
            t_ = tf()[:, :]
            k.act(t_, bS[:, 0:128], AF.Copy, scale=col("ecs"))
            bV = nbank(); k.mm(bV[:, 0:128], Wb, R0)
            vnew = tb()[:, :]
            k.act(vnew, bV[:, 0:128], AF.Copy, scale=beta_)
            bU = nbank(); k.mm(bU[:, 0:128], kend, vnew)
            bO = nbank(); k.mm(bO[:, 0:128], QKm, vnew)
            k.stt(Sm[d][:, :], Sm[d][:, :], col("etot"), bU[:, 0:128], ALU.mult, ALU.add)
            last = (c % 2 == 1) if d == 0 else (c % 2 == 0)
            if last:
                k.dma("sp", V(sout.t[c // 2, d, h], sout[:].keys), Sm[d][:, :], is_output=True)
            cn = c + 1 if d == 0 else c - 1
            if 0 <= cn < NCH:
                fcol = (16 if d == 0 else 32) + cn
                k.ts(Sm[d][:, :], Sm[d][:, :], k.flags[:, fcol:fcol + 1], ALU.mult)
                k.copy(Sb[d][:, :], Sm[d][:, :], eng="act")
            oa = V(oacc.t[:, c, :], oacc[:, c].keys)
            first = (d == 0 and c < 8) or (d == 1 and c >= 8)
            if first:
                k.tt(oa, bO[:, 0:128], t_, ALU.add)
            else:
                k.tt(t_, bO[:, 0:128], t_, ALU.add)
                k.tt(oa, oa, t_, ALU.add, eng="pool")

        k.marks = getattr(k, "marks", [])
        mk_ = lambda lab: k.marks.append((lab, k.cnt["e_pe"]))
        for h in range(8):
            mk_("dn%d:proj" % h)
            sc1 = k.scope(); sc1.__enter__()
            k.pc_alloc()
            sqb = [k.sbuf("d_sq%d" % i, [128, 512], BF16) for i in range(2)]
            rr = [k.sbuf("d_rr%d" % i, [128, 512], F32) for i in range(2)]
            pw = [k.pc_prefetch(win, winv, coff + h * 128) for coff in (3104, 4128, 5152)]
            for i_, (buf, coff, cc) in enumerate(((qT, 3104, h), (kT, 4128, 8 + h), (vT, 5152, 16 + h))):
                k.proj_conv5(win, winv, coff + h * 128, lambda tap, cc=cc: cw[:, cc, tap:tap + 1], None,
                             lambda seg, buf=buf: buf[:, seg * 256:(seg + 1) * 256], w=pw[i_])
            mk_("dn%d:l2" % h)
            for (buf, isq) in ((qT, True), (kT, False)):
                for tt in range(NTT):
                    sl = slice(tt * 512, (tt + 1) * 512)
                    s2 = sqb[tt % 2]; r_ = rr[tt % 2]
                    k.act(s2[:, :], buf[:, sl], AF.Square)
                    b = nbank()
                    k.mm(b[:, :], k.ones_b[:, :], s2[:, :])
                    k.act(r_[:, :], b[:, :], AF.Ln, bias=k.eps_t[:, 0:1])
                    k.act(r_[:, :], r_[:, :], AF.Exp, scale=-0.5, bias=(lnsc[:, 0:1] if isq else zero1[:, 0:1]))
                    k.tt(buf[:, sl], buf[:, sl], r_[:, :], ALU.mult)
            w = wg[h % 2]
            k.dma("pool", w[:, :, :], V(winv[:, :, 6176 + h * 128:6176 + (h + 1) * 128], win[:].keys))
            for c in range(NCH):
                cs_ = slice(c * 128, (c + 1) * 128)
                k.transpose(pbf[:, 0:128], kT[:, cs_], k.ident_b[:, :])
                k.transpose(pbf[:, 128:256], vT[:, cs_], k.ident_b[:, :])
                k.copy(V(khtok.t[:, c, :], khtok[:, c].keys), pbf[:, 0:128], eng="dve")
                k.copy(V(vtok.t[:, c, :], vtok[:, c].keys), pbf[:, 128:256], eng="dve")
            sc1.__exit__(None, None, None)
            mk_("dn%d:P" % h)
            sc2 = k.scope(); sc2.__enter__()
            scr = [[k.sbuf("d_s%d_%d" % (g_, i), [128, 4, 128], IDT) for i in range(10)] for g_ in range(G)]
            quads = [(d, c0) for d in range(2) for c0 in range(0, NCH, 4)]
            for g0 in range(0, len(quads), G):
                gens = [quad_gen(h, d, c0, scr[i], scr[i][2 + 6]) for i, (d, c0) in enumerate(quads[g0:g0 + G])]
                while gens:
                    for g_ in list(gens):
                        try:
                            next(g_)
                        except StopIteration:
                            gens.remove(g_)
            sc2.__exit__(None, None, None)
            mk_("dn%d:C" % h)
            for d in range(2):
                k.dma("sp", Sm[d][:, :], V(s0d.t[d, h], s0d[:].keys))
                k.copy(Sb[d][:, :], Sm[d][:, :], eng="act")
            for step in range(NCH):
                chain_step(h, 0, step)
                chain_step(h, 1, NCH - 1 - step)
            mk_("dn%d:fin" % h)
            for c in range(NCH):
                oa = V(oacc.t[:, c, :], oacc[:, c].keys)
                junk = tf()[:, :]
                k.act(junk, oa, AF.Square, accum_out=fin16[:, c:c + 1])
            k.act(fin16b[:, :], fin16[:, :], AF.Sqrt, bias=k.eps_t[:, 0:1], scale=1.0 / 128.0)
            k.recip(fin16[:, :], fin16b[:, :])
            for c in range(NCH):
                cs_ = slice(c * 128, (c + 1) * 128)
                b = nbank()
                for kc in range(8):
                    k.mm(b[:, 0:128], k.hT[kc][:, PAD + c * 128:PAD + (c + 1) * 128], w[:, kc, :], start=(kc == 0), stop=(kc == 7))
                sg = tb()[:, :]
                k.act(sg, b[:, 0:128], AF.Silu)
                oa = V(oacc.t[:, c, :], oacc[:, c].keys)
                o1 = tf()[:, :]
                k.stt(o1, oa, fin16[:, c:c + 1], nwb[:, :], ALU.mult, ALU.mult)
                ob = tb()[:, :]
                k.tt(ob, o1, sg, ALU.mult)
                k.transpose(pbf[:, 0:128], ob, k.ident_b[:, :])
                k.copy(V(k.mixedT.t[:, 8 + h, cs_], k.mixedT[:, 8 + h].keys), pbf[:, 0:128], eng="dve")


def core_inputs(inputs, core):
    f32 = np.float32
    d = {}
    flags = np.zeros(128, f32)
    rope = np.zeros((2, 128, T), f32)
    if core < 4:
        rope[0] = 1.0
        d["ctx_k"] = np.zeros((512, 256), f32)
        d["ctx_v"] = np.zeros((512, 256), f32)
        flags[112] = -30000.0
        for c in range(16):
            flags[48 + c] = 0.0; flags[64 + c] = 1.0 if c % 2 == 1 else 0.0
            flags[80 + c] = 0.0; flags[96 + c] = 1.0 if c % 2 == 0 else 0.0
        d["x"] = np.ascontiguousarray(inputs["x_prompt"][core * 8:(core + 1) * 8].reshape(T, D))
        d["cond"] = np.ascontiguousarray(inputs["c_ctx"])
        for c in range(16):
            flags[16 + c] = 0.0 if c % 2 == 0 else 1.0
            flags[32 + c] = 0.0 if c % 2 == 1 else 1.0
    else:
        b = (core - 4) % 2
        d["ctx_k"] = np.ascontiguousarray(inputs["cache_l1_k"][b].reshape(512, 256))
        d["ctx_v"] = np.ascontiguousarray(inputs["cache_l1_v"][b].reshape(512, 256))
        pos = np.arange(T)
        inv = (10000.0 ** (-np.arange(32, dtype=np.float64) / 32.0))
        ang = np.concatenate([(pos // 64)[:, None] * inv[None, :], (pos % 64)[:, None] * inv[None, :]], axis=1)
        ang = ang.astype(f32)
        rope[0] = np.concatenate([np.cos(ang), np.cos(ang)], axis=1).T
        rope[1] = np.concatenate([np.sin(ang), np.sin(ang)], axis=1).T
        for c in range(16):
            flags[48 + c] = 1.0; flags[80 + c] = 1.0
        d["x"] = np.ascontiguousarray(inputs["x_sample"][b])
        d["cond"] = np.ascontiguousarray(inputs["c"][b])
        for s in range(8):
            flags[2 * s] = 0.0 if s == 0 else 1.0
            flags[2 * s + 1] = 0.0 if s == 7 else 1.0
        for c in range(16):
            flags[16 + c] = 0.0 if c == 0 else 1.0
            flags[32 + c] = 0.0 if c == 15 else 1.0
    d["flags"] = flags
    d["rope"] = rope
    d["cst_ident"] = np.eye(128, dtype=f32)
    sq = np.arange(128)
    band = np.zeros((2, 128, 512), f32)
    band[0] = np.tile((sq[:, None] >= sq[None, :]).astype(f32), (1, 4))
    band[1] = np.tile((sq[:, None] <= sq[None, :]).astype(f32), (1, 4))
    d["cst_band"] = band
    rot = np.zeros((128, 128), f32)
    for dp in range(64):
        rot[dp + 64, dp] = -1.0
        rot[dp, dp + 64] = 1.0
    d["cst_rot"] = rot
    tri = np.zeros((2, 128, 128), f32)
    tri[0] = (sq[:, None] <= sq[None, :]).astype(f32)
    tri[1] = (sq[:, None] >= sq[None, :]).astype(f32)
    d["cst_tri"] = tri
    mbias = np.zeros((2, 128, 128), f32)
    mbias[0] = np.where(sq[None, :] >= sq[:, None], 0.0, -30000.0)
    mbias[1] = np.where(sq[None, :] <= sq[:, None], 0.0, -30000.0)
    d["cst_mbias"] = mbias
    strict = np.zeros((2, 128, 128), f32)
    strict[0] = (sq[None, :] > sq[:, None]).astype(f32)
    strict[1] = (sq[None, :] < sq[:, None]).astype(f32)
    d["cst_strict"] = strict
    blk = np.zeros((4, 128, 128), f32)
    bd = lambda b: ((sq[:, None] // b) == (sq[None, :] // b)).astype(f32)
    blk[0] = bd(16); blk[1] = bd(32) - bd(16); blk[2] = bd(64) - bd(32); blk[3] = 1.0 - bd(64)
    d["cst_blk"] = blk
    if core < 4:
        d["ssd_h0"] = np.zeros((2, 16, 64, 128), f32)
        d["dn_h0"] = np.zeros((2, 8, 128, 128), f32)
    else:
        d["ssd_h0"] = np.ascontiguousarray(inputs["state_l0_ssd"][(core - 4) % 2])
        d["dn_h0"] = np.ascontiguousarray(inputs["state_l0_dn"][(core - 4) % 2])
    return d


def build(stages):
    k = Net()
    k.load_consts()
    k.load_x()
    for st in stages:
        if st[0] == "ffn":
            l = st[1]
            cols = k.modulation(l)
            k.ffn(l, cols)
        elif st[0] == "mix0":
            cols = k.modulation(0)
            k.mixer_l0(cols)
        elif st[0] == "mix1":
            cols = k.modulation(1)
            k.mixer_l1(cols)
        elif st[0] == "mod":
            cols = k.modulation(st[1])
        elif st[0] == "pre":
            cols = k.modulation(st[1])
            k.prenorm(cols, 3, 4)
    k.store_y()
    k.finish()
    return k


def run(inputs, stages, n_cores=8):
    from concourse.bass_utils import run_bass_kernel_spmd
    k = build(stages)
    in_maps = []
    for c in range(n_cores):
        d = core_inputs(inputs, c)
        m = {}
        for name in k.inp:
            if name in d:
                m[name] = d[name]
            else:
                m[name] = np.ascontiguousarray(np.asarray(inputs[name], dtype=np.float32))
        in_maps.append(m)
    res = run_bass_kernel_spmd(k.nc, in_maps, core_ids=list(range(n_cores)))
    return k, res


FULL = [("full",)]


def build_full():
    k = Net()
    k.load_consts()
    k.load_x()
    import os
    nsub = int(os.environ.get("FULL_N", 4))
    cols0 = k.modulation(0)
    k.mixer_l0(cols0)
    if nsub >= 2:
        k.ffn(0, cols0)
    if nsub >= 3:
        cols1 = k.modulation(1)
        k.mixer_l1(cols1)
    if nsub >= 4:
        k.ffn(1, cols1)
    k.store_y()
    k.finish()
    return k


def kernel(**inputs):
    from concourse.bass_utils import run_bass_kernel_spmd
    inputs = {n: np.asarray(v) for n, v in inputs.items()}
    used = ("x_prompt", "x_sample", "state_l0_ssd", "state_l0_dn", "cache_l1_k", "cache_l1_v", "c", "c_ctx",
            "mod_w_l0", "mod_b_l0", "norm_mix_pre_l0", "norm_mix_post_l0", "norm_ffn_pre_l0", "norm_ffn_post_l0",
            "ffn_up_l0", "ffn_conv_w_l0", "ffn_conv_b_l0", "ffn_down_l0",
            "mod_w_l1", "mod_b_l1", "norm_mix_pre_l1", "norm_mix_post_l1", "norm_ffn_pre_l1", "norm_ffn_post_l1",
            "ffn_up_l1", "ffn_conv_w_l1", "ffn_conv_b_l1", "ffn_down_l1",
            "mix_in_l0", "mix_out_l0", "ssd_conv_w", "ssd_conv_b", "ssd_dt_bias", "ssd_A_log", "ssd_D", "ssd_norm_w",
            "dn_conv_w", "dn_dt_bias", "dn_A_log", "dn_norm_w",
            "mix_in_l1", "mix_out_l1", "sg_ln_w", "sg_ln_b", "sg_w_s", "sg_b_s", "attn_sink")
    assert all(n in inputs for n in used)
    k = build_full()
    in_maps = []
    for c in range(8):
        d = core_inputs(inputs, c)
        m = {}
        for name in k.inp:
            m[name] = d[name] if name in d else np.ascontiguousarray(np.asarray(inputs[name], dtype=np.float32))
        in_maps.append(m)
    res = run_bass_kernel_spmd(k.nc, in_maps, core_ids=list(range(8)))
    r = res.results
    f32 = np.float32
    y_prompt = np.concatenate([r[c]["y"].reshape(8, 256, D) for c in range(4)], axis=0).astype(f32)
    y_sample = np.stack([r[4]["y"], r[5]["y"]], axis=0).astype(f32)
    new_ssd = np.concatenate([r[c]["ssd_out"] for c in range(4)], axis=0).astype(f32)
    new_dn = np.concatenate([r[c]["dn_out"] for c in range(4)], axis=0).astype(f32)
    new_k = np.concatenate([r[c]["nk"].reshape(8, 256, 2, 128) for c in range(4)], axis=0).astype(f32)
    new_v = np.concatenate([r[c]["nv"].reshape(8, 256, 2, 128) for c in range(4)], axis=0).astype(f32)
    return (y_prompt, y_sample, new_ssd, new_dn, new_k, new_v)
```
